# Optimizing a Trainium2 kernel written in Bass

```python
import jax, jax.numpy as jnp
from jax import lax
import numpy as np

D_MODEL = 1024
BATCH = 4
SEQ = 8192
DEPTH = 2

D_MIX = D_MODEL
LRU_WIDTH = D_MIX // 2
LRU_BLOCKS = 8
LRU_BLOCK = LRU_WIDTH // LRU_BLOCKS
LRU_CONV = 4
LRU_C = 8.0
RET_HEADS = 4
RET_WIDTH = D_MIX // 4
RET_HEAD_DIM = RET_WIDTH // RET_HEADS
RET_CHUNK = 128
ROPE_BASE = 10000.0
HG_HEADS = 4
HG_WIDTH = D_MIX // 4
HG_HEAD_DIM = HG_WIDTH // HG_HEADS
HG_CHUNK = 64
IN_SPLITS = [LRU_WIDTH, LRU_WIDTH] + [RET_WIDTH] * 4 + [HG_WIDTH] * 4
IN_COLS = sum(IN_SPLITS)
D_FF = ((8 * D_MODEL // 3 + 255) // 256) * 256
FFN_CONV = 3
NORM_EPS = 1e-6

kernel_name = "hymba_style_rglru_retention_hgrn2_convffn"


def rmsnorm(x, w):
    xf = x.astype(jnp.float32)
    y = xf * lax.rsqrt(jnp.mean(xf * xf, axis=-1, keepdims=True) + NORM_EPS)
    return (y * w.astype(jnp.float32)).astype(x.dtype)


def head_rmsnorm(o, w):
    B, S, H, d = o.shape
    y = o * lax.rsqrt(jnp.mean(o * o, axis=-1, keepdims=True) + NORM_EPS)
    return y.reshape(B, S, H * d) * w.astype(jnp.float32)


def causal_dwconv(x, w, b):
    K = w.shape[0]
    S = x.shape[1]
    xp = jnp.pad(x, ((0, 0), (K - 1, 0), (0, 0)))
    y = b + xp[:, 0:S] * w[0]
    for k in range(1, K):
        y = y + xp[:, k:k + S] * w[k]
    return y


def rotary(x):
    S, d = x.shape[1], x.shape[-1]
    inv = ROPE_BASE ** (-jnp.arange(0, d, 2, dtype=jnp.float32) / d)
    ang = jnp.arange(S, dtype=jnp.float32)[:, None] * inv[None, :]
    cos = jnp.cos(ang)[None, :, None, :]
    sin = jnp.sin(ang)[None, :, None, :]
    x1, x2 = x[..., : d // 2], x[..., d // 2:]
    return jnp.concatenate([x1 * cos - x2 * sin, x2 * cos + x1 * sin], axis=-1)


def to_chunks(x, C):
    B, S, H, d = x.shape
    return x.reshape(B, S // C, C, H, d).transpose(1, 0, 3, 2, 4)


def from_chunks(x):
    NC, B, H, C, d = x.shape
    return x.transpose(1, 0, 3, 2, 4).reshape(B, NC * C, H, d)


def rglru_group(xr, gr, conv_w, conv_b, wa, ba, wx, bx, lam):
    dt = xr.dtype
    xc = causal_dwconv(xr, conv_w, conv_b).astype(jnp.float32)
    B, S, _ = xc.shape
    xb = xc.reshape(B, S, LRU_BLOCKS, LRU_BLOCK)
    r = jax.nn.sigmoid(jnp.einsum('bshi,hij->bshj', xb, wa.astype(jnp.float32)).reshape(B, S, LRU_WIDTH) + ba.astype(jnp.float32))
    i = jax.nn.sigmoid(jnp.einsum('bshi,hij->bshj', xb, wx.astype(jnp.float32)).reshape(B, S, LRU_WIDTH) + bx.astype(jnp.float32))
    log_a = -LRU_C * r * jax.nn.softplus(-lam.astype(jnp.float32))
    a = jnp.exp(log_a)
    u = jnp.sqrt(-jnp.expm1(2.0 * log_a)) * (i * xc)

    def combine(c1, c2):
        a1, b1 = c1
        a2, b2 = c2
        return a1 * a2, a2 * b1 + b2

    _, h = lax.associative_scan(combine, (a, u), axis=1)
    return (h * jax.nn.gelu(gr.astype(jnp.float32))).astype(dt)


def retention_group(q, k, v, g, norm_w):
    dt = q.dtype
    B, S, _ = q.shape
    H, dh, C = RET_HEADS, RET_HEAD_DIM, RET_CHUNK
    q = rotary(q.astype(jnp.float32).reshape(B, S, H, dh))
    k = rotary(k.astype(jnp.float32).reshape(B, S, H, dh)) * (dh ** -0.5)
    v = v.astype(jnp.float32).reshape(B, S, H, dh)
    log_gamma = jnp.log1p(-jnp.exp2(-5.0 - jnp.arange(H, dtype=jnp.float32)))
    idx = jnp.arange(C, dtype=jnp.float32)
    rel = idx[:, None] - idx[None, :]
    intra = jnp.where(rel >= 0, jnp.exp(jnp.maximum(rel, 0.0) * log_gamma[:, None, None]), 0.0)
    q_dec = jnp.exp((idx + 1.0)[None, :] * log_gamma[:, None])[..., None]
    k_dec = jnp.exp((C - 1.0 - idx)[None, :] * log_gamma[:, None])[..., None]
    chunk_dec = jnp.exp(C * log_gamma)[:, None, None]

    def step(state, inp):
        qc, kc, vc = inp
        scores = jnp.einsum('bhnd,bhmd->bhnm', qc, kc) * intra
        o = jnp.einsum('bhnm,bhme->bhne', scores, vc) + jnp.einsum('bhnd,bhde->bhne', qc, state) * q_dec
        state = state * chunk_dec + jnp.einsum('bhmd,bhme->bhde', kc * k_dec, vc)
        return state, o

    state0 = jnp.zeros((B, H, dh, dh), jnp.float32)
    _, o = lax.scan(step, state0, (to_chunks(q, C), to_chunks(k, C), to_chunks(v, C)))
    y = head_rmsnorm(from_chunks(o), norm_w)
    return (y * jax.nn.silu(g.astype(jnp.float32))).astype(dt)


def hgrn2_group(q, fpre, i, g, lb, norm_w):
    dt = q.dtype
    B, S, _ = q.shape
    H, dh, C = HG_HEADS, HG_HEAD_DIM, HG_CHUNK
    q = jax.nn.silu(q.astype(jnp.float32)).reshape(B, S, H, dh)
    fp = fpre.astype(jnp.float32).reshape(B, S, H, dh)
    lb = lb.reshape(H, dh)
    log_f = jnp.logaddexp(jnp.log(lb), jnp.log1p(-lb) + jax.nn.log_sigmoid(fp))
    k = (1.0 - lb) * jax.nn.sigmoid(-fp)
    v = i.astype(jnp.float32).reshape(B, S, H, dh)
    mask = jnp.tril(jnp.ones((C, C), dtype=bool))[None, None, :, :, None]

    def step(state, inp):
        qc, kc, gc, vc = inp
        b = jnp.cumsum(gc, axis=2)
        diff = b[:, :, :, None, :] - b[:, :, None, :, :]
        decay = jnp.exp(jnp.where(mask, diff, -jnp.inf))
        scores = jnp.einsum('bhnd,bhmd,bhnmd->bhnm', qc, kc, decay)
        o = jnp.einsum('bhnm,bhme->bhne', scores, vc) + jnp.einsum('bhnd,bhde->bhne', qc * jnp.exp(b), state)
        b_last = b[:, :, -1:, :]
        state = state * jnp.exp(b_last[:, :, 0, :, None]) + jnp.einsum('bhmd,bhme->bhde', kc * jnp.exp(b_last - b), vc)
        return state, o

    state0 = jnp.zeros((B, H, dh, dh), jnp.float32)
    _, o = lax.scan(step, state0, (to_chunks(q, C), to_chunks(k, C), to_chunks(log_f, C), to_chunks(v, C)))
    y = head_rmsnorm(from_chunks(o), norm_w)
    return (y * jax.nn.silu(g.astype(jnp.float32))).astype(dt)


def conv_ffn(h, w_up, conv_w, conv_b, w_down):
    u = causal_dwconv(h @ w_up, conv_w, conv_b)
    gate, val = jnp.split(u, 2, axis=-1)
    return (jax.nn.silu(gate) * val) @ w_down


def setup_inputs(seed: int = 0) -> dict:
    key = jax.random.key(seed)
    ks = jax.random.split(key, 24)
    f32 = jnp.float32
    nrm = lambda k, shape, s: jax.random.normal(k, shape, f32) * s
    u = jax.random.uniform(ks[9], (DEPTH, LRU_WIDTH), f32, 0.9, 0.999)
    a0 = u ** (1.0 / LRU_C)
    lam = jnp.log(a0) - jnp.log1p(-a0)
    return {
        "x": nrm(ks[0], (BATCH, SEQ, D_MODEL), 1.0),
        "norm1_w": 1.0 + nrm(ks[1], (DEPTH, D_MODEL), 0.02),
        "w_in": nrm(ks[2], (DEPTH, D_MODEL, IN_COLS), D_MODEL ** -0.5),
        "lru_conv_w": nrm(ks[3], (DEPTH, LRU_CONV, LRU_WIDTH), LRU_CONV ** -0.5),
        "lru_conv_b": nrm(ks[4], (DEPTH, LRU_WIDTH), 0.01),
        "lru_wa": nrm(ks[5], (DEPTH, LRU_BLOCKS, LRU_BLOCK, LRU_BLOCK), LRU_BLOCK ** -0.5),
        "lru_ba": nrm(ks[6], (DEPTH, LRU_WIDTH), 0.01),
        "lru_wx": nrm(ks[7], (DEPTH, LRU_BLOCKS, LRU_BLOCK, LRU_BLOCK), LRU_BLOCK ** -0.5),
        "lru_bx": nrm(ks[8], (DEPTH, LRU_WIDTH), 0.01),
        "lru_lambda": lam,
        "ret_norm_w": 1.0 + nrm(ks[10], (DEPTH, RET_WIDTH), 0.02),
        "hg_lower_bounds": nrm(ks[11], (DEPTH, HG_WIDTH), 0.5),
        "hg_norm_w": 1.0 + nrm(ks[12], (DEPTH, HG_WIDTH), 0.02),
        "w_out": nrm(ks[13], (DEPTH, D_MIX, D_MODEL), D_MIX ** -0.5),
        "norm2_w": 1.0 + nrm(ks[14], (DEPTH, D_MODEL), 0.02),
        "ffn_w_up": nrm(ks[15], (DEPTH, D_MODEL, 2 * D_FF), D_MODEL ** -0.5),
        "ffn_conv_w": nrm(ks[16], (DEPTH, FFN_CONV, 2 * D_FF), FFN_CONV ** -0.5),
        "ffn_conv_b": nrm(ks[17], (DEPTH, 2 * D_FF), 0.01),
        "ffn_w_down": nrm(ks[18], (DEPTH, D_FF, D_MODEL), D_FF ** -0.5),
        "final_norm_w": 1.0 + nrm(ks[19], (D_MODEL,), 0.02),
    }


def reference(x, norm1_w, w_in, lru_conv_w, lru_conv_b, lru_wa, lru_ba, lru_wx, lru_bx,
              lru_lambda, ret_norm_w, hg_lower_bounds, hg_norm_w, w_out, norm2_w,
              ffn_w_up, ffn_conv_w, ffn_conv_b, ffn_w_down, final_norm_w):
    lb_all = jnp.cumsum(jax.nn.softmax(hg_lower_bounds.astype(jnp.float32), axis=0), axis=0)
    lb_all = lb_all - lb_all[0:1]
    split_idx = list(np.cumsum(IN_SPLITS)[:-1])
    for l in range(DEPTH):
        h = rmsnorm(x, norm1_w[l])
        proj = h @ w_in[l]
        lx, lg, rq, rk, rv, rg, hq, hf, hi, hg = jnp.split(proj, split_idx, axis=-1)
        y_lru = rglru_group(lx, lg, lru_conv_w[l], lru_conv_b[l], lru_wa[l], lru_ba[l],
                            lru_wx[l], lru_bx[l], lru_lambda[l])
        y_ret = retention_group(rq, rk, rv, rg, ret_norm_w[l])
        y_hg = hgrn2_group(hq, hf, hi, hg, lb_all[l], hg_norm_w[l])
        x = x + jnp.concatenate([y_lru, y_ret, y_hg], axis=-1) @ w_out[l]
        h = rmsnorm(x, norm2_w[l])
        x = x + conv_ffn(h, ffn_w_up[l], ffn_conv_w[l], ffn_conv_b[l], ffn_w_down[l])
    return rmsnorm(x, final_norm_w)
```

```python
import contextlib
import math
import numpy as np
import concourse.bass as bass
import concourse.mybir as mybir
from concourse.bass_utils import run_bass_kernel_spmd

F32 = mybir.dt.float32
BF16 = mybir.dt.bfloat16
AF = mybir.ActivationFunctionType
ALU = mybir.AluOpType

EPOCH = 4000

D = 1024
SEQ = 8192
BATCH = 4
DEPTH = 2
T = 512
DFF = 2816
NFC = 44
EPS = 1e-6
NCOL_IN = 3584


class Buf:
    def __init__(self, t, name):
        self.t = t
        self.name = name
        self.last_write = None
        self.reads = []
        self.dma_cnt = 0

    def __getitem__(self, key):
        return self.t[key]

    def sub(self, name):
        return Buf(self.t, name)


class Sched:
    ENG = ("pe", "act", "dve", "pool", "sp")

    def __init__(self, nc):
        self.nc = nc
        self.stack = contextlib.ExitStack()
        self.ops = {e: [] for e in self.ENG}
        self.cnt = {e: 0 for e in self.ENG}
        self.epoch = {e: 0 for e in self.ENG}
        self.sems = {}
        self.waited = {e: {} for e in self.ENG}
        self.nsem = 0
        self.final_waits = []

    def sem(self, key):
        if key not in self.sems:
            self.sems[key] = self.stack.enter_context(self.nc.semaphore("s%d" % self.nsem))
            self.nsem += 1
        return self.sems[key]

    def sb(self, name, shape, dtype=F32):
        t = self.stack.enter_context(self.nc.sbuf_tensor(name, list(shape), dtype))
        return Buf(t, name)

    def ps(self, name, shape, dtype=F32):
        t = self.stack.enter_context(self.nc.psum_tensor(name, list(shape), dtype))
        return Buf(t, name)

    def _deps(self, r, w):
        deps = []
        for b in r:
            if b.last_write is not None:
                deps.append(b.last_write)
        for b in w:
            if b.last_write is not None:
                deps.append(b.last_write)
            deps.extend(b.reads)
        return deps

    def _waits(self, eng, deps):
        need = {}
        wd = self.waited[eng]
        for (k, v) in deps:
            if wd.get(k, 0) >= v:
                continue
            if need.get(k, 0) < v:
                need[k] = v
        for k, v in need.items():
            wd[k] = v
        return [(self.sem(k), v) for k, v in need.items()]

    def _commit(self, r, w, tok):
        for b in w:
            b.last_write = tok
            b.reads = []
        for b in r:
            if b in w:
                continue
            b.reads = [x for x in b.reads if x[0] != tok[0]] + [tok]

    def op(self, eng, fn, r=(), w=()):
        deps = self._deps(r, w)
        if eng == "pe":
            deps = [d for d in deps if not (d[0][0] == "e" and d[0][1] == "pe")]
        waits = self._waits(eng, deps)
        if self.cnt[eng] >= EPOCH:
            self.epoch[eng] += 1
            self.cnt[eng] = 0
        self.cnt[eng] += 1
        key = ("e", eng, self.epoch[eng])
        tok = (key, self.cnt[eng])
        self.ops[eng].append((fn, waits, self.sem(key), 1))
        self._commit(r, w, tok)
        return tok

    def dma(self, q, out_ap, in_ap, r=(), w=(), sem_buf=None):
        sb_ = sem_buf if sem_buf is not None else (w[0] if len(w) else r[0])
        deps = self._deps(r, w)
        waits = self._waits(q, deps)
        key = ("d", id(sb_))
        sb_.dma_cnt += 16
        tok = (key, sb_.dma_cnt)
        s = self.sem(key)

        def fn(e, out_ap=out_ap, in_ap=in_ap):
            return e.dma_start(out=out_ap, in_=in_ap)

        self.ops[q].append((fn, waits, s, 16))
        self._commit(r, w, tok)
        return tok

    def wait_all_on(self, eng, toks):
        self.final_waits.append((eng, toks))

    def emit(self):
        nc = self.nc
        for eng, toks in self.final_waits:
            waits = self._waits(eng, toks)
            self.ops[eng].append((None, waits, None, 0))

        def replay(engobj, lst):
            for (fn, waits, s, inc) in lst:
                for (ws, v) in waits:
                    engobj.wait_ge(ws, v)
                if fn is not None:
                    fn(engobj).then_inc(s, inc)

        with nc.Block() as block:
            @block.tensor
            def _(e):
                replay(e, self.ops["pe"])

            @block.scalar
            def _(e):
                replay(e, self.ops["act"])

            @block.vector
            def _(e):
                replay(e, self.ops["dve"])

            @block.gpsimd
            def _(e):
                replay(e, self.ops["pool"])

            @block.sync
            def _(e):
                replay(e, self.ops["sp"])

    def close(self):
        self.stack.close()


def _pvec_layout():
    lay = {}
    col = 0

    def add(name, n):
        nonlocal col
        lay[name] = col
        col += n

    for l in range(DEPTH):
        add("n1w%d" % l, 8)
        add("n2w%d" % l, 8)
        for k in range(4):
            add("lcw%d_%d" % (l, k), 4)
        add("lcb%d" % l, 4)
        add("ba%d" % l, 4)
        add("bx%d" % l, 4)
        add("lam%d" % l, 4)
        add("rnw%d" % l, 2)
        add("hnw%d" % l, 2)
        add("hb%d" % l, 2)
        for k in range(3):
            add("fcw%d_%d" % (l, k), NFC)
        add("fcb%d" % l, NFC)
    add("fnw", 8)
    add("eps", 1)
    add("one", 1)
    lay["_n"] = col
    return lay


PV = _pvec_layout()
RET_GAMMA = [1.0 - 2.0 ** (-5.0 - h) for h in range(4)]


def build(NT, nlayers=DEPTH, debug_out=None):
    S = NT * T
    nc = bass.Bass("TRN2", target_bir_lowering=False)
    xT_d = nc.dram_tensor("xT", [D, S], F32, kind="ExternalInput").ap()
    win_d = nc.dram_tensor("win", [DEPTH, 128, 8, NCOL_IN], F32, kind="ExternalInput").ap()
    wout_d = nc.dram_tensor("wout", [DEPTH, 128, 8, D], F32, kind="ExternalInput").ap()
    wup_d = nc.dram_tensor("wup", [DEPTH, 128, 8, 2 * DFF], F32, kind="ExternalInput").ap()
    wdn_d = nc.dram_tensor("wdn", [DEPTH, 8, 128, 22, 128], F32, kind="ExternalInput").ap()
    gw_d = nc.dram_tensor("gatew", [DEPTH, 128, 8, 128], F32, kind="ExternalInput").ap()
    pv_d = nc.dram_tensor("pvec", [128, PV["_n"]], F32, kind="ExternalInput").ap()
    tab_d = nc.dram_tensor("tabs", [128, 8, S], F32, kind="ExternalInput").ap()
    cst_d = nc.dram_tensor("cst", [128, 5, 128], F32, kind="ExternalInput").ap()
    rm_d = nc.dram_tensor("rmask", [128, T], F32, kind="ExternalInput").ap()
    yT_d = nc.dram_tensor("yT", [D, S], F32, kind="ExternalOutput").ap()

    sch = Sched(nc)
    sb, ps = sch.sb, sch.ps

    pv = sb("pv", [128, PV["_n"]])
    sch.dma("sp", pv[:, :], pv_d[:, :], w=[pv])
    cstf = sb("cstf", [128, 5, 128])
    sch.dma("sp", cstf[:, :, :], cst_d[:, :, :], w=[cstf])
    cst = sb("cstb", [128, 5, 128], BF16)
    sch.op("dve", lambda e: e.tensor_copy(out=cst[:, :, :], in_=cstf[:, :, :]), r=[cstf], w=[cst])
    ONES, IDENT, BD64, MASK128, MASK32 = range(5)
    rmask = sb("rmask_s", [128, T])
    sch.dma("sp", rmask[:, :], rm_d[:, :], w=[rmask])
    gwf = sb("gwf", [128, DEPTH * 8, 128])
    gwb = sb("gwb", [128, DEPTH * 8, 128], BF16)
    for l in range(DEPTH):
        sch.dma("sp", gwf[:, l * 8:(l + 1) * 8, :], gw_d[l], w=[gwf])
    sch.op("dve", lambda e: e.tensor_copy(out=gwb[:, :, :], in_=gwf[:, :, :]), r=[gwf], w=[gwb])

    def pcol(name, c=0, lo=0, hi=128):
        k = PV[name] + c
        return pv[lo:hi, k:k + 1]

    der = sb("der", [128, 32])
    for l in range(DEPTH):
        k = PV["lam%d" % l]
        sch.op("act", lambda e, k=k, l=l: e.activation(out=der[:, l * 4:l * 4 + 4], in_=pv[:, k:k + 4], func=AF.Exp, scale=-1.0), r=[pv], w=[der])
        sch.op("act", lambda e, l=l: e.activation(out=der[:, l * 4:l * 4 + 4], in_=der[:, l * 4:l * 4 + 4], func=AF.Ln, bias=pcol("one"), scale=1.0), r=[der, pv], w=[der])
        sch.op("dve", lambda e, l=l: e.tensor_scalar(out=der[:, l * 4:l * 4 + 4], in0=der[:, l * 4:l * 4 + 4], scalar1=-8.0, scalar2=None, op0=ALU.mult), r=[der], w=[der])
    LB, OML, NOML = 8, 12, 16
    sch.op("dve", lambda e: e.memset(der[:, LB:LB + 2], 0.0), w=[der])
    sch.op("dve", lambda e: e.memset(der[:, OML:OML + 2], 1.0), w=[der])
    sch.op("dve", lambda e: e.memset(der[:, NOML:NOML + 2], -1.0), w=[der])
    k0, k1 = PV["hb0"], PV["hb1"]
    sch.op("dve", lambda e: e.tensor_tensor(out=der[:, 20:22], in0=pv[:, k1:k1 + 2], in1=pv[:, k0:k0 + 2], op=ALU.subtract), r=[pv, der], w=[der])
    sch.op("act", lambda e: e.activation(out=der[:, LB + 2:LB + 4], in_=der[:, 20:22], func=AF.Sigmoid), r=[der], w=[der])
    sch.op("act", lambda e: e.activation(out=der[:, OML + 2:OML + 4], in_=der[:, 20:22], func=AF.Sigmoid, scale=-1.0), r=[der], w=[der])
    sch.op("dve", lambda e: e.tensor_scalar(out=der[:, NOML + 2:NOML + 4], in0=der[:, OML + 2:OML + 4], scalar1=-1.0, scalar2=None, op0=ALU.mult), r=[der], w=[der])

    def dcol(base, l, c, lo=0, hi=128):
        k = base + l * (4 if base == 0 else 2) + c
        return der[lo:hi, k:k + 1]

    hst = [sb("hst%d" % l, [128, 4]) for l in range(DEPTH)]
    lxh = [sb("lxh%d" % l, [128, 4, 4]) for l in range(DEPTH)]
    fh = [sb("fh%d" % l, [128, NFC, 2]) for l in range(DEPTH)]
    st_f = {}
    st_b = {}
    for l in range(DEPTH):
        sch.op("dve", lambda e, l=l: e.memset(hst[l][:, :], 0.0), w=[hst[l]])
        sch.op("dve", lambda e, l=l: e.memset(lxh[l][:, :, :], 0.0), w=[lxh[l]])
        sch.op("dve", lambda e, l=l: e.memset(fh[l][:, :, :], 0.0), w=[fh[l]])
        for mix in ("r", "h"):
            for p in range(2):
                tf = sb("stf_%s%d%d" % (mix, l, p), [128, 64])
                tb = sb("stb_%s%d%d" % (mix, l, p), [128, 64], BF16)
                for hh in range(2):
                    bf_ = tf.sub("stf_%s%d%d%d" % (mix, l, p, hh))
                    bb_ = tb.sub("stb_%s%d%d%d" % (mix, l, p, hh))
                    st_f[(mix, l, p, hh)] = bf_
                    st_b[(mix, l, p, hh)] = bb_
                    lo = hh * 64
                    sch.op("dve", lambda e, bf_=bf_, lo=lo: e.memset(bf_[lo:lo + 64, :], 0.0), w=[bf_])
                    sch.op("dve", lambda e, bb_=bb_, lo=lo: e.memset(bb_[lo:lo + 64, :], 0.0), w=[bb_])

    xres = [sb("xres%d" % i, [128, 8, T]) for i in range(2)]
    tabs = sb("tabs_s", [128, 8, T])
    hT = sb("hT", [128, 8, T], BF16)
    yT = sb("yTs", [128, 8, T], BF16)
    gT = sb("gT", [128, 22, T], BF16)
    NSLOT = 4
    wsl = [sb("wsl%d" % i, [128, 4096], BF16) for i in range(NSLOT)]
    Fb = [sb("F%d" % i, [128, T + 4]) for i in range(10)]
    Bb = [sb("B%d" % i, [128, T], BF16) for i in range(6)]
    vtok = sb("vtok", [128, 8, 256], BF16)
    ktok = sb("ktok", [128, 8, 2, 128], BF16)
    ATb = [sb("AT%d" % i, [128, 128], BF16) for i in range(2)]
    ebl = [sb("ebl%d" % p, [128, 16]) for p in range(2)]
    stmp = [sb("stmp%d" % i, [128, 64]) for i in range(2)]

    pA = ps("pA", [128, T])
    pB = ps("pB", [128, T])
    pS = ps("pS", [128, T])
    pG = ps("pG", [128, T])
    pO = [ps("pO%d" % p, [128, T]) for p in range(2)]
    pXt = ps("pX", [128, T])
    pX = [pXt.sub("pX%d" % i) for i in range(4)]
    pUt = ps("pU", [128, T])
    pU = [pUt.sub("pU%d" % i) for i in range(4)]
    pTr = Buf(pUt[:, 256:512].bitcast(BF16), "pTr")

    wstate = {"n": 0}

    def wload(src_ap, shape):
        s = wsl[wstate["n"] % NSLOT]
        wstate["n"] += 1
        n = 1
        for d_ in shape[1:]:
            n *= d_
        if len(shape) == 3:
            view = s[:, 0:n].rearrange("p (a b) -> p a b", b=shape[2])
        else:
            view = s[:, 0:n]
        sch.dma("pool", view, src_ap, w=[s])
        return s, view

    def proj(pbuf, slot, wview, col, rhs_buf, rhs_of_kc, nk=8, m=128):
        for kc in range(nk):
            sch.op("pe", lambda e, kc=kc: e.matmul(pbuf[0:m, :], lhsT=wview[:, kc, col:col + m], rhs=rhs_of_kc(kc),
                                                     start=(kc == 0), stop=(kc == nk - 1)), r=[slot, rhs_buf], w=[pbuf])

    def rms_to_hT(xr, nwname):
        sch.op("act", lambda e: e.activation(out=gT[:, 0:8, :], in_=xr[:, :, :], func=AF.Square), r=[xr], w=[gT])
        for c in range(8):
            sch.op("pe", lambda e, c=c: e.matmul(pS[:, :], lhsT=cst[:, ONES, :], rhs=gT[:, c, :], start=(c == 0), stop=(c == 7)), r=[cst, gT], w=[pS])
        rs = Fb[9]
        sch.op("act", lambda e: e.activation(out=rs[:, 0:T], in_=pS[:, :], func=AF.Sqrt, bias=pcol("eps"), scale=1.0 / D), r=[pS, pv], w=[rs])
        sch.op("dve", lambda e: e.reciprocal(out=rs[:, 0:T], in_=rs[:, 0:T]), r=[rs], w=[rs])
        return rs

    def emit_layer(l, xr, ti):
        rs = rms_to_hT(xr, "n1w%d" % l)
        for c in range(8):
            sch.op("dve", lambda e, c=c: e.scalar_tensor_tensor(out=hT[:, c, :], in0=xr[:, c, :], scalar=pcol("n1w%d" % l, c), in1=rs[:, 0:T],
                                                                  op0=ALU.mult, op1=ALU.mult), r=[xr, rs, pv], w=[hT])
        hk = lambda kc: hT[:, kc, :]

        s0, w0 = wload(win_d[l, :, :, 0:512], [128, 8, 512])
        s1, w1 = wload(win_d[l, :, :, 512:1024], [128, 8, 512])
        for c in range(4):
            lxb, xc, rr, ii, aa, mm = Fb[0], Fb[1], Fb[2], Fb[3], Fb[4], Fb[5]
            xcb = Bb[0]
            proj(pA, s0, w0, c * 128, hT, hk)
            sch.op("dve", lambda e, c=c: e.tensor_copy(out=lxb[:, 0:3], in_=lxh[l][:, c, 0:3]), r=[lxh[l]], w=[lxb])
            sch.op("act", lambda e: e.activation(out=lxb[:, 3:3 + T], in_=pA[:, :], func=AF.Copy), r=[pA], w=[lxb])
            sch.op("dve", lambda e, c=c: e.tensor_copy(out=lxh[l][:, c, 0:3], in_=lxb[:, T:T + 3]), r=[lxb], w=[lxh[l]])
            sch.op("dve", lambda e, c=c: e.tensor_scalar(out=xc[:, 0:T], in0=lxb[:, 0:T], scalar1=pcol("lcw%d_0" % l, c), scalar2=pcol("lcb%d" % l, c),
                                                           op0=ALU.mult, op1=ALU.add), r=[lxb, pv], w=[xc])
            for k in range(1, 4):
                sch.op("dve", lambda e, c=c, k=k: e.scalar_tensor_tensor(out=xc[:, 0:T], in0=lxb[:, k:k + T], scalar=pcol("lcw%d_%d" % (l, k), c), in1=xc[:, 0:T],
                                                                           op0=ALU.mult, op1=ALU.add), r=[lxb, xc, pv], w=[xc])
            sch.op("act", lambda e: e.activation(out=xcb[:, :], in_=xc[:, 0:T], func=AF.Copy), r=[xc], w=[xcb])
            sch.op("pe", lambda e, c=c: e.matmul(pG[:, :], lhsT=gwb[:, l * 8 + c * 2, :], rhs=xcb[:, :], start=True, stop=True), r=[gwb, xcb], w=[pG])
            sch.op("pe", lambda e, c=c: e.matmul(pS[:, :], lhsT=gwb[:, l * 8 + c * 2 + 1, :], rhs=xcb[:, :], start=True, stop=True), r=[gwb, xcb], w=[pS])
            sch.op("act", lambda e, c=c: e.activation(out=rr[:, 0:T], in_=pG[:, :], func=AF.Sigmoid, bias=pcol("ba%d" % l, c)), r=[pG, pv], w=[rr])
            sch.op("act", lambda e, c=c: e.activation(out=ii[:, 0:T], in_=pS[:, :], func=AF.Sigmoid, bias=pcol("bx%d" % l, c)), r=[pS, pv], w=[ii])
            sch.op("act", lambda e, c=c: e.activation(out=aa[:, 0:T], in_=rr[:, 0:T], func=AF.Exp, scale=dcol(0, l, c)), r=[rr, der], w=[aa])
            sch.op("act", lambda e: e.activation(out=mm[:, 0:T], in_=aa[:, 0:T], func=AF.Square), r=[aa], w=[mm])
            sch.op("act", lambda e: e.activation(out=mm[:, 0:T], in_=mm[:, 0:T], func=AF.Sqrt, bias=pcol("one"), scale=-1.0), r=[mm, pv], w=[mm])
            sch.op("dve", lambda e: e.tensor_tensor(out=ii[:, 0:T], in0=ii[:, 0:T], in1=xc[:, 0:T], op=ALU.mult), r=[ii, xc], w=[ii])
            sch.op("dve", lambda e: e.tensor_tensor(out=ii[:, 0:T], in0=ii[:, 0:T], in1=mm[:, 0:T], op=ALU.mult), r=[ii, mm], w=[ii])
            sch.op("dve", lambda e, c=c: e.tensor_tensor_scan(out=rr[:, 0:T], data0=aa[:, 0:T], data1=ii[:, 0:T], initial=hst[l][:, c:c + 1],
                                                                op0=ALU.mult, op1=ALU.add), r=[aa, ii, hst[l]], w=[rr])
            sch.op("dve", lambda e, c=c: e.tensor_copy(out=hst[l][:, c:c + 1], in_=rr[:, T - 1:T]), r=[rr], w=[hst[l]])
            proj(pB, s1, w1, c * 128, hT, hk)
            sch.op("act", lambda e: e.activation(out=mm[:, 0:T], in_=pB[:, :], func=AF.Gelu_apprx_tanh), r=[pB], w=[mm])
            sch.op("dve", lambda e, c=c: e.tensor_tensor(out=yT[:, c, :], in0=rr[:, 0:T], in1=mm[:, 0:T], op=ALU.mult), r=[rr, mm], w=[yT])

        def gla(mix, qT, kT, C, BLK, sgate, nwname, ycol0, decay_const):
            nblk = T // BLK
            nch = T // C
            msk = MASK128 if C == 128 else MASK32
            for p in range(2):
                for b_ in range(nblk):
                    sch.op("pe", lambda e, p=p, b_=b_: e.transpose(pTr[0:BLK, 0:128], kT[p][:, b_ * BLK:(b_ + 1) * BLK], cst[:, IDENT, :]), r=[kT[p], cst], w=[pTr])
                    sch.op("act", lambda e, p=p, b_=b_: e.activation(out=ktok[0:BLK, b_, p, :], in_=pTr[0:BLK, 0:128], func=AF.Copy), r=[pTr], w=[ktok])
            qi = 0
            for j in range(nch):
                blk = (j * C) // BLK
                base = (j * C) % BLK
                for h in range(4):
                    p, hh = h // 2, h % 2
                    off = hh * 64
                    px = pX[qi % 4]
                    pu = pU[qi % 4]
                    at = ATb[qi % 2]
                    xcol = (qi % 4) * 128
                    ucol = (qi % 4) * 64
                    qi += 1
                    Sf = st_f[(mix, l, p, hh)]
                    Sb = st_b[(mix, l, p, hh)]
                    tsl = slice(j * C, (j + 1) * C)
                    sch.op("pe", lambda e, p=p, off=off, tsl=tsl, px=px, xcol=xcol, base=base: e.matmul(
                        px[base:base + C, xcol:xcol + C], lhsT=kT[p][off:off + 64, tsl], rhs=qT[p][off:off + 64, tsl], start=True, stop=True),
                        r=[kT[p], qT[p]], w=[px])
                    sch.op("dve", lambda e, px=px, at=at, xcol=xcol, base=base: e.tensor_tensor(
                        out=at[base:base + C, 0:C], in0=px[base:base + C, xcol:xcol + C], in1=cst[base:base + C, msk, 0:C], op=ALU.mult),
                        r=[px, cst], w=[at])
                    sch.op("pe", lambda e, p=p, off=off, tsl=tsl, at=at, base=base, blk=blk, h=h: e.matmul(
                        pO[p][off:off + 64, tsl], lhsT=vtok[base:base + C, blk, h * 64:(h + 1) * 64], rhs=at[base:base + C, 0:C], start=True, stop=False),
                        r=[vtok, at], w=[pO[p]])
                    sch.op("pe", lambda e, p=p, off=off, tsl=tsl, Sb=Sb: e.matmul(
                        pO[p][off:off + 64, tsl], lhsT=Sb[off:off + 64, :], rhs=qT[p][off:off + 64, tsl], start=False, stop=True),
                        r=[Sb, qT[p]], w=[pO[p]])
                    sch.op("pe", lambda e, p=p, off=off, pu=pu, ucol=ucol, base=base, blk=blk, h=h: e.matmul(
                        pu[off:off + 64, ucol:ucol + 64], lhsT=ktok[base:base + C, blk, p, off:off + 64], rhs=vtok[base:base + C, blk, h * 64:(h + 1) * 64],
                        start=True, stop=True), r=[ktok, vtok], w=[pu])
                    if decay_const is not None:
                        g = decay_const[h]
                        sch.op("dve", lambda e, Sf=Sf, pu=pu, off=off, ucol=ucol, g=g: e.scalar_tensor_tensor(
                            out=Sf[off:off + 64, :], in0=Sf[off:off + 64, :], scalar=g, in1=pu[off:off + 64, ucol:ucol + 64], op0=ALU.mult, op1=ALU.add),
                            r=[Sf, pu], w=[Sf])
                        sch.op("dve", lambda e, Sf=Sf, Sb=Sb, off=off, g=g: e.tensor_scalar(
                            out=Sb[off:off + 64, :], in0=Sf[off:off + 64, :], scalar1=g, scalar2=None, op0=ALU.mult), r=[Sf], w=[Sb])
                    else:
                        tmp = stmp[hh]
                        sch.op("dve", lambda e, Sf=Sf, pu=pu, off=off, ucol=ucol, tmp=tmp: e.tensor_tensor(
                            out=tmp[off:off + 64, :], in0=pu[off:off + 64, ucol:ucol + 64], in1=Sf[off:off + 64, :], op=ALU.add), r=[Sf, pu], w=[tmp])
                        sch.op("dve", lambda e, Sf=Sf, off=off, tmp=tmp, p=p, j=j: e.tensor_scalar(
                            out=Sf[off:off + 64, :], in0=tmp[off:off + 64, :], scalar1=ebl[p][off:off + 64, j:j + 1], scalar2=None, op0=ALU.mult),
                            r=[tmp, ebl[p]], w=[Sf])
                        sch.op("act", lambda e, Sf=Sf, Sb=Sb, off=off: e.activation(out=Sb[off:off + 64, :], in_=Sf[off:off + 64, :], func=AF.Copy), r=[Sf], w=[Sb])
            for p in range(2):
                o, rt = Fb[4], Fb[5]
                osq = Bb[4]
                sch.op("act", lambda e, p=p: e.activation(out=o[:, 0:T], in_=pO[p][:, :], func=AF.Copy), r=[pO[p]], w=[o])
                sch.op("act", lambda e, p=p: e.activation(out=osq[:, :], in_=pO[p][:, :], func=AF.Square), r=[pO[p]], w=[osq])
                sch.op("pe", lambda e: e.matmul(pS[:, :], lhsT=cst[:, BD64, :], rhs=osq[:, :], start=True, stop=True), r=[cst, osq], w=[pS])
                sch.op("act", lambda e: e.activation(out=rt[:, 0:T], in_=pS[:, :], func=AF.Sqrt, bias=pcol("eps"), scale=1.0), r=[pS, pv], w=[rt])
                sch.op("dve", lambda e: e.reciprocal(out=rt[:, 0:T], in_=rt[:, 0:T]), r=[rt], w=[rt])
                sch.op("dve", lambda e, p=p: e.scalar_tensor_tensor(out=o[:, 0:T], in0=o[:, 0:T], scalar=pcol(nwname, p), in1=rt[:, 0:T], op0=ALU.mult, op1=ALU.mult),
                       r=[o, rt, pv], w=[o])
                sch.op("dve", lambda e, p=p: e.tensor_tensor(out=yT[:, ycol0 + p, :], in0=o[:, 0:T], in1=sgate[p][:, 0:T], op=ALU.mult), r=[o, sgate[p]], w=[yT])

        s2, w2 = wload(win_d[l, :, :, 1024:1536], [128, 8, 512])
        s6, w6 = wload(win_d[l, :, :, 3072:3584], [128, 8, 512])
        s3, w3 = wload(win_d[l, :, :, 1536:2048], [128, 8, 512])
        if l == 0:
            sch.dma("sp", tabs[:, :, :], tab_d[:, :, ti * T:(ti + 1) * T], w=[tabs])
        qT = [Bb[0], Bb[1]]
        kT = [Bb[2], Bb[3]]
        for which, dst, cbase, tb in (("q", qT, 0, 0), ("k", kT, 256, 2)):
            for p in range(2):
                proj(pA, s2, w2, cbase + p * 128, hT, hk)
                proj(pB, s6, w6, cbase + p * 128, hT, hk)
                t1, t2 = Fb[0], Fb[1]
                sch.op("dve", lambda e, p=p, tb=tb: e.tensor_tensor(out=t1[:, 0:T], in0=pA[:, :], in1=tabs[:, p * 4 + tb, :], op=ALU.mult), r=[pA, tabs], w=[t1])
                sch.op("dve", lambda e, p=p, tb=tb: e.tensor_tensor(out=t2[:, 0:T], in0=pB[:, :], in1=tabs[:, p * 4 + tb + 1, :], op=ALU.mult), r=[pB, tabs], w=[t2])
                sch.op("dve", lambda e, p=p, dst=dst: e.tensor_tensor(out=dst[p][:, :], in0=t1[:, 0:T], in1=t2[:, 0:T], op=ALU.add), r=[t1, t2], w=[dst[p]])
        for b_ in range(4):
            for kc in range(8):
                sch.op("pe", lambda e, b_=b_, kc=kc: e.matmul(pG[:, 0:256], lhsT=hT[:, kc, b_ * 128:(b_ + 1) * 128], rhs=w3[:, kc, 0:256],
                                                               start=(kc == 0), stop=(kc == 7)), r=[hT, s3], w=[pG])
            sch.op("act", lambda e, b_=b_: e.activation(out=vtok[:, b_, :], in_=pG[:, 0:256], func=AF.Copy), r=[pG], w=[vtok])
        sg = [Fb[6], Fb[7]]
        for p in range(2):
            proj(pA, s3, w3, 256 + p * 128, hT, hk)
            sch.op("act", lambda e, p=p: e.activation(out=sg[p][:, 0:T], in_=pA[:, :], func=AF.Silu), r=[pA], w=[sg[p]])
        gla("r", qT, kT, 128, 128, sg, "rnw%d" % l, 4, [g ** 128 for g in RET_GAMMA])

        s4, w4 = wload(win_d[l, :, :, 2048:2560], [128, 8, 512])
        s5, w5 = wload(win_d[l, :, :, 2560:3072], [128, 8, 512])
        qT = [Bb[0], Bb[1]]
        kT = [Bb[2], Bb[3]]
        for p in range(2):
            sig, ff, bb, en = Fb[0], Fb[1], Fb[2], Fb[3]
            proj(pA, s4, w4, 256 + p * 128, hT, hk)
            sch.op("act", lambda e: e.activation(out=sig[:, 0:T], in_=pA[:, :], func=AF.Sigmoid), r=[pA], w=[sig])
            sch.op("dve", lambda e, p=p: e.tensor_scalar(out=ff[:, 0:T], in0=sig[:, 0:T], scalar1=dcol(OML, l, p), scalar2=dcol(LB, l, p), op0=ALU.mult, op1=ALU.add),
                   r=[sig, der], w=[ff])
            sch.op("act", lambda e: e.activation(out=ff[:, 0:T], in_=ff[:, 0:T], func=AF.Ln), r=[ff], w=[ff])
            sch.op("dve", lambda e: e.tensor_tensor_scan(out=bb[:, 0:T], data0=rmask[:, :], data1=ff[:, 0:T], initial=0.0, op0=ALU.mult, op1=ALU.add),
                   r=[rmask, ff], w=[bb])
            sch.op("dve", lambda e, p=p: e.tensor_scalar(out=sig[:, 0:T], in0=sig[:, 0:T], scalar1=dcol(NOML, l, p), scalar2=dcol(OML, l, p), op0=ALU.mult, op1=ALU.add),
                   r=[sig, der], w=[sig])
            sch.op("act", lambda e: e.activation(out=en[:, 0:T], in_=bb[:, 0:T], func=AF.Exp, scale=-1.0), r=[bb], w=[en])
            sch.op("dve", lambda e, p=p: e.tensor_tensor(out=kT[p][:, :], in0=sig[:, 0:T], in1=en[:, 0:T], op=ALU.mult), r=[sig, en], w=[kT[p]])
            sch.op("act", lambda e: e.activation(out=ff[:, 0:T], in_=bb[:, 0:T], func=AF.Exp), r=[bb], w=[ff])
            sch.op("dve", lambda e, p=p: e.tensor_copy(out=ebl[p][:, :], in_=ff[:, 0:T].rearrange("p (c t) -> p c t", t=32)[:, :, 31]), r=[ff], w=[ebl[p]])
            proj(pB, s4, w4, p * 128, hT, hk)
            sch.op("act", lambda e: e.activation(out=en[:, 0:T], in_=pB[:, :], func=AF.Silu), r=[pB], w=[en])
            sch.op("dve", lambda e, p=p: e.tensor_tensor(out=qT[p][:, :], in0=en[:, 0:T], in1=ff[:, 0:T], op=ALU.mult), r=[en, ff], w=[qT[p]])
        for b_ in range(8):
            for kc in range(8):
                sch.op("pe", lambda e, b_=b_, kc=kc: e.matmul(pG[0:64, 0:256], lhsT=hT[:, kc, b_ * 64:(b_ + 1) * 64], rhs=w5[:, kc, 0:256],
                                                               start=(kc == 0), stop=(kc == 7)), r=[hT, s5], w=[pG])
            sch.op("act", lambda e, b_=b_: e.activation(out=vtok[0:64, b_, :], in_=pG[0:64, 0:256], func=AF.Copy), r=[pG], w=[vtok])
        for p in range(2):
            proj(pA, s5, w5, 256 + p * 128, hT, hk)
            sch.op("act", lambda e, p=p: e.activation(out=sg[p][:, 0:T], in_=pA[:, :], func=AF.Silu), r=[pA], w=[sg[p]])
        gla("h", qT, kT, 32, 64, sg, "hnw%d" % l, 6, None)

        for half in range(2):
            so, wo = wload(wout_d[l, :, :, half * 512:(half + 1) * 512], [128, 8, 512])
            for mq in range(4):
                m = half * 4 + mq
                pb = pA if (m % 2 == 0) else pB
                proj(pb, so, wo, mq * 128, yT, lambda kc: yT[:, kc, :])
                sch.op("dve", lambda e, m=m, pb=pb: e.tensor_tensor(out=xr[:, m, :], in0=pb[:, :], in1=xr[:, m, :], op=ALU.add), r=[pb, xr], w=[xr])

        rs = rms_to_hT(xr, "n2w%d" % l)
        for c in range(8):
            sch.op("dve", lambda e, c=c: e.scalar_tensor_tensor(out=hT[:, c, :], in0=xr[:, c, :], scalar=pcol("n2w%d" % l, c), in1=rs[:, 0:T],
                                                                  op0=ALU.mult, op1=ALU.mult), r=[xr, rs, pv], w=[hT])
        for q in range(11):
            su, wu = wload(wup_d[l, :, :, q * 512:(q + 1) * 512], [128, 8, 512])
            for jj in range(2):
                j = 2 * q + jj
                outs = []
                for (half, pb, ub, cb, col) in ((0, pA, Fb[0], Fb[1], jj * 128), (1, pB, Fb[2], Fb[3], 256 + jj * 128)):
                    ch = half * 22 + j
                    proj(pb, su, wu, col, hT, hk)
                    sch.op("dve", lambda e, ub=ub, ch=ch: e.tensor_copy(out=ub[:, 0:2], in_=fh[l][:, ch, :]), r=[fh[l]], w=[ub])
                    sch.op("act", lambda e, ub=ub, pb=pb: e.activation(out=ub[:, 2:2 + T], in_=pb[:, :], func=AF.Copy), r=[pb], w=[ub])
                    sch.op("dve", lambda e, ub=ub, ch=ch: e.tensor_copy(out=fh[l][:, ch, :], in_=ub[:, T:T + 2]), r=[ub], w=[fh[l]])
                    sch.op("dve", lambda e, ub=ub, cb=cb, ch=ch: e.tensor_scalar(out=cb[:, 0:T], in0=ub[:, 2:2 + T], scalar1=pcol("fcw%d_2" % l, ch), scalar2=pcol("fcb%d" % l, ch),
                                                                                   op0=ALU.mult, op1=ALU.add), r=[ub, pv], w=[cb])
                    for k in (1, 0):
                        sch.op("dve", lambda e, ub=ub, cb=cb, ch=ch, k=k: e.scalar_tensor_tensor(out=cb[:, 0:T], in0=ub[:, k:k + T], scalar=pcol("fcw%d_%d" % (l, k), ch), in1=cb[:, 0:T],
                                                                                                   op0=ALU.mult, op1=ALU.add), r=[ub, cb, pv], w=[cb])
                    outs.append(cb)
                cg, cv = outs
                sch.op("act", lambda e, cg=cg: e.activation(out=cg[:, 0:T], in_=cg[:, 0:T], func=AF.Silu), r=[cg], w=[cg])
                sch.op("dve", lambda e, cg=cg, cv=cv, j=j: e.tensor_tensor(out=gT[:, j, :], in0=cg[:, 0:T], in1=cv[:, 0:T], op=ALU.mult), r=[cg, cv], w=[gT])
        for m in range(8):
            sd, wd = wload(wdn_d[l, m], [128, 22, 128])
            pb = pA if (m % 2 == 0) else pB
            for kc in range(22):
                sch.op("pe", lambda e, kc=kc, pb=pb, wd=wd: e.matmul(pb[:, :], lhsT=wd[:, kc, :], rhs=gT[:, kc, :], start=(kc == 0), stop=(kc == 21)), r=[sd, gT], w=[pb])
            sch.op("dve", lambda e, m=m, pb=pb: e.tensor_tensor(out=xr[:, m, :], in0=pb[:, :], in1=xr[:, m, :], op=ALU.add), r=[pb, xr], w=[xr])

    xv = xT_d.rearrange("(c p) s -> p c s", p=128)
    yv = yT_d.rearrange("(c p) s -> p c s", p=128)
    out_toks = []
    for ti in range(NT):
        xr = xres[ti % 2]
        sch.dma("sp", xr[:, :, :], xv[:, :, ti * T:(ti + 1) * T], w=[xr])
        for l in range(nlayers):
            emit_layer(l, xr, ti)
        rs = rms_to_hT(xr, "fnw")
        for c in range(8):
            sch.op("dve", lambda e, c=c, xr=xr, rs=rs: e.scalar_tensor_tensor(out=xr[:, c, :], in0=xr[:, c, :], scalar=pcol("fnw", c), in1=rs[:, 0:T],
                                                                               op0=ALU.mult, op1=ALU.mult), r=[xr, rs, pv], w=[xr])
        out_toks.append(sch.dma("sp", yv[:, :, ti * T:(ti + 1) * T], xr[:, :, :], r=[xr]))
    sch.wait_all_on("sp", out_toks)
    sch.emit()
    return nc, sch


def _chunks(v):
    v = np.asarray(v, np.float32)
    return np.ascontiguousarray(v.reshape(-1, 128).T)


def make_tables(S):
    t = np.arange(S, dtype=np.float64)
    inv = 10000.0 ** (-np.arange(0, 64, 2, dtype=np.float64) / 64.0)
    ang = t[None, :] * inv[:, None]
    cos = np.concatenate([np.cos(ang), np.cos(ang)], 0)
    sin = np.concatenate([-np.sin(ang), np.sin(ang)], 0)
    n1 = (t % 128) + 1.0
    tabs = np.zeros((128, 8, S), np.float64)
    for p in range(2):
        for hh in range(2):
            h = 2 * p + hh
            lg = math.log(RET_GAMMA[h])
            qd = np.exp(n1 * lg)[None, :]
            kd = np.exp(-n1 * lg)[None, :] * (64.0 ** -0.5)
            sl = slice(hh * 64, hh * 64 + 64)
            tabs[sl, p * 4 + 0] = cos * qd
            tabs[sl, p * 4 + 1] = sin * qd
            tabs[sl, p * 4 + 2] = cos * kd
            tabs[sl, p * 4 + 3] = sin * kd
    return tabs.astype(np.float32)


def make_consts():
    cst = np.zeros((128, 5, 128), np.float32)
    cst[:, 0, :] = 1.0
    cst[:, 1, :] = np.eye(128, dtype=np.float32)
    for b in range(2):
        cst[b * 64:(b + 1) * 64, 2, b * 64:(b + 1) * 64] = 1.0 / 64.0
    m = np.arange(128)[:, None]
    n = np.arange(128)[None, :]
    cst[:, 3, :] = (n >= m).astype(np.float32)
    m32 = (np.arange(32)[None, :] >= np.arange(32)[:, None]).astype(np.float32)
    for b in range(4):
        cst[b * 32:(b + 1) * 32, 4, 0:32] = m32
    rmask = np.ones((128, T), np.float32)
    rmask[:, ::32] = 0.0
    return cst, rmask


def prep_shared(inp, S):
    L = DEPTH
    w_in = np.asarray(inp["w_in"], np.float32)
    swap = lambda base: [base + h * 64 + ((d + 32) % 64) for h in range(4) for d in range(64)]
    idx = list(range(3072)) + swap(1024) + swap(1280)
    win = w_in[:, :, idx].reshape(L, 8, 128, NCOL_IN).transpose(0, 2, 1, 3)
    wout = np.asarray(inp["w_out"], np.float32).reshape(L, 8, 128, D).transpose(0, 2, 1, 3)
    perm = []
    for q in range(11):
        perm += list(range(2 * q * 128, (2 * q + 2) * 128))
        perm += list(range(DFF + 2 * q * 128, DFF + (2 * q + 2) * 128))
    wup = np.asarray(inp["ffn_w_up"], np.float32)[:, :, perm].reshape(L, 8, 128, 2 * DFF).transpose(0, 2, 1, 3)
    wdn = np.asarray(inp["ffn_w_down"], np.float32).reshape(L, 22, 128, 8, 128).transpose(0, 3, 2, 1, 4)
    gw = np.zeros((L, 128, 8, 128), np.float32)
    wa = np.asarray(inp["lru_wa"], np.float32)
    wx = np.asarray(inp["lru_wx"], np.float32)
    for l in range(L):
        for c in range(4):
            for b2 in range(2):
                sl = slice(b2 * 64, b2 * 64 + 64)
                gw[l, sl, c * 2 + 0, sl] = wa[l, 2 * c + b2]
                gw[l, sl, c * 2 + 1, sl] = wx[l, 2 * c + b2]
    pvec = np.zeros((128, PV["_n"]), np.float32)

    def put(name, v):
        a = _chunks(v)
        pvec[:, PV[name]:PV[name] + a.shape[1]] = a

    for l in range(L):
        put("n1w%d" % l, inp["norm1_w"][l])
        put("n2w%d" % l, inp["norm2_w"][l])
        for k in range(4):
            put("lcw%d_%d" % (l, k), inp["lru_conv_w"][l][k])
        put("lcb%d" % l, inp["lru_conv_b"][l])
        put("ba%d" % l, inp["lru_ba"][l])
        put("bx%d" % l, inp["lru_bx"][l])
        put("lam%d" % l, inp["lru_lambda"][l])
        put("rnw%d" % l, inp["ret_norm_w"][l])
        put("hnw%d" % l, inp["hg_norm_w"][l])
        put("hb%d" % l, inp["hg_lower_bounds"][l])
        for k in range(3):
            put("fcw%d_%d" % (l, k), inp["ffn_conv_w"][l][k])
        put("fcb%d" % l, inp["ffn_conv_b"][l])
    put("fnw", inp["final_norm_w"])
    pvec[:, PV["eps"]] = EPS
    pvec[:, PV["one"]] = 1.0
    cst, rmask = make_consts()
    return {
        "win": np.ascontiguousarray(win), "wout": np.ascontiguousarray(wout),
        "wup": np.ascontiguousarray(wup), "wdn": np.ascontiguousarray(wdn),
        "gatew": gw, "pvec": pvec, "tabs": make_tables(S), "cst": cst, "rmask": rmask,
    }


_CACHE = {}


def kernel(**inputs):
    x = np.asarray(inputs["x"], np.float32)
    B, S, _ = x.shape
    NT = S // T
    shared = prep_shared(inputs, S)
    if NT not in _CACHE:
        _CACHE[NT] = build(NT)
    nc, sch = _CACHE[NT]
    ncores = 8
    in_maps = []
    for c in range(ncores):
        m = dict(shared)
        m["xT"] = np.ascontiguousarray(x[c % B].T)
        in_maps.append(m)
    res = run_bass_kernel_spmd(nc, in_maps, core_ids=list(range(ncores)))
    out = np.empty((B, S, D), np.float32)
    for b in range(B):
        out[b] = res.results[b]["yT"].T
    return out
```

```python
import contextlib
import math
import numpy as np
import concourse.bass as bass
import concourse.mybir as mybir
from concourse.bass_utils import run_bass_kernel_spmd

F32 = mybir.dt.float32
BF16 = mybir.dt.bfloat16
AF = mybir.ActivationFunctionType
ALU = mybir.AluOpType

EPOCH = 4000

D = 1024
SEQ = 8192
BATCH = 4
DEPTH = 2
T = 512
DFF = 2816
NFC = 44
EPS = 1e-6
NCOL_IN = 3584


import heapq


class Buf:
    def __init__(self, t, name):
        self.t = t
        self.name = name
        self.last_write = None
        self.reads = []
        self.dma_cnt = 0
        self.g_lw = None
        self.g_rd = []

    def __getitem__(self, key):
        return self.t[key]

    def sub(self, name):
        return Buf(self.t, name)


class _Op:
    __slots__ = ("eng", "fn", "r", "w", "occ", "lat", "kind", "sem_buf", "preds", "succs")


class Sched:
    ENG = ("pe", "act", "dve", "pool", "sp")

    def __init__(self, nc):
        self.nc = nc
        self.stack = contextlib.ExitStack()
        self.rec = []
        self.ops = {e: [] for e in self.ENG}
        self.cnt = {e: 0 for e in self.ENG}
        self.epoch = {e: 0 for e in self.ENG}
        self.sems = {}
        self.waited = {e: {} for e in self.ENG}
        self.nsem = 0
        self.final_waits = []
        self.sim_time = 0.0

    def sem(self, key):
        if key not in self.sems:
            self.sems[key] = self.stack.enter_context(self.nc.semaphore("s%d" % self.nsem))
            self.nsem += 1
        return self.sems[key]

    def sb(self, name, shape, dtype=F32):
        t = self.stack.enter_context(self.nc.sbuf_tensor(name, list(shape), dtype))
        return Buf(t, name)

    def ps(self, name, shape, dtype=F32):
        t = self.stack.enter_context(self.nc.psum_tensor(name, list(shape), dtype))
        return Buf(t, name)

    def _record(self, o):
        i = len(self.rec)
        preds = set()
        for b in o.r:
            if b.g_lw is not None:
                preds.add(b.g_lw)
        for b in o.w:
            if b.g_lw is not None:
                preds.add(b.g_lw)
            preds.update(b.g_rd)
        preds.discard(i)
        o.preds = preds
        o.succs = []
        for p in preds:
            self.rec[p].succs.append(i)
        for b in o.w:
            b.g_lw = i
            b.g_rd = []
        for b in o.r:
            if b not in o.w:
                b.g_rd.append(i)
        self.rec.append(o)
        return i

    def op(self, eng, fn, r=(), w=(), n=T, k=1.0, nmm=1):
        o = _Op()
        o.eng, o.fn, o.r, o.w, o.kind, o.sem_buf = eng, fn, tuple(r), tuple(w), "c", None
        if eng == "pe":
            o.occ = nmm * (0.19 + 0.0002 * n)
        elif eng == "act":
            o.occ = 0.22 + 0.00072 * n
        elif eng == "dve":
            o.occ = 0.07 + 0.00105 * n * k
        else:
            o.occ = 0.15 + 0.0017 * n * k
        o.lat = o.occ + 0.1
        return self._record(o)

    def dma(self, q, out_ap, in_ap, r=(), w=(), sem_buf=None, nbytes=1 << 20):
        o = _Op()
        o.eng, o.r, o.w, o.kind = q, tuple(r), tuple(w), "d"
        o.sem_buf = sem_buf if sem_buf is not None else (w[0] if len(w) else r[0])

        def fn(e, out_ap=out_ap, in_ap=in_ap):
            return e.dma_start(out=out_ap, in_=in_ap)

        o.fn = fn
        o.occ = 1.0 if q == "pool" else 0.15
        o.lat = o.occ + 2.0 + nbytes / 150e3
        return self._record(o)

    def wait_all_on(self, eng, idxs):
        self.final_waits.append((eng, list(idxs)))

    def _schedule(self):
        ops = self.rec
        ENG = self.ENG
        ready = {e: [] for e in ENG}
        busy = {e: 0.0 for e in ENG}
        npred = [len(o.preds) for o in ops]
        for i, o in enumerate(ops):
            if npred[i] == 0:
                heapq.heappush(ready[o.eng], i)
        events = []
        now = 0.0
        order = []
        while True:
            for e in ENG:
                if ready[e] and busy[e] <= now + 1e-9:
                    i = heapq.heappop(ready[e])
                    o = ops[i]
                    order.append(i)
                    busy[e] = now + o.occ
                    heapq.heappush(events, (now + o.lat, 1, i))
                    heapq.heappush(events, (now + o.occ, 0, i))
            if not events:
                break
            t, typ, i = heapq.heappop(events)
            now = t
            if typ == 1:
                for s_ in ops[i].succs:
                    npred[s_] -= 1
                    if npred[s_] == 0:
                        heapq.heappush(ready[ops[s_].eng], s_)
        assert len(order) == len(ops), (len(order), len(ops))
        self.sim_time = now
        return order

    def _deps(self, r, w):
        deps = []
        for b in r:
            if b.last_write is not None:
                deps.append(b.last_write)
        for b in w:
            if b.last_write is not None:
                deps.append(b.last_write)
            deps.extend(b.reads)
        return deps

    def _waits(self, eng, deps):
        need = {}
        wd = self.waited[eng]
        for (k, v) in deps:
            if wd.get(k, 0) >= v:
                continue
            if need.get(k, 0) < v:
                need[k] = v
        for k, v in need.items():
            wd[k] = v
        return [(self.sem(k), v) for k, v in need.items()]

    def _commit(self, r, w, tok):
        for b in w:
            b.last_write = tok
            b.reads = []
        for b in r:
            if b in w:
                continue
            b.reads = [x for x in b.reads if x[0] != tok[0]] + [tok]

    def _replay_one(self, o):
        eng = o.eng
        deps = self._deps(o.r, o.w)
        if o.kind == "c":
            if eng == "pe":
                deps = [d for d in deps if not (d[0][0] == "e" and d[0][1] == "pe")]
            waits = self._waits(eng, deps)
            if self.cnt[eng] >= EPOCH:
                self.epoch[eng] += 1
                self.cnt[eng] = 0
            self.cnt[eng] += 1
            key = ("e", eng, self.epoch[eng])
            tok = (key, self.cnt[eng])
            self.ops[eng].append((o.fn, waits, self.sem(key), 1))
        else:
            waits = self._waits(eng, deps)
            sb_ = o.sem_buf
            key = ("d", id(sb_))
            sb_.dma_cnt += 16
            tok = (key, sb_.dma_cnt)
            self.ops[eng].append((o.fn, waits, self.sem(key), 16))
        self._commit(o.r, o.w, tok)
        return tok

    def emit(self):
        nc = self.nc
        order = self._schedule()
        toks = {}
        for i in order:
            toks[i] = self._replay_one(self.rec[i])
        for eng, idxs in self.final_waits:
            waits = self._waits(eng, [toks[i] for i in idxs])
            self.ops[eng].append((None, waits, None, 0))

        def replay(engobj, lst):
            for (fn, waits, s, inc) in lst:
                for (ws, v) in waits:
                    engobj.wait_ge(ws, v)
                if fn is not None:
                    fn(engobj).then_inc(s, inc)

        with nc.Block() as block:
            @block.tensor
            def _(e):
                replay(e, self.ops["pe"])

            @block.scalar
            def _(e):
                replay(e, self.ops["act"])

            @block.vector
            def _(e):
                replay(e, self.ops["dve"])

            @block.gpsimd
            def _(e):
                replay(e, self.ops["pool"])

            @block.sync
            def _(e):
                replay(e, self.ops["sp"])

    def close(self):
        self.stack.close()


def _pvec_layout():
    lay = {}
    col = 0

    def add(name, n):
        nonlocal col
        lay[name] = col
        col += n

    for l in range(DEPTH):
        add("n1w%d" % l, 8)
        add("n2w%d" % l, 8)
        for k in range(4):
            add("lcw%d_%d" % (l, k), 4)
        add("lcb%d" % l, 4)
        add("ba%d" % l, 4)
        add("bx%d" % l, 4)
        add("lam%d" % l, 4)
        add("rnw%d" % l, 2)
        add("hnw%d" % l, 2)
        add("hb%d" % l, 2)
        for k in range(3):
            add("fcw%d_%d" % (l, k), NFC)
        add("fcb%d" % l, NFC)
    add("fnw", 8)
    add("eps", 1)
    add("one", 1)
    lay["_n"] = col
    return lay


PV = _pvec_layout()
RET_GAMMA = [1.0 - 2.0 ** (-5.0 - h) for h in range(4)]


def build(NT, nlayers=DEPTH, debug_out=None):
    S = NT * T
    nc = bass.Bass("TRN2", target_bir_lowering=False)
    xT_d = nc.dram_tensor("xT", [D, S], F32, kind="ExternalInput").ap()
    win_d = nc.dram_tensor("win", [DEPTH, 128, 8, NCOL_IN], F32, kind="ExternalInput").ap()
    wout_d = nc.dram_tensor("wout", [DEPTH, 128, 8, D], F32, kind="ExternalInput").ap()
    wup_d = nc.dram_tensor("wup", [DEPTH, 128, 8, 2 * DFF], F32, kind="ExternalInput").ap()
    wdn_d = nc.dram_tensor("wdn", [DEPTH, 8, 128, 22, 128], F32, kind="ExternalInput").ap()
    gw_d = nc.dram_tensor("gatew", [DEPTH, 128, 8, 128], F32, kind="ExternalInput").ap()
    pv_d = nc.dram_tensor("pvec", [128, PV["_n"]], F32, kind="ExternalInput").ap()
    tab_d = nc.dram_tensor("tabs", [128, 8, S], F32, kind="ExternalInput").ap()
    cst_d = nc.dram_tensor("cst", [128, 5, 128], F32, kind="ExternalInput").ap()
    rm_d = nc.dram_tensor("rmask", [128, T], F32, kind="ExternalInput").ap()
    yT_d = nc.dram_tensor("yT", [D, S], F32, kind="ExternalOutput").ap()

    sch = Sched(nc)
    sb, ps = sch.sb, sch.ps
    op = sch.op

    pv = sb("pv", [128, PV["_n"]])
    sch.dma("sp", pv[:, :], pv_d[:, :], w=[pv], nbytes=1 << 18)
    cstf = sb("cstf", [128, 5, 128])
    sch.dma("sp", cstf[:, :, :], cst_d[:, :, :], w=[cstf], nbytes=1 << 18)
    cst = sb("cstb", [128, 5, 128], BF16)
    op("dve", lambda e: e.tensor_copy(out=cst[:, :, :], in_=cstf[:, :, :]), r=[cstf], w=[cst], n=640)
    ONES, IDENT, BD64, MASK128, MASK32 = range(5)
    rmask = sb("rmask_s", [128, T])
    sch.dma("sp", rmask[:, :], rm_d[:, :], w=[rmask], nbytes=1 << 18)
    gwf = sb("gwf", [128, DEPTH * 8, 128])
    gwb = sb("gwb", [128, DEPTH * 8, 128], BF16)
    for l in range(DEPTH):
        sch.dma("sp", gwf[:, l * 8:(l + 1) * 8, :], gw_d[l], w=[gwf], nbytes=1 << 19)
    op("dve", lambda e: e.tensor_copy(out=gwb[:, :, :], in_=gwf[:, :, :]), r=[gwf], w=[gwb], n=2048)

    def pcol(name, c=0, lo=0, hi=128):
        k = PV[name] + c
        return pv[lo:hi, k:k + 1]

    der = sb("der", [128, 32])
    for l in range(DEPTH):
        k = PV["lam%d" % l]
        op("act", lambda e, k=k, l=l: e.activation(out=der[:, l * 4:l * 4 + 4], in_=pv[:, k:k + 4], func=AF.Exp, scale=-1.0), r=[pv], w=[der], n=4)
        op("act", lambda e, l=l: e.activation(out=der[:, l * 4:l * 4 + 4], in_=der[:, l * 4:l * 4 + 4], func=AF.Ln, bias=pcol("one"), scale=1.0), r=[der, pv], w=[der], n=4)
        op("dve", lambda e, l=l: e.tensor_scalar(out=der[:, l * 4:l * 4 + 4], in0=der[:, l * 4:l * 4 + 4], scalar1=-8.0, scalar2=None, op0=ALU.mult), r=[der], w=[der], n=4)
    LB, OML, NOML = 8, 12, 16
    op("dve", lambda e: e.memset(der[:, LB:LB + 2], 0.0), w=[der], n=2)
    op("dve", lambda e: e.memset(der[:, OML:OML + 2], 1.0), w=[der], n=2)
    op("dve", lambda e: e.memset(der[:, NOML:NOML + 2], -1.0), w=[der], n=2)
    k0, k1 = PV["hb0"], PV["hb1"]
    op("dve", lambda e: e.tensor_tensor(out=der[:, 20:22], in0=pv[:, k1:k1 + 2], in1=pv[:, k0:k0 + 2], op=ALU.subtract), r=[pv, der], w=[der], n=2)
    op("act", lambda e: e.activation(out=der[:, LB + 2:LB + 4], in_=der[:, 20:22], func=AF.Sigmoid), r=[der], w=[der], n=2)
    op("act", lambda e: e.activation(out=der[:, OML + 2:OML + 4], in_=der[:, 20:22], func=AF.Sigmoid, scale=-1.0), r=[der], w=[der], n=2)
    op("dve", lambda e: e.tensor_scalar(out=der[:, NOML + 2:NOML + 4], in0=der[:, OML + 2:OML + 4], scalar1=-1.0, scalar2=None, op0=ALU.mult), r=[der], w=[der], n=2)

    def dcol(base, l, c, lo=0, hi=128):
        k = base + l * (4 if base == 0 else 2) + c
        return der[lo:hi, k:k + 1]

    hst_t = [sb("hst%d" % l, [128, 4]) for l in range(DEPTH)]
    lxh_t = [sb("lxh%d" % l, [128, 4, 4]) for l in range(DEPTH)]
    fh_t = [sb("fh%d" % l, [128, NFC, 2]) for l in range(DEPTH)]
    hst = {}
    lxh = {}
    fh = {}
    st_f = {}
    st_b = {}
    for l in range(DEPTH):
        for c in range(4):
            hst[(l, c)] = hst_t[l].sub("hst%d_%d" % (l, c))
            lxh[(l, c)] = lxh_t[l].sub("lxh%d_%d" % (l, c))
        for ch in range(NFC):
            fh[(l, ch)] = fh_t[l].sub("fh%d_%d" % (l, ch))
        op("dve", lambda e, l=l: e.memset(hst_t[l][:, :], 0.0), w=[hst[(l, c)] for c in range(4)], n=4)
        op("dve", lambda e, l=l: e.memset(lxh_t[l][:, :, :], 0.0), w=[lxh[(l, c)] for c in range(4)], n=16)
        op("dve", lambda e, l=l: e.memset(fh_t[l][:, :, :], 0.0), w=[fh[(l, ch)] for ch in range(NFC)], n=88)
        for mix in ("r", "h"):
            for p in range(2):
                tf = sb("stf_%s%d%d" % (mix, l, p), [128, 64])
                tb = sb("stb_%s%d%d" % (mix, l, p), [128, 64], BF16)
                for hh in range(2):
                    bf_ = tf.sub("stf_%s%d%d%d" % (mix, l, p, hh))
                    bb_ = tb.sub("stb_%s%d%d%d" % (mix, l, p, hh))
                    st_f[(mix, l, p, hh)] = bf_
                    st_b[(mix, l, p, hh)] = bb_
                    lo = hh * 64
                    op("dve", lambda e, bf_=bf_, lo=lo: e.memset(bf_[lo:lo + 64, :], 0.0), w=[bf_], n=64)
                    op("dve", lambda e, bb_=bb_, lo=lo: e.memset(bb_[lo:lo + 64, :], 0.0), w=[bb_], n=64)

    xres = [sb("xres%d" % i, [128, 8, T]) for i in range(2)]
    tabs = sb("tabs_s", [128, 8, T])
    hT = sb("hT", [128, 8, T], BF16)
    yT = sb("yTs", [128, 8, T], BF16)
    yTc = [yT.sub("yT%d" % c) for c in range(8)]
    gT = sb("gT", [128, 22, T], BF16)
    gTc = [gT.sub("gT%d" % c) for c in range(22)]
    NSLOT = 4
    wsl = [sb("wsl%d" % i, [128, 4096], BF16) for i in range(NSLOT)]

    class FB:
        def __init__(self, name):
            self.b = sb(name, [128, T + 4])
            self.h = self.b.sub(name + "_h")

        @property
        def d(self):
            return self.b[:, 4:4 + T]

    NF = 16
    Fring = [FB("F%d" % i) for i in range(NF)]
    fstate = {"f": 0, "b": 0, "a": 0, "x": 0, "at": 0}

    def getF():
        f = Fring[fstate["f"] % NF]
        fstate["f"] += 1
        return f

    NB = 10
    Bring = [sb("B%d" % i, [128, T], BF16) for i in range(NB)]

    def getB():
        b = Bring[fstate["b"] % NB]
        fstate["b"] += 1
        return b

    vtok = sb("vtok", [128, 8, 256], BF16)
    ktok = sb("ktok", [128, 8, 2, 128], BF16)
    NAT = 4
    ATb = [sb("AT%d" % i, [128, 128], BF16) for i in range(NAT)]
    ebl = [sb("ebl%d" % p, [128, 16]) for p in range(2)]
    stmp = [sb("stmp%d" % i, [128, 64]) for i in range(4)]

    NACC = 3
    acc = [ps("acc%d" % i, [128, T]) for i in range(NACC)]
    pO = [ps("pO%d" % p, [128, T]) for p in range(2)]
    pX = [ps("pX%d" % i, [128, T]) for i in range(2)]
    pUT = ps("pUT", [128, T])
    pTr = pUT[:, 256:512].bitcast(BF16)

    def getA():
        a = acc[fstate["a"] % NACC]
        fstate["a"] += 1
        return a

    wstate = {"n": 0}

    def wload(src_ap, shape):
        s = wsl[wstate["n"] % NSLOT]
        wstate["n"] += 1
        n = 1
        for d_ in shape[1:]:
            n *= d_
        view = s[:, 0:n].rearrange("p (a b) -> p a b", b=shape[2])
        sch.dma("pool", view, src_ap, w=[s], nbytes=128 * n * 4)
        return s, view

    def proj(slot, wview, col, rhs_bufs, rhs_of_kc, nk=8, m=128, ncols=T):
        pbuf = getA()

        def fn(e):
            ins = None
            for kc in range(nk):
                ins = e.matmul(pbuf[0:m, 0:ncols], lhsT=wview[:, kc, col:col + m], rhs=rhs_of_kc(kc), start=(kc == 0), stop=(kc == nk - 1))
            return ins

        op("pe", fn, r=[slot] + list(rhs_bufs), w=[pbuf], n=ncols, nmm=nk)
        return pbuf

    def rms_rstd(xr):
        op("act", lambda e: e.activation(out=gT[:, 0:8, :], in_=xr[:, :, :], func=AF.Square), r=[xr], w=gTc[0:8], n=8 * T)
        pS_ = getA()

        def fn(e):
            ins = None
            for c in range(8):
                ins = e.matmul(pS_[:, :], lhsT=cst[:, ONES, :], rhs=gT[:, c, :], start=(c == 0), stop=(c == 7))
            return ins

        op("pe", fn, r=[cst] + gTc[0:8], w=[pS_], nmm=8)
        rs = getF()
        op("act", lambda e: e.activation(out=rs.d, in_=pS_[:, :], func=AF.Ln, bias=pcol("eps"), scale=1.0 / D), r=[pv], w=[rs.b, pS_])
        op("act", lambda e: e.activation(out=rs.d, in_=rs.d, func=AF.Exp, scale=-0.5), r=[], w=[rs.b])
        return rs

    def norm_to_hT(xr, nwname):
        rs = rms_rstd(xr)
        for c in range(8):
            op("dve", lambda e, c=c: e.scalar_tensor_tensor(out=hT[:, c, :], in0=xr[:, c, :], scalar=pcol(nwname, c), in1=rs.d,
                                                              op0=ALU.mult, op1=ALU.mult), r=[xr, rs.b, pv], w=[hT], k=2)

    hk = lambda kc: hT[:, kc, :]

    def emit_layer(l, xr, ti):
        norm_to_hT(xr, "n1w%d" % l)

        s0, w0 = wload(win_d[l, :, :, 0:512], [128, 8, 512])
        s1, w1 = wload(win_d[l, :, :, 512:1024], [128, 8, 512])
        for c in range(4):
            lxb, xc, rr, ii, aa, mm = getF(), getF(), getF(), getF(), getF(), getF()
            xcb = getB()
            pa = proj(s0, w0, c * 128, [hT], hk)
            op("pool", lambda e, c=c, lxb=lxb: e.tensor_copy(out=lxb.b[:, 1:4], in_=lxh_t[l][:, c, 0:3]), r=[lxh[(l, c)]], w=[lxb.h], n=3)
            op("act", lambda e, lxb=lxb, pa=pa: e.activation(out=lxb.d, in_=pa[:, :], func=AF.Copy), r=[], w=[lxb.b, pa])
            op("pool", lambda e, c=c, lxb=lxb: e.tensor_copy(out=lxh_t[l][:, c, 0:3], in_=lxb.b[:, T + 1:T + 4]), r=[lxb.b], w=[lxh[(l, c)]], n=3)
            op("dve", lambda e, c=c, lxb=lxb, xc=xc: e.tensor_scalar(out=xc.d, in0=lxb.b[:, 1:1 + T], scalar1=pcol("lcw%d_0" % l, c), scalar2=pcol("lcb%d" % l, c),
                                                                      op0=ALU.mult, op1=ALU.add), r=[lxb.b, lxb.h, pv], w=[xc.b])
            for k in range(1, 4):
                op("dve", lambda e, c=c, k=k, lxb=lxb, xc=xc: e.scalar_tensor_tensor(out=xc.d, in0=lxb.b[:, 1 + k:1 + k + T], scalar=pcol("lcw%d_%d" % (l, k), c), in1=xc.d,
                                                                                      op0=ALU.mult, op1=ALU.add), r=[lxb.b, lxb.h, pv], w=[xc.b], k=2)
            op("act", lambda e, xc=xc, xcb=xcb: e.activation(out=xcb[:, :], in_=xc.d, func=AF.Copy), r=[xc.b], w=[xcb])
            pg = getA()
            op("pe", lambda e, c=c, pg=pg, xcb=xcb: e.matmul(pg[:, :], lhsT=gwb[:, l * 8 + c * 2, :], rhs=xcb[:, :], start=True, stop=True), r=[gwb, xcb], w=[pg])
            op("act", lambda e, c=c, pg=pg, rr=rr: e.activation(out=rr.d, in_=pg[:, :], func=AF.Sigmoid, bias=pcol("ba%d" % l, c)), r=[pv], w=[rr.b, pg])
            pg2 = getA()
            op("pe", lambda e, c=c, pg2=pg2, xcb=xcb: e.matmul(pg2[:, :], lhsT=gwb[:, l * 8 + c * 2 + 1, :], rhs=xcb[:, :], start=True, stop=True), r=[gwb, xcb], w=[pg2])
            op("act", lambda e, c=c, pg2=pg2, ii=ii: e.activation(out=ii.d, in_=pg2[:, :], func=AF.Sigmoid, bias=pcol("bx%d" % l, c)), r=[pv], w=[ii.b, pg2])
            op("act", lambda e, c=c, rr=rr, aa=aa: e.activation(out=aa.d, in_=rr.d, func=AF.Exp, scale=dcol(0, l, c)), r=[rr.b, der], w=[aa.b])
            op("act", lambda e, aa=aa, mm=mm: e.activation(out=mm.d, in_=aa.d, func=AF.Square), r=[aa.b], w=[mm.b])
            op("act", lambda e, mm=mm: e.activation(out=mm.d, in_=mm.d, func=AF.Sqrt, bias=pcol("one"), scale=-1.0), r=[pv], w=[mm.b])
            op("pool", lambda e, ii=ii, xc=xc: e.tensor_tensor(out=ii.d, in0=ii.d, in1=xc.d, op=ALU.mult), r=[xc.b], w=[ii.b], k=2)
            op("pool", lambda e, ii=ii, mm=mm: e.tensor_tensor(out=ii.d, in0=ii.d, in1=mm.d, op=ALU.mult), r=[mm.b], w=[ii.b], k=2)
            hs = hst[(l, c)]
            op("dve", lambda e, c=c, rr=rr, aa=aa, ii=ii: e.tensor_tensor_scan(out=rr.d, data0=aa.d, data1=ii.d, initial=hst_t[l][:, c:c + 1],
                                                                                op0=ALU.mult, op1=ALU.add), r=[aa.b, ii.b, hs], w=[rr.b], k=2)
            op("dve", lambda e, c=c, rr=rr: e.tensor_copy(out=hst_t[l][:, c:c + 1], in_=rr.b[:, T + 3:T + 4]), r=[rr.b], w=[hs], n=1)
            pb = proj(s1, w1, c * 128, [hT], hk)
            op("act", lambda e, mm=mm, pb=pb: e.activation(out=mm.d, in_=pb[:, :], func=AF.Gelu_apprx_tanh), r=[], w=[mm.b, pb])
            op("dve", lambda e, c=c, rr=rr, mm=mm: e.tensor_tensor(out=yT[:, c, :], in0=rr.d, in1=mm.d, op=ALU.mult), r=[rr.b, mm.b], w=[yTc[c]], k=2)

        def gla(mix, qT, kT, C, BLK, sgate, nwname, ycol0, decay_const):
            nblk = T // BLK
            nch = T // C
            msk = MASK128 if C == 128 else MASK32
            for p in range(2):
                for b_ in range(nblk):
                    op("pe", lambda e, p=p, b_=b_: e.transpose(pTr[0:BLK, 0:128], kT[p][:, b_ * BLK:(b_ + 1) * BLK], cst[:, IDENT, :]), r=[kT[p], cst], w=[pUT], n=128)
                    op("act", lambda e, p=p, b_=b_: e.activation(out=ktok[0:BLK, b_, p, :], in_=pTr[0:BLK, 0:128], func=AF.Copy), r=[], w=[ktok, pUT], n=128)
            for j in range(nch):
                blk = (j * C) // BLK
                base = (j * C) % BLK
                for h in range(4):
                    p, hh = h // 2, h % 2
                    off = hh * 64
                    px = pX[fstate["x"] % 2]
                    fstate["x"] += 1
                    at = ATb[fstate["at"] % NAT]
                    fstate["at"] += 1
                    Sf = st_f[(mix, l, p, hh)]
                    Sb = st_b[(mix, l, p, hh)]
                    tsl = slice(j * C, (j + 1) * C)
                    op("pe", lambda e, p=p, off=off, tsl=tsl, px=px, base=base: e.matmul(
                        px[base:base + C, 0:C], lhsT=kT[p][off:off + 64, tsl], rhs=qT[p][off:off + 64, tsl], start=True, stop=True),
                        r=[kT[p], qT[p]], w=[px], n=C)
                    op("dve", lambda e, px=px, at=at, base=base: e.tensor_tensor(
                        out=at[base:base + C, 0:C], in0=px[base:base + C, 0:C], in1=cst[base:base + C, msk, 0:C], op=ALU.mult),
                        r=[cst], w=[at, px], n=C)

                    def fn_o(e, p=p, off=off, tsl=tsl, at=at, base=base, blk=blk, h=h, Sb=Sb):
                        e.matmul(pO[p][off:off + 64, tsl], lhsT=vtok[base:base + C, blk, h * 64:(h + 1) * 64], rhs=at[base:base + C, 0:C], start=True, stop=False)
                        return e.matmul(pO[p][off:off + 64, tsl], lhsT=Sb[off:off + 64, :], rhs=qT[p][off:off + 64, tsl], start=False, stop=True)

                    op("pe", fn_o, r=[vtok, at, Sb, qT[p]], w=[pO[p]], n=C, nmm=2)
                    op("pe", lambda e, p=p, off=off, base=base, blk=blk, h=h: e.matmul(
                        pUT[off:off + 64, 0:64], lhsT=ktok[base:base + C, blk, p, off:off + 64], rhs=vtok[base:base + C, blk, h * 64:(h + 1) * 64],
                        start=True, stop=True), r=[ktok, vtok], w=[pUT], n=64)
                    if decay_const is not None:
                        g = decay_const[h]
                        op("dve", lambda e, Sf=Sf, off=off, g=g: e.scalar_tensor_tensor(
                            out=Sf[off:off + 64, :], in0=Sf[off:off + 64, :], scalar=g, in1=pUT[off:off + 64, 0:64], op0=ALU.mult, op1=ALU.add),
                            r=[], w=[Sf, pUT], n=64)
                        op("dve", lambda e, Sf=Sf, Sb=Sb, off=off, g=g: e.tensor_scalar(
                            out=Sb[off:off + 64, :], in0=Sf[off:off + 64, :], scalar1=g, scalar2=None, op0=ALU.mult), r=[Sf], w=[Sb], n=64)
                    else:
                        tmp = stmp[h]
                        op("dve", lambda e, Sf=Sf, off=off, tmp=tmp: e.tensor_tensor(
                            out=tmp[off:off + 64, :], in0=pUT[off:off + 64, 0:64], in1=Sf[off:off + 64, :], op=ALU.add), r=[Sf], w=[tmp, pUT], n=64)
                        op("dve", lambda e, Sf=Sf, off=off, tmp=tmp, p=p, j=j: e.tensor_scalar(
                            out=Sf[off:off + 64, :], in0=tmp[off:off + 64, :], scalar1=ebl[p][off:off + 64, j:j + 1], scalar2=None, op0=ALU.mult),
                            r=[tmp, ebl[p]], w=[Sf], n=64)
                        op("act", lambda e, Sf=Sf, Sb=Sb, off=off: e.activation(out=Sb[off:off + 64, :], in_=Sf[off:off + 64, :], func=AF.Copy), r=[Sf], w=[Sb], n=64)
            for p in range(2):
                o, rt = getF(), getF()
                osq = getB()
                op("act", lambda e, p=p, o=o: e.activation(out=o.d, in_=pO[p][:, :], func=AF.Copy), r=[], w=[o.b, pO[p]])
                op("act", lambda e, o=o, osq=osq: e.activation(out=osq[:, :], in_=o.d, func=AF.Square), r=[o.b], w=[osq])
                pS_ = getA()
                op("pe", lambda e, pS_=pS_, osq=osq: e.matmul(pS_[:, :], lhsT=cst[:, BD64, :], rhs=osq[:, :], start=True, stop=True), r=[cst, osq], w=[pS_])
                op("act", lambda e, pS_=pS_, rt=rt: e.activation(out=rt.d, in_=pS_[:, :], func=AF.Ln, bias=pcol("eps"), scale=1.0), r=[pv], w=[rt.b, pS_])
                op("act", lambda e, rt=rt: e.activation(out=rt.d, in_=rt.d, func=AF.Exp, scale=-0.5), r=[], w=[rt.b])
                op("dve", lambda e, p=p, o=o, rt=rt: e.scalar_tensor_tensor(out=o.d, in0=o.d, scalar=pcol(nwname, p), in1=rt.d, op0=ALU.mult, op1=ALU.mult),
                   r=[rt.b, pv], w=[o.b], k=2)
                op("pool", lambda e, p=p, o=o: e.tensor_tensor(out=yT[:, ycol0 + p, :], in0=o.d, in1=sgate[p].d, op=ALU.mult), r=[o.b, sgate[p].b], w=[yTc[ycol0 + p]], k=2)

        s2, w2 = wload(win_d[l, :, :, 1024:1536], [128, 8, 512])
        s6, w6 = wload(win_d[l, :, :, 3072:3584], [128, 8, 512])
        s3, w3 = wload(win_d[l, :, :, 1536:2048], [128, 8, 512])
        if l == 0:
            sch.dma("sp", tabs[:, :, :], tab_d[:, :, ti * T:(ti + 1) * T], w=[tabs], nbytes=2 << 20)
        qT = [getB(), getB()]
        kT = [getB(), getB()]
        for dst, cbase, tb in ((qT, 0, 0), (kT, 256, 2)):
            for p in range(2):
                pa = proj(s2, w2, cbase + p * 128, [hT], hk)
                pb = proj(s6, w6, cbase + p * 128, [hT], hk)
                t1, t2 = getF(), getF()
                op("dve", lambda e, p=p, tb=tb, t1=t1, pa=pa: e.tensor_tensor(out=t1.d, in0=pa[:, :], in1=tabs[:, p * 4 + tb, :], op=ALU.mult), r=[tabs], w=[t1.b, pa])
                op("dve", lambda e, p=p, tb=tb, t2=t2, pb=pb: e.tensor_tensor(out=t2.d, in0=pb[:, :], in1=tabs[:, p * 4 + tb + 1, :], op=ALU.mult), r=[tabs], w=[t2.b, pb])
                op("pool", lambda e, p=p, dst=dst, t1=t1, t2=t2: e.tensor_tensor(out=dst[p][:, :], in0=t1.d, in1=t2.d, op=ALU.add), r=[t1.b, t2.b], w=[dst[p]], k=2)
        for b_ in range(4):
            pg = getA()

            def fn_v(e, b_=b_, pg=pg):
                ins = None
                for kc in range(8):
                    ins = e.matmul(pg[:, 0:256], lhsT=hT[:, kc, b_ * 128:(b_ + 1) * 128], rhs=w3[:, kc, 0:256], start=(kc == 0), stop=(kc == 7))
                return ins

            op("pe", fn_v, r=[hT, s3], w=[pg], n=256, nmm=8)
            op("act", lambda e, b_=b_, pg=pg: e.activation(out=vtok[:, b_, :], in_=pg[:, 0:256], func=AF.Copy), r=[], w=[vtok, pg], n=256)
        sg = [getF(), getF()]
        for p in range(2):
            pa = proj(s3, w3, 256 + p * 128, [hT], hk)
            op("act", lambda e, p=p, pa=pa, sg=sg: e.activation(out=sg[p].d, in_=pa[:, :], func=AF.Silu), r=[], w=[sg[p].b, pa])
        gla("r", qT, kT, 128, 128, sg, "rnw%d" % l, 4, [g ** 128 for g in RET_GAMMA])

        s4, w4 = wload(win_d[l, :, :, 2048:2560], [128, 8, 512])
        s5, w5 = wload(win_d[l, :, :, 2560:3072], [128, 8, 512])
        qT = [getB(), getB()]
        kT = [getB(), getB()]
        for p in range(2):
            sig, ff, bb, en = getF(), getF(), getF(), getF()
            pa = proj(s4, w4, 256 + p * 128, [hT], hk)
            op("act", lambda e, sig=sig, pa=pa: e.activation(out=sig.d, in_=pa[:, :], func=AF.Sigmoid), r=[], w=[sig.b, pa])
            op("dve", lambda e, p=p, sig=sig, ff=ff: e.tensor_scalar(out=ff.d, in0=sig.d, scalar1=dcol(OML, l, p), scalar2=dcol(LB, l, p), op0=ALU.mult, op1=ALU.add),
               r=[sig.b, der], w=[ff.b])
            op("act", lambda e, ff=ff: e.activation(out=ff.d, in_=ff.d, func=AF.Ln), r=[], w=[ff.b])
            op("dve", lambda e, ff=ff, bb=bb: e.tensor_tensor_scan(out=bb.d, data0=rmask[:, :], data1=ff.d, initial=0.0, op0=ALU.mult, op1=ALU.add),
               r=[rmask, ff.b], w=[bb.b], k=2)
            op("pool", lambda e, p=p, sig=sig: e.tensor_scalar(out=sig.d, in0=sig.d, scalar1=dcol(NOML, l, p), scalar2=dcol(OML, l, p), op0=ALU.mult, op1=ALU.add),
               r=[der], w=[sig.b])
            op("act", lambda e, bb=bb, en=en: e.activation(out=en.d, in_=bb.d, func=AF.Exp, scale=-1.0), r=[bb.b], w=[en.b])
            op("dve", lambda e, p=p, sig=sig, en=en, kT=kT: e.tensor_tensor(out=kT[p][:, :], in0=sig.d, in1=en.d, op=ALU.mult), r=[sig.b, en.b], w=[kT[p]], k=2)
            op("act", lambda e, bb=bb, ff=ff: e.activation(out=ff.d, in_=bb.d, func=AF.Exp), r=[bb.b], w=[ff.b])
            op("dve", lambda e, p=p, ff=ff: e.tensor_copy(out=ebl[p][:, :], in_=ff.d.rearrange("p (c t) -> p c t", t=32)[:, :, 31]), r=[ff.b], w=[ebl[p]], n=16)
            pb = proj(s4, w4, p * 128, [hT], hk)
            op("act", lambda e, en=en, pb=pb: e.activation(out=en.d, in_=pb[:, :], func=AF.Silu), r=[], w=[en.b, pb])
            op("dve", lambda e, p=p, en=en, ff=ff, qT=qT: e.tensor_tensor(out=qT[p][:, :], in0=en.d, in1=ff.d, op=ALU.mult), r=[en.b, ff.b], w=[qT[p]], k=2)
        for b_ in range(8):
            pg = getA()

            def fn_v2(e, b_=b_, pg=pg):
                ins = None
                for kc in range(8):
                    ins = e.matmul(pg[0:64, 0:256], lhsT=hT[:, kc, b_ * 64:(b_ + 1) * 64], rhs=w5[:, kc, 0:256], start=(kc == 0), stop=(kc == 7))
                return ins

            op("pe", fn_v2, r=[hT, s5], w=[pg], n=256, nmm=8)
            op("act", lambda e, b_=b_, pg=pg: e.activation(out=vtok[0:64, b_, :], in_=pg[0:64, 0:256], func=AF.Copy), r=[], w=[vtok, pg], n=256)
        sg = [getF(), getF()]
        for p in range(2):
            pa = proj(s5, w5, 256 + p * 128, [hT], hk)
            op("act", lambda e, p=p, pa=pa, sg=sg: e.activation(out=sg[p].d, in_=pa[:, :], func=AF.Silu), r=[], w=[sg[p].b, pa])
        gla("h", qT, kT, 32, 64, sg, "hnw%d" % l, 6, None)

        for half in range(2):
            so, wo = wload(wout_d[l, :, :, half * 512:(half + 1) * 512], [128, 8, 512])
            for mq in range(4):
                m = half * 4 + mq
                pb = proj(so, wo, mq * 128, yTc, lambda kc: yT[:, kc, :])
                op("dve", lambda e, m=m, pb=pb: e.tensor_tensor(out=xr[:, m, :], in0=pb[:, :], in1=xr[:, m, :], op=ALU.add), r=[], w=[xr, pb])

        norm_to_hT(xr, "n2w%d" % l)
        for q in range(11):
            su, wu = wload(wup_d[l, :, :, q * 512:(q + 1) * 512], [128, 8, 512])
            for jj in range(2):
                j = 2 * q + jj
                outs = []
                for (half, col) in ((0, jj * 128), (1, 256 + jj * 128)):
                    ch = half * 22 + j
                    ub, cb = getF(), getF()
                    pb = proj(su, wu, col, [hT], hk)
                    fhb = fh[(l, ch)]
                    op("pool", lambda e, ub=ub, ch=ch: e.tensor_copy(out=ub.b[:, 2:4], in_=fh_t[l][:, ch, :]), r=[fhb], w=[ub.h], n=2)
                    op("act", lambda e, ub=ub, pb=pb: e.activation(out=ub.d, in_=pb[:, :], func=AF.Copy), r=[], w=[ub.b, pb])
                    op("pool", lambda e, ub=ub, ch=ch: e.tensor_copy(out=fh_t[l][:, ch, :], in_=ub.b[:, T + 2:T + 4]), r=[ub.b], w=[fhb], n=2)
                    op("dve", lambda e, ub=ub, cb=cb, ch=ch: e.tensor_scalar(out=cb.d, in0=ub.d, scalar1=pcol("fcw%d_2" % l, ch), scalar2=pcol("fcb%d" % l, ch),
                                                                              op0=ALU.mult, op1=ALU.add), r=[ub.b, pv], w=[cb.b])
                    for k in (1, 0):
                        op("dve", lambda e, ub=ub, cb=cb, ch=ch, k=k: e.scalar_tensor_tensor(out=cb.d, in0=ub.b[:, 2 + k:2 + k + T], scalar=pcol("fcw%d_%d" % (l, k), ch), in1=cb.d,
                                                                                              op0=ALU.mult, op1=ALU.add), r=[ub.b, ub.h, pv], w=[cb.b], k=2)
                    outs.append(cb)
                cg, cv = outs
                op("act", lambda e, cg=cg: e.activation(out=cg.d, in_=cg.d, func=AF.Silu), r=[], w=[cg.b])
                op("pool", lambda e, cg=cg, cv=cv, j=j: e.tensor_tensor(out=gT[:, j, :], in0=cg.d, in1=cv.d, op=ALU.mult), r=[cg.b, cv.b], w=[gTc[j]], k=2)
        for m in range(8):
            sd, wd = wload(wdn_d[l, m], [128, 22, 128])
            pb = getA()

            def fn_d(e, pb=pb, wd=wd):
                ins = None
                for kc in range(22):
                    ins = e.matmul(pb[:, :], lhsT=wd[:, kc, :], rhs=gT[:, kc, :], start=(kc == 0), stop=(kc == 21))
                return ins

            op("pe", fn_d, r=[sd] + gTc, w=[pb], nmm=22)
            op("dve", lambda e, m=m, pb=pb: e.tensor_tensor(out=xr[:, m, :], in0=pb[:, :], in1=xr[:, m, :], op=ALU.add), r=[], w=[xr, pb])

    xv = xT_d.rearrange("(c p) s -> p c s", p=128)
    yv = yT_d.rearrange("(c p) s -> p c s", p=128)
    out_toks = []
    for ti in range(NT):
        xr = xres[ti % 2]
        sch.dma("sp", xr[:, :, :], xv[:, :, ti * T:(ti + 1) * T], w=[xr], nbytes=2 << 20)
        for l in range(nlayers):
            emit_layer(l, xr, ti)
        rs = rms_rstd(xr)
        for c in range(8):
            op("dve", lambda e, c=c, xr=xr, rs=rs: e.scalar_tensor_tensor(out=xr[:, c, :], in0=xr[:, c, :], scalar=pcol("fnw", c), in1=rs.d,
                                                                           op0=ALU.mult, op1=ALU.mult), r=[rs.b, pv], w=[xr], k=2)
        out_toks.append(sch.dma("sp", yv[:, :, ti * T:(ti + 1) * T], xr[:, :, :], r=[xr], nbytes=2 << 20))
    sch.wait_all_on("sp", out_toks)
    sch.emit()
    return nc, sch


def _chunks(v):
    v = np.asarray(v, np.float32)
    return np.ascontiguousarray(v.reshape(-1, 128).T)


def make_tables(S):
    t = np.arange(S, dtype=np.float64)
    inv = 10000.0 ** (-np.arange(0, 64, 2, dtype=np.float64) / 64.0)
    ang = t[None, :] * inv[:, None]
    cos = np.concatenate([np.cos(ang), np.cos(ang)], 0)
    sin = np.concatenate([-np.sin(ang), np.sin(ang)], 0)
    n1 = (t % 128) + 1.0
    tabs = np.zeros((128, 8, S), np.float64)
    for p in range(2):
        for hh in range(2):
            h = 2 * p + hh
            lg = math.log(RET_GAMMA[h])
            qd = np.exp(n1 * lg)[None, :]
            kd = np.exp(-n1 * lg)[None, :] * (64.0 ** -0.5)
            sl = slice(hh * 64, hh * 64 + 64)
            tabs[sl, p * 4 + 0] = cos * qd
            tabs[sl, p * 4 + 1] = sin * qd
            tabs[sl, p * 4 + 2] = cos * kd
            tabs[sl, p * 4 + 3] = sin * kd
    return tabs.astype(np.float32)


def make_consts():
    cst = np.zeros((128, 5, 128), np.float32)
    cst[:, 0, :] = 1.0
    cst[:, 1, :] = np.eye(128, dtype=np.float32)
    for b in range(2):
        cst[b * 64:(b + 1) * 64, 2, b * 64:(b + 1) * 64] = 1.0 / 64.0
    m = np.arange(128)[:, None]
    n = np.arange(128)[None, :]
    cst[:, 3, :] = (n >= m).astype(np.float32)
    m32 = (np.arange(32)[None, :] >= np.arange(32)[:, None]).astype(np.float32)
    for b in range(4):
        cst[b * 32:(b + 1) * 32, 4, 0:32] = m32
    rmask = np.ones((128, T), np.float32)
    rmask[:, ::32] = 0.0
    return cst, rmask


def prep_shared(inp, S):
    L = DEPTH
    w_in = np.asarray(inp["w_in"], np.float32)
    swap = lambda base: [base + h * 64 + ((d + 32) % 64) for h in range(4) for d in range(64)]
    idx = list(range(3072)) + swap(1024) + swap(1280)
    win = w_in[:, :, idx].reshape(L, 8, 128, NCOL_IN).transpose(0, 2, 1, 3)
    wout = np.asarray(inp["w_out"], np.float32).reshape(L, 8, 128, D).transpose(0, 2, 1, 3)
    perm = []
    for q in range(11):
        perm += list(range(2 * q * 128, (2 * q + 2) * 128))
        perm += list(range(DFF + 2 * q * 128, DFF + (2 * q + 2) * 128))
    wup = np.asarray(inp["ffn_w_up"], np.float32)[:, :, perm].reshape(L, 8, 128, 2 * DFF).transpose(0, 2, 1, 3)
    wdn = np.asarray(inp["ffn_w_down"], np.float32).reshape(L, 22, 128, 8, 128).transpose(0, 3, 2, 1, 4)
    gw = np.zeros((L, 128, 8, 128), np.float32)
    wa = np.asarray(inp["lru_wa"], np.float32)
    wx = np.asarray(inp["lru_wx"], np.float32)
    for l in range(L):
        for c in range(4):
            for b2 in range(2):
                sl = slice(b2 * 64, b2 * 64 + 64)
                gw[l, sl, c * 2 + 0, sl] = wa[l, 2 * c + b2]
                gw[l, sl, c * 2 + 1, sl] = wx[l, 2 * c + b2]
    pvec = np.zeros((128, PV["_n"]), np.float32)

    def put(name, v):
        a = _chunks(v)
        pvec[:, PV[name]:PV[name] + a.shape[1]] = a

    for l in range(L):
        put("n1w%d" % l, inp["norm1_w"][l])
        put("n2w%d" % l, inp["norm2_w"][l])
        for k in range(4):
            put("lcw%d_%d" % (l, k), inp["lru_conv_w"][l][k])
        put("lcb%d" % l, inp["lru_conv_b"][l])
        put("ba%d" % l, inp["lru_ba"][l])
        put("bx%d" % l, inp["lru_bx"][l])
        put("lam%d" % l, inp["lru_lambda"][l])
        put("rnw%d" % l, inp["ret_norm_w"][l])
        put("hnw%d" % l, inp["hg_norm_w"][l])
        put("hb%d" % l, inp["hg_lower_bounds"][l])
        for k in range(3):
            put("fcw%d_%d" % (l, k), inp["ffn_conv_w"][l][k])
        put("fcb%d" % l, inp["ffn_conv_b"][l])
    put("fnw", inp["final_norm_w"])
    pvec[:, PV["eps"]] = EPS
    pvec[:, PV["one"]] = 1.0
    cst, rmask = make_consts()
    return {
        "win": np.ascontiguousarray(win), "wout": np.ascontiguousarray(wout),
        "wup": np.ascontiguousarray(wup), "wdn": np.ascontiguousarray(wdn),
        "gatew": gw, "pvec": pvec, "tabs": make_tables(S), "cst": cst, "rmask": rmask,
    }


_CACHE = {}


def kernel(**inputs):
    x = np.asarray(inputs["x"], np.float32)
    B, S, _ = x.shape
    NT = S // T
    shared = prep_shared(inputs, S)
    if NT not in _CACHE:
        _CACHE[NT] = build(NT)
    nc, sch = _CACHE[NT]
    ncores = 8
    in_maps = []
    for c in range(ncores):
        m = dict(shared)
        m["xT"] = np.ascontiguousarray(x[c % B].T)
        in_maps.append(m)
    res = run_bass_kernel_spmd(nc, in_maps, core_ids=list(range(ncores)))
    out = np.empty((B, S, D), np.float32)
    for b in range(B):
        out[b] = res.results[b]["yT"].T
    return out
```

```python
import contextlib
import math
import numpy as np
import concourse.bass as bass
import concourse.mybir as mybir
from concourse.bass_utils import run_bass_kernel_spmd

F32 = mybir.dt.float32
BF16 = mybir.dt.bfloat16
AF = mybir.ActivationFunctionType
ALU = mybir.AluOpType

EPOCH = 4000

D = 1024
SEQ = 8192
BATCH = 4
DEPTH = 2
NL = 1
SKEW = 2
T = 512
DFF = 2816
NFC = 44
EPS = 1e-6
NCOL_IN = 3584


import heapq


class Buf:
    def __init__(self, t, name):
        self.t = t
        self.name = name
        self.last_write = None
        self.reads = []
        self.dma_cnt = 0
        self.g_lw = None
        self.g_rd = []

    def __getitem__(self, key):
        return self.t[key]

    def sub(self, name):
        return Buf(self.t, name)


class _Op:
    __slots__ = ("eng", "fn", "r", "w", "occ", "lat", "kind", "sem_buf", "preds", "succs", "inc")


class Sched:
    ENG = ("pe", "act", "dve", "pool", "sp")

    def __init__(self, nc):
        self.nc = nc
        self.stack = contextlib.ExitStack()
        self.rec = []
        self.ops = {e: [] for e in self.ENG}
        self.cnt = {e: 0 for e in self.ENG}
        self.epoch = {e: 0 for e in self.ENG}
        self.sems = {}
        self.waited = {e: {} for e in self.ENG}
        self.nsem = 0
        self.final_waits = []
        self.sim_time = 0.0

    def sem(self, key):
        if key not in self.sems:
            self.sems[key] = self.stack.enter_context(self.nc.semaphore("s%d" % self.nsem))
            self.nsem += 1
        return self.sems[key]

    def sb(self, name, shape, dtype=F32):
        t = self.stack.enter_context(self.nc.sbuf_tensor(name, list(shape), dtype))
        return Buf(t, name)

    def ps(self, name, shape, dtype=F32):
        t = self.stack.enter_context(self.nc.psum_tensor(name, list(shape), dtype))
        return Buf(t, name)

    def _record(self, o):
        i = len(self.rec)
        preds = set()
        for b in o.r:
            if b.g_lw is not None:
                preds.add(b.g_lw)
        for b in o.w:
            if b.g_lw is not None:
                preds.add(b.g_lw)
            preds.update(b.g_rd)
        preds.discard(i)
        o.preds = preds
        o.succs = []
        for p in preds:
            self.rec[p].succs.append(i)
        for b in o.w:
            b.g_lw = i
            b.g_rd = []
        for b in o.r:
            if b not in o.w:
                b.g_rd.append(i)
        self.rec.append(o)
        return i

    def op(self, eng, fn, r=(), w=(), n=T, k=1.0, nmm=1):
        o = _Op()
        o.eng, o.fn, o.r, o.w, o.kind, o.sem_buf = eng, fn, tuple(r), tuple(w), "c", None
        if eng == "pe":
            o.occ = nmm * (0.19 + 0.0002 * n)
        elif eng == "act":
            o.occ = 0.22 + 0.00072 * n
        elif eng == "dve":
            o.occ = 0.07 + 0.00105 * n * k
        else:
            o.occ = 0.15 + 0.0017 * n * k
        o.lat = o.occ + 0.1
        return self._record(o)

    def dma(self, q, out_ap, in_ap, r=(), w=(), sem_buf=None, nbytes=1 << 20):
        o = _Op()
        o.eng, o.r, o.w, o.kind = q, tuple(r), tuple(w), "d"
        o.sem_buf = sem_buf if sem_buf is not None else (w[0] if len(w) else r[0])

        def fn(e, out_ap=out_ap, in_ap=in_ap):
            return e.dma_start(out=out_ap, in_=in_ap)

        o.fn = fn
        o.occ = 1.0 if q == "pool" else 0.15
        o.lat = o.occ + 2.0 + nbytes / 150e3
        o.inc = 16
        return self._record(o)

    def raw(self, q, fn, r=(), w=(), sem_buf=None, inc=1, occ=1.0, lat=50.0):
        o = _Op()
        o.eng, o.r, o.w, o.kind = q, tuple(r), tuple(w), "d"
        o.sem_buf = sem_buf
        o.fn = fn
        o.occ, o.lat = occ, lat
        o.inc = inc
        return self._record(o)

    def wait_all_on(self, eng, idxs):
        self.final_waits.append((eng, list(idxs)))

    def _schedule(self):
        ops = self.rec
        ENG = self.ENG
        ready = {e: [] for e in ENG}
        busy = {e: 0.0 for e in ENG}
        npred = [len(o.preds) for o in ops]
        for i, o in enumerate(ops):
            if npred[i] == 0:
                heapq.heappush(ready[o.eng], i)
        events = []
        now = 0.0
        order = []
        while True:
            for e in ENG:
                if ready[e] and busy[e] <= now + 1e-9:
                    i = heapq.heappop(ready[e])
                    o = ops[i]
                    order.append(i)
                    busy[e] = now + o.occ
                    heapq.heappush(events, (now + o.lat, 1, i))
                    heapq.heappush(events, (now + o.occ, 0, i))
            if not events:
                break
            t, typ, i = heapq.heappop(events)
            now = t
            if typ == 1:
                for s_ in ops[i].succs:
                    npred[s_] -= 1
                    if npred[s_] == 0:
                        heapq.heappush(ready[ops[s_].eng], s_)
        assert len(order) == len(ops), (len(order), len(ops))
        self.sim_time = now
        return order

    def _deps(self, r, w):
        deps = []
        for b in r:
            if b.last_write is not None:
                deps.append(b.last_write)
        for b in w:
            if b.last_write is not None:
                deps.append(b.last_write)
            deps.extend(b.reads)
        return deps

    def _waits(self, eng, deps):
        need = {}
        wd = self.waited[eng]
        for (k, v) in deps:
            if wd.get(k, 0) >= v:
                continue
            if need.get(k, 0) < v:
                need[k] = v
        for k, v in need.items():
            wd[k] = v
        return [(self.sem(k), v) for k, v in need.items()]

    def _commit(self, r, w, tok):
        for b in w:
            b.last_write = tok
            b.reads = []
        for b in r:
            if b in w:
                continue
            b.reads = [x for x in b.reads if x[0] != tok[0]] + [tok]

    def _replay_one(self, o):
        eng = o.eng
        deps = self._deps(o.r, o.w)
        if o.kind == "c":
            if eng == "pe":
                deps = [d for d in deps if not (d[0][0] == "e" and d[0][1] == "pe")]
            waits = self._waits(eng, deps)
            if self.cnt[eng] >= EPOCH:
                self.epoch[eng] += 1
                self.cnt[eng] = 0
            self.cnt[eng] += 1
            key = ("e", eng, self.epoch[eng])
            tok = (key, self.cnt[eng])
            self.ops[eng].append((o.fn, waits, self.sem(key), 1))
        else:
            waits = self._waits(eng, deps)
            sb_ = o.sem_buf
            key = ("d", id(sb_))
            inc = getattr(o, "inc", 16)
            sb_.dma_cnt += inc
            tok = (key, sb_.dma_cnt)
            self.ops[eng].append((o.fn, waits, self.sem(key), inc))
        self._commit(o.r, o.w, tok)
        return tok

    def emit(self):
        nc = self.nc
        order = self._schedule()
        toks = {}
        for i in order:
            toks[i] = self._replay_one(self.rec[i])
        for eng, idxs in self.final_waits:
            waits = self._waits(eng, [toks[i] for i in idxs])
            self.ops[eng].append((None, waits, None, 0))

        def replay(engobj, lst):
            for (fn, waits, s, inc) in lst:
                for (ws, v) in waits:
                    engobj.wait_ge(ws, v)
                if fn is not None:
                    fn(engobj).then_inc(s, inc)

        with nc.Block() as block:
            @block.tensor
            def _(e):
                replay(e, self.ops["pe"])

            @block.scalar
            def _(e):
                replay(e, self.ops["act"])

            @block.vector
            def _(e):
                replay(e, self.ops["dve"])

            @block.gpsimd
            def _(e):
                replay(e, self.ops["pool"])

            @block.sync
            def _(e):
                replay(e, self.ops["sp"])

    def close(self):
        self.stack.close()


def _pvec_layout():
    lay = {}
    col = 0

    def add(name, n):
        nonlocal col
        lay[name] = col
        col += n

    for l in range(NL):
        add("n1w%d" % l, 8)
        add("n2w%d" % l, 8)
        for k in range(4):
            add("lcw%d_%d" % (l, k), 4)
        add("lcb%d" % l, 4)
        add("ba%d" % l, 4)
        add("bx%d" % l, 4)
        add("lam%d" % l, 4)
        add("rnw%d" % l, 2)
        add("hnw%d" % l, 2)
        for k in range(3):
            add("fcw%d_%d" % (l, k), NFC)
        add("fcb%d" % l, NFC)
    add("fnw", 8)
    add("hbA", 2)
    add("hbB", 2)
    add("m", 1)
    add("om", 1)
    add("keep", 1)
    add("lbm", 1)
    add("eps", 1)
    add("one", 1)
    lay["_n"] = col
    return lay


PV = _pvec_layout()
RET_GAMMA = [1.0 - 2.0 ** (-5.0 - h) for h in range(4)]


def build(NT, nlayers=NL, debug_out=None):
    NS = NT + SKEW
    S = NS * T
    nc = bass.Bass("TRN2", target_bir_lowering=False)
    xT_d = nc.dram_tensor("xT", [D, S], F32, kind="ExternalInput").ap()
    win_d = nc.dram_tensor("win", [NL, 128, 8, NCOL_IN], F32, kind="ExternalInput").ap()
    wout_d = nc.dram_tensor("wout", [NL, 128, 8, D], F32, kind="ExternalInput").ap()
    wup_d = nc.dram_tensor("wup", [NL, 128, 8, 2 * DFF], F32, kind="ExternalInput").ap()
    wdn_d = nc.dram_tensor("wdn", [NL, 8, 128, 22, 128], F32, kind="ExternalInput").ap()
    gw_d = nc.dram_tensor("gatew", [NL, 128, 8, 128], F32, kind="ExternalInput").ap()
    pv_d = nc.dram_tensor("pvec", [128, PV["_n"]], F32, kind="ExternalInput").ap()
    tab_d = nc.dram_tensor("tabs", [128, 8, S], F32, kind="ExternalInput").ap()
    cst_d = nc.dram_tensor("cst", [128, 5, 128], F32, kind="ExternalInput").ap()
    rm_d = nc.dram_tensor("rmask", [128, T], F32, kind="ExternalInput").ap()
    yT_d = nc.dram_tensor("yT", [D, S], F32, kind="ExternalOutput").ap()
    snd_t = nc.dram_tensor("snd", [D, T], F32)
    rcv_t = [nc.dram_tensor("rcv%d" % i, [2 * D, T], F32) for i in range(2)]
    GROUPS = [[0, 1], [2, 3], [4, 5], [6, 7]]

    sch = Sched(nc)
    sb, ps = sch.sb, sch.ps
    op = sch.op

    pv = sb("pv", [128, PV["_n"]])
    sch.dma("sp", pv[:, :], pv_d[:, :], w=[pv], nbytes=1 << 18)
    cstf = sb("cstf", [128, 5, 128])
    sch.dma("sp", cstf[:, :, :], cst_d[:, :, :], w=[cstf], nbytes=1 << 18)
    cst = sb("cstb", [128, 5, 128], BF16)
    op("dve", lambda e: e.tensor_copy(out=cst[:, :, :], in_=cstf[:, :, :]), r=[cstf], w=[cst], n=640)
    ONES, IDENT, BD64, MASK128, MASK32 = range(5)
    rmask = sb("rmask_s", [128, T])
    sch.dma("sp", rmask[:, :], rm_d[:, :], w=[rmask], nbytes=1 << 18)
    gwf = sb("gwf", [128, NL * 8, 128])
    gwb = sb("gwb", [128, NL * 8, 128], BF16)
    for l in range(NL):
        sch.dma("sp", gwf[:, l * 8:(l + 1) * 8, :], gw_d[l], w=[gwf], nbytes=1 << 19)
    op("dve", lambda e: e.tensor_copy(out=gwb[:, :, :], in_=gwf[:, :, :]), r=[gwf], w=[gwb], n=2048)

    def pcol(name, c=0, lo=0, hi=128):
        k = PV[name] + c
        return pv[lo:hi, k:k + 1]

    der = sb("der", [128, 32])
    for l in range(NL):
        k = PV["lam%d" % l]
        op("act", lambda e, k=k, l=l: e.activation(out=der[:, l * 4:l * 4 + 4], in_=pv[:, k:k + 4], func=AF.Exp, scale=-1.0), r=[pv], w=[der], n=4)
        op("act", lambda e, l=l: e.activation(out=der[:, l * 4:l * 4 + 4], in_=der[:, l * 4:l * 4 + 4], func=AF.Ln, bias=pcol("one"), scale=1.0), r=[der, pv], w=[der], n=4)
        op("dve", lambda e, l=l: e.tensor_scalar(out=der[:, l * 4:l * 4 + 4], in0=der[:, l * 4:l * 4 + 4], scalar1=-8.0, scalar2=None, op0=ALU.mult), r=[der], w=[der], n=4)
    LB, OML, NOML = 8, 12, 16
    kA, kB = PV["hbA"], PV["hbB"]
    op("dve", lambda e: e.tensor_tensor(out=der[:, 20:22], in0=pv[:, kB:kB + 2], in1=pv[:, kA:kA + 2], op=ALU.subtract), r=[pv, der], w=[der], n=2)
    op("act", lambda e: e.activation(out=der[:, 22:24], in_=der[:, 20:22], func=AF.Sigmoid), r=[der], w=[der], n=2)
    op("dve", lambda e: e.tensor_scalar(out=der[:, LB:LB + 2], in0=der[:, 22:24], scalar1=pcol("lbm"), scalar2=None, op0=ALU.mult), r=[pv], w=[der], n=2)
    op("dve", lambda e: e.tensor_scalar(out=der[:, OML:OML + 2], in0=der[:, LB:LB + 2], scalar1=-1.0, scalar2=1.0, op0=ALU.mult, op1=ALU.add), r=[], w=[der], n=2)
    op("dve", lambda e: e.tensor_scalar(out=der[:, NOML:NOML + 2], in0=der[:, OML:OML + 2], scalar1=-1.0, scalar2=None, op0=ALU.mult), r=[], w=[der], n=2)

    def dcol(base, l, c, lo=0, hi=128):
        k = base + l * (4 if base == 0 else 2) + c
        return der[lo:hi, k:k + 1]

    hst_t = [sb("hst%d" % l, [128, 4]) for l in range(NL)]
    lxh_t = [sb("lxh%d" % l, [128, 4, 4]) for l in range(NL)]
    fh_t = [sb("fh%d" % l, [128, NFC, 2]) for l in range(NL)]
    hst = {}
    lxh = {}
    fh = {}
    st_f = {}
    st_b = {}
    for l in range(NL):
        for c in range(4):
            hst[(l, c)] = hst_t[l].sub("hst%d_%d" % (l, c))
            lxh[(l, c)] = lxh_t[l].sub("lxh%d_%d" % (l, c))
        for ch in range(NFC):
            fh[(l, ch)] = fh_t[l].sub("fh%d_%d" % (l, ch))
        op("dve", lambda e, l=l: e.memset(hst_t[l][:, :], 0.0), w=[hst[(l, c)] for c in range(4)], n=4)
        op("dve", lambda e, l=l: e.memset(lxh_t[l][:, :, :], 0.0), w=[lxh[(l, c)] for c in range(4)], n=16)
        op("dve", lambda e, l=l: e.memset(fh_t[l][:, :, :], 0.0), w=[fh[(l, ch)] for ch in range(NFC)], n=88)
        for mix in ("r", "h"):
            for p in range(2):
                tf = sb("stf_%s%d%d" % (mix, l, p), [128, 64])
                tb = sb("stb_%s%d%d" % (mix, l, p), [128, 64], BF16)
                for hh in range(2):
                    bf_ = tf.sub("stf_%s%d%d%d" % (mix, l, p, hh))
                    bb_ = tb.sub("stb_%s%d%d%d" % (mix, l, p, hh))
                    st_f[(mix, l, p, hh)] = bf_
                    st_b[(mix, l, p, hh)] = bb_
                    lo = hh * 64
                    op("dve", lambda e, bf_=bf_, lo=lo: e.memset(bf_[lo:lo + 64, :], 0.0), w=[bf_], n=64)
                    op("dve", lambda e, bb_=bb_, lo=lo: e.memset(bb_[lo:lo + 64, :], 0.0), w=[bb_], n=64)

    xres = [sb("xres%d" % i, [128, 8, T]) for i in range(2)]
    rv = sb("rv", [128, 8, T])
    tabs = sb("tabs_s", [128, 8, T])
    hT = sb("hT", [128, 8, T], BF16)
    yT = sb("yTs", [128, 8, T], BF16)
    yTc = [yT.sub("yT%d" % c) for c in range(8)]
    gT = sb("gT", [128, 22, T], BF16)
    gTc = [gT.sub("gT%d" % c) for c in range(22)]
    NSLOT = 4
    wsl = [sb("wsl%d" % i, [128, 4096], BF16) for i in range(NSLOT)]

    class FB:
        def __init__(self, name):
            self.b = sb(name, [128, T + 4])
            self.h = self.b.sub(name + "_h")

        @property
        def d(self):
            return self.b[:, 4:4 + T]

    NF = 14
    Fring = [FB("F%d" % i) for i in range(NF)]
    fstate = {"f": 0, "b": 0, "a": 0, "x": 0, "at": 0}

    def getF():
        f = Fring[fstate["f"] % NF]
        fstate["f"] += 1
        return f

    NB = 8
    Bring = [sb("B%d" % i, [128, T], BF16) for i in range(NB)]

    def getB():
        b = Bring[fstate["b"] % NB]
        fstate["b"] += 1
        return b

    vtok = sb("vtok", [128, 8, 256], BF16)
    ktok = sb("ktok", [128, 8, 2, 128], BF16)
    NAT = 4
    ATb = [sb("AT%d" % i, [128, 128], BF16) for i in range(NAT)]
    ebl = [sb("ebl%d" % p, [128, 16]) for p in range(2)]
    stmp = [sb("stmp%d" % i, [128, 64]) for i in range(4)]

    NACC = 3
    acc = [ps("acc%d" % i, [128, T]) for i in range(NACC)]
    pO = [ps("pO%d" % p, [128, T]) for p in range(2)]
    pX = [ps("pX%d" % i, [128, T]) for i in range(2)]
    pUT = ps("pUT", [128, T])
    pTr = pUT[:, 256:512].bitcast(BF16)

    def getA():
        a = acc[fstate["a"] % NACC]
        fstate["a"] += 1
        return a

    wstate = {"n": 0}

    def wload(src_ap, shape):
        s = wsl[wstate["n"] % NSLOT]
        wstate["n"] += 1
        n = 1
        for d_ in shape[1:]:
            n *= d_
        view = s[:, 0:n].rearrange("p (a b) -> p a b", b=shape[2])
        sch.dma("pool", view, src_ap, w=[s], nbytes=128 * n * 4)
        return s, view

    def proj(slot, wview, col, rhs_bufs, rhs_of_kc, nk=8, m=128, ncols=T):
        pbuf = getA()

        def fn(e):
            ins = None
            for kc in range(nk):
                ins = e.matmul(pbuf[0:m, 0:ncols], lhsT=wview[:, kc, col:col + m], rhs=rhs_of_kc(kc), start=(kc == 0), stop=(kc == nk - 1))
            return ins

        op("pe", fn, r=[slot] + list(rhs_bufs), w=[pbuf], n=ncols, nmm=nk)
        return pbuf

    def rms_rstd(xr):
        op("act", lambda e: e.activation(out=gT[:, 0:8, :], in_=xr[:, :, :], func=AF.Square), r=[xr], w=gTc[0:8], n=8 * T)
        pS_ = getA()

        def fn(e):
            ins = None
            for c in range(8):
                ins = e.matmul(pS_[:, :], lhsT=cst[:, ONES, :], rhs=gT[:, c, :], start=(c == 0), stop=(c == 7))
            return ins

        op("pe", fn, r=[cst] + gTc[0:8], w=[pS_], nmm=8)
        rs = getF()
        op("act", lambda e: e.activation(out=rs.d, in_=pS_[:, :], func=AF.Ln, bias=pcol("eps"), scale=1.0 / D), r=[pv], w=[rs.b, pS_])
        op("act", lambda e: e.activation(out=rs.d, in_=rs.d, func=AF.Exp, scale=-0.5), r=[], w=[rs.b])
        return rs

    def norm_to_hT(xr, nwname):
        rs = rms_rstd(xr)
        for c in range(8):
            op("dve", lambda e, c=c: e.scalar_tensor_tensor(out=hT[:, c, :], in0=xr[:, c, :], scalar=pcol(nwname, c), in1=rs.d,
                                                              op0=ALU.mult, op1=ALU.mult), r=[xr, rs.b, pv], w=[hT], k=2)

    hk = lambda kc: hT[:, kc, :]

    def emit_layer(l, xr, ti):
        norm_to_hT(xr, "n1w%d" % l)

        s0, w0 = wload(win_d[l, :, :, 0:512], [128, 8, 512])
        s1, w1 = wload(win_d[l, :, :, 512:1024], [128, 8, 512])
        for c in range(4):
            lxb, xc, rr, ii, aa, mm = getF(), getF(), getF(), getF(), getF(), getF()
            xcb = getB()
            pa = proj(s0, w0, c * 128, [hT], hk)
            op("pool", lambda e, c=c, lxb=lxb: e.tensor_copy(out=lxb.b[:, 1:4], in_=lxh_t[l][:, c, 0:3]), r=[lxh[(l, c)]], w=[lxb.h], n=3)
            op("act", lambda e, lxb=lxb, pa=pa: e.activation(out=lxb.d, in_=pa[:, :], func=AF.Copy), r=[], w=[lxb.b, pa])
            op("pool", lambda e, c=c, lxb=lxb: e.tensor_copy(out=lxh_t[l][:, c, 0:3], in_=lxb.b[:, T + 1:T + 4]), r=[lxb.b], w=[lxh[(l, c)]], n=3)
            op("dve", lambda e, c=c, lxb=lxb, xc=xc: e.tensor_scalar(out=xc.d, in0=lxb.b[:, 1:1 + T], scalar1=pcol("lcw%d_0" % l, c), scalar2=pcol("lcb%d" % l, c),
                                                                      op0=ALU.mult, op1=ALU.add), r=[lxb.b, lxb.h, pv], w=[xc.b])
            for k in range(1, 4):
                op("dve", lambda e, c=c, k=k, lxb=lxb, xc=xc: e.scalar_tensor_tensor(out=xc.d, in0=lxb.b[:, 1 + k:1 + k + T], scalar=pcol("lcw%d_%d" % (l, k), c), in1=xc.d,
                                                                                      op0=ALU.mult, op1=ALU.add), r=[lxb.b, lxb.h, pv], w=[xc.b], k=2)
            op("act", lambda e, xc=xc, xcb=xcb: e.activation(out=xcb[:, :], in_=xc.d, func=AF.Copy), r=[xc.b], w=[xcb])
            pg = getA()
            op("pe", lambda e, c=c, pg=pg, xcb=xcb: e.matmul(pg[:, :], lhsT=gwb[:, l * 8 + c * 2, :], rhs=xcb[:, :], start=True, stop=True), r=[gwb, xcb], w=[pg])
            op("act", lambda e, c=c, pg=pg, rr=rr: e.activation(out=rr.d, in_=pg[:, :], func=AF.Sigmoid, bias=pcol("ba%d" % l, c)), r=[pv], w=[rr.b, pg])
            pg2 = getA()
            op("pe", lambda e, c=c, pg2=pg2, xcb=xcb: e.matmul(pg2[:, :], lhsT=gwb[:, l * 8 + c * 2 + 1, :], rhs=xcb[:, :], start=True, stop=True), r=[gwb, xcb], w=[pg2])
            op("act", lambda e, c=c, pg2=pg2, ii=ii: e.activation(out=ii.d, in_=pg2[:, :], func=AF.Sigmoid, bias=pcol("bx%d" % l, c)), r=[pv], w=[ii.b, pg2])
            op("act", lambda e, c=c, rr=rr, aa=aa: e.activation(out=aa.d, in_=rr.d, func=AF.Exp, scale=dcol(0, l, c)), r=[rr.b, der], w=[aa.b])
            op("act", lambda e, aa=aa, mm=mm: e.activation(out=mm.d, in_=aa.d, func=AF.Square), r=[aa.b], w=[mm.b])
            op("act", lambda e, mm=mm: e.activation(out=mm.d, in_=mm.d, func=AF.Sqrt, bias=pcol("one"), scale=-1.0), r=[pv], w=[mm.b])
            op("pool", lambda e, ii=ii, xc=xc: e.tensor_tensor(out=ii.d, in0=ii.d, in1=xc.d, op=ALU.mult), r=[xc.b], w=[ii.b], k=2)
            op("pool", lambda e, ii=ii, mm=mm: e.tensor_tensor(out=ii.d, in0=ii.d, in1=mm.d, op=ALU.mult), r=[mm.b], w=[ii.b], k=2)
            hs = hst[(l, c)]
            op("dve", lambda e, c=c, rr=rr, aa=aa, ii=ii: e.tensor_tensor_scan(out=rr.d, data0=aa.d, data1=ii.d, initial=hst_t[l][:, c:c + 1],
                                                                                op0=ALU.mult, op1=ALU.add), r=[aa.b, ii.b, hs], w=[rr.b], k=2)
            op("dve", lambda e, c=c, rr=rr: e.tensor_copy(out=hst_t[l][:, c:c + 1], in_=rr.b[:, T + 3:T + 4]), r=[rr.b], w=[hs], n=1)
            pb = proj(s1, w1, c * 128, [hT], hk)
            op("act", lambda e, mm=mm, pb=pb: e.activation(out=mm.d, in_=pb[:, :], func=AF.Gelu_apprx_tanh), r=[], w=[mm.b, pb])
            op("dve", lambda e, c=c, rr=rr, mm=mm: e.tensor_tensor(out=yT[:, c, :], in0=rr.d, in1=mm.d, op=ALU.mult), r=[rr.b, mm.b], w=[yTc[c]], k=2)

        def gla(mix, qT, kT, C, BLK, sgate, nwname, ycol0, decay_const):
            nblk = T // BLK
            nch = T // C
            msk = MASK128 if C == 128 else MASK32
            for p in range(2):
                for b_ in range(nblk):
                    op("pe", lambda e, p=p, b_=b_: e.transpose(pTr[0:BLK, 0:128], kT[p][:, b_ * BLK:(b_ + 1) * BLK], cst[:, IDENT, :]), r=[kT[p], cst], w=[pUT], n=128)
                    op("act", lambda e, p=p, b_=b_: e.activation(out=ktok[0:BLK, b_, p, :], in_=pTr[0:BLK, 0:128], func=AF.Copy), r=[], w=[ktok, pUT], n=128)
            for j in range(nch):
                blk = (j * C) // BLK
                base = (j * C) % BLK
                for h in range(4):
                    p, hh = h // 2, h % 2
                    off = hh * 64
                    px = pX[fstate["x"] % 2]
                    fstate["x"] += 1
                    at = ATb[fstate["at"] % NAT]
                    fstate["at"] += 1
                    Sf = st_f[(mix, l, p, hh)]
                    Sb = st_b[(mix, l, p, hh)]
                    tsl = slice(j * C, (j + 1) * C)
                    op("pe", lambda e, p=p, off=off, tsl=tsl, px=px, base=base: e.matmul(
                        px[base:base + C, 0:C], lhsT=kT[p][off:off + 64, tsl], rhs=qT[p][off:off + 64, tsl], start=True, stop=True),
                        r=[kT[p], qT[p]], w=[px], n=C)
                    op("dve", lambda e, px=px, at=at, base=base: e.tensor_tensor(
                        out=at[base:base + C, 0:C], in0=px[base:base + C, 0:C], in1=cst[base:base + C, msk, 0:C], op=ALU.mult),
                        r=[cst], w=[at, px], n=C)

                    def fn_o(e, p=p, off=off, tsl=tsl, at=at, base=base, blk=blk, h=h, Sb=Sb):
                        e.matmul(pO[p][off:off + 64, tsl], lhsT=vtok[base:base + C, blk, h * 64:(h + 1) * 64], rhs=at[base:base + C, 0:C], start=True, stop=False)
                        return e.matmul(pO[p][off:off + 64, tsl], lhsT=Sb[off:off + 64, :], rhs=qT[p][off:off + 64, tsl], start=False, stop=True)

                    op("pe", fn_o, r=[vtok, at, Sb, qT[p]], w=[pO[p]], n=C, nmm=2)
                    op("pe", lambda e, p=p, off=off, base=base, blk=blk, h=h: e.matmul(
                        pUT[off:off + 64, 0:64], lhsT=ktok[base:base + C, blk, p, off:off + 64], rhs=vtok[base:base + C, blk, h * 64:(h + 1) * 64],
                        start=True, stop=True), r=[ktok, vtok], w=[pUT], n=64)
                    if decay_const is not None:
                        g = decay_const[h]
                        op("dve", lambda e, Sf=Sf, off=off, g=g: e.scalar_tensor_tensor(
                            out=Sf[off:off + 64, :], in0=Sf[off:off + 64, :], scalar=g, in1=pUT[off:off + 64, 0:64], op0=ALU.mult, op1=ALU.add),
                            r=[], w=[Sf, pUT], n=64)
                        op("dve", lambda e, Sf=Sf, Sb=Sb, off=off, g=g: e.tensor_scalar(
                            out=Sb[off:off + 64, :], in0=Sf[off:off + 64, :], scalar1=g, scalar2=None, op0=ALU.mult), r=[Sf], w=[Sb], n=64)
                    else:
                        tmp = stmp[h]
                        op("dve", lambda e, Sf=Sf, off=off, tmp=tmp: e.tensor_tensor(
                            out=tmp[off:off + 64, :], in0=pUT[off:off + 64, 0:64], in1=Sf[off:off + 64, :], op=ALU.add), r=[Sf], w=[tmp, pUT], n=64)
                        op("dve", lambda e, Sf=Sf, off=off, tmp=tmp, p=p, j=j: e.tensor_scalar(
                            out=Sf[off:off + 64, :], in0=tmp[off:off + 64, :], scalar1=ebl[p][off:off + 64, j:j + 1], scalar2=None, op0=ALU.mult),
                            r=[tmp, ebl[p]], w=[Sf], n=64)
                        op("act", lambda e, Sf=Sf, Sb=Sb, off=off: e.activation(out=Sb[off:off + 64, :], in_=Sf[off:off + 64, :], func=AF.Copy), r=[Sf], w=[Sb], n=64)
            for p in range(2):
                o, rt = getF(), getF()
                osq = getB()
                op("act", lambda e, p=p, o=o: e.activation(out=o.d, in_=pO[p][:, :], func=AF.Copy), r=[], w=[o.b, pO[p]])
                op("act", lambda e, o=o, osq=osq: e.activation(out=osq[:, :], in_=o.d, func=AF.Square), r=[o.b], w=[osq])
                pS_ = getA()
                op("pe", lambda e, pS_=pS_, osq=osq: e.matmul(pS_[:, :], lhsT=cst[:, BD64, :], rhs=osq[:, :], start=True, stop=True), r=[cst, osq], w=[pS_])
                op("act", lambda e, pS_=pS_, rt=rt: e.activation(out=rt.d, in_=pS_[:, :], func=AF.Ln, bias=pcol("eps"), scale=1.0), r=[pv], w=[rt.b, pS_])
                op("act", lambda e, rt=rt: e.activation(out=rt.d, in_=rt.d, func=AF.Exp, scale=-0.5), r=[], w=[rt.b])
                op("dve", lambda e, p=p, o=o, rt=rt: e.scalar_tensor_tensor(out=o.d, in0=o.d, scalar=pcol(nwname, p), in1=rt.d, op0=ALU.mult, op1=ALU.mult),
                   r=[rt.b, pv], w=[o.b], k=2)
                op("pool", lambda e, p=p, o=o: e.tensor_tensor(out=yT[:, ycol0 + p, :], in0=o.d, in1=sgate[p].d, op=ALU.mult), r=[o.b, sgate[p].b], w=[yTc[ycol0 + p]], k=2)

        s2, w2 = wload(win_d[l, :, :, 1024:1536], [128, 8, 512])
        s6, w6 = wload(win_d[l, :, :, 3072:3584], [128, 8, 512])
        s3, w3 = wload(win_d[l, :, :, 1536:2048], [128, 8, 512])
        if l == 0:
            sch.dma("sp", tabs[:, :, :], tab_d[:, :, ti * T:(ti + 1) * T], w=[tabs], nbytes=2 << 20)
        qT = [getB(), getB()]
        kT = [getB(), getB()]
        for dst, cbase, tb in ((qT, 0, 0), (kT, 256, 2)):
            for p in range(2):
                pa = proj(s2, w2, cbase + p * 128, [hT], hk)
                pb = proj(s6, w6, cbase + p * 128, [hT], hk)
                t1, t2 = getF(), getF()
                op("dve", lambda e, p=p, tb=tb, t1=t1, pa=pa: e.tensor_tensor(out=t1.d, in0=pa[:, :], in1=tabs[:, p * 4 + tb, :], op=ALU.mult), r=[tabs], w=[t1.b, pa])
                op("dve", lambda e, p=p, tb=tb, t2=t2, pb=pb: e.tensor_tensor(out=t2.d, in0=pb[:, :], in1=tabs[:, p * 4 + tb + 1, :], op=ALU.mult), r=[tabs], w=[t2.b, pb])
                op("pool", lambda e, p=p, dst=dst, t1=t1, t2=t2: e.tensor_tensor(out=dst[p][:, :], in0=t1.d, in1=t2.d, op=ALU.add), r=[t1.b, t2.b], w=[dst[p]], k=2)
        for b_ in range(4):
            pg = getA()

            def fn_v(e, b_=b_, pg=pg):
                ins = None
                for kc in range(8):
                    ins = e.matmul(pg[:, 0:256], lhsT=hT[:, kc, b_ * 128:(b_ + 1) * 128], rhs=w3[:, kc, 0:256], start=(kc == 0), stop=(kc == 7))
                return ins

            op("pe", fn_v, r=[hT, s3], w=[pg], n=256, nmm=8)
            op("act", lambda e, b_=b_, pg=pg: e.activation(out=vtok[:, b_, :], in_=pg[:, 0:256], func=AF.Copy), r=[], w=[vtok, pg], n=256)
        sg = [getF(), getF()]
        for p in range(2):
            pa = proj(s3, w3, 256 + p * 128, [hT], hk)
            op("act", lambda e, p=p, pa=pa, sg=sg: e.activation(out=sg[p].d, in_=pa[:, :], func=AF.Silu), r=[], w=[sg[p].b, pa])
        gla("r", qT, kT, 128, 128, sg, "rnw%d" % l, 4, [g ** 128 for g in RET_GAMMA])

        s4, w4 = wload(win_d[l, :, :, 2048:2560], [128, 8, 512])
        s5, w5 = wload(win_d[l, :, :, 2560:3072], [128, 8, 512])
        qT = [getB(), getB()]
        kT = [getB(), getB()]
        for p in range(2):
            sig, ff, bb, en = getF(), getF(), getF(), getF()
            pa = proj(s4, w4, 256 + p * 128, [hT], hk)
            op("act", lambda e, sig=sig, pa=pa: e.activation(out=sig.d, in_=pa[:, :], func=AF.Sigmoid), r=[], w=[sig.b, pa])
            op("dve", lambda e, p=p, sig=sig, ff=ff: e.tensor_scalar(out=ff.d, in0=sig.d, scalar1=dcol(OML, l, p), scalar2=dcol(LB, l, p), op0=ALU.mult, op1=ALU.add),
               r=[sig.b, der], w=[ff.b])
            op("act", lambda e, ff=ff: e.activation(out=ff.d, in_=ff.d, func=AF.Ln), r=[], w=[ff.b])
            op("dve", lambda e, ff=ff, bb=bb: e.tensor_tensor_scan(out=bb.d, data0=rmask[:, :], data1=ff.d, initial=0.0, op0=ALU.mult, op1=ALU.add),
               r=[rmask, ff.b], w=[bb.b], k=2)
            op("pool", lambda e, p=p, sig=sig: e.tensor_scalar(out=sig.d, in0=sig.d, scalar1=dcol(NOML, l, p), scalar2=dcol(OML, l, p), op0=ALU.mult, op1=ALU.add),
               r=[der], w=[sig.b])
            op("act", lambda e, bb=bb, en=en: e.activation(out=en.d, in_=bb.d, func=AF.Exp, scale=-1.0), r=[bb.b], w=[en.b])
            op("dve", lambda e, p=p, sig=sig, en=en, kT=kT: e.tensor_tensor(out=kT[p][:, :], in0=sig.d, in1=en.d, op=ALU.mult), r=[sig.b, en.b], w=[kT[p]], k=2)
            op("act", lambda e, bb=bb, ff=ff: e.activation(out=ff.d, in_=bb.d, func=AF.Exp), r=[bb.b], w=[ff.b])
            op("dve", lambda e, p=p, ff=ff: e.tensor_copy(out=ebl[p][:, :], in_=ff.d.rearrange("p (c t) -> p c t", t=32)[:, :, 31]), r=[ff.b], w=[ebl[p]], n=16)
            pb = proj(s4, w4, p * 128, [hT], hk)
            op("act", lambda e, en=en, pb=pb: e.activation(out=en.d, in_=pb[:, :], func=AF.Silu), r=[], w=[en.b, pb])
            op("dve", lambda e, p=p, en=en, ff=ff, qT=qT: e.tensor_tensor(out=qT[p][:, :], in0=en.d, in1=ff.d, op=ALU.mult), r=[en.b, ff.b], w=[qT[p]], k=2)
        for b_ in range(8):
            pg = getA()

            def fn_v2(e, b_=b_, pg=pg):
                ins = None
                for kc in range(8):
                    ins = e.matmul(pg[0:64, 0:256], lhsT=hT[:, kc, b_ * 64:(b_ + 1) * 64], rhs=w5[:, kc, 0:256], start=(kc == 0), stop=(kc == 7))
                return ins

            op("pe", fn_v2, r=[hT, s5], w=[pg], n=256, nmm=8)
            op("act", lambda e, b_=b_, pg=pg: e.activation(out=vtok[0:64, b_, :], in_=pg[0:64, 0:256], func=AF.Copy), r=[], w=[vtok, pg], n=256)
        sg = [getF(), getF()]
        for p in range(2):
            pa = proj(s5, w5, 256 + p * 128, [hT], hk)
            op("act", lambda e, p=p, pa=pa, sg=sg: e.activation(out=sg[p].d, in_=pa[:, :], func=AF.Silu), r=[], w=[sg[p].b, pa])
        gla("h", qT, kT, 32, 64, sg, "hnw%d" % l, 6, None)

        for half in range(2):
            so, wo = wload(wout_d[l, :, :, half * 512:(half + 1) * 512], [128, 8, 512])
            for mq in range(4):
                m = half * 4 + mq
                pb = proj(so, wo, mq * 128, yTc, lambda kc: yT[:, kc, :])
                op("dve", lambda e, m=m, pb=pb: e.tensor_tensor(out=xr[:, m, :], in0=pb[:, :], in1=xr[:, m, :], op=ALU.add), r=[], w=[xr, pb])

        norm_to_hT(xr, "n2w%d" % l)
        for q in range(11):
            su, wu = wload(wup_d[l, :, :, q * 512:(q + 1) * 512], [128, 8, 512])
            for jj in range(2):
                j = 2 * q + jj
                outs = []
                for (half, col) in ((0, jj * 128), (1, 256 + jj * 128)):
                    ch = half * 22 + j
                    ub, cb = getF(), getF()
                    pb = proj(su, wu, col, [hT], hk)
                    fhb = fh[(l, ch)]
                    op("pool", lambda e, ub=ub, ch=ch: e.tensor_copy(out=ub.b[:, 2:4], in_=fh_t[l][:, ch, :]), r=[fhb], w=[ub.h], n=2)
                    op("act", lambda e, ub=ub, pb=pb: e.activation(out=ub.d, in_=pb[:, :], func=AF.Copy), r=[], w=[ub.b, pb])
                    op("pool", lambda e, ub=ub, ch=ch: e.tensor_copy(out=fh_t[l][:, ch, :], in_=ub.b[:, T + 2:T + 4]), r=[ub.b], w=[fhb], n=2)
                    op("dve", lambda e, ub=ub, cb=cb, ch=ch: e.tensor_scalar(out=cb.d, in0=ub.d, scalar1=pcol("fcw%d_2" % l, ch), scalar2=pcol("fcb%d" % l, ch),
                                                                              op0=ALU.mult, op1=ALU.add), r=[ub.b, pv], w=[cb.b])
                    for k in (1, 0):
                        op("dve", lambda e, ub=ub, cb=cb, ch=ch, k=k: e.scalar_tensor_tensor(out=cb.d, in0=ub.b[:, 2 + k:2 + k + T], scalar=pcol("fcw%d_%d" % (l, k), ch), in1=cb.d,
                                                                                              op0=ALU.mult, op1=ALU.add), r=[ub.b, ub.h, pv], w=[cb.b], k=2)
                    outs.append(cb)
                cg, cv = outs
                op("act", lambda e, cg=cg: e.activation(out=cg.d, in_=cg.d, func=AF.Silu), r=[], w=[cg.b])
                op("pool", lambda e, cg=cg, cv=cv, j=j: e.tensor_tensor(out=gT[:, j, :], in0=cg.d, in1=cv.d, op=ALU.mult), r=[cg.b, cv.b], w=[gTc[j]], k=2)
        for m in range(8):
            sd, wd = wload(wdn_d[l, m], [128, 22, 128])
            pb = getA()

            def fn_d(e, pb=pb, wd=wd):
                ins = None
                for kc in range(22):
                    ins = e.matmul(pb[:, :], lhsT=wd[:, kc, :], rhs=gT[:, kc, :], start=(kc == 0), stop=(kc == 21))
                return ins

            op("pe", fn_d, r=[sd] + gTc, w=[pb], nmm=22)
            op("dve", lambda e, m=m, pb=pb: e.tensor_tensor(out=xr[:, m, :], in0=pb[:, :], in1=xr[:, m, :], op=ALU.add), r=[], w=[xr, pb])

    xv = xT_d.rearrange("(c p) s -> p c s", p=128)
    yv = yT_d.rearrange("(c p) s -> p c s", p=128)
    sndB = Buf(snd_t, "snd")
    rcvB = [Buf(rcv_t[i], "rcv%d" % i) for i in range(2)]
    snd_v = snd_t.ap().rearrange("(c p) s -> p c s", p=128)
    out_toks = []
    for st in range(NS):
        xr = xres[st % 2]
        sch.dma("sp", xr[:, :, :], xv[:, :, st * T:(st + 1) * T], w=[xr], nbytes=2 << 20)
        op("pool", lambda e, xr=xr: e.tensor_scalar(out=xr[:, :, :], in0=xr[:, :, :], scalar1=pcol("m"), scalar2=None, op0=ALU.mult), r=[pv], w=[xr], n=8 * T)
        if st >= SKEW:
            rb = rcvB[(st - SKEW) % 2]
            rcv_v = rcv_t[(st - SKEW) % 2].ap()[0:D, :].rearrange("(c p) s -> p c s", p=128)
            sch.dma("sp", rv[:, :, :], rcv_v, r=[rb], w=[rv], nbytes=2 << 20)
            op("dve", lambda e, xr=xr: e.scalar_tensor_tensor(out=xr[:, :, :], in0=rv[:, :, :], scalar=pcol("om"), in1=xr[:, :, :], op0=ALU.mult, op1=ALU.add),
               r=[rv, pv], w=[xr], n=8 * T, k=2)
        emit_layer(0, xr, st)
        if st < NT:
            sch.dma("sp", snd_v, xr[:, :, :], r=[xr], w=[sndB], nbytes=2 << 20)
            rb = rcvB[st % 2]
            sch.raw("pool", lambda e, rb=rb: e.collective_compute("AllGather", ALU.bypass, replica_groups=GROUPS,
                                                                   ins=[snd_t.ap().opt()], outs=[rb.t.ap().opt()]),
                    r=[sndB], w=[rb], sem_buf=rb, inc=1, occ=1.0, lat=120.0)
        if st == SKEW - 1:
            kp = pcol("keep")
            op("dve", lambda e: e.tensor_scalar(out=hst_t[0][:, :], in0=hst_t[0][:, :], scalar1=kp, scalar2=None, op0=ALU.mult), r=[pv], w=[hst[(0, c)] for c in range(4)], n=4)
            op("dve", lambda e: e.tensor_scalar(out=lxh_t[0][:, :, :], in0=lxh_t[0][:, :, :], scalar1=kp, scalar2=None, op0=ALU.mult), r=[pv], w=[lxh[(0, c)] for c in range(4)], n=16)
            op("dve", lambda e: e.tensor_scalar(out=fh_t[0][:, :, :], in0=fh_t[0][:, :, :], scalar1=kp, scalar2=None, op0=ALU.mult), r=[pv], w=[fh[(0, ch)] for ch in range(NFC)], n=88)
            for key in list(st_f.keys()):
                lo = key[3] * 64
                for tbl in (st_f, st_b):
                    bb_ = tbl[key]
                    op("dve", lambda e, bb_=bb_, lo=lo: e.tensor_scalar(out=bb_[lo:lo + 64, :], in0=bb_[lo:lo + 64, :], scalar1=pv[lo:lo + 64, PV["keep"]:PV["keep"] + 1],
                                                                          scalar2=None, op0=ALU.mult), r=[pv], w=[bb_], n=64)
        rs = rms_rstd(xr)
        for c in range(8):
            op("dve", lambda e, c=c, xr=xr, rs=rs: e.scalar_tensor_tensor(out=xr[:, c, :], in0=xr[:, c, :], scalar=pcol("fnw", c), in1=rs.d,
                                                                           op0=ALU.mult, op1=ALU.mult), r=[rs.b, pv], w=[xr], k=2)
        out_toks.append(sch.dma("sp", yv[:, :, st * T:(st + 1) * T], xr[:, :, :], r=[xr], nbytes=2 << 20))
    sch.wait_all_on("sp", out_toks)
    sch.emit()
    return nc, sch


def _chunks(v):
    v = np.asarray(v, np.float32)
    return np.ascontiguousarray(v.reshape(-1, 128).T)


def make_tables(S):
    t = np.arange(S, dtype=np.float64)
    inv = 10000.0 ** (-np.arange(0, 64, 2, dtype=np.float64) / 64.0)
    ang = t[None, :] * inv[:, None]
    cos = np.concatenate([np.cos(ang), np.cos(ang)], 0)
    sin = np.concatenate([-np.sin(ang), np.sin(ang)], 0)
    n1 = (t % 128) + 1.0
    tabs = np.zeros((128, 8, S), np.float64)
    for p in range(2):
        for hh in range(2):
            h = 2 * p + hh
            lg = math.log(RET_GAMMA[h])
            qd = np.exp(n1 * lg)[None, :]
            kd = np.exp(-n1 * lg)[None, :] * (64.0 ** -0.5)
            sl = slice(hh * 64, hh * 64 + 64)
            tabs[sl, p * 4 + 0] = cos * qd
            tabs[sl, p * 4 + 1] = sin * qd
            tabs[sl, p * 4 + 2] = cos * kd
            tabs[sl, p * 4 + 3] = sin * kd
    return tabs.astype(np.float32)


def make_consts():
    cst = np.zeros((128, 5, 128), np.float32)
    cst[:, 0, :] = 1.0
    cst[:, 1, :] = np.eye(128, dtype=np.float32)
    for b in range(2):
        cst[b * 64:(b + 1) * 64, 2, b * 64:(b + 1) * 64] = 1.0 / 64.0
    m = np.arange(128)[:, None]
    n = np.arange(128)[None, :]
    cst[:, 3, :] = (n >= m).astype(np.float32)
    m32 = (np.arange(32)[None, :] >= np.arange(32)[:, None]).astype(np.float32)
    for b in range(4):
        cst[b * 32:(b + 1) * 32, 4, 0:32] = m32
    rmask = np.ones((128, T), np.float32)
    rmask[:, ::32] = 0.0
    return cst, rmask


def prep_role(inp, role, NT):
    lyr = role
    NS = NT + SKEW
    w_in = np.asarray(inp["w_in"], np.float32)[lyr:lyr + 1]
    swap = lambda base: [base + h * 64 + ((d + 32) % 64) for h in range(4) for d in range(64)]
    idx = list(range(3072)) + swap(1024) + swap(1280)
    win = w_in[:, :, idx].reshape(1, 8, 128, NCOL_IN).transpose(0, 2, 1, 3)
    wout = np.asarray(inp["w_out"], np.float32)[lyr:lyr + 1].reshape(1, 8, 128, D).transpose(0, 2, 1, 3)
    perm = []
    for q in range(11):
        perm += list(range(2 * q * 128, (2 * q + 2) * 128))
        perm += list(range(DFF + 2 * q * 128, DFF + (2 * q + 2) * 128))
    wup = np.asarray(inp["ffn_w_up"], np.float32)[lyr:lyr + 1][:, :, perm].reshape(1, 8, 128, 2 * DFF).transpose(0, 2, 1, 3)
    wdn = np.asarray(inp["ffn_w_down"], np.float32)[lyr:lyr + 1].reshape(1, 22, 128, 8, 128).transpose(0, 3, 2, 1, 4)
    gw = np.zeros((1, 128, 8, 128), np.float32)
    wa = np.asarray(inp["lru_wa"], np.float32)
    wx = np.asarray(inp["lru_wx"], np.float32)
    for c in range(4):
        for b2 in range(2):
            sl = slice(b2 * 64, b2 * 64 + 64)
            gw[0, sl, c * 2 + 0, sl] = wa[lyr, 2 * c + b2]
            gw[0, sl, c * 2 + 1, sl] = wx[lyr, 2 * c + b2]
    pvec = np.zeros((128, PV["_n"]), np.float32)

    def put(name, v):
        a = _chunks(v)
        pvec[:, PV[name]:PV[name] + a.shape[1]] = a

    put("n1w0", inp["norm1_w"][lyr])
    put("n2w0", inp["norm2_w"][lyr])
    for k in range(4):
        put("lcw0_%d" % k, inp["lru_conv_w"][lyr][k])
    put("lcb0", inp["lru_conv_b"][lyr])
    put("ba0", inp["lru_ba"][lyr])
    put("bx0", inp["lru_bx"][lyr])
    put("lam0", inp["lru_lambda"][lyr])
    put("rnw0", inp["ret_norm_w"][lyr])
    put("hnw0", inp["hg_norm_w"][lyr])
    put("hbA", inp["hg_lower_bounds"][0])
    put("hbB", inp["hg_lower_bounds"][1])
    for k in range(3):
        put("fcw0_%d" % k, inp["ffn_conv_w"][lyr][k])
    put("fcb0", inp["ffn_conv_b"][lyr])
    put("fnw", inp["final_norm_w"])
    pvec[:, PV["eps"]] = EPS
    pvec[:, PV["one"]] = 1.0
    pvec[:, PV["m"]] = 1.0 if role == 0 else 0.0
    pvec[:, PV["om"]] = 0.0 if role == 0 else 1.0
    pvec[:, PV["keep"]] = 1.0 if role == 0 else 0.0
    pvec[:, PV["lbm"]] = 0.0 if role == 0 else 1.0
    cst, rmask = make_consts()
    tb = make_tables(NT * T)
    tabs = np.zeros((128, 8, NS * T), np.float32)
    for st in range(NS):
        ti = st if role == 0 else st - SKEW
        if ti < 0 or ti >= NT:
            ti = 0
        tabs[:, :, st * T:(st + 1) * T] = tb[:, :, ti * T:(ti + 1) * T]
    return {
        "win": np.ascontiguousarray(win), "wout": np.ascontiguousarray(wout),
        "wup": np.ascontiguousarray(wup), "wdn": np.ascontiguousarray(wdn),
        "gatew": gw, "pvec": pvec, "tabs": tabs, "cst": cst, "rmask": rmask,
    }


_CACHE = {}


def kernel(**inputs):
    x = np.asarray(inputs["x"], np.float32)
    B, S, _ = x.shape
    NT = S // T
    NS = NT + SKEW
    roles = [prep_role(inputs, r, NT) for r in range(2)]
    if NT not in _CACHE:
        _CACHE[NT] = build(NT)
    nc, sch = _CACHE[NT]
    ncores = 2 * B
    zeros = np.zeros((D, NS * T), np.float32)
    in_maps = []
    for c in range(ncores):
        b, r = c // 2, c % 2
        m = dict(roles[r])
        if r == 0:
            xp = np.zeros((D, NS * T), np.float32)
            xp[:, :S] = x[b].T
            m["xT"] = xp
        else:
            m["xT"] = zeros
        in_maps.append(m)
    res = run_bass_kernel_spmd(nc, in_maps, core_ids=list(range(ncores)))
    out = np.empty((B, S, D), np.float32)
    for b in range(B):
        out[b] = res.results[2 * b + 1]["yT"][:, SKEW * T:].T
    return out
```

```python
import contextlib
import math
import numpy as np
import concourse.bass as bass
import concourse.mybir as mybir
from concourse.bass_utils import run_bass_kernel_spmd

F32 = mybir.dt.float32
BF16 = mybir.dt.bfloat16
AF = mybir.ActivationFunctionType
ALU = mybir.AluOpType

EPOCH = 4000

D = 1024
SEQ = 8192
BATCH = 4
DEPTH = 2
NL = 1
SKEW = 2
T = 512
DFF = 2816
NFC = 44
EPS = 1e-6
NCOL_IN = 3584


import heapq
import sys


class Buf:
    def __init__(self, t, name):
        self.t = t
        self.name = name
        self.last_write = None
        self.reads = []
        self.dma_cnt = 0
        self.g_lw = None
        self.g_rd = []
        self.held = False

    def __getitem__(self, key):
        return self.t[key]

    def sub(self, name):
        return Buf(self.t, name)


class _Op:
    __slots__ = ("eng", "fn", "r", "w", "occ", "lat", "kind", "sem_buf", "preds", "succs", "inc", "line", "start")


class Sched:
    ENG = ("pe", "act", "dve", "pool", "sp")

    def __init__(self, nc):
        self.nc = nc
        self.stack = contextlib.ExitStack()
        self.rec = []
        self.ops = {e: [] for e in self.ENG}
        self.cnt = {e: 0 for e in self.ENG}
        self.epoch = {e: 0 for e in self.ENG}
        self.sems = {}
        self.waited = {e: {} for e in self.ENG}
        self.nsem = 0
        self.final_waits = []
        self.sim_time = 0.0

    def sem(self, key):
        if key not in self.sems:
            self.sems[key] = self.stack.enter_context(self.nc.semaphore("s%d" % self.nsem))
            self.nsem += 1
        return self.sems[key]

    def sb(self, name, shape, dtype=F32):
        t = self.stack.enter_context(self.nc.sbuf_tensor(name, list(shape), dtype))
        return Buf(t, name)

    def ps(self, name, shape, dtype=F32):
        t = self.stack.enter_context(self.nc.psum_tensor(name, list(shape), dtype))
        return Buf(t, name)

    def _record(self, o):
        i = len(self.rec)
        try:
            f = sys._getframe(2)
            while f.f_code.co_name in ("proj", "wload", "getA", "op", "dma", "raw"):
                f = f.f_back
            o.line = f.f_lineno
        except Exception:
            o.line = 0
        preds = set()
        for b in o.r:
            if b.g_lw is not None:
                preds.add(b.g_lw)
        for b in o.w:
            if b.g_lw is not None:
                preds.add(b.g_lw)
            preds.update(b.g_rd)
        preds.discard(i)
        o.preds = preds
        o.succs = []
        for p in preds:
            self.rec[p].succs.append(i)
        for b in o.w:
            b.g_lw = i
            b.g_rd = []
            if hasattr(b, "touch"):
                b.touch = i
        for b in o.r:
            if b not in o.w:
                b.g_rd.append(i)
        self.rec.append(o)
        return i

    def op(self, eng, fn, r=(), w=(), n=T, k=1.0, nmm=1):
        o = _Op()
        o.eng, o.fn, o.r, o.w, o.kind, o.sem_buf = eng, fn, tuple(r), tuple(w), "c", None
        if eng == "pe":
            o.occ = nmm * (0.19 + 0.0002 * n)
        elif eng == "act":
            o.occ = 0.22 + 0.00072 * n
        elif eng == "dve":
            o.occ = 0.07 + 0.00105 * n * k
        else:
            o.occ = 0.4 + 0.002 * n
        o.lat = o.occ + 0.1
        return self._record(o)

    def dma(self, q, out_ap, in_ap, r=(), w=(), sem_buf=None, nbytes=1 << 20):
        o = _Op()
        o.eng, o.r, o.w, o.kind = q, tuple(r), tuple(w), "d"
        o.sem_buf = sem_buf if sem_buf is not None else (w[0] if len(w) else r[0])

        def fn(e, out_ap=out_ap, in_ap=in_ap):
            return e.dma_start(out=out_ap, in_=in_ap)

        o.fn = fn
        o.occ = 1.0 if q == "pool" else 0.15
        o.lat = o.occ + 2.0 + nbytes / 150e3
        o.inc = 16
        return self._record(o)

    def raw(self, q, fn, r=(), w=(), sem_buf=None, inc=1, occ=1.0, lat=50.0):
        o = _Op()
        o.eng, o.r, o.w, o.kind = q, tuple(r), tuple(w), "d"
        o.sem_buf = sem_buf
        o.fn = fn
        o.occ, o.lat = occ, lat
        o.inc = inc
        return self._record(o)

    def wait_all_on(self, eng, idxs):
        self.final_waits.append((eng, list(idxs)))

    def _schedule(self):
        ops = self.rec
        ENG = self.ENG
        ready = {e: [] for e in ENG}
        busy = {e: 0.0 for e in ENG}
        npred = [len(o.preds) for o in ops]
        for i, o in enumerate(ops):
            if npred[i] == 0:
                heapq.heappush(ready[o.eng], i)
        events = []
        now = 0.0
        order = []
        while True:
            for e in ENG:
                if ready[e] and busy[e] <= now + 1e-9:
                    i = heapq.heappop(ready[e])
                    o = ops[i]
                    order.append(i)
                    o.start = now
                    busy[e] = now + o.occ
                    heapq.heappush(events, (now + o.lat, 1, i))
                    heapq.heappush(events, (now + o.occ, 0, i))
            if not events:
                break
            t, typ, i = heapq.heappop(events)
            now = t
            if typ == 1:
                for s_ in ops[i].succs:
                    npred[s_] -= 1
                    if npred[s_] == 0:
                        heapq.heappush(ready[ops[s_].eng], s_)
        assert len(order) == len(ops), (len(order), len(ops))
        self.sim_time = now
        return order

    def _deps(self, r, w):
        deps = []
        for b in r:
            if b.last_write is not None:
                deps.append(b.last_write)
        for b in w:
            if b.last_write is not None:
                deps.append(b.last_write)
            deps.extend(b.reads)
        return deps

    def _waits(self, eng, deps):
        need = {}
        wd = self.waited[eng]
        for (k, v) in deps:
            if wd.get(k, 0) >= v:
                continue
            if need.get(k, 0) < v:
                need[k] = v
        for k, v in need.items():
            wd[k] = v
        return [(self.sem(k), v) for k, v in need.items()]

    def _commit(self, r, w, tok):
        for b in w:
            b.last_write = tok
            b.reads = []
        for b in r:
            if b in w:
                continue
            b.reads = [x for x in b.reads if x[0] != tok[0]] + [tok]

    def _replay_one(self, o):
        eng = o.eng
        deps = self._deps(o.r, o.w)
        if o.kind == "c":
            if eng == "pe":
                deps = [d for d in deps if not (d[0][0] == "e" and d[0][1] == "pe")]
            waits = self._waits(eng, deps)
            if self.cnt[eng] >= EPOCH:
                self.epoch[eng] += 1
                self.cnt[eng] = 0
            self.cnt[eng] += 1
            key = ("e", eng, self.epoch[eng])
            tok = (key, self.cnt[eng])
            self.ops[eng].append((o.fn, waits, self.sem(key), 1))
        else:
            waits = self._waits(eng, deps)
            sb_ = o.sem_buf
            key = ("d", id(sb_))
            inc = getattr(o, "inc", 16)
            sb_.dma_cnt += inc
            tok = (key, sb_.dma_cnt)
            self.ops[eng].append((o.fn, waits, self.sem(key), inc))
        self._commit(o.r, o.w, tok)
        return tok

    def emit(self):
        nc = self.nc
        order = self._schedule()
        toks = {}
        for i in order:
            toks[i] = self._replay_one(self.rec[i])
        for eng, idxs in self.final_waits:
            waits = self._waits(eng, [toks[i] for i in idxs])
            self.ops[eng].append((None, waits, None, 0))

        def replay(engobj, lst):
            for (fn, waits, s, inc) in lst:
                for (ws, v) in waits:
                    engobj.wait_ge(ws, v)
                if fn is not None:
                    fn(engobj).then_inc(s, inc)

        with nc.Block() as block:
            @block.tensor
            def _(e):
                replay(e, self.ops["pe"])

            @block.scalar
            def _(e):
                replay(e, self.ops["act"])

            @block.vector
            def _(e):
                replay(e, self.ops["dve"])

            @block.gpsimd
            def _(e):
                replay(e, self.ops["pool"])

            @block.sync
            def _(e):
                replay(e, self.ops["sp"])

    def close(self):
        self.stack.close()


def _pvec_layout():
    lay = {}
    col = 0

    def add(name, n):
        nonlocal col
        lay[name] = col
        col += n

    for l in range(NL):
        add("n1w%d" % l, 8)
        add("n2w%d" % l, 8)
        for k in range(4):
            add("lcw%d_%d" % (l, k), 4)
        add("lcb%d" % l, 4)
        add("ba%d" % l, 4)
        add("bx%d" % l, 4)
        add("lam%d" % l, 4)
        add("rnw%d" % l, 2)
        add("hnw%d" % l, 2)
        for k in range(3):
            add("fcw%d_%d" % (l, k), NFC)
        add("fcb%d" % l, NFC)
    add("fnw", 8)
    add("hbA", 2)
    add("hbB", 2)
    add("m", 1)
    add("om", 1)
    add("keep", 1)
    add("lbm", 1)
    add("eps", 1)
    add("one", 1)
    lay["_n"] = col
    return lay


PV = _pvec_layout()
RET_GAMMA = [1.0 - 2.0 ** (-5.0 - h) for h in range(4)]


def build(NT, nlayers=NL, debug_out=None):
    NS = NT + SKEW
    S = NS * T
    nc = bass.Bass("TRN2", target_bir_lowering=False)
    xT_d = nc.dram_tensor("xT", [D, S], F32, kind="ExternalInput").ap()
    win_d = nc.dram_tensor("win", [NL, 128, 8, NCOL_IN], F32, kind="ExternalInput").ap()
    wout_d = nc.dram_tensor("wout", [NL, 128, 8, D], F32, kind="ExternalInput").ap()
    wup_d = nc.dram_tensor("wup", [NL, 128, 8, 2 * DFF], F32, kind="ExternalInput").ap()
    wdn_d = nc.dram_tensor("wdn", [NL, 8, 2, 128, 11, 128], F32, kind="ExternalInput").ap()
    gw_d = nc.dram_tensor("gatew", [NL, 128, 8, 128], F32, kind="ExternalInput").ap()
    pv_d = nc.dram_tensor("pvec", [128, PV["_n"]], F32, kind="ExternalInput").ap()
    tab_d = nc.dram_tensor("tabs", [128, 8, S], F32, kind="ExternalInput").ap()
    cst_d = nc.dram_tensor("cst", [128, 5, 128], F32, kind="ExternalInput").ap()
    rm_d = nc.dram_tensor("rmask", [128, T], F32, kind="ExternalInput").ap()
    yT_d = nc.dram_tensor("yT", [D, S], F32, kind="ExternalOutput").ap()
    snd_t = nc.dram_tensor("snd", [D, T], F32)
    rcv_t = [nc.dram_tensor("rcv%d" % i, [2 * D, T], F32) for i in range(2)]
    GROUPS = [[0, 1], [2, 3], [4, 5], [6, 7]]

    sch = Sched(nc)
    sb, ps = sch.sb, sch.ps
    op = sch.op

    pv = sb("pv", [128, PV["_n"]])
    sch.dma("sp", pv[:, :], pv_d[:, :], w=[pv], nbytes=1 << 18)
    cstf = sb("cstf", [128, 5, 128])
    sch.dma("sp", cstf[:, :, :], cst_d[:, :, :], w=[cstf], nbytes=1 << 18)
    cst = sb("cstb", [128, 5, 128], BF16)
    op("dve", lambda e: e.tensor_copy(out=cst[:, :, :], in_=cstf[:, :, :]), r=[cstf], w=[cst], n=640)
    ONES, IDENT, BD64, MASK128, MASK32 = range(5)
    rmask = sb("rmask_s", [128, T])
    sch.dma("sp", rmask[:, :], rm_d[:, :], w=[rmask], nbytes=1 << 18)
    gwf = sb("gwf", [128, NL * 8, 128])
    gwb = sb("gwb", [128, NL * 8, 128], BF16)
    for l in range(NL):
        sch.dma("sp", gwf[:, l * 8:(l + 1) * 8, :], gw_d[l], w=[gwf], nbytes=1 << 19)
    op("dve", lambda e: e.tensor_copy(out=gwb[:, :, :], in_=gwf[:, :, :]), r=[gwf], w=[gwb], n=2048)

    def pcol(name, c=0, lo=0, hi=128):
        k = PV[name] + c
        return pv[lo:hi, k:k + 1]

    der = sb("der", [128, 32])
    for l in range(NL):
        k = PV["lam%d" % l]
        op("act", lambda e, k=k, l=l: e.activation(out=der[:, l * 4:l * 4 + 4], in_=pv[:, k:k + 4], func=AF.Exp, scale=-1.0), r=[pv], w=[der], n=4)
        op("act", lambda e, l=l: e.activation(out=der[:, l * 4:l * 4 + 4], in_=der[:, l * 4:l * 4 + 4], func=AF.Ln, bias=pcol("one"), scale=1.0), r=[der, pv], w=[der], n=4)
        op("dve", lambda e, l=l: e.tensor_scalar(out=der[:, l * 4:l * 4 + 4], in0=der[:, l * 4:l * 4 + 4], scalar1=-8.0, scalar2=None, op0=ALU.mult), r=[der], w=[der], n=4)
    LB, OML, NOML = 8, 12, 16
    kA, kB = PV["hbA"], PV["hbB"]
    op("dve", lambda e: e.tensor_tensor(out=der[:, 20:22], in0=pv[:, kB:kB + 2], in1=pv[:, kA:kA + 2], op=ALU.subtract), r=[pv, der], w=[der], n=2)
    op("act", lambda e: e.activation(out=der[:, 22:24], in_=der[:, 20:22], func=AF.Sigmoid), r=[der], w=[der], n=2)
    op("dve", lambda e: e.tensor_scalar(out=der[:, LB:LB + 2], in0=der[:, 22:24], scalar1=pcol("lbm"), scalar2=None, op0=ALU.mult), r=[pv], w=[der], n=2)
    op("dve", lambda e: e.tensor_scalar(out=der[:, OML:OML + 2], in0=der[:, LB:LB + 2], scalar1=-1.0, scalar2=1.0, op0=ALU.mult, op1=ALU.add), r=[], w=[der], n=2)
    op("dve", lambda e: e.tensor_scalar(out=der[:, NOML:NOML + 2], in0=der[:, OML:OML + 2], scalar1=-1.0, scalar2=None, op0=ALU.mult), r=[], w=[der], n=2)

    def dcol(base, l, c, lo=0, hi=128):
        k = base + l * (4 if base == 0 else 2) + c
        return der[lo:hi, k:k + 1]

    hst_t = [sb("hst%d" % l, [128, 4]) for l in range(NL)]
    lxh_t = [sb("lxh%d" % l, [128, 4, 4]) for l in range(NL)]
    fh_t = [sb("fh%d" % l, [128, NFC, 2]) for l in range(NL)]
    hst = {}
    lxh = {}
    fh = {}
    st_f = {}
    st_b = {}
    for l in range(NL):
        for c in range(4):
            hst[(l, c)] = hst_t[l].sub("hst%d_%d" % (l, c))
            lxh[(l, c)] = lxh_t[l].sub("lxh%d_%d" % (l, c))
        for ch in range(NFC):
            fh[(l, ch)] = fh_t[l].sub("fh%d_%d" % (l, ch))
        op("dve", lambda e, l=l: e.memset(hst_t[l][:, :], 0.0), w=[hst[(l, c)] for c in range(4)], n=4)
        op("dve", lambda e, l=l: e.memset(lxh_t[l][:, :, :], 0.0), w=[lxh[(l, c)] for c in range(4)], n=16)
        op("dve", lambda e, l=l: e.memset(fh_t[l][:, :, :], 0.0), w=[fh[(l, ch)] for ch in range(NFC)], n=88)
        for mix in ("r", "h"):
            for p in range(2):
                tf = sb("stf_%s%d%d" % (mix, l, p), [128, 64])
                tb = sb("stb_%s%d%d" % (mix, l, p), [128, 64], BF16)
                for hh in range(2):
                    bf_ = tf.sub("stf_%s%d%d%d" % (mix, l, p, hh))
                    bb_ = tb.sub("stb_%s%d%d%d" % (mix, l, p, hh))
                    st_f[(mix, l, p, hh)] = bf_
                    st_b[(mix, l, p, hh)] = bb_
                    lo = hh * 64
                    op("dve", lambda e, bf_=bf_, lo=lo: e.memset(bf_[lo:lo + 64, :], 0.0), w=[bf_], n=64)
                    op("dve", lambda e, bb_=bb_, lo=lo: e.memset(bb_[lo:lo + 64, :], 0.0), w=[bb_], n=64)

    xres = [sb("xres%d" % i, [128, 8, T]) for i in range(2)]
    rv = sb("rv", [128, 8, T])
    tabs = sb("tabs_s", [128, 8, T])
    hT = sb("hT", [128, 8, T], BF16)
    yT = sb("yTs", [128, 8, T], BF16)
    yTc = [yT.sub("yT%d" % c) for c in range(8)]
    gT = sb("gT", [128, 22, T], BF16)
    gTc = [gT.sub("gT%d" % c) for c in range(22)]
    NSLOT = 4
    wsl = [sb("wsl%d" % i, [128, 4096], BF16) for i in range(NSLOT)]

    class FB:
        def __init__(self, name):
            self.b = sb(name, [128, T + 4])
            self.h = self.b.sub(name + "_h")

        @property
        def d(self):
            return self.b[:, 4:4 + T]

    NF = 14
    Fring = [FB("F%d" % i) for i in range(NF)]
    fstate = {"f": 0, "b": 0, "a": 0, "x": 0, "at": 0}

    def getF():
        f = Fring[fstate["f"] % NF]
        fstate["f"] += 1
        return f

    NB = 8
    Bring = [sb("B%d" % i, [128, T], BF16) for i in range(NB)]

    def getB():
        b = Bring[fstate["b"] % NB]
        fstate["b"] += 1
        return b

    vtok = sb("vtok", [128, 8, 256], BF16)
    ktok = sb("ktok", [128, 8, 2, 128], BF16)
    NAT = 4
    ATb = [sb("AT%d" % i, [128, 128], BF16) for i in range(NAT)]
    ebl = [sb("ebl%d" % p, [128, 16]) for p in range(2)]
    stmp = [sb("stmp%d" % i, [128, 64]) for i in range(4)]

    banks = [ps("bank%d" % i, [128, T]) for i in range(8)]
    for i_, b_ in enumerate(banks):
        b_.touch = -100 + i_
        b_.held = False

    def getA(hold=False):
        a = min((b for b in banks if not b.held), key=lambda b: b.touch)
        a.touch = len(sch.rec)
        a.held = hold
        return a

    wstate = {"n": 0}

    def wload(src_ap, shape):
        s = wsl[wstate["n"] % NSLOT]
        wstate["n"] += 1
        n = 1
        for d_ in shape[1:]:
            n *= d_
        view = s[:, 0:n].rearrange("p (a b) -> p a b", b=shape[2])
        sch.dma("pool", view, src_ap, w=[s], nbytes=128 * n * 4)
        return s, view

    def proj(slot, wview, col, rhs_bufs, rhs_of_kc, nk=8, m=128, ncols=T):
        pbuf = getA()

        def fn(e):
            ins = None
            for kc in range(nk):
                ins = e.matmul(pbuf[0:m, 0:ncols], lhsT=wview[:, kc, col:col + m], rhs=rhs_of_kc(kc), start=(kc == 0), stop=(kc == nk - 1))
            return ins

        op("pe", fn, r=[slot] + list(rhs_bufs), w=[pbuf], n=ncols, nmm=nk)
        return pbuf

    def sq_chunk(xr, m):
        op("act", lambda e, m=m: e.activation(out=gT[:, m, :], in_=xr[:, m, :], func=AF.Square), r=[xr], w=[gTc[m]])

    def rms_rstd(xr, use_hT=False, presq=False):
        scr = hT if use_hT else gT
        scr_b = [hT] if use_hT else gTc[0:8]
        if not presq:
            op("act", lambda e: e.activation(out=scr[:, 0:8, :], in_=xr[:, :, :], func=AF.Square), r=[xr], w=scr_b, n=8 * T)
        pS_ = getA()

        def fn(e):
            ins = None
            for c in range(8):
                ins = e.matmul(pS_[:, :], lhsT=cst[:, ONES, :], rhs=scr[:, c, :], start=(c == 0), stop=(c == 7))
            return ins

        op("pe", fn, r=[cst] + scr_b, w=[pS_], nmm=8)
        rs = getF()
        op("act", lambda e: e.activation(out=rs.d, in_=pS_[:, :], func=AF.Ln, bias=pcol("eps"), scale=1.0 / D), r=[pv], w=[rs.b, pS_])
        op("act", lambda e: e.activation(out=rs.d, in_=rs.d, func=AF.Exp, scale=-0.5), r=[], w=[rs.b])
        return rs

    def norm_to_hT(xr, nwname, use_hT=False, presq=False):
        rs = rms_rstd(xr, use_hT, presq)
        for c in range(8):
            op("dve", lambda e, c=c: e.scalar_tensor_tensor(out=hT[:, c, :], in0=xr[:, c, :], scalar=pcol(nwname, c), in1=rs.d,
                                                              op0=ALU.mult, op1=ALU.mult), r=[xr, rs.b, pv], w=[hT], k=2)

    hk = lambda kc: hT[:, kc, :]

    def emit_layer(l, xr, ti):
        norm_to_hT(xr, "n1w%d" % l, use_hT=True)

        s0, w0 = wload(win_d[l, :, :, 0:512], [128, 8, 512])
        s1, w1 = wload(win_d[l, :, :, 512:1024], [128, 8, 512])
        for c in range(4):
            lxb, xc, rr, ii, aa, mm = getF(), getF(), getF(), getF(), getF(), getF()
            xcb = getB()
            pa = proj(s0, w0, c * 128, [hT], hk)
            op("pool", lambda e, c=c, lxb=lxb: e.tensor_copy(out=lxb.b[:, 1:4], in_=lxh_t[l][:, c, 0:3]), r=[lxh[(l, c)]], w=[lxb.h], n=3)
            op("act", lambda e, lxb=lxb, pa=pa: e.activation(out=lxb.d, in_=pa[:, :], func=AF.Copy), r=[], w=[lxb.b, pa])
            op("pool", lambda e, c=c, lxb=lxb: e.tensor_copy(out=lxh_t[l][:, c, 0:3], in_=lxb.b[:, T + 1:T + 4]), r=[lxb.b], w=[lxh[(l, c)]], n=3)
            op("dve", lambda e, c=c, lxb=lxb, xc=xc: e.tensor_scalar(out=xc.d, in0=lxb.b[:, 1:1 + T], scalar1=pcol("lcw%d_0" % l, c), scalar2=pcol("lcb%d" % l, c),
                                                                      op0=ALU.mult, op1=ALU.add), r=[lxb.b, lxb.h, pv], w=[xc.b])
            for k in range(1, 4):
                op("dve", lambda e, c=c, k=k, lxb=lxb, xc=xc: e.scalar_tensor_tensor(out=xc.d, in0=lxb.b[:, 1 + k:1 + k + T], scalar=pcol("lcw%d_%d" % (l, k), c), in1=xc.d,
                                                                                      op0=ALU.mult, op1=ALU.add), r=[lxb.b, lxb.h, pv], w=[xc.b], k=2)
            op("act", lambda e, xc=xc, xcb=xcb: e.activation(out=xcb[:, :], in_=xc.d, func=AF.Copy), r=[xc.b], w=[xcb])
            pg = getA()
            op("pe", lambda e, c=c, pg=pg, xcb=xcb: e.matmul(pg[:, :], lhsT=gwb[:, l * 8 + c * 2, :], rhs=xcb[:, :], start=True, stop=True), r=[gwb, xcb], w=[pg])
            op("act", lambda e, c=c, pg=pg, rr=rr: e.activation(out=rr.d, in_=pg[:, :], func=AF.Sigmoid, bias=pcol("ba%d" % l, c)), r=[pv], w=[rr.b, pg])
            pg2 = getA()
            op("pe", lambda e, c=c, pg2=pg2, xcb=xcb: e.matmul(pg2[:, :], lhsT=gwb[:, l * 8 + c * 2 + 1, :], rhs=xcb[:, :], start=True, stop=True), r=[gwb, xcb], w=[pg2])
            op("act", lambda e, c=c, pg2=pg2, ii=ii: e.activation(out=ii.d, in_=pg2[:, :], func=AF.Sigmoid, bias=pcol("bx%d" % l, c)), r=[pv], w=[ii.b, pg2])
            op("act", lambda e, c=c, rr=rr, aa=aa: e.activation(out=aa.d, in_=rr.d, func=AF.Exp, scale=dcol(0, l, c)), r=[rr.b, der], w=[aa.b])
            op("act", lambda e, aa=aa, mm=mm: e.activation(out=mm.d, in_=aa.d, func=AF.Square), r=[aa.b], w=[mm.b])
            op("act", lambda e, mm=mm: e.activation(out=mm.d, in_=mm.d, func=AF.Sqrt, bias=pcol("one"), scale=-1.0), r=[pv], w=[mm.b])
            op("pool", lambda e, ii=ii, xc=xc: e.tensor_tensor(out=ii.d, in0=ii.d, in1=xc.d, op=ALU.mult), r=[xc.b], w=[ii.b], k=2)
            op("pool", lambda e, ii=ii, mm=mm: e.tensor_tensor(out=ii.d, in0=ii.d, in1=mm.d, op=ALU.mult), r=[mm.b], w=[ii.b], k=2)
            hs = hst[(l, c)]
            op("dve", lambda e, c=c, rr=rr, aa=aa, ii=ii: e.tensor_tensor_scan(out=rr.d, data0=aa.d, data1=ii.d, initial=hst_t[l][:, c:c + 1],
                                                                                op0=ALU.mult, op1=ALU.add), r=[aa.b, ii.b, hs], w=[rr.b], k=2)
            op("dve", lambda e, c=c, rr=rr: e.tensor_copy(out=hst_t[l][:, c:c + 1], in_=rr.b[:, T + 3:T + 4]), r=[rr.b], w=[hs], n=1)
            pb = proj(s1, w1, c * 128, [hT], hk)
            op("act", lambda e, mm=mm, pb=pb: e.activation(out=mm.d, in_=pb[:, :], func=AF.Gelu_apprx_tanh), r=[], w=[mm.b, pb])
            op("dve", lambda e, c=c, rr=rr, mm=mm: e.tensor_tensor(out=yT[:, c, :], in0=rr.d, in1=mm.d, op=ALU.mult), r=[rr.b, mm.b], w=[yTc[c]], k=2)

        def gla(mix, qT, kT, C, BLK, sgate, nwname, ycol0, decay_const):
            nblk = T // BLK
            nch = T // C
            msk = MASK128 if C == 128 else MASK32
            pO = [getA(True), getA(True)]
            pX = [getA(True), getA(True)]
            pUT = getA(True)
            pTr = pUT[:, 256:512].bitcast(BF16)
            for p in range(2):
                for b_ in range(nblk):
                    op("pe", lambda e, p=p, b_=b_: e.transpose(pTr[0:BLK, 0:128], kT[p][:, b_ * BLK:(b_ + 1) * BLK], cst[:, IDENT, :]), r=[kT[p], cst], w=[pUT], n=128)
                    op("act", lambda e, p=p, b_=b_: e.activation(out=ktok[0:BLK, b_, p, :], in_=pTr[0:BLK, 0:128], func=AF.Copy), r=[], w=[ktok, pUT], n=128)
            for j in range(nch):
                blk = (j * C) // BLK
                base = (j * C) % BLK
                for h in range(4):
                    p, hh = h // 2, h % 2
                    off = hh * 64
                    px = pX[fstate["x"] % 2]
                    fstate["x"] += 1
                    at = ATb[fstate["at"] % NAT]
                    fstate["at"] += 1
                    Sf = st_f[(mix, l, p, hh)]
                    Sb = st_b[(mix, l, p, hh)]
                    tsl = slice(j * C, (j + 1) * C)
                    op("pe", lambda e, p=p, off=off, tsl=tsl, px=px, base=base: e.matmul(
                        px[base:base + C, 0:C], lhsT=kT[p][off:off + 64, tsl], rhs=qT[p][off:off + 64, tsl], start=True, stop=True),
                        r=[kT[p], qT[p]], w=[px], n=C)
                    op("dve", lambda e, px=px, at=at, base=base: e.tensor_tensor(
                        out=at[base:base + C, 0:C], in0=px[base:base + C, 0:C], in1=cst[base:base + C, msk, 0:C], op=ALU.mult),
                        r=[cst], w=[at, px], n=C)

                    def fn_o(e, p=p, off=off, tsl=tsl, at=at, base=base, blk=blk, h=h, Sb=Sb):
                        e.matmul(pO[p][off:off + 64, tsl], lhsT=vtok[base:base + C, blk, h * 64:(h + 1) * 64], rhs=at[base:base + C, 0:C], start=True, stop=False)
                        return e.matmul(pO[p][off:off + 64, tsl], lhsT=Sb[off:off + 64, :], rhs=qT[p][off:off + 64, tsl], start=False, stop=True)

                    op("pe", fn_o, r=[vtok, at, Sb, qT[p]], w=[pO[p]], n=C, nmm=2)
                    op("pe", lambda e, p=p, off=off, base=base, blk=blk, h=h: e.matmul(
                        pUT[off:off + 64, 0:64], lhsT=ktok[base:base + C, blk, p, off:off + 64], rhs=vtok[base:base + C, blk, h * 64:(h + 1) * 64],
                        start=True, stop=True), r=[ktok, vtok], w=[pUT], n=64)
                    if decay_const is not None:
                        g = decay_const[h]
                        op("dve", lambda e, Sf=Sf, off=off, g=g: e.scalar_tensor_tensor(
                            out=Sf[off:off + 64, :], in0=Sf[off:off + 64, :], scalar=g, in1=pUT[off:off + 64, 0:64], op0=ALU.mult, op1=ALU.add),
                            r=[], w=[Sf, pUT], n=64)
                        op("dve", lambda e, Sf=Sf, Sb=Sb, off=off, g=g: e.tensor_scalar(
                            out=Sb[off:off + 64, :], in0=Sf[off:off + 64, :], scalar1=g, scalar2=None, op0=ALU.mult), r=[Sf], w=[Sb], n=64)
                    else:
                        tmp = stmp[h]
                        op("dve", lambda e, Sf=Sf, off=off, tmp=tmp: e.tensor_tensor(
                            out=tmp[off:off + 64, :], in0=pUT[off:off + 64, 0:64], in1=Sf[off:off + 64, :], op=ALU.add), r=[Sf], w=[tmp, pUT], n=64)
                        op("dve", lambda e, Sf=Sf, off=off, tmp=tmp, p=p, j=j: e.tensor_scalar(
                            out=Sf[off:off + 64, :], in0=tmp[off:off + 64, :], scalar1=ebl[p][off:off + 64, j:j + 1], scalar2=None, op0=ALU.mult),
                            r=[tmp, ebl[p]], w=[Sf], n=64)
                        op("act", lambda e, Sf=Sf, Sb=Sb, off=off: e.activation(out=Sb[off:off + 64, :], in_=Sf[off:off + 64, :], func=AF.Copy), r=[Sf], w=[Sb], n=64)
            for p in range(2):
                o, rt = getF(), getF()
                osq = getB()
                op("act", lambda e, p=p, o=o: e.activation(out=o.d, in_=pO[p][:, :], func=AF.Copy), r=[], w=[o.b, pO[p]])
                op("act", lambda e, o=o, osq=osq: e.activation(out=osq[:, :], in_=o.d, func=AF.Square), r=[o.b], w=[osq])
                pS_ = getA()
                op("pe", lambda e, pS_=pS_, osq=osq: e.matmul(pS_[:, :], lhsT=cst[:, BD64, :], rhs=osq[:, :], start=True, stop=True), r=[cst, osq], w=[pS_])
                op("act", lambda e, pS_=pS_, rt=rt: e.activation(out=rt.d, in_=pS_[:, :], func=AF.Ln, bias=pcol("eps"), scale=1.0), r=[pv], w=[rt.b, pS_])
                op("act", lambda e, rt=rt: e.activation(out=rt.d, in_=rt.d, func=AF.Exp, scale=-0.5), r=[], w=[rt.b])
                op("dve", lambda e, p=p, o=o, rt=rt: e.scalar_tensor_tensor(out=o.d, in0=o.d, scalar=pcol(nwname, p), in1=rt.d, op0=ALU.mult, op1=ALU.mult),
                   r=[rt.b, pv], w=[o.b], k=2)
                op("pool", lambda e, p=p, o=o: e.tensor_tensor(out=yT[:, ycol0 + p, :], in0=o.d, in1=sgate[p].d, op=ALU.mult), r=[o.b, sgate[p].b], w=[yTc[ycol0 + p]], k=2)
            for b_ in pO + pX + [pUT]:
                b_.held = False

        s2, w2 = wload(win_d[l, :, :, 1024:1536], [128, 8, 512])
        s6, w6 = wload(win_d[l, :, :, 3072:3584], [128, 8, 512])
        s3, w3 = wload(win_d[l, :, :, 1536:2048], [128, 8, 512])
        if l == 0:
            sch.dma("sp", tabs[:, :, :], tab_d[:, :, ti * T:(ti + 1) * T], w=[tabs], nbytes=2 << 20)
        qT = [getB(), getB()]
        kT = [getB(), getB()]
        for dst, cbase, tb in ((qT, 0, 0), (kT, 256, 2)):
            for p in range(2):
                pa = proj(s2, w2, cbase + p * 128, [hT], hk)
                pb = proj(s6, w6, cbase + p * 128, [hT], hk)
                t1, t2 = getF(), getF()
                op("dve", lambda e, p=p, tb=tb, t1=t1, pa=pa: e.tensor_tensor(out=t1.d, in0=pa[:, :], in1=tabs[:, p * 4 + tb, :], op=ALU.mult), r=[tabs], w=[t1.b, pa])
                op("dve", lambda e, p=p, tb=tb, t2=t2, pb=pb: e.tensor_tensor(out=t2.d, in0=pb[:, :], in1=tabs[:, p * 4 + tb + 1, :], op=ALU.mult), r=[tabs], w=[t2.b, pb])
                op("pool", lambda e, p=p, dst=dst, t1=t1, t2=t2: e.tensor_tensor(out=dst[p][:, :], in0=t1.d, in1=t2.d, op=ALU.add), r=[t1.b, t2.b], w=[dst[p]], k=2)
        for b_ in range(4):
            pg = getA()

            def fn_v(e, b_=b_, pg=pg):
                ins = None
                for kc in range(8):
                    ins = e.matmul(pg[:, 0:256], lhsT=hT[:, kc, b_ * 128:(b_ + 1) * 128], rhs=w3[:, kc, 0:256], start=(kc == 0), stop=(kc == 7))
                return ins

            op("pe", fn_v, r=[hT, s3], w=[pg], n=256, nmm=8)
            op("act", lambda e, b_=b_, pg=pg: e.activation(out=vtok[:, b_, :], in_=pg[:, 0:256], func=AF.Copy), r=[], w=[vtok, pg], n=256)
        sg = [getF(), getF()]
        for p in range(2):
            pa = proj(s3, w3, 256 + p * 128, [hT], hk)
            op("act", lambda e, p=p, pa=pa, sg=sg: e.activation(out=sg[p].d, in_=pa[:, :], func=AF.Silu), r=[], w=[sg[p].b, pa])
        gla("r", qT, kT, 128, 128, sg, "rnw%d" % l, 4, [g ** 128 for g in RET_GAMMA])

        s4, w4 = wload(win_d[l, :, :, 2048:2560], [128, 8, 512])
        s5, w5 = wload(win_d[l, :, :, 2560:3072], [128, 8, 512])
        qT = [getB(), getB()]
        kT = [getB(), getB()]
        for p in range(2):
            sig, ff, bb, en = getF(), getF(), getF(), getF()
            pa = proj(s4, w4, 256 + p * 128, [hT], hk)
            op("act", lambda e, sig=sig, pa=pa: e.activation(out=sig.d, in_=pa[:, :], func=AF.Sigmoid), r=[], w=[sig.b, pa])
            op("dve", lambda e, p=p, sig=sig, ff=ff: e.tensor_scalar(out=ff.d, in0=sig.d, scalar1=dcol(OML, l, p), scalar2=dcol(LB, l, p), op0=ALU.mult, op1=ALU.add),
               r=[sig.b, der], w=[ff.b])
            op("act", lambda e, ff=ff: e.activation(out=ff.d, in_=ff.d, func=AF.Ln), r=[], w=[ff.b])
            op("dve", lambda e, ff=ff, bb=bb: e.tensor_tensor_scan(out=bb.d, data0=rmask[:, :], data1=ff.d, initial=0.0, op0=ALU.mult, op1=ALU.add),
               r=[rmask, ff.b], w=[bb.b], k=2)
            op("pool", lambda e, p=p, sig=sig: e.tensor_scalar(out=sig.d, in0=sig.d, scalar1=dcol(NOML, l, p), scalar2=dcol(OML, l, p), op0=ALU.mult, op1=ALU.add),
               r=[der], w=[sig.b])
            op("act", lambda e, bb=bb, en=en: e.activation(out=en.d, in_=bb.d, func=AF.Exp, scale=-1.0), r=[bb.b], w=[en.b])
            op("dve", lambda e, p=p, sig=sig, en=en, kT=kT: e.tensor_tensor(out=kT[p][:, :], in0=sig.d, in1=en.d, op=ALU.mult), r=[sig.b, en.b], w=[kT[p]], k=2)
            op("act", lambda e, bb=bb, ff=ff: e.activation(out=ff.d, in_=bb.d, func=AF.Exp), r=[bb.b], w=[ff.b])
            op("dve", lambda e, p=p, ff=ff: e.tensor_copy(out=ebl[p][:, :], in_=ff.d.rearrange("p (c t) -> p c t", t=32)[:, :, 31]), r=[ff.b], w=[ebl[p]], n=16)
            pb = proj(s4, w4, p * 128, [hT], hk)
            op("act", lambda e, en=en, pb=pb: e.activation(out=en.d, in_=pb[:, :], func=AF.Silu), r=[], w=[en.b, pb])
            op("dve", lambda e, p=p, en=en, ff=ff, qT=qT: e.tensor_tensor(out=qT[p][:, :], in0=en.d, in1=ff.d, op=ALU.mult), r=[en.b, ff.b], w=[qT[p]], k=2)
        for b_ in range(8):
            pg = getA()

            def fn_v2(e, b_=b_, pg=pg):
                ins = None
                for kc in range(8):
                    ins = e.matmul(pg[0:64, 0:256], lhsT=hT[:, kc, b_ * 64:(b_ + 1) * 64], rhs=w5[:, kc, 0:256], start=(kc == 0), stop=(kc == 7))
                return ins

            op("pe", fn_v2, r=[hT, s5], w=[pg], n=256, nmm=8)
            op("act", lambda e, b_=b_, pg=pg: e.activation(out=vtok[0:64, b_, :], in_=pg[0:64, 0:256], func=AF.Copy), r=[], w=[vtok, pg], n=256)
        sg = [getF(), getF()]
        for p in range(2):
            pa = proj(s5, w5, 256 + p * 128, [hT], hk)
            op("act", lambda e, p=p, pa=pa, sg=sg: e.activation(out=sg[p].d, in_=pa[:, :], func=AF.Silu), r=[], w=[sg[p].b, pa])
        gla("h", qT, kT, 32, 64, sg, "hnw%d" % l, 6, None)

        for half in range(2):
            so, wo = wload(wout_d[l, :, :, half * 512:(half + 1) * 512], [128, 8, 512])
            for mq in range(4):
                m = half * 4 + mq
                pb = proj(so, wo, mq * 128, yTc, lambda kc: yT[:, kc, :])
                op("dve", lambda e, m=m, pb=pb: e.tensor_tensor(out=xr[:, m, :], in0=pb[:, :], in1=xr[:, m, :], op=ALU.add), r=[], w=[xr, pb])
                sq_chunk(xr, m)

        def down_half(hf):
            for m in range(8):
                sd, wd = wload(wdn_d[l, m, hf], [128, 11, 128])
                pb = getA()

                def fn_d(e, pb=pb, wd=wd):
                    ins = None
                    for kc in range(11):
                        ins = e.matmul(pb[:, :], lhsT=wd[:, kc, :], rhs=gT[:, hf * 11 + kc, :], start=(kc == 0), stop=(kc == 10))
                    return ins

                op("pe", fn_d, r=[sd] + gTc[hf * 11:(hf + 1) * 11], w=[pb], nmm=11)
                op("dve", lambda e, m=m, pb=pb: e.tensor_tensor(out=xr[:, m, :], in0=pb[:, :], in1=xr[:, m, :], op=ALU.add), r=[], w=[xr, pb])
                if hf == 1:
                    sq_chunk(xr, m)

        norm_to_hT(xr, "n2w%d" % l, presq=True)
        for q in range(11):
            su, wu = wload(wup_d[l, :, :, q * 512:(q + 1) * 512], [128, 8, 512])
            for jj in range(2):
                j = 2 * q + jj
                outs = []
                for (half, col) in ((0, jj * 128), (1, 256 + jj * 128)):
                    ch = half * 22 + j
                    ub, cb = getF(), getF()
                    pb = proj(su, wu, col, [hT], hk)
                    fhb = fh[(l, ch)]
                    op("pool", lambda e, ub=ub, ch=ch: e.tensor_copy(out=ub.b[:, 2:4], in_=fh_t[l][:, ch, :]), r=[fhb], w=[ub.h], n=2)
                    op("act", lambda e, ub=ub, pb=pb: e.activation(out=ub.d, in_=pb[:, :], func=AF.Copy), r=[], w=[ub.b, pb])
                    op("act", lambda e, cb=cb, pb=pb, ch=ch: e.activation(out=cb.d, in_=pb[:, :], func=AF.Identity, bias=pcol("fcb%d" % l, ch), scale=pcol("fcw%d_2" % l, ch)),
                       r=[pv], w=[cb.b, pb])
                    op("pool", lambda e, ub=ub, ch=ch: e.tensor_copy(out=fh_t[l][:, ch, :], in_=ub.b[:, T + 2:T + 4]), r=[ub.b], w=[fhb], n=2)
                    for k in (1, 0):
                        op("dve", lambda e, ub=ub, cb=cb, ch=ch, k=k: e.scalar_tensor_tensor(out=cb.d, in0=ub.b[:, 2 + k:2 + k + T], scalar=pcol("fcw%d_%d" % (l, k), ch), in1=cb.d,
                                                                                              op0=ALU.mult, op1=ALU.add), r=[ub.b, ub.h, pv], w=[cb.b], k=2)
                    outs.append(cb)
                cg, cv = outs
                op("act", lambda e, cg=cg: e.activation(out=cg.d, in_=cg.d, func=AF.Silu), r=[], w=[cg.b])
                op("pool", lambda e, cg=cg, cv=cv, j=j: e.tensor_tensor(out=gT[:, j, :], in0=cg.d, in1=cv.d, op=ALU.mult), r=[cg.b, cv.b], w=[gTc[j]], k=2)
            if q == 5:
                down_half(0)
        down_half(1)

    xv = xT_d.rearrange("(c p) s -> p c s", p=128)
    yv = yT_d.rearrange("(c p) s -> p c s", p=128)
    sndB = Buf(snd_t, "snd")
    rcvB = [Buf(rcv_t[i], "rcv%d" % i) for i in range(2)]
    snd_v = snd_t.ap().rearrange("(c p) s -> p c s", p=128)
    out_toks = []
    for st in range(NS):
        xr = xres[st % 2]
        sch.dma("sp", xr[:, :, :], xv[:, :, st * T:(st + 1) * T], w=[xr], nbytes=2 << 20)
        op("act", lambda e, xr=xr: e.activation(out=xr[:, :, :], in_=xr[:, :, :], func=AF.Copy, scale=pcol("m")), r=[pv], w=[xr], n=8 * T)
        if st >= SKEW:
            rb = rcvB[(st - SKEW) % 2]
            rcv_v = rcv_t[(st - SKEW) % 2].ap()[0:D, :].rearrange("(c p) s -> p c s", p=128)
            sch.dma("sp", rv[:, :, :], rcv_v, r=[rb], w=[rv], nbytes=2 << 20)
            op("dve", lambda e, xr=xr: e.scalar_tensor_tensor(out=xr[:, :, :], in0=rv[:, :, :], scalar=pcol("om"), in1=xr[:, :, :], op0=ALU.mult, op1=ALU.add),
               r=[rv, pv], w=[xr], n=8 * T, k=2)
        emit_layer(0, xr, st)
        if st < NT:
            sch.dma("sp", snd_v, xr[:, :, :], r=[xr], w=[sndB], nbytes=2 << 20)
            rb = rcvB[st % 2]
            sch.raw("pool", lambda e, rb=rb: e.collective_compute("AllGather", ALU.bypass, replica_groups=GROUPS,
                                                                   ins=[snd_t.ap().opt()], outs=[rb.t.ap().opt()]),
                    r=[sndB], w=[rb], sem_buf=rb, inc=1, occ=1.0, lat=120.0)
        if st == SKEW - 1:
            kp = pcol("keep")
            op("dve", lambda e: e.tensor_scalar(out=hst_t[0][:, :], in0=hst_t[0][:, :], scalar1=kp, scalar2=None, op0=ALU.mult), r=[pv], w=[hst[(0, c)] for c in range(4)], n=4)
            op("dve", lambda e: e.tensor_scalar(out=lxh_t[0][:, :, :], in0=lxh_t[0][:, :, :], scalar1=kp, scalar2=None, op0=ALU.mult), r=[pv], w=[lxh[(0, c)] for c in range(4)], n=16)
            op("dve", lambda e: e.tensor_scalar(out=fh_t[0][:, :, :], in0=fh_t[0][:, :, :], scalar1=kp, scalar2=None, op0=ALU.mult), r=[pv], w=[fh[(0, ch)] for ch in range(NFC)], n=88)
            for key in list(st_f.keys()):
                lo = key[3] * 64
                for tbl in (st_f, st_b):
                    bb_ = tbl[key]
                    op("dve", lambda e, bb_=bb_, lo=lo: e.tensor_scalar(out=bb_[lo:lo + 64, :], in0=bb_[lo:lo + 64, :], scalar1=pv[lo:lo + 64, PV["keep"]:PV["keep"] + 1],
                                                                          scalar2=None, op0=ALU.mult), r=[pv], w=[bb_], n=64)
        rs = rms_rstd(xr, presq=True)
        for c in range(8):
            op("dve", lambda e, c=c, xr=xr, rs=rs: e.scalar_tensor_tensor(out=xr[:, c, :], in0=xr[:, c, :], scalar=pcol("fnw", c), in1=rs.d,
                                                                           op0=ALU.mult, op1=ALU.mult), r=[rs.b, pv], w=[xr], k=2)
        out_toks.append(sch.dma("sp", yv[:, :, st * T:(st + 1) * T], xr[:, :, :], r=[xr], nbytes=2 << 20))
    sch.wait_all_on("sp", out_toks)
    sch.emit()
    return nc, sch


def _chunks(v):
    v = np.asarray(v, np.float32)
    return np.ascontiguousarray(v.reshape(-1, 128).T)


def make_tables(S):
    t = np.arange(S, dtype=np.float64)
    inv = 10000.0 ** (-np.arange(0, 64, 2, dtype=np.float64) / 64.0)
    ang = t[None, :] * inv[:, None]
    cos = np.concatenate([np.cos(ang), np.cos(ang)], 0)
    sin = np.concatenate([-np.sin(ang), np.sin(ang)], 0)
    n1 = (t % 128) + 1.0
    tabs = np.zeros((128, 8, S), np.float64)
    for p in range(2):
        for hh in range(2):
            h = 2 * p + hh
            lg = math.log(RET_GAMMA[h])
            qd = np.exp(n1 * lg)[None, :]
            kd = np.exp(-n1 * lg)[None, :] * (64.0 ** -0.5)
            sl = slice(hh * 64, hh * 64 + 64)
            tabs[sl, p * 4 + 0] = cos * qd
            tabs[sl, p * 4 + 1] = sin * qd
            tabs[sl, p * 4 + 2] = cos * kd
            tabs[sl, p * 4 + 3] = sin * kd
    return tabs.astype(np.float32)


def make_consts():
    cst = np.zeros((128, 5, 128), np.float32)
    cst[:, 0, :] = 1.0
    cst[:, 1, :] = np.eye(128, dtype=np.float32)
    for b in range(2):
        cst[b * 64:(b + 1) * 64, 2, b * 64:(b + 1) * 64] = 1.0 / 64.0
    m = np.arange(128)[:, None]
    n = np.arange(128)[None, :]
    cst[:, 3, :] = (n >= m).astype(np.float32)
    m32 = (np.arange(32)[None, :] >= np.arange(32)[:, None]).astype(np.float32)
    for b in range(4):
        cst[b * 32:(b + 1) * 32, 4, 0:32] = m32
    rmask = np.ones((128, T), np.float32)
    rmask[:, ::32] = 0.0
    return cst, rmask


def prep_role(inp, role, NT):
    lyr = role
    NS = NT + SKEW
    w_in = np.asarray(inp["w_in"], np.float32)[lyr:lyr + 1]
    swap = lambda base: [base + h * 64 + ((d + 32) % 64) for h in range(4) for d in range(64)]
    idx = list(range(3072)) + swap(1024) + swap(1280)
    win = w_in[:, :, idx].reshape(1, 8, 128, NCOL_IN).transpose(0, 2, 1, 3)
    wout = np.asarray(inp["w_out"], np.float32)[lyr:lyr + 1].reshape(1, 8, 128, D).transpose(0, 2, 1, 3)
    perm = []
    for q in range(11):
        perm += list(range(2 * q * 128, (2 * q + 2) * 128))
        perm += list(range(DFF + 2 * q * 128, DFF + (2 * q + 2) * 128))
    wup = np.asarray(inp["ffn_w_up"], np.float32)[lyr:lyr + 1][:, :, perm].reshape(1, 8, 128, 2 * DFF).transpose(0, 2, 1, 3)
    wdn = np.asarray(inp["ffn_w_down"], np.float32)[lyr:lyr + 1].reshape(1, 2, 11, 128, 8, 128).transpose(0, 4, 1, 3, 2, 5)
    gw = np.zeros((1, 128, 8, 128), np.float32)
    wa = np.asarray(inp["lru_wa"], np.float32)
    wx = np.asarray(inp["lru_wx"], np.float32)
    for c in range(4):
        for b2 in range(2):
            sl = slice(b2 * 64, b2 * 64 + 64)
            gw[0, sl, c * 2 + 0, sl] = wa[lyr, 2 * c + b2]
            gw[0, sl, c * 2 + 1, sl] = wx[lyr, 2 * c + b2]
    pvec = np.zeros((128, PV["_n"]), np.float32)

    def put(name, v):
        a = _chunks(v)
        pvec[:, PV[name]:PV[name] + a.shape[1]] = a

    put("n1w0", inp["norm1_w"][lyr])
    put("n2w0", inp["norm2_w"][lyr])
    for k in range(4):
        put("lcw0_%d" % k, inp["lru_conv_w"][lyr][k])
    put("lcb0", inp["lru_conv_b"][lyr])
    put("ba0", inp["lru_ba"][lyr])
    put("bx0", inp["lru_bx"][lyr])
    put("lam0", inp["lru_lambda"][lyr])
    put("rnw0", inp["ret_norm_w"][lyr])
    put("hnw0", inp["hg_norm_w"][lyr])
    put("hbA", inp["hg_lower_bounds"][0])
    put("hbB", inp["hg_lower_bounds"][1])
    for k in range(3):
        put("fcw0_%d" % k, inp["ffn_conv_w"][lyr][k])
    put("fcb0", inp["ffn_conv_b"][lyr])
    put("fnw", inp["final_norm_w"])
    pvec[:, PV["eps"]] = EPS
    pvec[:, PV["one"]] = 1.0
    pvec[:, PV["m"]] = 1.0 if role == 0 else 0.0
    pvec[:, PV["om"]] = 0.0 if role == 0 else 1.0
    pvec[:, PV["keep"]] = 1.0 if role == 0 else 0.0
    pvec[:, PV["lbm"]] = 0.0 if role == 0 else 1.0
    cst, rmask = make_consts()
    tb = make_tables(NT * T)
    tabs = np.zeros((128, 8, NS * T), np.float32)
    for st in range(NS):
        ti = st if role == 0 else st - SKEW
        if ti < 0 or ti >= NT:
            ti = 0
        tabs[:, :, st * T:(st + 1) * T] = tb[:, :, ti * T:(ti + 1) * T]
    return {
        "win": np.ascontiguousarray(win), "wout": np.ascontiguousarray(wout),
        "wup": np.ascontiguousarray(wup), "wdn": np.ascontiguousarray(wdn),
        "gatew": gw, "pvec": pvec, "tabs": tabs, "cst": cst, "rmask": rmask,
    }


_CACHE = {}


def kernel(**inputs):
    x = np.asarray(inputs["x"], np.float32)
    B, S, _ = x.shape
    NT = S // T
    NS = NT + SKEW
    roles = [prep_role(inputs, r, NT) for r in range(2)]
    if NT not in _CACHE:
        _CACHE[NT] = build(NT)
    nc, sch = _CACHE[NT]
    ncores = 2 * B
    zeros = np.zeros((D, NS * T), np.float32)
    in_maps = []
    for c in range(ncores):
        b, r = c // 2, c % 2
        m = dict(roles[r])
        if r == 0:
            xp = np.zeros((D, NS * T), np.float32)
            xp[:, :S] = x[b].T
            m["xT"] = xp
        else:
            m["xT"] = zeros
        in_maps.append(m)
    res = run_bass_kernel_spmd(nc, in_maps, core_ids=list(range(ncores)))
    out = np.empty((B, S, D), np.float32)
    for b in range(B):
        out[b] = res.results[2 * b + 1]["yT"][:, SKEW * T:].T
    return out
```

```python
import contextlib
import math
import numpy as np
import concourse.bass as bass
import concourse.mybir as mybir
from concourse.bass_utils import run_bass_kernel_spmd

F32 = mybir.dt.float32
BF16 = mybir.dt.bfloat16
AF = mybir.ActivationFunctionType
ALU = mybir.AluOpType

EPOCH = 4000

D = 1024
SEQ = 8192
BATCH = 4
DEPTH = 2
NL = 1
SKEW = 2
T = 512
DFF = 2816
NFC = 44
EPS = 1e-6
NCOL_IN = 3584


import heapq
import sys


class Buf:
    def __init__(self, t, name):
        self.t = t
        self.name = name
        self.last_write = None
        self.reads = []
        self.dma_cnt = 0
        self.g_lw = None
        self.g_rd = []
        self.held = False

    def __getitem__(self, key):
        return self.t[key]

    def sub(self, name):
        return Buf(self.t, name)


class _Op:
    __slots__ = ("eng", "fn", "r", "w", "occ", "lat", "kind", "sem_buf", "preds", "succs", "inc", "line", "start")


class Sched:
    ENG = ("pe", "act", "dve", "pool", "sp")

    def __init__(self, nc):
        self.nc = nc
        self.stack = contextlib.ExitStack()
        self.rec = []
        self.ops = {e: [] for e in self.ENG}
        self.cnt = {e: 0 for e in self.ENG}
        self.epoch = {e: 0 for e in self.ENG}
        self.sems = {}
        self.waited = {e: {} for e in self.ENG}
        self.nsem = 0
        self.final_waits = []
        self.sim_time = 0.0

    def sem(self, key):
        if key not in self.sems:
            self.sems[key] = self.stack.enter_context(self.nc.semaphore("s%d" % self.nsem))
            self.nsem += 1
        return self.sems[key]

    def sb(self, name, shape, dtype=F32):
        t = self.stack.enter_context(self.nc.sbuf_tensor(name, list(shape), dtype))
        return Buf(t, name)

    def ps(self, name, shape, dtype=F32):
        t = self.stack.enter_context(self.nc.psum_tensor(name, list(shape), dtype))
        return Buf(t, name)

    def _record(self, o):
        i = len(self.rec)
        try:
            f = sys._getframe(2)
            while f.f_code.co_name in ("proj", "wload", "getA", "op", "dma", "raw"):
                f = f.f_back
            o.line = f.f_lineno
        except Exception:
            o.line = 0
        preds = set()
        for b in o.r:
            if b.g_lw is not None:
                preds.add(b.g_lw)
        for b in o.w:
            if b.g_lw is not None:
                preds.add(b.g_lw)
            preds.update(b.g_rd)
        preds.discard(i)
        o.preds = preds
        o.succs = []
        for p in preds:
            self.rec[p].succs.append(i)
        for b in o.w:
            b.g_lw = i
            b.g_rd = []
            if hasattr(b, "touch"):
                b.touch = i
        for b in o.r:
            if b not in o.w:
                b.g_rd.append(i)
        self.rec.append(o)
        return i

    def op(self, eng, fn, r=(), w=(), n=T, k=1.0, nmm=1):
        o = _Op()
        o.eng, o.fn, o.r, o.w, o.kind, o.sem_buf = eng, fn, tuple(r), tuple(w), "c", None
        if eng == "pe":
            o.occ = nmm * (0.19 + 0.0002 * n)
        elif eng == "act":
            o.occ = 0.22 + 0.00072 * n
        elif eng == "dve":
            o.occ = 0.07 + 0.00105 * n * k
        else:
            o.occ = 0.4 + 0.002 * n
        o.lat = o.occ + 0.1
        return self._record(o)

    def dma(self, q, out_ap, in_ap, r=(), w=(), sem_buf=None, nbytes=1 << 20):
        o = _Op()
        o.eng, o.r, o.w, o.kind = q, tuple(r), tuple(w), "d"
        o.sem_buf = sem_buf if sem_buf is not None else (w[0] if len(w) else r[0])

        def fn(e, out_ap=out_ap, in_ap=in_ap):
            return e.dma_start(out=out_ap, in_=in_ap)

        o.fn = fn
        o.occ = 1.0 if q == "pool" else 0.15
        o.lat = o.occ + 2.0 + nbytes / 150e3
        o.inc = 16
        return self._record(o)

    def raw(self, q, fn, r=(), w=(), sem_buf=None, inc=1, occ=1.0, lat=50.0):
        o = _Op()
        o.eng, o.r, o.w, o.kind = q, tuple(r), tuple(w), "d"
        o.sem_buf = sem_buf
        o.fn = fn
        o.occ, o.lat = occ, lat
        o.inc = inc
        return self._record(o)

    def wait_all_on(self, eng, idxs):
        self.final_waits.append((eng, list(idxs)))

    def _schedule(self):
        ops = self.rec
        ENG = self.ENG
        ready = {e: [] for e in ENG}
        busy = {e: 0.0 for e in ENG}
        npred = [len(o.preds) for o in ops]
        for i, o in enumerate(ops):
            if npred[i] == 0:
                heapq.heappush(ready[o.eng], i)
        events = []
        now = 0.0
        order = []
        while True:
            for e in ENG:
                if ready[e] and busy[e] <= now + 1e-9:
                    i = heapq.heappop(ready[e])
                    o = ops[i]
                    order.append(i)
                    o.start = now
                    busy[e] = now + o.occ
                    heapq.heappush(events, (now + o.lat, 1, i))
                    heapq.heappush(events, (now + o.occ, 0, i))
            if not events:
                break
            t, typ, i = heapq.heappop(events)
            now = t
            if typ == 1:
                for s_ in ops[i].succs:
                    npred[s_] -= 1
                    if npred[s_] == 0:
                        heapq.heappush(ready[ops[s_].eng], s_)
        assert len(order) == len(ops), (len(order), len(ops))
        self.sim_time = now
        return order

    def _deps(self, r, w):
        deps = []
        for b in r:
            if b.last_write is not None:
                deps.append(b.last_write)
        for b in w:
            if b.last_write is not None:
                deps.append(b.last_write)
            deps.extend(b.reads)
        return deps

    def _waits(self, eng, deps):
        need = {}
        wd = self.waited[eng]
        for (k, v) in deps:
            if wd.get(k, 0) >= v:
                continue
            if need.get(k, 0) < v:
                need[k] = v
        for k, v in need.items():
            wd[k] = v
        return [(self.sem(k), v) for k, v in need.items()]

    def _commit(self, r, w, tok):
        for b in w:
            b.last_write = tok
            b.reads = []
        for b in r:
            if b in w:
                continue
            b.reads = [x for x in b.reads if x[0] != tok[0]] + [tok]

    def _replay_one(self, o):
        eng = o.eng
        deps = self._deps(o.r, o.w)
        if o.kind == "c":
            if eng == "pe":
                deps = [d for d in deps if not (d[0][0] == "e" and d[0][1] == "pe")]
            waits = self._waits(eng, deps)
            if self.cnt[eng] >= EPOCH:
                self.epoch[eng] += 1
                self.cnt[eng] = 0
            self.cnt[eng] += 1
            key = ("e", eng, self.epoch[eng])
            tok = (key, self.cnt[eng])
            self.ops[eng].append((o.fn, waits, self.sem(key), 1))
        else:
            waits = self._waits(eng, deps)
            sb_ = o.sem_buf
            key = ("d", id(sb_))
            inc = getattr(o, "inc", 16)
            sb_.dma_cnt += inc
            tok = (key, sb_.dma_cnt)
            self.ops[eng].append((o.fn, waits, self.sem(key), inc))
        self._commit(o.r, o.w, tok)
        return tok

    def emit(self):
        nc = self.nc
        order = self._schedule()
        toks = {}
        for i in order:
            toks[i] = self._replay_one(self.rec[i])
        for eng, idxs in self.final_waits:
            waits = self._waits(eng, [toks[i] for i in idxs])
            self.ops[eng].append((None, waits, None, 0))

        def replay(engobj, lst):
            for (fn, waits, s, inc) in lst:
                for (ws, v) in waits:
                    engobj.wait_ge(ws, v)
                if fn is not None:
                    fn(engobj).then_inc(s, inc)

        with nc.Block() as block:
            @block.tensor
            def _(e):
                replay(e, self.ops["pe"])

            @block.scalar
            def _(e):
                replay(e, self.ops["act"])

            @block.vector
            def _(e):
                replay(e, self.ops["dve"])

            @block.gpsimd
            def _(e):
                replay(e, self.ops["pool"])

            @block.sync
            def _(e):
                replay(e, self.ops["sp"])

    def close(self):
        self.stack.close()


def _pvec_layout():
    lay = {}
    col = 0

    def add(name, n):
        nonlocal col
        lay[name] = col
        col += n

    for l in range(NL):
        add("n1w%d" % l, 8)
        add("n2w%d" % l, 8)
        for k in range(4):
            add("lcw%d_%d" % (l, k), 4)
        add("lcb%d" % l, 4)
        add("ba%d" % l, 4)
        add("bx%d" % l, 4)
        add("lam%d" % l, 4)
        add("rnw%d" % l, 2)
        add("hnw%d" % l, 2)
        for k in range(3):
            add("fcw%d_%d" % (l, k), NFC)
        add("fcb%d" % l, NFC)
    add("fnw", 8)
    add("hbA", 2)
    add("hbB", 2)
    add("m", 1)
    add("om", 1)
    add("keep", 1)
    add("lbm", 1)
    add("eps", 1)
    add("one", 1)
    lay["_n"] = col
    return lay


PV = _pvec_layout()
RET_GAMMA = [1.0 - 2.0 ** (-5.0 - h) for h in range(4)]


def build(NT, nlayers=NL, debug_out=None):
    NS = NT + SKEW
    S = NS * T
    nc = bass.Bass("TRN2", target_bir_lowering=False)
    xT_d = nc.dram_tensor("xT", [D, S], F32, kind="ExternalInput").ap()
    win_d = nc.dram_tensor("win", [NL, 128, 8, NCOL_IN], F32, kind="ExternalInput").ap()
    wout_d = nc.dram_tensor("wout", [NL, 128, 8, D], F32, kind="ExternalInput").ap()
    wup_d = nc.dram_tensor("wup", [NL, 128, 8, 2 * DFF], F32, kind="ExternalInput").ap()
    wdn_d = nc.dram_tensor("wdn", [NL, 8, 2, 128, 11, 128], F32, kind="ExternalInput").ap()
    gw_d = nc.dram_tensor("gatew", [NL, 128, 8, 128], F32, kind="ExternalInput").ap()
    pv_d = nc.dram_tensor("pvec", [128, PV["_n"]], F32, kind="ExternalInput").ap()
    tab_d = nc.dram_tensor("tabs", [128, 8, S], F32, kind="ExternalInput").ap()
    cst_d = nc.dram_tensor("cst", [128, 5, 128], F32, kind="ExternalInput").ap()
    rm_d = nc.dram_tensor("rmask", [128, T], F32, kind="ExternalInput").ap()
    yT_d = nc.dram_tensor("yT", [D, S], F32, kind="ExternalOutput").ap()
    snd_t = nc.dram_tensor("snd", [D, T], F32)
    rcv_t = [nc.dram_tensor("rcv%d" % i, [2 * D, T], F32) for i in range(2)]
    GROUPS = [[0, 1], [2, 3], [4, 5], [6, 7]]

    sch = Sched(nc)
    sb, ps = sch.sb, sch.ps
    op = sch.op

    pv = sb("pv", [128, PV["_n"]])
    sch.dma("sp", pv[:, :], pv_d[:, :], w=[pv], nbytes=1 << 18)
    cstf = sb("cstf", [128, 5, 128])
    sch.dma("sp", cstf[:, :, :], cst_d[:, :, :], w=[cstf], nbytes=1 << 18)
    cst = sb("cstb", [128, 5, 128], BF16)
    op("dve", lambda e: e.tensor_copy(out=cst[:, :, :], in_=cstf[:, :, :]), r=[cstf], w=[cst], n=640)
    ONES, IDENT, BD64, MASK128, MASK32 = range(5)
    rmask = sb("rmask_s", [128, T])
    sch.dma("sp", rmask[:, :], rm_d[:, :], w=[rmask], nbytes=1 << 18)
    gwf = sb("gwf", [128, NL * 8, 128])
    gwb = sb("gwb", [128, NL * 8, 128], BF16)
    for l in range(NL):
        sch.dma("sp", gwf[:, l * 8:(l + 1) * 8, :], gw_d[l], w=[gwf], nbytes=1 << 19)
    op("dve", lambda e: e.tensor_copy(out=gwb[:, :, :], in_=gwf[:, :, :]), r=[gwf], w=[gwb], n=2048)

    def pcol(name, c=0, lo=0, hi=128):
        k = PV[name] + c
        return pv[lo:hi, k:k + 1]

    der = sb("der", [128, 32])
    for l in range(NL):
        k = PV["lam%d" % l]
        op("act", lambda e, k=k, l=l: e.activation(out=der[:, l * 4:l * 4 + 4], in_=pv[:, k:k + 4], func=AF.Exp, scale=-1.0), r=[pv], w=[der], n=4)
        op("act", lambda e, l=l: e.activation(out=der[:, l * 4:l * 4 + 4], in_=der[:, l * 4:l * 4 + 4], func=AF.Ln, bias=pcol("one"), scale=1.0), r=[der, pv], w=[der], n=4)
        op("dve", lambda e, l=l: e.tensor_scalar(out=der[:, l * 4:l * 4 + 4], in0=der[:, l * 4:l * 4 + 4], scalar1=-8.0, scalar2=None, op0=ALU.mult), r=[der], w=[der], n=4)
    LB, OML, NOML = 8, 12, 16
    kA, kB = PV["hbA"], PV["hbB"]
    op("dve", lambda e: e.tensor_tensor(out=der[:, 20:22], in0=pv[:, kB:kB + 2], in1=pv[:, kA:kA + 2], op=ALU.subtract), r=[pv, der], w=[der], n=2)
    op("act", lambda e: e.activation(out=der[:, 22:24], in_=der[:, 20:22], func=AF.Sigmoid), r=[der], w=[der], n=2)
    op("dve", lambda e: e.tensor_scalar(out=der[:, LB:LB + 2], in0=der[:, 22:24], scalar1=pcol("lbm"), scalar2=None, op0=ALU.mult), r=[pv], w=[der], n=2)
    op("dve", lambda e: e.tensor_scalar(out=der[:, OML:OML + 2], in0=der[:, LB:LB + 2], scalar1=-1.0, scalar2=1.0, op0=ALU.mult, op1=ALU.add), r=[], w=[der], n=2)
    op("dve", lambda e: e.tensor_scalar(out=der[:, NOML:NOML + 2], in0=der[:, OML:OML + 2], scalar1=-1.0, scalar2=None, op0=ALU.mult), r=[], w=[der], n=2)

    def dcol(base, l, c, lo=0, hi=128):
        k = base + l * (4 if base == 0 else 2) + c
        return der[lo:hi, k:k + 1]

    hst_t = [sb("hst%d" % l, [128, 4]) for l in range(NL)]
    lxh_t = [sb("lxh%d" % l, [128, 4, 4]) for l in range(NL)]
    fh_t = [sb("fh%d" % l, [128, NFC, 2]) for l in range(NL)]
    hst = {}
    lxh = {}
    fh = {}
    st_f = {}
    st_b = {}
    for l in range(NL):
        for c in range(4):
            hst[(l, c)] = hst_t[l].sub("hst%d_%d" % (l, c))
            lxh[(l, c)] = lxh_t[l].sub("lxh%d_%d" % (l, c))
        for ch in range(NFC):
            fh[(l, ch)] = fh_t[l].sub("fh%d_%d" % (l, ch))
        op("dve", lambda e, l=l: e.memset(hst_t[l][:, :], 0.0), w=[hst[(l, c)] for c in range(4)], n=4)
        op("dve", lambda e, l=l: e.memset(lxh_t[l][:, :, :], 0.0), w=[lxh[(l, c)] for c in range(4)], n=16)
        op("dve", lambda e, l=l: e.memset(fh_t[l][:, :, :], 0.0), w=[fh[(l, ch)] for ch in range(NFC)], n=88)
        for mix in ("r", "h"):
            for p in range(2):
                tf = sb("stf_%s%d%d" % (mix, l, p), [128, 64])
                tb = sb("stb_%s%d%d" % (mix, l, p), [128, 64], BF16)
                for hh in range(2):
                    bf_ = tf.sub("stf_%s%d%d%d" % (mix, l, p, hh))
                    bb_ = tb.sub("stb_%s%d%d%d" % (mix, l, p, hh))
                    st_f[(mix, l, p, hh)] = bf_
                    st_b[(mix, l, p, hh)] = bb_
                    lo = hh * 64
                    op("dve", lambda e, bf_=bf_, lo=lo: e.memset(bf_[lo:lo + 64, :], 0.0), w=[bf_], n=64)
                    op("dve", lambda e, bb_=bb_, lo=lo: e.memset(bb_[lo:lo + 64, :], 0.0), w=[bb_], n=64)

    xres = [sb("xres%d" % i, [128, 8, T]) for i in range(2)]
    rv = sb("rv", [128, 8, T])
    tabs = sb("tabs_s", [128, 8, T])
    hT = sb("hT", [128, 8, T], BF16)
    yT = sb("yTs", [128, 8, T], BF16)
    yTc = [yT.sub("yT%d" % c) for c in range(8)]
    gT = sb("gT", [128, 22, T], BF16)
    gTc = [gT.sub("gT%d" % c) for c in range(22)]
    NSLOT = 4
    wsl = [sb("wsl%d" % i, [128, 4096], BF16) for i in range(NSLOT)]

    class FB:
        def __init__(self, name):
            self.b = sb(name, [128, T + 4])
            self.h = self.b.sub(name + "_h")

        @property
        def d(self):
            return self.b[:, 4:4 + T]

    NF = 14
    Fring = [FB("F%d" % i) for i in range(NF)]
    fstate = {"f": 0, "b": 0, "a": 0, "x": 0, "at": 0}

    def getF():
        f = Fring[fstate["f"] % NF]
        fstate["f"] += 1
        return f

    NB = 8
    Bring = [sb("B%d" % i, [128, T], BF16) for i in range(NB)]

    def getB():
        b = Bring[fstate["b"] % NB]
        fstate["b"] += 1
        return b

    vtok = sb("vtok", [128, 8, 256], BF16)
    ktok = sb("ktok", [128, 8, 2, 128], BF16)
    NAT = 4
    ATb = [sb("AT%d" % i, [128, 128], BF16) for i in range(NAT)]
    ebl = [sb("ebl%d" % p, [128, 16]) for p in range(2)]
    stmp = [sb("stmp%d" % i, [128, 64]) for i in range(4)]

    banks = [ps("bank%d" % i, [128, T]) for i in range(8)]
    for i_, b_ in enumerate(banks):
        b_.touch = -100 + i_
        b_.held = False

    def getA(hold=False):
        a = min((b for b in banks if not b.held), key=lambda b: b.touch)
        a.touch = len(sch.rec)
        a.held = hold
        return a

    wstate = {"n": 0}

    def wload(src_ap, shape):
        s = wsl[wstate["n"] % NSLOT]
        wstate["n"] += 1
        n = 1
        for d_ in shape[1:]:
            n *= d_
        view = s[:, 0:n].rearrange("p (a b) -> p a b", b=shape[2])
        sch.dma("pool", view, src_ap, w=[s], nbytes=128 * n * 4)
        return s, view

    def proj(slot, wview, col, rhs_bufs, rhs_of_kc, nk=8, m=128, ncols=T):
        pbuf = getA()

        def fn(e):
            ins = None
            for kc in range(nk):
                ins = e.matmul(pbuf[0:m, 0:ncols], lhsT=wview[:, kc, col:col + m], rhs=rhs_of_kc(kc), start=(kc == 0), stop=(kc == nk - 1))
            return ins

        op("pe", fn, r=[slot] + list(rhs_bufs), w=[pbuf], n=ncols, nmm=nk)
        return pbuf

    def sq_chunk(xr, m):
        op("act", lambda e, m=m: e.activation(out=gT[:, m, :], in_=xr[:, m, :], func=AF.Square), r=[xr], w=[gTc[m]])

    def rms_rstd(xr, use_hT=False, presq=False):
        scr = hT if use_hT else gT
        scr_b = [hT] if use_hT else gTc[0:8]
        if not presq:
            op("act", lambda e: e.activation(out=scr[:, 0:8, :], in_=xr[:, :, :], func=AF.Square), r=[xr], w=scr_b, n=8 * T)
        pS_ = getA()

        def fn(e):
            ins = None
            for c in range(8):
                ins = e.matmul(pS_[:, :], lhsT=cst[:, ONES, :], rhs=scr[:, c, :], start=(c == 0), stop=(c == 7))
            return ins

        op("pe", fn, r=[cst] + scr_b, w=[pS_], nmm=8)
        rs = getF()
        op("act", lambda e: e.activation(out=rs.d, in_=pS_[:, :], func=AF.Ln, bias=pcol("eps"), scale=1.0 / D), r=[pv], w=[rs.b, pS_])
        op("act", lambda e: e.activation(out=rs.d, in_=rs.d, func=AF.Exp, scale=-0.5), r=[], w=[rs.b])
        return rs

    def norm_to_hT(xr, nwname, use_hT=False, presq=False):
        rs = rms_rstd(xr, use_hT, presq)
        for c in range(8):
            op("dve", lambda e, c=c: e.scalar_tensor_tensor(out=hT[:, c, :], in0=xr[:, c, :], scalar=pcol(nwname, c), in1=rs.d,
                                                              op0=ALU.mult, op1=ALU.mult), r=[xr, rs.b, pv], w=[hT], k=2)

    hk = lambda kc: hT[:, kc, :]

    def emit_layer(l, xr, ti):
        norm_to_hT(xr, "n1w%d" % l, use_hT=True)

        s0, w0 = wload(win_d[l, :, :, 0:512], [128, 8, 512])
        s1, w1 = wload(win_d[l, :, :, 512:1024], [128, 8, 512])
        for c in range(4):
            lxb, xc, rr, ii, aa, mm = getF(), getF(), getF(), getF(), getF(), getF()
            xcb = getB()
            pa = proj(s0, w0, c * 128, [hT], hk)
            op("pool", lambda e, c=c, lxb=lxb: e.tensor_copy(out=lxb.b[:, 1:4], in_=lxh_t[l][:, c, 0:3]), r=[lxh[(l, c)]], w=[lxb.h], n=3)
            op("act", lambda e, lxb=lxb, pa=pa: e.activation(out=lxb.d, in_=pa[:, :], func=AF.Copy), r=[], w=[lxb.b, pa])
            op("pool", lambda e, c=c, lxb=lxb: e.tensor_copy(out=lxh_t[l][:, c, 0:3], in_=lxb.b[:, T + 1:T + 4]), r=[lxb.b], w=[lxh[(l, c)]], n=3)
            op("dve", lambda e, c=c, lxb=lxb, xc=xc: e.tensor_scalar(out=xc.d, in0=lxb.b[:, 1:1 + T], scalar1=pcol("lcw%d_0" % l, c), scalar2=pcol("lcb%d" % l, c),
                                                                      op0=ALU.mult, op1=ALU.add), r=[lxb.b, lxb.h, pv], w=[xc.b])
            for k in range(1, 4):
                op("dve", lambda e, c=c, k=k, lxb=lxb, xc=xc: e.scalar_tensor_tensor(out=xc.d, in0=lxb.b[:, 1 + k:1 + k + T], scalar=pcol("lcw%d_%d" % (l, k), c), in1=xc.d,
                                                                                      op0=ALU.mult, op1=ALU.add), r=[lxb.b, lxb.h, pv], w=[xc.b], k=2)
            op("act", lambda e, xc=xc, xcb=xcb: e.activation(out=xcb[:, :], in_=xc.d, func=AF.Copy), r=[xc.b], w=[xcb])
            pg = getA()
            op("pe", lambda e, c=c, pg=pg, xcb=xcb: e.matmul(pg[:, :], lhsT=gwb[:, l * 8 + c * 2, :], rhs=xcb[:, :], start=True, stop=True), r=[gwb, xcb], w=[pg])
            op("act", lambda e, c=c, pg=pg, rr=rr: e.activation(out=rr.d, in_=pg[:, :], func=AF.Sigmoid, bias=pcol("ba%d" % l, c)), r=[pv], w=[rr.b, pg])
            pg2 = getA()
            op("pe", lambda e, c=c, pg2=pg2, xcb=xcb: e.matmul(pg2[:, :], lhsT=gwb[:, l * 8 + c * 2 + 1, :], rhs=xcb[:, :], start=True, stop=True), r=[gwb, xcb], w=[pg2])
            op("act", lambda e, c=c, pg2=pg2, ii=ii: e.activation(out=ii.d, in_=pg2[:, :], func=AF.Sigmoid, bias=pcol("bx%d" % l, c)), r=[pv], w=[ii.b, pg2])
            op("act", lambda e, c=c, rr=rr, aa=aa: e.activation(out=aa.d, in_=rr.d, func=AF.Exp, scale=dcol(0, l, c)), r=[rr.b, der], w=[aa.b])
            op("act", lambda e, aa=aa, mm=mm: e.activation(out=mm.d, in_=aa.d, func=AF.Square), r=[aa.b], w=[mm.b])
            op("act", lambda e, mm=mm: e.activation(out=mm.d, in_=mm.d, func=AF.Sqrt, bias=pcol("one"), scale=-1.0), r=[pv], w=[mm.b])
            op("pool", lambda e, ii=ii, xc=xc: e.tensor_tensor(out=ii.d, in0=ii.d, in1=xc.d, op=ALU.mult), r=[xc.b], w=[ii.b], k=2)
            op("pool", lambda e, ii=ii, mm=mm: e.tensor_tensor(out=ii.d, in0=ii.d, in1=mm.d, op=ALU.mult), r=[mm.b], w=[ii.b], k=2)
            hs = hst[(l, c)]
            op("dve", lambda e, c=c, rr=rr, aa=aa, ii=ii: e.tensor_tensor_scan(out=rr.d, data0=aa.d, data1=ii.d, initial=hst_t[l][:, c:c + 1],
                                                                                op0=ALU.mult, op1=ALU.add), r=[aa.b, ii.b, hs], w=[rr.b], k=2)
            op("dve", lambda e, c=c, rr=rr: e.tensor_copy(out=hst_t[l][:, c:c + 1], in_=rr.b[:, T + 3:T + 4]), r=[rr.b], w=[hs], n=1)
            pb = proj(s1, w1, c * 128, [hT], hk)
            op("act", lambda e, mm=mm, pb=pb: e.activation(out=mm.d, in_=pb[:, :], func=AF.Gelu_apprx_tanh), r=[], w=[mm.b, pb])
            op("dve", lambda e, c=c, rr=rr, mm=mm: e.tensor_tensor(out=yT[:, c, :], in0=rr.d, in1=mm.d, op=ALU.mult), r=[rr.b, mm.b], w=[yTc[c]], k=2)

        def gla(mix, qT, kT, C, BLK, sgate, nwname, ycol0, decay_const):
            nblk = T // BLK
            nch = T // C
            msk = MASK128 if C == 128 else MASK32
            pO = [getA(True), getA(True)]
            pX = [getA(True), getA(True)]
            pUT = getA(True)
            pTr = pUT[:, 256:512].bitcast(BF16)
            for p in range(2):
                for b_ in range(nblk):
                    op("pe", lambda e, p=p, b_=b_: e.transpose(pTr[0:BLK, 0:128], kT[p][:, b_ * BLK:(b_ + 1) * BLK], cst[:, IDENT, :]), r=[kT[p], cst], w=[pUT], n=128)
                    op("act", lambda e, p=p, b_=b_: e.activation(out=ktok[0:BLK, b_, p, :], in_=pTr[0:BLK, 0:128], func=AF.Copy), r=[], w=[ktok, pUT], n=128)
            for j in range(nch):
                blk = (j * C) // BLK
                base = (j * C) % BLK
                tsl = slice(j * C, (j + 1) * C)
                for p in range(2):
                    for hh in range(2):
                        h = 2 * p + hh
                        off = hh * 64
                        px = pX[fstate["x"] % 2]
                        fstate["x"] += 1
                        at = ATb[fstate["at"] % NAT]
                        fstate["at"] += 1
                        Sb = st_b[(mix, l, p, hh)]
                        op("pe", lambda e, p=p, off=off, tsl=tsl, px=px, base=base: e.matmul(
                            px[base:base + C, 0:C], lhsT=kT[p][off:off + 64, tsl], rhs=qT[p][off:off + 64, tsl], start=True, stop=True),
                            r=[kT[p], qT[p]], w=[px], n=C)
                        op("dve", lambda e, px=px, at=at, base=base: e.tensor_tensor(
                            out=at[base:base + C, 0:C], in0=px[base:base + C, 0:C], in1=cst[base:base + C, msk, 0:C], op=ALU.mult),
                            r=[cst], w=[at, px], n=C)

                        def fn_o(e, p=p, off=off, tsl=tsl, at=at, base=base, blk=blk, h=h, Sb=Sb):
                            e.matmul(pO[p][off:off + 64, tsl], lhsT=vtok[base:base + C, blk, h * 64:(h + 1) * 64], rhs=at[base:base + C, 0:C], start=True, stop=False)
                            return e.matmul(pO[p][off:off + 64, tsl], lhsT=Sb[off:off + 64, :], rhs=qT[p][off:off + 64, tsl], start=False, stop=True)

                        op("pe", fn_o, r=[vtok, at, Sb, qT[p]], w=[pO[p]], n=C, nmm=2)
                    op("pe", lambda e, p=p, base=base, blk=blk: e.matmul(
                        pUT[:, 0:128], lhsT=ktok[base:base + C, blk, p, :], rhs=vtok[base:base + C, blk, p * 128:(p + 1) * 128],
                        start=True, stop=True), r=[ktok, vtok], w=[pUT], n=128)
                    for hh in range(2):
                        h = 2 * p + hh
                        off = hh * 64
                        Sf = st_f[(mix, l, p, hh)]
                        Sb = st_b[(mix, l, p, hh)]
                        if decay_const is not None:
                            g = decay_const[h]
                            op("dve", lambda e, Sf=Sf, off=off, g=g: e.scalar_tensor_tensor(
                                out=Sf[off:off + 64, :], in0=Sf[off:off + 64, :], scalar=g, in1=pUT[off:off + 64, off:off + 64], op0=ALU.mult, op1=ALU.add),
                                r=[], w=[Sf, pUT], n=64)
                            op("dve", lambda e, Sf=Sf, Sb=Sb, off=off, g=g: e.tensor_scalar(
                                out=Sb[off:off + 64, :], in0=Sf[off:off + 64, :], scalar1=g, scalar2=None, op0=ALU.mult), r=[Sf], w=[Sb], n=64)
                        else:
                            tmp = stmp[h]
                            op("dve", lambda e, Sf=Sf, off=off, tmp=tmp: e.tensor_tensor(
                                out=tmp[off:off + 64, :], in0=pUT[off:off + 64, off:off + 64], in1=Sf[off:off + 64, :], op=ALU.add), r=[Sf], w=[tmp, pUT], n=64)
                            op("dve", lambda e, Sf=Sf, off=off, tmp=tmp, p=p, j=j: e.tensor_scalar(
                                out=Sf[off:off + 64, :], in0=tmp[off:off + 64, :], scalar1=ebl[p][off:off + 64, j:j + 1], scalar2=None, op0=ALU.mult),
                                r=[tmp, ebl[p]], w=[Sf], n=64)
                            op("act", lambda e, Sf=Sf, Sb=Sb, off=off: e.activation(out=Sb[off:off + 64, :], in_=Sf[off:off + 64, :], func=AF.Copy), r=[Sf], w=[Sb], n=64)
            for p in range(2):
                o, rt = getF(), getF()
                osq = getB()
                op("act", lambda e, p=p, o=o: e.activation(out=o.d, in_=pO[p][:, :], func=AF.Copy), r=[], w=[o.b, pO[p]])
                op("act", lambda e, o=o, osq=osq: e.activation(out=osq[:, :], in_=o.d, func=AF.Square), r=[o.b], w=[osq])
                pS_ = getA()
                op("pe", lambda e, pS_=pS_, osq=osq: e.matmul(pS_[:, :], lhsT=cst[:, BD64, :], rhs=osq[:, :], start=True, stop=True), r=[cst, osq], w=[pS_])
                op("act", lambda e, pS_=pS_, rt=rt: e.activation(out=rt.d, in_=pS_[:, :], func=AF.Ln, bias=pcol("eps"), scale=1.0), r=[pv], w=[rt.b, pS_])
                op("act", lambda e, rt=rt: e.activation(out=rt.d, in_=rt.d, func=AF.Exp, scale=-0.5), r=[], w=[rt.b])
                op("dve", lambda e, p=p, o=o, rt=rt: e.scalar_tensor_tensor(out=o.d, in0=o.d, scalar=pcol(nwname, p), in1=rt.d, op0=ALU.mult, op1=ALU.mult),
                   r=[rt.b, pv], w=[o.b], k=2)
                op("pool", lambda e, p=p, o=o: e.tensor_tensor(out=yT[:, ycol0 + p, :], in0=o.d, in1=sgate[p].d, op=ALU.mult), r=[o.b, sgate[p].b], w=[yTc[ycol0 + p]], k=2)
            for b_ in pO + pX + [pUT]:
                b_.held = False

        s2, w2 = wload(win_d[l, :, :, 1024:1536], [128, 8, 512])
        s6, w6 = wload(win_d[l, :, :, 3072:3584], [128, 8, 512])
        s3, w3 = wload(win_d[l, :, :, 1536:2048], [128, 8, 512])
        if l == 0:
            sch.dma("sp", tabs[:, :, :], tab_d[:, :, ti * T:(ti + 1) * T], w=[tabs], nbytes=2 << 20)
        qT = [getB(), getB()]
        kT = [getB(), getB()]
        for dst, cbase, tb in ((qT, 0, 0), (kT, 256, 2)):
            for p in range(2):
                pa = proj(s2, w2, cbase + p * 128, [hT], hk)
                pb = proj(s6, w6, cbase + p * 128, [hT], hk)
                t1, t2 = getF(), getF()
                op("dve", lambda e, p=p, tb=tb, t1=t1, pa=pa: e.tensor_tensor(out=t1.d, in0=pa[:, :], in1=tabs[:, p * 4 + tb, :], op=ALU.mult), r=[tabs], w=[t1.b, pa])
                op("dve", lambda e, p=p, tb=tb, t2=t2, pb=pb: e.tensor_tensor(out=t2.d, in0=pb[:, :], in1=tabs[:, p * 4 + tb + 1, :], op=ALU.mult), r=[tabs], w=[t2.b, pb])
                op("pool", lambda e, p=p, dst=dst, t1=t1, t2=t2: e.tensor_tensor(out=dst[p][:, :], in0=t1.d, in1=t2.d, op=ALU.add), r=[t1.b, t2.b], w=[dst[p]], k=2)
        for b_ in range(4):
            pg = getA()

            def fn_v(e, b_=b_, pg=pg):
                ins = None
                for kc in range(8):
                    ins = e.matmul(pg[:, 0:256], lhsT=hT[:, kc, b_ * 128:(b_ + 1) * 128], rhs=w3[:, kc, 0:256], start=(kc == 0), stop=(kc == 7))
                return ins

            op("pe", fn_v, r=[hT, s3], w=[pg], n=256, nmm=8)
            op("act", lambda e, b_=b_, pg=pg: e.activation(out=vtok[:, b_, :], in_=pg[:, 0:256], func=AF.Copy), r=[], w=[vtok, pg], n=256)
        sg = [getF(), getF()]
        for p in range(2):
            pa = proj(s3, w3, 256 + p * 128, [hT], hk)
            op("act", lambda e, p=p, pa=pa, sg=sg: e.activation(out=sg[p].d, in_=pa[:, :], func=AF.Silu), r=[], w=[sg[p].b, pa])
        gla("r", qT, kT, 128, 128, sg, "rnw%d" % l, 4, [g ** 128 for g in RET_GAMMA])

        s4, w4 = wload(win_d[l, :, :, 2048:2560], [128, 8, 512])
        s5, w5 = wload(win_d[l, :, :, 2560:3072], [128, 8, 512])
        qT = [getB(), getB()]
        kT = [getB(), getB()]
        for p in range(2):
            sig, ff, bb, en = getF(), getF(), getF(), getF()
            pa = proj(s4, w4, 256 + p * 128, [hT], hk)
            op("act", lambda e, sig=sig, pa=pa: e.activation(out=sig.d, in_=pa[:, :], func=AF.Sigmoid), r=[], w=[sig.b, pa])
            op("dve", lambda e, p=p, sig=sig, ff=ff: e.tensor_scalar(out=ff.d, in0=sig.d, scalar1=dcol(OML, l, p), scalar2=dcol(LB, l, p), op0=ALU.mult, op1=ALU.add),
               r=[sig.b, der], w=[ff.b])
            op("act", lambda e, ff=ff: e.activation(out=ff.d, in_=ff.d, func=AF.Ln), r=[], w=[ff.b])
            op("dve", lambda e, ff=ff, bb=bb: e.tensor_tensor_scan(out=bb.d, data0=rmask[:, :], data1=ff.d, initial=0.0, op0=ALU.mult, op1=ALU.add),
               r=[rmask, ff.b], w=[bb.b], k=2)
            op("pool", lambda e, p=p, sig=sig: e.tensor_scalar(out=sig.d, in0=sig.d, scalar1=dcol(NOML, l, p), scalar2=dcol(OML, l, p), op0=ALU.mult, op1=ALU.add),
               r=[der], w=[sig.b])
            op("act", lambda e, bb=bb, en=en: e.activation(out=en.d, in_=bb.d, func=AF.Exp, scale=-1.0), r=[bb.b], w=[en.b])
            op("dve", lambda e, p=p, sig=sig, en=en, kT=kT: e.tensor_tensor(out=kT[p][:, :], in0=sig.d, in1=en.d, op=ALU.mult), r=[sig.b, en.b], w=[kT[p]], k=2)
            op("act", lambda e, bb=bb, ff=ff: e.activation(out=ff.d, in_=bb.d, func=AF.Exp), r=[bb.b], w=[ff.b])
            op("dve", lambda e, p=p, ff=ff: e.tensor_copy(out=ebl[p][:, :], in_=ff.d.rearrange("p (c t) -> p c t", t=32)[:, :, 31]), r=[ff.b], w=[ebl[p]], n=16)
            pb = proj(s4, w4, p * 128, [hT], hk)
            op("act", lambda e, en=en, pb=pb: e.activation(out=en.d, in_=pb[:, :], func=AF.Silu), r=[], w=[en.b, pb])
            op("dve", lambda e, p=p, en=en, ff=ff, qT=qT: e.tensor_tensor(out=qT[p][:, :], in0=en.d, in1=ff.d, op=ALU.mult), r=[en.b, ff.b], w=[qT[p]], k=2)
        for b_ in range(8):
            pg = getA()

            def fn_v2(e, b_=b_, pg=pg):
                ins = None
                for kc in range(8):
                    ins = e.matmul(pg[0:64, 0:256], lhsT=hT[:, kc, b_ * 64:(b_ + 1) * 64], rhs=w5[:, kc, 0:256], start=(kc == 0), stop=(kc == 7))
                return ins

            op("pe", fn_v2, r=[hT, s5], w=[pg], n=256, nmm=8)
            op("act", lambda e, b_=b_, pg=pg: e.activation(out=vtok[0:64, b_, :], in_=pg[0:64, 0:256], func=AF.Copy), r=[], w=[vtok, pg], n=256)
        sg = [getF(), getF()]
        for p in range(2):
            pa = proj(s5, w5, 256 + p * 128, [hT], hk)
            op("act", lambda e, p=p, pa=pa, sg=sg: e.activation(out=sg[p].d, in_=pa[:, :], func=AF.Silu), r=[], w=[sg[p].b, pa])
        gla("h", qT, kT, 32, 64, sg, "hnw%d" % l, 6, None)

        for half in range(2):
            so, wo = wload(wout_d[l, :, :, half * 512:(half + 1) * 512], [128, 8, 512])
            for mq in range(4):
                m = half * 4 + mq
                pb = proj(so, wo, mq * 128, yTc, lambda kc: yT[:, kc, :])
                op("dve", lambda e, m=m, pb=pb: e.tensor_tensor(out=xr[:, m, :], in0=pb[:, :], in1=xr[:, m, :], op=ALU.add), r=[], w=[xr, pb])
                sq_chunk(xr, m)

        def down_half(hf):
            for m in range(8):
                sd, wd = wload(wdn_d[l, m, hf], [128, 11, 128])
                pb = getA()

                def fn_d(e, pb=pb, wd=wd):
                    ins = None
                    for kc in range(11):
                        ins = e.matmul(pb[:, :], lhsT=wd[:, kc, :], rhs=gT[:, hf * 11 + kc, :], start=(kc == 0), stop=(kc == 10))
                    return ins

                op("pe", fn_d, r=[sd] + gTc[hf * 11:(hf + 1) * 11], w=[pb], nmm=11)
                op("dve", lambda e, m=m, pb=pb: e.tensor_tensor(out=xr[:, m, :], in0=pb[:, :], in1=xr[:, m, :], op=ALU.add), r=[], w=[xr, pb])
                if hf == 1:
                    sq_chunk(xr, m)

        norm_to_hT(xr, "n2w%d" % l, presq=True)
        for q in range(11):
            su, wu = wload(wup_d[l, :, :, q * 512:(q + 1) * 512], [128, 8, 512])
            for jj in range(2):
                j = 2 * q + jj
                outs = []
                for (half, col) in ((0, jj * 128), (1, 256 + jj * 128)):
                    ch = half * 22 + j
                    cb = getF()
                    pb = proj(su, wu, col, [hT], hk)
                    fhb = fh[(l, ch)]
                    op("act", lambda e, cb=cb, pb=pb, ch=ch: e.activation(out=cb.d, in_=pb[:, :], func=AF.Identity, bias=pcol("fcb%d" % l, ch), scale=pcol("fcw%d_2" % l, ch)),
                       r=[pv], w=[cb.b, pb])
                    op("dve", lambda e, cb=cb, pb=pb, ch=ch: e.scalar_tensor_tensor(out=cb.b[:, 5:4 + T], in0=pb[:, 0:T - 1], scalar=pcol("fcw%d_1" % l, ch), in1=cb.b[:, 5:4 + T],
                                                                                     op0=ALU.mult, op1=ALU.add), r=[pv], w=[cb.b, pb])
                    op("dve", lambda e, cb=cb, pb=pb, ch=ch: e.scalar_tensor_tensor(out=cb.b[:, 6:4 + T], in0=pb[:, 0:T - 2], scalar=pcol("fcw%d_0" % l, ch), in1=cb.b[:, 6:4 + T],
                                                                                     op0=ALU.mult, op1=ALU.add), r=[pv], w=[cb.b, pb])
                    op("dve", lambda e, cb=cb, ch=ch: e.scalar_tensor_tensor(out=cb.b[:, 4:5], in0=fh_t[l][:, ch, 1:2], scalar=pcol("fcw%d_1" % l, ch), in1=cb.b[:, 4:5],
                                                                              op0=ALU.mult, op1=ALU.add), r=[pv, fhb], w=[cb.b], n=1)
                    op("dve", lambda e, cb=cb, ch=ch: e.scalar_tensor_tensor(out=cb.b[:, 4:6], in0=fh_t[l][:, ch, 0:2], scalar=pcol("fcw%d_0" % l, ch), in1=cb.b[:, 4:6],
                                                                              op0=ALU.mult, op1=ALU.add), r=[pv, fhb], w=[cb.b], n=2)
                    op("act", lambda e, pb=pb, ch=ch: e.activation(out=fh_t[l][:, ch, :], in_=pb[:, T - 2:T], func=AF.Copy), r=[], w=[fhb, pb], n=2)
                    outs.append(cb)
                cg, cv = outs
                op("act", lambda e, cg=cg: e.activation(out=cg.d, in_=cg.d, func=AF.Silu), r=[], w=[cg.b])
                op("pool", lambda e, cg=cg, cv=cv, j=j: e.tensor_tensor(out=gT[:, j, :], in0=cg.d, in1=cv.d, op=ALU.mult), r=[cg.b, cv.b], w=[gTc[j]], k=2)
            if q == 5:
                down_half(0)
        down_half(1)

    xv = xT_d.rearrange("(c p) s -> p c s", p=128)
    yv = yT_d.rearrange("(c p) s -> p c s", p=128)
    sndB = Buf(snd_t, "snd")
    rcvB = [Buf(rcv_t[i], "rcv%d" % i) for i in range(2)]
    snd_v = snd_t.ap().rearrange("(c p) s -> p c s", p=128)
    out_toks = []
    for st in range(NS):
        xr = xres[st % 2]
        sch.dma("sp", xr[:, :, :], xv[:, :, st * T:(st + 1) * T], w=[xr], nbytes=2 << 20)
        op("act", lambda e, xr=xr: e.activation(out=xr[:, :, :], in_=xr[:, :, :], func=AF.Copy, scale=pcol("m")), r=[pv], w=[xr], n=8 * T)
        if st >= SKEW:
            rb = rcvB[(st - SKEW) % 2]
            rcv_v = rcv_t[(st - SKEW) % 2].ap()[0:D, :].rearrange("(c p) s -> p c s", p=128)
            sch.dma("sp", rv[:, :, :], rcv_v, r=[rb], w=[rv], nbytes=2 << 20)
            op("dve", lambda e, xr=xr: e.scalar_tensor_tensor(out=xr[:, :, :], in0=rv[:, :, :], scalar=pcol("om"), in1=xr[:, :, :], op0=ALU.mult, op1=ALU.add),
               r=[rv, pv], w=[xr], n=8 * T, k=2)
        emit_layer(0, xr, st)
        if st < NT:
            sch.dma("sp", snd_v, xr[:, :, :], r=[xr], w=[sndB], nbytes=2 << 20)
            rb = rcvB[st % 2]
            sch.raw("pool", lambda e, rb=rb: e.collective_compute("AllGather", ALU.bypass, replica_groups=GROUPS,
                                                                   ins=[snd_t.ap().opt()], outs=[rb.t.ap().opt()]),
                    r=[sndB], w=[rb], sem_buf=rb, inc=1, occ=1.0, lat=120.0)
        if st == SKEW - 1:
            kp = pcol("keep")
            op("dve", lambda e: e.tensor_scalar(out=hst_t[0][:, :], in0=hst_t[0][:, :], scalar1=kp, scalar2=None, op0=ALU.mult), r=[pv], w=[hst[(0, c)] for c in range(4)], n=4)
            op("dve", lambda e: e.tensor_scalar(out=lxh_t[0][:, :, :], in0=lxh_t[0][:, :, :], scalar1=kp, scalar2=None, op0=ALU.mult), r=[pv], w=[lxh[(0, c)] for c in range(4)], n=16)
            op("dve", lambda e: e.tensor_scalar(out=fh_t[0][:, :, :], in0=fh_t[0][:, :, :], scalar1=kp, scalar2=None, op0=ALU.mult), r=[pv], w=[fh[(0, ch)] for ch in range(NFC)], n=88)
            for key in list(st_f.keys()):
                lo = key[3] * 64
                for tbl in (st_f, st_b):
                    bb_ = tbl[key]
                    op("dve", lambda e, bb_=bb_, lo=lo: e.tensor_scalar(out=bb_[lo:lo + 64, :], in0=bb_[lo:lo + 64, :], scalar1=pv[lo:lo + 64, PV["keep"]:PV["keep"] + 1],
                                                                          scalar2=None, op0=ALU.mult), r=[pv], w=[bb_], n=64)
        rs = rms_rstd(xr, presq=True)
        for c in range(8):
            op("dve", lambda e, c=c, xr=xr, rs=rs: e.scalar_tensor_tensor(out=xr[:, c, :], in0=xr[:, c, :], scalar=pcol("fnw", c), in1=rs.d,
                                                                           op0=ALU.mult, op1=ALU.mult), r=[rs.b, pv], w=[xr], k=2)
        out_toks.append(sch.dma("sp", yv[:, :, st * T:(st + 1) * T], xr[:, :, :], r=[xr], nbytes=2 << 20))
    sch.wait_all_on("sp", out_toks)
    sch.emit()
    return nc, sch


def _chunks(v):
    v = np.asarray(v, np.float32)
    return np.ascontiguousarray(v.reshape(-1, 128).T)


def make_tables(S):
    t = np.arange(S, dtype=np.float64)
    inv = 10000.0 ** (-np.arange(0, 64, 2, dtype=np.float64) / 64.0)
    ang = t[None, :] * inv[:, None]
    cos = np.concatenate([np.cos(ang), np.cos(ang)], 0)
    sin = np.concatenate([-np.sin(ang), np.sin(ang)], 0)
    n1 = (t % 128) + 1.0
    tabs = np.zeros((128, 8, S), np.float64)
    for p in range(2):
        for hh in range(2):
            h = 2 * p + hh
            lg = math.log(RET_GAMMA[h])
            qd = np.exp(n1 * lg)[None, :]
            kd = np.exp(-n1 * lg)[None, :] * (64.0 ** -0.5)
            sl = slice(hh * 64, hh * 64 + 64)
            tabs[sl, p * 4 + 0] = cos * qd
            tabs[sl, p * 4 + 1] = sin * qd
            tabs[sl, p * 4 + 2] = cos * kd
            tabs[sl, p * 4 + 3] = sin * kd
    return tabs.astype(np.float32)


def make_consts():
    cst = np.zeros((128, 5, 128), np.float32)
    cst[:, 0, :] = 1.0
    cst[:, 1, :] = np.eye(128, dtype=np.float32)
    for b in range(2):
        cst[b * 64:(b + 1) * 64, 2, b * 64:(b + 1) * 64] = 1.0 / 64.0
    m = np.arange(128)[:, None]
    n = np.arange(128)[None, :]
    cst[:, 3, :] = (n >= m).astype(np.float32)
    m32 = (np.arange(32)[None, :] >= np.arange(32)[:, None]).astype(np.float32)
    for b in range(4):
        cst[b * 32:(b + 1) * 32, 4, 0:32] = m32
    rmask = np.ones((128, T), np.float32)
    rmask[:, ::32] = 0.0
    return cst, rmask


def prep_role(inp, role, NT):
    lyr = role
    NS = NT + SKEW
    w_in = np.asarray(inp["w_in"], np.float32)[lyr:lyr + 1]
    swap = lambda base: [base + h * 64 + ((d + 32) % 64) for h in range(4) for d in range(64)]
    idx = list(range(3072)) + swap(1024) + swap(1280)
    win = w_in[:, :, idx].reshape(1, 8, 128, NCOL_IN).transpose(0, 2, 1, 3)
    wout = np.asarray(inp["w_out"], np.float32)[lyr:lyr + 1].reshape(1, 8, 128, D).transpose(0, 2, 1, 3)
    perm = []
    for q in range(11):
        perm += list(range(2 * q * 128, (2 * q + 2) * 128))
        perm += list(range(DFF + 2 * q * 128, DFF + (2 * q + 2) * 128))
    wup = np.asarray(inp["ffn_w_up"], np.float32)[lyr:lyr + 1][:, :, perm].reshape(1, 8, 128, 2 * DFF).transpose(0, 2, 1, 3)
    wdn = np.asarray(inp["ffn_w_down"], np.float32)[lyr:lyr + 1].reshape(1, 2, 11, 128, 8, 128).transpose(0, 4, 1, 3, 2, 5)
    gw = np.zeros((1, 128, 8, 128), np.float32)
    wa = np.asarray(inp["lru_wa"], np.float32)
    wx = np.asarray(inp["lru_wx"], np.float32)
    for c in range(4):
        for b2 in range(2):
            sl = slice(b2 * 64, b2 * 64 + 64)
            gw[0, sl, c * 2 + 0, sl] = wa[lyr, 2 * c + b2]
            gw[0, sl, c * 2 + 1, sl] = wx[lyr, 2 * c + b2]
    pvec = np.zeros((128, PV["_n"]), np.float32)

    def put(name, v):
        a = _chunks(v)
        pvec[:, PV[name]:PV[name] + a.shape[1]] = a

    put("n1w0", inp["norm1_w"][lyr])
    put("n2w0", inp["norm2_w"][lyr])
    for k in range(4):
        put("lcw0_%d" % k, inp["lru_conv_w"][lyr][k])
    put("lcb0", inp["lru_conv_b"][lyr])
    put("ba0", inp["lru_ba"][lyr])
    put("bx0", inp["lru_bx"][lyr])
    put("lam0", inp["lru_lambda"][lyr])
    put("rnw0", inp["ret_norm_w"][lyr])
    put("hnw0", inp["hg_norm_w"][lyr])
    put("hbA", inp["hg_lower_bounds"][0])
    put("hbB", inp["hg_lower_bounds"][1])
    for k in range(3):
        put("fcw0_%d" % k, inp["ffn_conv_w"][lyr][k])
    put("fcb0", inp["ffn_conv_b"][lyr])
    put("fnw", inp["final_norm_w"])
    pvec[:, PV["eps"]] = EPS
    pvec[:, PV["one"]] = 1.0
    pvec[:, PV["m"]] = 1.0 if role == 0 else 0.0
    pvec[:, PV["om"]] = 0.0 if role == 0 else 1.0
    pvec[:, PV["keep"]] = 1.0 if role == 0 else 0.0
    pvec[:, PV["lbm"]] = 0.0 if role == 0 else 1.0
    cst, rmask = make_consts()
    tb = make_tables(NT * T)
    tabs = np.zeros((128, 8, NS * T), np.float32)
    for st in range(NS):
        ti = st if role == 0 else st - SKEW
        if ti < 0 or ti >= NT:
            ti = 0
        tabs[:, :, st * T:(st + 1) * T] = tb[:, :, ti * T:(ti + 1) * T]
    return {
        "win": np.ascontiguousarray(win), "wout": np.ascontiguousarray(wout),
        "wup": np.ascontiguousarray(wup), "wdn": np.ascontiguousarray(wdn),
        "gatew": gw, "pvec": pvec, "tabs": tabs, "cst": cst, "rmask": rmask,
    }


_CACHE = {}


def kernel(**inputs):
    x = np.asarray(inputs["x"], np.float32)
    B, S, _ = x.shape
    NT = S // T
    NS = NT + SKEW
    roles = [prep_role(inputs, r, NT) for r in range(2)]
    if NT not in _CACHE:
        _CACHE[NT] = build(NT)
    nc, sch = _CACHE[NT]
    ncores = 2 * B
    zeros = np.zeros((D, NS * T), np.float32)
    in_maps = []
    for c in range(ncores):
        b, r = c // 2, c % 2
        m = dict(roles[r])
        if r == 0:
            xp = np.zeros((D, NS * T), np.float32)
            xp[:, :S] = x[b].T
            m["xT"] = xp
        else:
            m["xT"] = zeros
        in_maps.append(m)
    res = run_bass_kernel_spmd(nc, in_maps, core_ids=list(range(ncores)))
    out = np.empty((B, S, D), np.float32)
    for b in range(B):
        out[b] = res.results[2 * b + 1]["yT"][:, SKEW * T:].T
    return out
```

```python
import contextlib
import math
import numpy as np
import concourse.bass as bass
import concourse.mybir as mybir
from concourse.bass_utils import run_bass_kernel_spmd

F32 = mybir.dt.float32
BF16 = mybir.dt.bfloat16
AF = mybir.ActivationFunctionType
ALU = mybir.AluOpType

EPOCH = 4000
PRIO = "cp"

D = 1024
SEQ = 8192
BATCH = 4
DEPTH = 2
NL = 1
SKEW = 2
T = 512
DFF = 2816
NFC = 44
EPS = 1e-6
NCOL_IN = 3584


import heapq
import sys


class Buf:
    def __init__(self, t, name):
        self.t = t
        self.name = name
        self.last_write = None
        self.reads = []
        self.dma_cnt = 0
        self.g_lw = None
        self.g_rd = []
        self.held = False

    def __getitem__(self, key):
        return self.t[key]

    def sub(self, name):
        return Buf(self.t, name)


class _Op:
    __slots__ = ("eng", "fn", "r", "w", "occ", "lat", "kind", "sem_buf", "preds", "succs", "inc", "line", "start", "tbl")


class Sched:
    ENG = ("pe", "act", "dve", "pool", "sp")

    def __init__(self, nc):
        self.nc = nc
        self.stack = contextlib.ExitStack()
        self.rec = []
        self.ops = {e: [] for e in self.ENG}
        self.cnt = {e: 0 for e in self.ENG}
        self.epoch = {e: 0 for e in self.ENG}
        self.sems = {}
        self.waited = {e: {} for e in self.ENG}
        self.nsem = 0
        self.final_waits = []
        self.sim_time = 0.0

    def sem(self, key):
        if key not in self.sems:
            self.sems[key] = self.stack.enter_context(self.nc.semaphore("s%d" % self.nsem))
            self.nsem += 1
        return self.sems[key]

    def sb(self, name, shape, dtype=F32):
        t = self.stack.enter_context(self.nc.sbuf_tensor(name, list(shape), dtype))
        return Buf(t, name)

    def ps(self, name, shape, dtype=F32):
        t = self.stack.enter_context(self.nc.psum_tensor(name, list(shape), dtype))
        return Buf(t, name)

    def _record(self, o):
        i = len(self.rec)
        try:
            f = sys._getframe(2)
            while f.f_code.co_name in ("proj", "wload", "getA", "op", "dma", "raw"):
                f = f.f_back
            o.line = f.f_lineno
        except Exception:
            o.line = 0
        preds = set()
        for b in o.r:
            if b.g_lw is not None:
                preds.add(b.g_lw)
        for b in o.w:
            if b.g_lw is not None:
                preds.add(b.g_lw)
            preds.update(b.g_rd)
        preds.discard(i)
        o.preds = preds
        o.succs = []
        for p in preds:
            self.rec[p].succs.append(i)
        for b in o.w:
            b.g_lw = i
            b.g_rd = []
            if hasattr(b, "touch"):
                b.touch = i
        for b in o.r:
            if b not in o.w:
                b.g_rd.append(i)
        self.rec.append(o)
        return i

    def op(self, eng, fn, r=(), w=(), n=T, k=1.0, nmm=1, tbl=None):
        o = _Op()
        o.tbl = tbl
        o.eng, o.fn, o.r, o.w, o.kind, o.sem_buf = eng, fn, tuple(r), tuple(w), "c", None
        if eng == "pe":
            o.occ = nmm * (0.19 + 0.0002 * n)
        elif eng == "act":
            o.occ = 0.22 + 0.00072 * n
        elif eng == "dve":
            o.occ = 0.07 + 0.00105 * n * k
        else:
            o.occ = 0.4 + 0.002 * n
        o.lat = o.occ + (0.1 if eng == "pe" else 0.3)
        return self._record(o)

    def dma(self, q, out_ap, in_ap, r=(), w=(), sem_buf=None, nbytes=1 << 20):
        o = _Op()
        o.eng, o.r, o.w, o.kind = q, tuple(r), tuple(w), "d"
        o.sem_buf = sem_buf if sem_buf is not None else (w[0] if len(w) else r[0])

        def fn(e, out_ap=out_ap, in_ap=in_ap):
            return e.dma_start(out=out_ap, in_=in_ap)

        o.fn = fn
        o.occ = 1.0 if q == "pool" else 0.15
        o.lat = o.occ + 2.0 + nbytes / 150e3
        o.inc = 16
        return self._record(o)

    def raw(self, q, fn, r=(), w=(), sem_buf=None, inc=1, occ=1.0, lat=50.0):
        o = _Op()
        o.eng, o.r, o.w, o.kind = q, tuple(r), tuple(w), "d"
        o.sem_buf = sem_buf
        o.fn = fn
        o.occ, o.lat = occ, lat
        o.inc = inc
        return self._record(o)

    def wait_all_on(self, eng, idxs):
        self.final_waits.append((eng, list(idxs)))

    def _schedule(self):
        ops = self.rec
        ENG = self.ENG
        ready = {e: [] for e in ENG}
        busy = {e: 0.0 for e in ENG}
        npred = [len(o.preds) for o in ops]
        N = len(ops)
        tail = [0.0] * N
        for i in range(N - 1, -1, -1):
            o = ops[i]
            t = 0.0
            for s_ in o.succs:
                if tail[s_] > t:
                    t = tail[s_]
            tail[i] = t + o.lat
        if PRIO == "cp":
            key = [(-tail[i], i) for i in range(N)]
        else:
            key = [(i, i) for i in range(N)]
        for i, o in enumerate(ops):
            if npred[i] == 0:
                heapq.heappush(ready[o.eng], (key[i], i))
        events = []
        now = 0.0
        order = []
        cur_tbl = None
        while True:
            for e in ENG:
                if ready[e] and busy[e] <= now + 1e-9:
                    extra = 0.0
                    if e == "act":
                        cand = [heapq.heappop(ready[e]) for _ in range(min(5, len(ready[e])))]
                        pick = 0
                        for ci, (_, ii) in enumerate(cand):
                            tb = getattr(ops[ii], "tbl", None)
                            if tb is None or tb == cur_tbl:
                                pick = ci
                                break
                        i = cand[pick][1]
                        for ci, c_ in enumerate(cand):
                            if ci != pick:
                                heapq.heappush(ready[e], c_)
                        tb = getattr(ops[i], "tbl", None)
                        if tb is not None and tb != cur_tbl:
                            extra = 1.3
                            cur_tbl = tb
                    else:
                        i = heapq.heappop(ready[e])[1]
                    o = ops[i]
                    order.append(i)
                    o.start = now
                    busy[e] = now + o.occ + extra
                    heapq.heappush(events, (now + o.lat + extra, 1, i))
                    heapq.heappush(events, (now + o.occ + extra, 0, i))
            if not events:
                break
            t, typ, i = heapq.heappop(events)
            now = t
            if typ == 1:
                for s_ in ops[i].succs:
                    npred[s_] -= 1
                    if npred[s_] == 0:
                        heapq.heappush(ready[ops[s_].eng], (key[s_], s_))
        assert len(order) == len(ops), (len(order), len(ops))
        self.sim_time = now
        return order

    def _deps(self, r, w):
        deps = []
        for b in r:
            if b.last_write is not None:
                deps.append(b.last_write)
        for b in w:
            if b.last_write is not None:
                deps.append(b.last_write)
            deps.extend(b.reads)
        return deps

    def _waits(self, eng, deps):
        need = {}
        wd = self.waited[eng]
        for (k, v) in deps:
            if wd.get(k, 0) >= v:
                continue
            if need.get(k, 0) < v:
                need[k] = v
        for k, v in need.items():
            wd[k] = v
        return [(self.sem(k), v) for k, v in need.items()]

    def _commit(self, r, w, tok):
        for b in w:
            b.last_write = tok
            b.reads = []
        for b in r:
            if b in w:
                continue
            b.reads = [x for x in b.reads if x[0] != tok[0]] + [tok]

    def _replay_one(self, o):
        eng = o.eng
        deps = self._deps(o.r, o.w)
        if o.kind == "c":
            if eng == "pe":
                deps = [d for d in deps if not (d[0][0] == "e" and d[0][1] == "pe")]
            waits = self._waits(eng, deps)
            if self.cnt[eng] >= EPOCH:
                self.epoch[eng] += 1
                self.cnt[eng] = 0
            self.cnt[eng] += 1
            key = ("e", eng, self.epoch[eng])
            tok = (key, self.cnt[eng])
            self.ops[eng].append((o.fn, waits, self.sem(key), 1))
        else:
            waits = self._waits(eng, deps)
            sb_ = o.sem_buf
            key = ("d", id(sb_))
            inc = getattr(o, "inc", 16)
            sb_.dma_cnt += inc
            tok = (key, sb_.dma_cnt)
            self.ops[eng].append((o.fn, waits, self.sem(key), inc))
        self._commit(o.r, o.w, tok)
        return tok

    def emit(self):
        nc = self.nc
        order = self._schedule()
        toks = {}
        for i in order:
            toks[i] = self._replay_one(self.rec[i])
        for eng, idxs in self.final_waits:
            waits = self._waits(eng, [toks[i] for i in idxs])
            self.ops[eng].append((None, waits, None, 0))

        def replay(engobj, lst):
            for (fn, waits, s, inc) in lst:
                for (ws, v) in waits:
                    engobj.wait_ge(ws, v)
                if fn is not None:
                    fn(engobj).then_inc(s, inc)

        with nc.Block() as block:
            @block.tensor
            def _(e):
                replay(e, self.ops["pe"])

            @block.scalar
            def _(e):
                replay(e, self.ops["act"])

            @block.vector
            def _(e):
                replay(e, self.ops["dve"])

            @block.gpsimd
            def _(e):
                replay(e, self.ops["pool"])

            @block.sync
            def _(e):
                replay(e, self.ops["sp"])

    def close(self):
        self.stack.close()


def _pvec_layout():
    lay = {}
    col = 0

    def add(name, n):
        nonlocal col
        lay[name] = col
        col += n

    for l in range(NL):
        add("n1w%d" % l, 8)
        add("n2w%d" % l, 8)
        for k in range(4):
            add("lcw%d_%d" % (l, k), 4)
        add("lcb%d" % l, 4)
        add("ba%d" % l, 4)
        add("bx%d" % l, 4)
        add("lam%d" % l, 4)
        add("rnw%d" % l, 2)
        add("hnw%d" % l, 2)
        for k in range(3):
            add("fcw%d_%d" % (l, k), NFC)
        add("fcb%d" % l, NFC)
    add("fnw", 8)
    add("hbA", 2)
    add("hbB", 2)
    add("m", 1)
    add("om", 1)
    add("keep", 1)
    add("lbm", 1)
    add("eps", 1)
    add("one", 1)
    lay["_n"] = col
    return lay


PV = _pvec_layout()
RET_GAMMA = [1.0 - 2.0 ** (-5.0 - h) for h in range(4)]


def build(NT, nlayers=NL, debug_out=None):
    NS = NT + SKEW
    S = NS * T
    nc = bass.Bass("TRN2", target_bir_lowering=False)
    xT_d = nc.dram_tensor("xT", [D, S], F32, kind="ExternalInput").ap()
    win_d = nc.dram_tensor("win", [NL, 128, 8, NCOL_IN], F32, kind="ExternalInput").ap()
    wout_d = nc.dram_tensor("wout", [NL, 128, 8, D], F32, kind="ExternalInput").ap()
    wup_d = nc.dram_tensor("wup", [NL, 128, 8, 2 * DFF], F32, kind="ExternalInput").ap()
    wdn_d = nc.dram_tensor("wdn", [NL, 8, 2, 128, 11, 128], F32, kind="ExternalInput").ap()
    gw_d = nc.dram_tensor("gatew", [NL, 128, 8, 128], F32, kind="ExternalInput").ap()
    pv_d = nc.dram_tensor("pvec", [128, PV["_n"]], F32, kind="ExternalInput").ap()
    tab_d = nc.dram_tensor("tabs", [128, 8, S], F32, kind="ExternalInput").ap()
    cst_d = nc.dram_tensor("cst", [128, 5, 128], F32, kind="ExternalInput").ap()
    rm_d = nc.dram_tensor("rmask", [128, T], F32, kind="ExternalInput").ap()
    yT_d = nc.dram_tensor("yT", [D, S], F32, kind="ExternalOutput").ap()
    snd_t = nc.dram_tensor("snd", [D, T], F32)
    rcv_t = [nc.dram_tensor("rcv%d" % i, [2 * D, T], F32) for i in range(2)]
    GROUPS = [[0, 1], [2, 3], [4, 5], [6, 7]]

    sch = Sched(nc)
    sb, ps = sch.sb, sch.ps
    op = sch.op

    pv = sb("pv", [128, PV["_n"]])
    sch.dma("sp", pv[:, :], pv_d[:, :], w=[pv], nbytes=1 << 18)
    cstf = sb("cstf", [128, 5, 128])
    sch.dma("sp", cstf[:, :, :], cst_d[:, :, :], w=[cstf], nbytes=1 << 18)
    cst = sb("cstb", [128, 5, 128], BF16)
    op("dve", lambda e: e.tensor_copy(out=cst[:, :, :], in_=cstf[:, :, :]), r=[cstf], w=[cst], n=640)
    ONES, IDENT, BD64, MASK128, MASK32 = range(5)
    rmask = sb("rmask_s", [128, T])
    sch.dma("sp", rmask[:, :], rm_d[:, :], w=[rmask], nbytes=1 << 18)
    gwf = sb("gwf", [128, NL * 8, 128])
    gwb = sb("gwb", [128, NL * 8, 128], BF16)
    for l in range(NL):
        sch.dma("sp", gwf[:, l * 8:(l + 1) * 8, :], gw_d[l], w=[gwf], nbytes=1 << 19)
    op("dve", lambda e: e.tensor_copy(out=gwb[:, :, :], in_=gwf[:, :, :]), r=[gwf], w=[gwb], n=2048)

    def pcol(name, c=0, lo=0, hi=128):
        k = PV[name] + c
        return pv[lo:hi, k:k + 1]

    der = sb("der", [128, 32])
    for l in range(NL):
        k = PV["lam%d" % l]
        op("act", lambda e, k=k, l=l: e.activation(out=der[:, l * 4:l * 4 + 4], in_=pv[:, k:k + 4], func=AF.Exp, scale=-1.0), r=[pv], w=[der], n=4, tbl="exp")
        op("act", lambda e, l=l: e.activation(out=der[:, l * 4:l * 4 + 4], in_=der[:, l * 4:l * 4 + 4], func=AF.Ln, bias=pcol("one"), scale=1.0), r=[der, pv], w=[der], n=4, tbl="exp")
        op("dve", lambda e, l=l: e.tensor_scalar(out=der[:, l * 4:l * 4 + 4], in0=der[:, l * 4:l * 4 + 4], scalar1=-8.0, scalar2=None, op0=ALU.mult), r=[der], w=[der], n=4)
    LB, OML, NOML = 8, 12, 16
    kA, kB = PV["hbA"], PV["hbB"]
    op("dve", lambda e: e.tensor_tensor(out=der[:, 20:22], in0=pv[:, kB:kB + 2], in1=pv[:, kA:kA + 2], op=ALU.subtract), r=[pv, der], w=[der], n=2)
    op("act", lambda e: e.activation(out=der[:, 22:24], in_=der[:, 20:22], func=AF.Sigmoid), r=[der], w=[der], n=2, tbl="sig")
    op("dve", lambda e: e.tensor_scalar(out=der[:, LB:LB + 2], in0=der[:, 22:24], scalar1=pcol("lbm"), scalar2=None, op0=ALU.mult), r=[pv], w=[der], n=2)
    op("dve", lambda e: e.tensor_scalar(out=der[:, OML:OML + 2], in0=der[:, LB:LB + 2], scalar1=-1.0, scalar2=1.0, op0=ALU.mult, op1=ALU.add), r=[], w=[der], n=2)
    op("dve", lambda e: e.tensor_scalar(out=der[:, NOML:NOML + 2], in0=der[:, OML:OML + 2], scalar1=-1.0, scalar2=None, op0=ALU.mult), r=[], w=[der], n=2)

    def dcol(base, l, c, lo=0, hi=128):
        k = base + l * (4 if base == 0 else 2) + c
        return der[lo:hi, k:k + 1]

    hst_t = [sb("hst%d" % l, [128, 4]) for l in range(NL)]
    lxh_t = [sb("lxh%d" % l, [128, 4, 4]) for l in range(NL)]
    fh_t = [sb("fh%d" % l, [128, NFC, 2]) for l in range(NL)]
    hst = {}
    lxh = {}
    fh = {}
    st_f = {}
    st_b = {}
    for l in range(NL):
        for c in range(4):
            hst[(l, c)] = hst_t[l].sub("hst%d_%d" % (l, c))
            lxh[(l, c)] = lxh_t[l].sub("lxh%d_%d" % (l, c))
        for ch in range(NFC):
            fh[(l, ch)] = fh_t[l].sub("fh%d_%d" % (l, ch))
        op("dve", lambda e, l=l: e.memset(hst_t[l][:, :], 0.0), w=[hst[(l, c)] for c in range(4)], n=4)
        op("dve", lambda e, l=l: e.memset(lxh_t[l][:, :, :], 0.0), w=[lxh[(l, c)] for c in range(4)], n=16)
        op("dve", lambda e, l=l: e.memset(fh_t[l][:, :, :], 0.0), w=[fh[(l, ch)] for ch in range(NFC)], n=88)
        for mix in ("r", "h"):
            for p in range(2):
                tf = sb("stf_%s%d%d" % (mix, l, p), [128, 64])
                tb = sb("stb_%s%d%d" % (mix, l, p), [128, 64], BF16)
                for hh in range(2):
                    bf_ = tf.sub("stf_%s%d%d%d" % (mix, l, p, hh))
                    bb_ = tb.sub("stb_%s%d%d%d" % (mix, l, p, hh))
                    st_f[(mix, l, p, hh)] = bf_
                    st_b[(mix, l, p, hh)] = bb_
                    lo = hh * 64
                    op("dve", lambda e, bf_=bf_, lo=lo: e.memset(bf_[lo:lo + 64, :], 0.0), w=[bf_], n=64)
                    op("dve", lambda e, bb_=bb_, lo=lo: e.memset(bb_[lo:lo + 64, :], 0.0), w=[bb_], n=64)

    xres = [sb("xres%d" % i, [128, 8, T]) for i in range(2)]
    rv = sb("rv", [128, 8, T])
    tabs = sb("tabs_s", [128, 8, T])
    hT = sb("hT", [128, 8, T], BF16)
    yT = sb("yTs", [128, 8, T], BF16)
    yTc = [yT.sub("yT%d" % c) for c in range(8)]
    gT = sb("gT", [128, 22, T], BF16)
    gTc = [gT.sub("gT%d" % c) for c in range(22)]
    NSLOT = 4
    wsl = [sb("wsl%d" % i, [128, 4096], BF16) for i in range(NSLOT)]

    class FB:
        def __init__(self, name):
            self.b = sb(name, [128, T + 4])
            self.h = self.b.sub(name + "_h")

        @property
        def d(self):
            return self.b[:, 4:4 + T]

    NF = 14
    Fring = [FB("F%d" % i) for i in range(NF)]
    fstate = {"f": 0, "b": 0, "a": 0, "x": 0, "at": 0}

    def getF():
        f = Fring[fstate["f"] % NF]
        fstate["f"] += 1
        return f

    NB = 8
    Bring = [sb("B%d" % i, [128, T], BF16) for i in range(NB)]

    def getB():
        b = Bring[fstate["b"] % NB]
        fstate["b"] += 1
        return b

    vtok = sb("vtok", [128, 8, 256], BF16)
    ktok = sb("ktok", [128, 8, 2, 128], BF16)
    NAT = 4
    ATb = [sb("AT%d" % i, [128, 128], BF16) for i in range(NAT)]
    ebl = [sb("ebl%d" % p, [128, 16]) for p in range(2)]
    stmp = [sb("stmp%d" % i, [128, 64]) for i in range(4)]

    banks = [ps("bank%d" % i, [128, T]) for i in range(8)]
    for i_, b_ in enumerate(banks):
        b_.touch = -100 + i_
        b_.held = False

    def getA(hold=False):
        a = min((b for b in banks if not b.held), key=lambda b: b.touch)
        a.touch = len(sch.rec)
        a.held = hold
        return a

    wstate = {"n": 0}

    def wload(src_ap, shape):
        s = wsl[wstate["n"] % NSLOT]
        wstate["n"] += 1
        n = 1
        for d_ in shape[1:]:
            n *= d_
        view = s[:, 0:n].rearrange("p (a b) -> p a b", b=shape[2])
        sch.dma("pool", view, src_ap, w=[s], nbytes=128 * n * 4)
        return s, view

    def proj(slot, wview, col, rhs_bufs, rhs_of_kc, nk=8, m=128, ncols=T):
        pbuf = getA()

        def fn(e):
            ins = None
            for kc in range(nk):
                ins = e.matmul(pbuf[0:m, 0:ncols], lhsT=wview[:, kc, col:col + m], rhs=rhs_of_kc(kc), start=(kc == 0), stop=(kc == nk - 1))
            return ins

        op("pe", fn, r=[slot] + list(rhs_bufs), w=[pbuf], n=ncols, nmm=nk)
        return pbuf

    def sq_chunk(xr, m):
        op("act", lambda e, m=m: e.activation(out=gT[:, m, :], in_=xr[:, m, :], func=AF.Square), r=[xr], w=[gTc[m]])

    def rms_rstd(xr, use_hT=False, presq=False):
        scr = hT if use_hT else gT
        scr_b = [hT] if use_hT else gTc[0:8]
        if not presq:
            op("act", lambda e: e.activation(out=scr[:, 0:8, :], in_=xr[:, :, :], func=AF.Square), r=[xr], w=scr_b, n=8 * T)
        pS_ = getA()

        def fn(e):
            ins = None
            for c in range(8):
                ins = e.matmul(pS_[:, :], lhsT=cst[:, ONES, :], rhs=scr[:, c, :], start=(c == 0), stop=(c == 7))
            return ins

        op("pe", fn, r=[cst] + scr_b, w=[pS_], nmm=8)
        op("act", lambda e: e.activation(out=pS_[:, :], in_=pS_[:, :], func=AF.Ln, bias=pcol("eps"), scale=1.0 / D), r=[pv], w=[pS_], tbl="exp")
        op("act", lambda e: e.activation(out=pS_[:, :], in_=pS_[:, :], func=AF.Exp, scale=-0.5), r=[], w=[pS_], tbl="exp")
        pS_.held = True
        return pS_

    def norm_to_hT(xr, nwname, use_hT=False, presq=False):
        rs = rms_rstd(xr, use_hT, presq)
        for c in range(8):
            op("dve", lambda e, c=c: e.scalar_tensor_tensor(out=hT[:, c, :], in0=xr[:, c, :], scalar=pcol(nwname, c), in1=rs[:, :],
                                                              op0=ALU.mult, op1=ALU.mult), r=[xr, pv], w=[hT, rs])
        rs.held = False

    hk = lambda kc: hT[:, kc, :]

    def emit_layer(l, xr, ti):
        norm_to_hT(xr, "n1w%d" % l, use_hT=True)

        s0, w0 = wload(win_d[l, :, :, 0:512], [128, 8, 512])
        s1, w1 = wload(win_d[l, :, :, 512:1024], [128, 8, 512])
        for c in range(4):
            lxb, xc, rr, ii, aa, mm = getF(), getF(), getF(), getF(), getF(), getF()
            xcb = getB()
            pa = proj(s0, w0, c * 128, [hT], hk)
            op("pool", lambda e, c=c, lxb=lxb: e.tensor_copy(out=lxb.b[:, 1:4], in_=lxh_t[l][:, c, 0:3]), r=[lxh[(l, c)]], w=[lxb.h], n=3)
            op("act", lambda e, lxb=lxb, pa=pa: e.activation(out=lxb.d, in_=pa[:, :], func=AF.Copy), r=[], w=[lxb.b, pa])
            op("pool", lambda e, c=c, lxb=lxb: e.tensor_copy(out=lxh_t[l][:, c, 0:3], in_=lxb.b[:, T + 1:T + 4]), r=[lxb.b], w=[lxh[(l, c)]], n=3)
            op("dve", lambda e, c=c, lxb=lxb, xc=xc: e.tensor_scalar(out=xc.d, in0=lxb.b[:, 1:1 + T], scalar1=pcol("lcw%d_0" % l, c), scalar2=pcol("lcb%d" % l, c),
                                                                      op0=ALU.mult, op1=ALU.add), r=[lxb.b, lxb.h, pv], w=[xc.b])
            for k in range(1, 4):
                op("dve", lambda e, c=c, k=k, lxb=lxb, xc=xc: e.scalar_tensor_tensor(out=xc.d, in0=lxb.b[:, 1 + k:1 + k + T], scalar=pcol("lcw%d_%d" % (l, k), c), in1=xc.d,
                                                                                      op0=ALU.mult, op1=ALU.add), r=[lxb.b, lxb.h, pv], w=[xc.b], k=2)
            op("act", lambda e, xc=xc, xcb=xcb: e.activation(out=xcb[:, :], in_=xc.d, func=AF.Copy), r=[xc.b], w=[xcb])
            pg = getA()
            op("pe", lambda e, c=c, pg=pg, xcb=xcb: e.matmul(pg[:, :], lhsT=gwb[:, l * 8 + c * 2, :], rhs=xcb[:, :], start=True, stop=True), r=[gwb, xcb], w=[pg])
            op("act", lambda e, c=c, pg=pg, rr=rr: e.activation(out=rr.d, in_=pg[:, :], func=AF.Sigmoid, bias=pcol("ba%d" % l, c)), r=[pv], w=[rr.b, pg], tbl="sig")
            pg2 = getA()
            op("pe", lambda e, c=c, pg2=pg2, xcb=xcb: e.matmul(pg2[:, :], lhsT=gwb[:, l * 8 + c * 2 + 1, :], rhs=xcb[:, :], start=True, stop=True), r=[gwb, xcb], w=[pg2])
            op("act", lambda e, c=c, pg2=pg2, ii=ii: e.activation(out=ii.d, in_=pg2[:, :], func=AF.Sigmoid, bias=pcol("bx%d" % l, c)), r=[pv], w=[ii.b, pg2], tbl="sig")
            op("act", lambda e, c=c, rr=rr, aa=aa: e.activation(out=aa.d, in_=rr.d, func=AF.Exp, scale=dcol(0, l, c)), r=[rr.b, der], w=[aa.b], tbl="exp")
            op("act", lambda e, aa=aa, mm=mm: e.activation(out=mm.d, in_=aa.d, func=AF.Square), r=[aa.b], w=[mm.b])
            op("act", lambda e, mm=mm: e.activation(out=mm.d, in_=mm.d, func=AF.Sqrt, bias=pcol("one"), scale=-1.0), r=[pv], w=[mm.b], tbl="sqrt")
            op("pool", lambda e, ii=ii, xc=xc: e.tensor_tensor(out=ii.d, in0=ii.d, in1=xc.d, op=ALU.mult), r=[xc.b], w=[ii.b], k=2)
            op("pool", lambda e, ii=ii, mm=mm: e.tensor_tensor(out=ii.d, in0=ii.d, in1=mm.d, op=ALU.mult), r=[mm.b], w=[ii.b], k=2)
            hs = hst[(l, c)]
            op("dve", lambda e, c=c, rr=rr, aa=aa, ii=ii: e.tensor_tensor_scan(out=rr.d, data0=aa.d, data1=ii.d, initial=hst_t[l][:, c:c + 1],
                                                                                op0=ALU.mult, op1=ALU.add), r=[aa.b, ii.b, hs], w=[rr.b], k=2)
            op("dve", lambda e, c=c, rr=rr: e.tensor_copy(out=hst_t[l][:, c:c + 1], in_=rr.b[:, T + 3:T + 4]), r=[rr.b], w=[hs], n=1)
            pb = proj(s1, w1, c * 128, [hT], hk)
            op("act", lambda e, mm=mm, pb=pb: e.activation(out=mm.d, in_=pb[:, :], func=AF.Gelu_apprx_tanh), r=[], w=[mm.b, pb], tbl="gelu")
            op("dve", lambda e, c=c, rr=rr, mm=mm: e.tensor_tensor(out=yT[:, c, :], in0=rr.d, in1=mm.d, op=ALU.mult), r=[rr.b, mm.b], w=[yTc[c]], k=2)

        def gla(mix, qT, kT, C, BLK, sgate, nwname, ycol0, decay_const):
            nblk = T // BLK
            nch = T // C
            msk = MASK128 if C == 128 else MASK32
            pO = [getA(True), getA(True)]
            pX = [getA(True), getA(True)]
            pUT = getA(True)
            pTr = pUT[:, 256:512].bitcast(BF16)
            for p in range(2):
                for b_ in range(nblk):
                    op("pe", lambda e, p=p, b_=b_: e.transpose(pTr[0:BLK, 0:128], kT[p][:, b_ * BLK:(b_ + 1) * BLK], cst[:, IDENT, :]), r=[kT[p], cst], w=[pUT], n=128)
                    op("act", lambda e, p=p, b_=b_: e.activation(out=ktok[0:BLK, b_, p, :], in_=pTr[0:BLK, 0:128], func=AF.Copy), r=[], w=[ktok, pUT], n=128)
            for j in range(nch):
                blk = (j * C) // BLK
                base = (j * C) % BLK
                tsl = slice(j * C, (j + 1) * C)
                for p in range(2):
                    for hh in range(2):
                        h = 2 * p + hh
                        off = hh * 64
                        px = pX[fstate["x"] % 2]
                        fstate["x"] += 1
                        at = ATb[fstate["at"] % NAT]
                        fstate["at"] += 1
                        Sb = st_b[(mix, l, p, hh)]
                        op("pe", lambda e, p=p, off=off, tsl=tsl, px=px, base=base: e.matmul(
                            px[base:base + C, 0:C], lhsT=kT[p][off:off + 64, tsl], rhs=qT[p][off:off + 64, tsl], start=True, stop=True),
                            r=[kT[p], qT[p]], w=[px], n=C)
                        op("dve", lambda e, px=px, at=at, base=base: e.tensor_tensor(
                            out=at[base:base + C, 0:C], in0=px[base:base + C, 0:C], in1=cst[base:base + C, msk, 0:C], op=ALU.mult),
                            r=[cst], w=[at, px], n=C)

                        def fn_o(e, p=p, off=off, tsl=tsl, at=at, base=base, blk=blk, h=h, Sb=Sb):
                            e.matmul(pO[p][off:off + 64, tsl], lhsT=vtok[base:base + C, blk, h * 64:(h + 1) * 64], rhs=at[base:base + C, 0:C], start=True, stop=False)
                            return e.matmul(pO[p][off:off + 64, tsl], lhsT=Sb[off:off + 64, :], rhs=qT[p][off:off + 64, tsl], start=False, stop=True)

                        op("pe", fn_o, r=[vtok, at, Sb, qT[p]], w=[pO[p]], n=C, nmm=2)
                    op("pe", lambda e, p=p, base=base, blk=blk: e.matmul(
                        pUT[:, 0:128], lhsT=ktok[base:base + C, blk, p, :], rhs=vtok[base:base + C, blk, p * 128:(p + 1) * 128],
                        start=True, stop=True), r=[ktok, vtok], w=[pUT], n=128)
                    for hh in range(2):
                        h = 2 * p + hh
                        off = hh * 64
                        Sf = st_f[(mix, l, p, hh)]
                        Sb = st_b[(mix, l, p, hh)]
                        if decay_const is not None:
                            g = decay_const[h]
                            op("dve", lambda e, Sf=Sf, off=off, g=g: e.scalar_tensor_tensor(
                                out=Sf[off:off + 64, :], in0=Sf[off:off + 64, :], scalar=g, in1=pUT[off:off + 64, off:off + 64], op0=ALU.mult, op1=ALU.add),
                                r=[], w=[Sf, pUT], n=64)
                            op("dve", lambda e, Sf=Sf, Sb=Sb, off=off, g=g: e.tensor_scalar(
                                out=Sb[off:off + 64, :], in0=Sf[off:off + 64, :], scalar1=g, scalar2=None, op0=ALU.mult), r=[Sf], w=[Sb], n=64)
                        else:
                            tmp = stmp[h]
                            op("dve", lambda e, Sf=Sf, off=off, tmp=tmp: e.tensor_tensor(
                                out=tmp[off:off + 64, :], in0=pUT[off:off + 64, off:off + 64], in1=Sf[off:off + 64, :], op=ALU.add), r=[Sf], w=[tmp, pUT], n=64)
                            op("dve", lambda e, Sf=Sf, off=off, tmp=tmp, p=p, j=j: e.tensor_scalar(
                                out=Sf[off:off + 64, :], in0=tmp[off:off + 64, :], scalar1=ebl[p][off:off + 64, j:j + 1], scalar2=None, op0=ALU.mult),
                                r=[tmp, ebl[p]], w=[Sf], n=64)
                            op("act", lambda e, Sf=Sf, Sb=Sb, off=off: e.activation(out=Sb[off:off + 64, :], in_=Sf[off:off + 64, :], func=AF.Copy), r=[Sf], w=[Sb], n=64)
            for p in range(2):
                o, rt = getF(), getF()
                osq = getB()
                op("act", lambda e, p=p, o=o: e.activation(out=o.d, in_=pO[p][:, :], func=AF.Copy), r=[], w=[o.b, pO[p]])
                op("act", lambda e, o=o, osq=osq: e.activation(out=osq[:, :], in_=o.d, func=AF.Square), r=[o.b], w=[osq])
                pS_ = getA()
                op("pe", lambda e, pS_=pS_, osq=osq: e.matmul(pS_[:, :], lhsT=cst[:, BD64, :], rhs=osq[:, :], start=True, stop=True), r=[cst, osq], w=[pS_])
                op("act", lambda e, pS_=pS_, rt=rt: e.activation(out=rt.d, in_=pS_[:, :], func=AF.Ln, bias=pcol("eps"), scale=1.0), r=[pv], w=[rt.b, pS_], tbl="exp")
                op("act", lambda e, rt=rt: e.activation(out=rt.d, in_=rt.d, func=AF.Exp, scale=-0.5), r=[], w=[rt.b], tbl="exp")
                op("dve", lambda e, p=p, o=o, rt=rt: e.scalar_tensor_tensor(out=o.d, in0=o.d, scalar=pcol(nwname, p), in1=rt.d, op0=ALU.mult, op1=ALU.mult),
                   r=[rt.b, pv], w=[o.b], k=2)
                op("pool", lambda e, p=p, o=o: e.tensor_tensor(out=yT[:, ycol0 + p, :], in0=o.d, in1=sgate[p].d, op=ALU.mult), r=[o.b, sgate[p].b], w=[yTc[ycol0 + p]], k=2)
            for b_ in pO + pX + [pUT]:
                b_.held = False

        s2, w2 = wload(win_d[l, :, :, 1024:1536], [128, 8, 512])
        s6, w6 = wload(win_d[l, :, :, 3072:3584], [128, 8, 512])
        s3, w3 = wload(win_d[l, :, :, 1536:2048], [128, 8, 512])
        if l == 0:
            sch.dma("sp", tabs[:, :, :], tab_d[:, :, ti * T:(ti + 1) * T], w=[tabs], nbytes=2 << 20)
        qT = [getB(), getB()]
        kT = [getB(), getB()]
        for dst, cbase, tb in ((qT, 0, 0), (kT, 256, 2)):
            for p in range(2):
                pa = proj(s2, w2, cbase + p * 128, [hT], hk)
                pb = proj(s6, w6, cbase + p * 128, [hT], hk)
                t1, t2 = getF(), getF()
                op("dve", lambda e, p=p, tb=tb, t1=t1, pa=pa: e.tensor_tensor(out=t1.d, in0=pa[:, :], in1=tabs[:, p * 4 + tb, :], op=ALU.mult), r=[tabs], w=[t1.b, pa])
                op("dve", lambda e, p=p, tb=tb, t2=t2, pb=pb: e.tensor_tensor(out=t2.d, in0=pb[:, :], in1=tabs[:, p * 4 + tb + 1, :], op=ALU.mult), r=[tabs], w=[t2.b, pb])
                op("pool", lambda e, p=p, dst=dst, t1=t1, t2=t2: e.tensor_tensor(out=dst[p][:, :], in0=t1.d, in1=t2.d, op=ALU.add), r=[t1.b, t2.b], w=[dst[p]], k=2)
        for b_ in range(4):
            pg = getA()

            def fn_v(e, b_=b_, pg=pg):
                ins = None
                for kc in range(8):
                    ins = e.matmul(pg[:, 0:256], lhsT=hT[:, kc, b_ * 128:(b_ + 1) * 128], rhs=w3[:, kc, 0:256], start=(kc == 0), stop=(kc == 7))
                return ins

            op("pe", fn_v, r=[hT, s3], w=[pg], n=256, nmm=8)
            op("act", lambda e, b_=b_, pg=pg: e.activation(out=vtok[:, b_, :], in_=pg[:, 0:256], func=AF.Copy), r=[], w=[vtok, pg], n=256)
        sg = [getF(), getF()]
        for p in range(2):
            pa = proj(s3, w3, 256 + p * 128, [hT], hk)
            op("act", lambda e, p=p, pa=pa, sg=sg: e.activation(out=sg[p].d, in_=pa[:, :], func=AF.Silu), r=[], w=[sg[p].b, pa], tbl="silu")
        gla("r", qT, kT, 128, 128, sg, "rnw%d" % l, 4, [g ** 128 for g in RET_GAMMA])

        s4, w4 = wload(win_d[l, :, :, 2048:2560], [128, 8, 512])
        s5, w5 = wload(win_d[l, :, :, 2560:3072], [128, 8, 512])
        qT = [getB(), getB()]
        kT = [getB(), getB()]
        for p in range(2):
            sig, ff, bb, en = getF(), getF(), getF(), getF()
            pa = proj(s4, w4, 256 + p * 128, [hT], hk)
            op("act", lambda e, sig=sig, pa=pa: e.activation(out=sig.d, in_=pa[:, :], func=AF.Sigmoid), r=[], w=[sig.b, pa], tbl="sig")
            op("dve", lambda e, p=p, sig=sig, ff=ff: e.tensor_scalar(out=ff.d, in0=sig.d, scalar1=dcol(OML, l, p), scalar2=dcol(LB, l, p), op0=ALU.mult, op1=ALU.add),
               r=[sig.b, der], w=[ff.b])
            op("act", lambda e, ff=ff: e.activation(out=ff.d, in_=ff.d, func=AF.Ln), r=[], w=[ff.b], tbl="exp")
            op("dve", lambda e, ff=ff, bb=bb: e.tensor_tensor_scan(out=bb.d, data0=rmask[:, :], data1=ff.d, initial=0.0, op0=ALU.mult, op1=ALU.add),
               r=[rmask, ff.b], w=[bb.b], k=2)
            op("pool", lambda e, p=p, sig=sig: e.tensor_scalar(out=sig.d, in0=sig.d, scalar1=dcol(NOML, l, p), scalar2=dcol(OML, l, p), op0=ALU.mult, op1=ALU.add),
               r=[der], w=[sig.b])
            op("act", lambda e, bb=bb, en=en: e.activation(out=en.d, in_=bb.d, func=AF.Exp, scale=-1.0), r=[bb.b], w=[en.b], tbl="exp")
            op("dve", lambda e, p=p, sig=sig, en=en, kT=kT: e.tensor_tensor(out=kT[p][:, :], in0=sig.d, in1=en.d, op=ALU.mult), r=[sig.b, en.b], w=[kT[p]], k=2)
            op("act", lambda e, bb=bb, ff=ff: e.activation(out=ff.d, in_=bb.d, func=AF.Exp), r=[bb.b], w=[ff.b], tbl="exp")
            op("dve", lambda e, p=p, ff=ff: e.tensor_copy(out=ebl[p][:, :], in_=ff.d.rearrange("p (c t) -> p c t", t=32)[:, :, 31]), r=[ff.b], w=[ebl[p]], n=16)
            pb = proj(s4, w4, p * 128, [hT], hk)
            op("act", lambda e, en=en, pb=pb: e.activation(out=en.d, in_=pb[:, :], func=AF.Silu), r=[], w=[en.b, pb], tbl="silu")
            op("dve", lambda e, p=p, en=en, ff=ff, qT=qT: e.tensor_tensor(out=qT[p][:, :], in0=en.d, in1=ff.d, op=ALU.mult), r=[en.b, ff.b], w=[qT[p]], k=2)
        for b_ in range(8):
            pg = getA()

            def fn_v2(e, b_=b_, pg=pg):
                ins = None
                for kc in range(8):
                    ins = e.matmul(pg[0:64, 0:256], lhsT=hT[:, kc, b_ * 64:(b_ + 1) * 64], rhs=w5[:, kc, 0:256], start=(kc == 0), stop=(kc == 7))
                return ins

            op("pe", fn_v2, r=[hT, s5], w=[pg], n=256, nmm=8)
            op("act", lambda e, b_=b_, pg=pg: e.activation(out=vtok[0:64, b_, :], in_=pg[0:64, 0:256], func=AF.Copy), r=[], w=[vtok, pg], n=256)
        sg = [getF(), getF()]
        for p in range(2):
            pa = proj(s5, w5, 256 + p * 128, [hT], hk)
            op("act", lambda e, p=p, pa=pa, sg=sg: e.activation(out=sg[p].d, in_=pa[:, :], func=AF.Silu), r=[], w=[sg[p].b, pa], tbl="silu")
        gla("h", qT, kT, 32, 64, sg, "hnw%d" % l, 6, None)

        for half in range(2):
            so, wo = wload(wout_d[l, :, :, half * 512:(half + 1) * 512], [128, 8, 512])
            for mq in range(4):
                m = half * 4 + mq
                pb = proj(so, wo, mq * 128, yTc, lambda kc: yT[:, kc, :])
                op("dve", lambda e, m=m, pb=pb: e.tensor_tensor(out=xr[:, m, :], in0=pb[:, :], in1=xr[:, m, :], op=ALU.add), r=[], w=[xr, pb])
                sq_chunk(xr, m)

        def down_half(hf):
            for m in range(8):
                sd, wd = wload(wdn_d[l, m, hf], [128, 11, 128])
                pb = getA()

                def fn_d(e, pb=pb, wd=wd):
                    ins = None
                    for kc in range(11):
                        ins = e.matmul(pb[:, :], lhsT=wd[:, kc, :], rhs=gT[:, hf * 11 + kc, :], start=(kc == 0), stop=(kc == 10))
                    return ins

                op("pe", fn_d, r=[sd] + gTc[hf * 11:(hf + 1) * 11], w=[pb], nmm=11)
                op("dve", lambda e, m=m, pb=pb: e.tensor_tensor(out=xr[:, m, :], in0=pb[:, :], in1=xr[:, m, :], op=ALU.add), r=[], w=[xr, pb])
                if hf == 1:
                    sq_chunk(xr, m)

        norm_to_hT(xr, "n2w%d" % l, presq=True)
        for q in range(11):
            su, wu = wload(wup_d[l, :, :, q * 512:(q + 1) * 512], [128, 8, 512])
            for jj in range(2):
                j = 2 * q + jj
                outs = []
                for (half, col) in ((0, jj * 128), (1, 256 + jj * 128)):
                    ch = half * 22 + j
                    cb = getF()
                    pb = proj(su, wu, col, [hT], hk)
                    fhb = fh[(l, ch)]
                    op("act", lambda e, cb=cb, pb=pb, ch=ch: e.activation(out=cb.d, in_=pb[:, :], func=AF.Identity, bias=pcol("fcb%d" % l, ch), scale=pcol("fcw%d_2" % l, ch)),
                       r=[pv], w=[cb.b, pb])
                    op("dve", lambda e, cb=cb, pb=pb, ch=ch: e.scalar_tensor_tensor(out=cb.b[:, 5:4 + T], in0=pb[:, 0:T - 1], scalar=pcol("fcw%d_1" % l, ch), in1=cb.b[:, 5:4 + T],
                                                                                     op0=ALU.mult, op1=ALU.add), r=[pv], w=[cb.b, pb])
                    op("dve", lambda e, cb=cb, pb=pb, ch=ch: e.scalar_tensor_tensor(out=cb.b[:, 6:4 + T], in0=pb[:, 0:T - 2], scalar=pcol("fcw%d_0" % l, ch), in1=cb.b[:, 6:4 + T],
                                                                                     op0=ALU.mult, op1=ALU.add), r=[pv], w=[cb.b, pb])
                    op("dve", lambda e, cb=cb, ch=ch: e.scalar_tensor_tensor(out=cb.b[:, 4:5], in0=fh_t[l][:, ch, 1:2], scalar=pcol("fcw%d_1" % l, ch), in1=cb.b[:, 4:5],
                                                                              op0=ALU.mult, op1=ALU.add), r=[pv, fhb], w=[cb.b], n=1)
                    op("dve", lambda e, cb=cb, ch=ch: e.scalar_tensor_tensor(out=cb.b[:, 4:6], in0=fh_t[l][:, ch, 0:2], scalar=pcol("fcw%d_0" % l, ch), in1=cb.b[:, 4:6],
                                                                              op0=ALU.mult, op1=ALU.add), r=[pv, fhb], w=[cb.b], n=2)
                    op("act", lambda e, pb=pb, ch=ch: e.activation(out=fh_t[l][:, ch, :], in_=pb[:, T - 2:T], func=AF.Copy), r=[], w=[fhb, pb], n=2)
                    outs.append(cb)
                cg, cv = outs
                op("act", lambda e, cg=cg: e.activation(out=cg.d, in_=cg.d, func=AF.Silu), r=[], w=[cg.b], tbl="silu")
                op("pool", lambda e, cg=cg, cv=cv, j=j: e.tensor_tensor(out=gT[:, j, :], in0=cg.d, in1=cv.d, op=ALU.mult), r=[cg.b, cv.b], w=[gTc[j]], k=2)
            if q == 5:
                down_half(0)
        down_half(1)

    xv = xT_d.rearrange("(c p) s -> p c s", p=128)
    yv = yT_d.rearrange("(c p) s -> p c s", p=128)
    sndB = Buf(snd_t, "snd")
    rcvB = [Buf(rcv_t[i], "rcv%d" % i) for i in range(2)]
    snd_v = snd_t.ap().rearrange("(c p) s -> p c s", p=128)
    out_toks = []
    for st in range(NS):
        xr = xres[st % 2]
        sch.dma("sp", xr[:, :, :], xv[:, :, st * T:(st + 1) * T], w=[xr], nbytes=2 << 20)
        op("act", lambda e, xr=xr: e.activation(out=xr[:, :, :], in_=xr[:, :, :], func=AF.Copy, scale=pcol("m")), r=[pv], w=[xr], n=8 * T)
        if st >= SKEW:
            rb = rcvB[(st - SKEW) % 2]
            rcv_v = rcv_t[(st - SKEW) % 2].ap()[0:D, :].rearrange("(c p) s -> p c s", p=128)
            sch.dma("sp", rv[:, :, :], rcv_v, r=[rb], w=[rv], nbytes=2 << 20)
            op("dve", lambda e, xr=xr: e.scalar_tensor_tensor(out=xr[:, :, :], in0=rv[:, :, :], scalar=pcol("om"), in1=xr[:, :, :], op0=ALU.mult, op1=ALU.add),
               r=[rv, pv], w=[xr], n=8 * T, k=2)
        emit_layer(0, xr, st)
        if st < NT:
            sch.dma("sp", snd_v, xr[:, :, :], r=[xr], w=[sndB], nbytes=2 << 20)
            rb = rcvB[st % 2]
            sch.raw("pool", lambda e, rb=rb: e.collective_compute("AllGather", ALU.bypass, replica_groups=GROUPS,
                                                                   ins=[snd_t.ap().opt()], outs=[rb.t.ap().opt()]),
                    r=[sndB], w=[rb], sem_buf=rb, inc=1, occ=1.0, lat=120.0)
        if st == SKEW - 1:
            kp = pcol("keep")
            op("dve", lambda e: e.tensor_scalar(out=hst_t[0][:, :], in0=hst_t[0][:, :], scalar1=kp, scalar2=None, op0=ALU.mult), r=[pv], w=[hst[(0, c)] for c in range(4)], n=4)
            op("dve", lambda e: e.tensor_scalar(out=lxh_t[0][:, :, :], in0=lxh_t[0][:, :, :], scalar1=kp, scalar2=None, op0=ALU.mult), r=[pv], w=[lxh[(0, c)] for c in range(4)], n=16)
            op("dve", lambda e: e.tensor_scalar(out=fh_t[0][:, :, :], in0=fh_t[0][:, :, :], scalar1=kp, scalar2=None, op0=ALU.mult), r=[pv], w=[fh[(0, ch)] for ch in range(NFC)], n=88)
            for key in list(st_f.keys()):
                lo = key[3] * 64
                for tbl in (st_f, st_b):
                    bb_ = tbl[key]
                    op("dve", lambda e, bb_=bb_, lo=lo: e.tensor_scalar(out=bb_[lo:lo + 64, :], in0=bb_[lo:lo + 64, :], scalar1=pv[lo:lo + 64, PV["keep"]:PV["keep"] + 1],
                                                                          scalar2=None, op0=ALU.mult), r=[pv], w=[bb_], n=64)
        rs = rms_rstd(xr, presq=True)
        for c in range(8):
            op("dve", lambda e, c=c, xr=xr, rs=rs: e.scalar_tensor_tensor(out=xr[:, c, :], in0=xr[:, c, :], scalar=pcol("fnw", c), in1=rs[:, :],
                                                                           op0=ALU.mult, op1=ALU.mult), r=[pv], w=[xr, rs])
        rs.held = False
        out_toks.append(sch.dma("sp", yv[:, :, st * T:(st + 1) * T], xr[:, :, :], r=[xr], nbytes=2 << 20))
    sch.wait_all_on("sp", out_toks)
    sch.emit()
    return nc, sch


def _chunks(v):
    v = np.asarray(v, np.float32)
    return np.ascontiguousarray(v.reshape(-1, 128).T)


def make_tables(S):
    t = np.arange(S, dtype=np.float64)
    inv = 10000.0 ** (-np.arange(0, 64, 2, dtype=np.float64) / 64.0)
    ang = t[None, :] * inv[:, None]
    cos = np.concatenate([np.cos(ang), np.cos(ang)], 0)
    sin = np.concatenate([-np.sin(ang), np.sin(ang)], 0)
    n1 = (t % 128) + 1.0
    tabs = np.zeros((128, 8, S), np.float64)
    for p in range(2):
        for hh in range(2):
            h = 2 * p + hh
            lg = math.log(RET_GAMMA[h])
            qd = np.exp(n1 * lg)[None, :]
            kd = np.exp(-n1 * lg)[None, :] * (64.0 ** -0.5)
            sl = slice(hh * 64, hh * 64 + 64)
            tabs[sl, p * 4 + 0] = cos * qd
            tabs[sl, p * 4 + 1] = sin * qd
            tabs[sl, p * 4 + 2] = cos * kd
            tabs[sl, p * 4 + 3] = sin * kd
    return tabs.astype(np.float32)


def make_consts():
    cst = np.zeros((128, 5, 128), np.float32)
    cst[:, 0, :] = 1.0
    cst[:, 1, :] = np.eye(128, dtype=np.float32)
    for b in range(2):
        cst[b * 64:(b + 1) * 64, 2, b * 64:(b + 1) * 64] = 1.0 / 64.0
    m = np.arange(128)[:, None]
    n = np.arange(128)[None, :]
    cst[:, 3, :] = (n >= m).astype(np.float32)
    m32 = (np.arange(32)[None, :] >= np.arange(32)[:, None]).astype(np.float32)
    for b in range(4):
        cst[b * 32:(b + 1) * 32, 4, 0:32] = m32
    rmask = np.ones((128, T), np.float32)
    rmask[:, ::32] = 0.0
    return cst, rmask


def prep_role(inp, role, NT):
    lyr = role
    NS = NT + SKEW
    w_in = np.asarray(inp["w_in"], np.float32)[lyr:lyr + 1]
    swap = lambda base: [base + h * 64 + ((d + 32) % 64) for h in range(4) for d in range(64)]
    idx = list(range(3072)) + swap(1024) + swap(1280)
    win = w_in[:, :, idx].reshape(1, 8, 128, NCOL_IN).transpose(0, 2, 1, 3)
    wout = np.asarray(inp["w_out"], np.float32)[lyr:lyr + 1].reshape(1, 8, 128, D).transpose(0, 2, 1, 3)
    perm = []
    for q in range(11):
        perm += list(range(2 * q * 128, (2 * q + 2) * 128))
        perm += list(range(DFF + 2 * q * 128, DFF + (2 * q + 2) * 128))
    wup = np.asarray(inp["ffn_w_up"], np.float32)[lyr:lyr + 1][:, :, perm].reshape(1, 8, 128, 2 * DFF).transpose(0, 2, 1, 3)
    wdn = np.asarray(inp["ffn_w_down"], np.float32)[lyr:lyr + 1].reshape(1, 2, 11, 128, 8, 128).transpose(0, 4, 1, 3, 2, 5)
    gw = np.zeros((1, 128, 8, 128), np.float32)
    wa = np.asarray(inp["lru_wa"], np.float32)
    wx = np.asarray(inp["lru_wx"], np.float32)
    for c in range(4):
        for b2 in range(2):
            sl = slice(b2 * 64, b2 * 64 + 64)
            gw[0, sl, c * 2 + 0, sl] = wa[lyr, 2 * c + b2]
            gw[0, sl, c * 2 + 1, sl] = wx[lyr, 2 * c + b2]
    pvec = np.zeros((128, PV["_n"]), np.float32)

    def put(name, v):
        a = _chunks(v)
        pvec[:, PV[name]:PV[name] + a.shape[1]] = a

    put("n1w0", inp["norm1_w"][lyr])
    put("n2w0", inp["norm2_w"][lyr])
    for k in range(4):
        put("lcw0_%d" % k, inp["lru_conv_w"][lyr][k])
    put("lcb0", inp["lru_conv_b"][lyr])
    put("ba0", inp["lru_ba"][lyr])
    put("bx0", inp["lru_bx"][lyr])
    put("lam0", inp["lru_lambda"][lyr])
    put("rnw0", inp["ret_norm_w"][lyr])
    put("hnw0", inp["hg_norm_w"][lyr])
    put("hbA", inp["hg_lower_bounds"][0])
    put("hbB", inp["hg_lower_bounds"][1])
    for k in range(3):
        put("fcw0_%d" % k, inp["ffn_conv_w"][lyr][k])
    put("fcb0", inp["ffn_conv_b"][lyr])
    put("fnw", inp["final_norm_w"])
    pvec[:, PV["eps"]] = EPS
    pvec[:, PV["one"]] = 1.0
    pvec[:, PV["m"]] = 1.0 if role == 0 else 0.0
    pvec[:, PV["om"]] = 0.0 if role == 0 else 1.0
    pvec[:, PV["keep"]] = 1.0 if role == 0 else 0.0
    pvec[:, PV["lbm"]] = 0.0 if role == 0 else 1.0
    cst, rmask = make_consts()
    tb = make_tables(NT * T)
    tabs = np.zeros((128, 8, NS * T), np.float32)
    for st in range(NS):
        ti = st if role == 0 else st - SKEW
        if ti < 0 or ti >= NT:
            ti = 0
        tabs[:, :, st * T:(st + 1) * T] = tb[:, :, ti * T:(ti + 1) * T]
    return {
        "win": np.ascontiguousarray(win), "wout": np.ascontiguousarray(wout),
        "wup": np.ascontiguousarray(wup), "wdn": np.ascontiguousarray(wdn),
        "gatew": gw, "pvec": pvec, "tabs": tabs, "cst": cst, "rmask": rmask,
    }


_CACHE = {}


def kernel(**inputs):
    x = np.asarray(inputs["x"], np.float32)
    B, S, _ = x.shape
    NT = S // T
    NS = NT + SKEW
    roles = [prep_role(inputs, r, NT) for r in range(2)]
    if NT not in _CACHE:
        _CACHE[NT] = build(NT)
    nc, sch = _CACHE[NT]
    ncores = 2 * B
    zeros = np.zeros((D, NS * T), np.float32)
    in_maps = []
    for c in range(ncores):
        b, r = c // 2, c % 2
        m = dict(roles[r])
        if r == 0:
            xp = np.zeros((D, NS * T), np.float32)
            xp[:, :S] = x[b].T
            m["xT"] = xp
        else:
            m["xT"] = zeros
        in_maps.append(m)
    res = run_bass_kernel_spmd(nc, in_maps, core_ids=list(range(ncores)))
    out = np.empty((B, S, D), np.float32)
    for b in range(B):
        out[b] = res.results[2 * b + 1]["yT"][:, SKEW * T:].T
    return out
```

```python
import contextlib
import math
import numpy as np
import concourse.bass as bass
import concourse.mybir as mybir
from concourse.bass_utils import run_bass_kernel_spmd

F32 = mybir.dt.float32
BF16 = mybir.dt.bfloat16
AF = mybir.ActivationFunctionType
ALU = mybir.AluOpType

EPOCH = 4000
PRIO = "cp"

D = 1024
SEQ = 8192
BATCH = 4
DEPTH = 2
NL = 1
SKEW = 2
T = 512
DFF = 2816
NFC = 44
EPS = 1e-6
NCOL_IN = 3072


import heapq
import sys


class Buf:
    def __init__(self, t, name):
        self.t = t
        self.name = name
        self.last_write = None
        self.reads = []
        self.dma_cnt = 0
        self.g_lw = None
        self.g_rd = []
        self.held = False

    def __getitem__(self, key):
        return self.t[key]

    def sub(self, name):
        return Buf(self.t, name)


class _Op:
    __slots__ = ("eng", "fn", "r", "w", "occ", "lat", "kind", "sem_buf", "preds", "succs", "inc", "line", "start", "tbl")


class Sched:
    ENG = ("pe", "act", "dve", "pool", "sp")

    def __init__(self, nc):
        self.nc = nc
        self.stack = contextlib.ExitStack()
        self.rec = []
        self.ops = {e: [] for e in self.ENG}
        self.cnt = {e: 0 for e in self.ENG}
        self.epoch = {e: 0 for e in self.ENG}
        self.sems = {}
        self.waited = {e: {} for e in self.ENG}
        self.nsem = 0
        self.final_waits = []
        self.sim_time = 0.0

    def sem(self, key):
        if key not in self.sems:
            self.sems[key] = self.stack.enter_context(self.nc.semaphore("s%d" % self.nsem))
            self.nsem += 1
        return self.sems[key]

    def sb(self, name, shape, dtype=F32):
        t = self.stack.enter_context(self.nc.sbuf_tensor(name, list(shape), dtype))
        return Buf(t, name)

    def ps(self, name, shape, dtype=F32):
        t = self.stack.enter_context(self.nc.psum_tensor(name, list(shape), dtype))
        return Buf(t, name)

    def _record(self, o):
        i = len(self.rec)
        try:
            f = sys._getframe(2)
            while f.f_code.co_name in ("proj", "wload", "getA", "op", "dma", "raw"):
                f = f.f_back
            o.line = f.f_lineno
        except Exception:
            o.line = 0
        preds = set()
        for b in o.r:
            if b.g_lw is not None:
                preds.add(b.g_lw)
        for b in o.w:
            if b.g_lw is not None:
                preds.add(b.g_lw)
            preds.update(b.g_rd)
        preds.discard(i)
        o.preds = preds
        o.succs = []
        for p in preds:
            self.rec[p].succs.append(i)
        for b in o.w:
            b.g_lw = i
            b.g_rd = []
            if hasattr(b, "touch"):
                b.touch = i
        for b in o.r:
            if b not in o.w:
                b.g_rd.append(i)
        self.rec.append(o)
        return i

    def op(self, eng, fn, r=(), w=(), n=T, k=1.0, nmm=1, tbl=None):
        o = _Op()
        o.tbl = tbl
        o.eng, o.fn, o.r, o.w, o.kind, o.sem_buf = eng, fn, tuple(r), tuple(w), "c", None
        if eng == "pe":
            o.occ = nmm * (0.19 + 0.0002 * n)
        elif eng == "act":
            o.occ = 0.22 + 0.00072 * n
        elif eng == "dve":
            o.occ = 0.07 + 0.00105 * n * k
        else:
            o.occ = 0.4 + 0.002 * n
        o.lat = o.occ + (0.1 if eng == "pe" else 0.3)
        return self._record(o)

    def dma(self, q, out_ap, in_ap, r=(), w=(), sem_buf=None, nbytes=1 << 20):
        o = _Op()
        o.eng, o.r, o.w, o.kind = q, tuple(r), tuple(w), "d"
        o.sem_buf = sem_buf if sem_buf is not None else (w[0] if len(w) else r[0])

        def fn(e, out_ap=out_ap, in_ap=in_ap):
            return e.dma_start(out=out_ap, in_=in_ap)

        o.fn = fn
        o.occ = 1.0 if q == "pool" else 0.15
        o.lat = o.occ + 2.0 + nbytes / 150e3
        o.inc = 16
        return self._record(o)

    def raw(self, q, fn, r=(), w=(), sem_buf=None, inc=1, occ=1.0, lat=50.0):
        o = _Op()
        o.eng, o.r, o.w, o.kind = q, tuple(r), tuple(w), "d"
        o.sem_buf = sem_buf
        o.fn = fn
        o.occ, o.lat = occ, lat
        o.inc = inc
        return self._record(o)

    def wait_all_on(self, eng, idxs):
        self.final_waits.append((eng, list(idxs)))

    def _schedule(self):
        ops = self.rec
        ENG = self.ENG
        ready = {e: [] for e in ENG}
        busy = {e: 0.0 for e in ENG}
        npred = [len(o.preds) for o in ops]
        N = len(ops)
        tail = [0.0] * N
        for i in range(N - 1, -1, -1):
            o = ops[i]
            t = 0.0
            for s_ in o.succs:
                if tail[s_] > t:
                    t = tail[s_]
            tail[i] = t + o.lat
        if PRIO == "cp":
            key = [(-tail[i], i) for i in range(N)]
        else:
            key = [(i, i) for i in range(N)]
        for i, o in enumerate(ops):
            if npred[i] == 0:
                heapq.heappush(ready[o.eng], (key[i], i))
        events = []
        now = 0.0
        order = []
        cur_tbl = None
        while True:
            for e in ENG:
                if ready[e] and busy[e] <= now + 1e-9:
                    extra = 0.0
                    if e == "act":
                        cand = [heapq.heappop(ready[e]) for _ in range(min(5, len(ready[e])))]
                        pick = 0
                        for ci, (_, ii) in enumerate(cand):
                            tb = getattr(ops[ii], "tbl", None)
                            if tb is None or tb == cur_tbl:
                                pick = ci
                                break
                        i = cand[pick][1]
                        for ci, c_ in enumerate(cand):
                            if ci != pick:
                                heapq.heappush(ready[e], c_)
                        tb = getattr(ops[i], "tbl", None)
                        if tb is not None and tb != cur_tbl:
                            extra = 1.3
                            cur_tbl = tb
                    else:
                        i = heapq.heappop(ready[e])[1]
                    o = ops[i]
                    order.append(i)
                    o.start = now
                    busy[e] = now + o.occ + extra
                    heapq.heappush(events, (now + o.lat + extra, 1, i))
                    heapq.heappush(events, (now + o.occ + extra, 0, i))
            if not events:
                break
            t, typ, i = heapq.heappop(events)
            now = t
            if typ == 1:
                for s_ in ops[i].succs:
                    npred[s_] -= 1
                    if npred[s_] == 0:
                        heapq.heappush(ready[ops[s_].eng], (key[s_], s_))
        assert len(order) == len(ops), (len(order), len(ops))
        self.sim_time = now
        return order

    def _deps(self, r, w):
        deps = []
        for b in r:
            if b.last_write is not None:
                deps.append(b.last_write)
        for b in w:
            if b.last_write is not None:
                deps.append(b.last_write)
            deps.extend(b.reads)
        return deps

    def _waits(self, eng, deps):
        need = {}
        wd = self.waited[eng]
        for (k, v) in deps:
            if wd.get(k, 0) >= v:
                continue
            if need.get(k, 0) < v:
                need[k] = v
        for k, v in need.items():
            wd[k] = v
        return [(self.sem(k), v) for k, v in need.items()]

    def _commit(self, r, w, tok):
        for b in w:
            b.last_write = tok
            b.reads = []
        for b in r:
            if b in w:
                continue
            b.reads = [x for x in b.reads if x[0] != tok[0]] + [tok]

    def _replay_one(self, o):
        eng = o.eng
        deps = self._deps(o.r, o.w)
        if o.kind == "c":
            if eng == "pe":
                deps = [d for d in deps if not (d[0][0] == "e" and d[0][1] == "pe")]
            waits = self._waits(eng, deps)
            if self.cnt[eng] >= EPOCH:
                self.epoch[eng] += 1
                self.cnt[eng] = 0
            self.cnt[eng] += 1
            key = ("e", eng, self.epoch[eng])
            tok = (key, self.cnt[eng])
            self.ops[eng].append((o.fn, waits, self.sem(key), 1))
        else:
            waits = self._waits(eng, deps)
            sb_ = o.sem_buf
            key = ("d", id(sb_))
            inc = getattr(o, "inc", 16)
            sb_.dma_cnt += inc
            tok = (key, sb_.dma_cnt)
            self.ops[eng].append((o.fn, waits, self.sem(key), inc))
        self._commit(o.r, o.w, tok)
        return tok

    def emit(self):
        nc = self.nc
        order = self._schedule()
        toks = {}
        for i in order:
            toks[i] = self._replay_one(self.rec[i])
        for eng, idxs in self.final_waits:
            waits = self._waits(eng, [toks[i] for i in idxs])
            self.ops[eng].append((None, waits, None, 0))

        def replay(engobj, lst):
            for (fn, waits, s, inc) in lst:
                for (ws, v) in waits:
                    engobj.wait_ge(ws, v)
                if fn is not None:
                    fn(engobj).then_inc(s, inc)

        with nc.Block() as block:
            @block.tensor
            def _(e):
                replay(e, self.ops["pe"])

            @block.scalar
            def _(e):
                replay(e, self.ops["act"])

            @block.vector
            def _(e):
                replay(e, self.ops["dve"])

            @block.gpsimd
            def _(e):
                replay(e, self.ops["pool"])

            @block.sync
            def _(e):
                replay(e, self.ops["sp"])

    def close(self):
        self.stack.close()


def _pvec_layout():
    lay = {}
    col = 0

    def add(name, n):
        nonlocal col
        lay[name] = col
        col += n

    for l in range(NL):
        add("n1w%d" % l, 8)
        add("n2w%d" % l, 8)
        for k in range(4):
            add("lcw%d_%d" % (l, k), 4)
        add("lcb%d" % l, 4)
        add("ba%d" % l, 4)
        add("bx%d" % l, 4)
        add("lam%d" % l, 4)
        add("rnw%d" % l, 2)
        add("hnw%d" % l, 2)
        for k in range(3):
            add("fcw%d_%d" % (l, k), NFC)
        add("fcb%d" % l, NFC)
    add("fnw", 8)
    add("hbA", 2)
    add("hbB", 2)
    add("m", 1)
    add("om", 1)
    add("keep", 1)
    add("lbm", 1)
    add("eps", 1)
    add("one", 1)
    lay["_n"] = col
    return lay


PV = _pvec_layout()
RET_GAMMA = [1.0 - 2.0 ** (-5.0 - h) for h in range(4)]


def build(NT, nlayers=NL, debug_out=None):
    NS = NT + SKEW
    S = NS * T
    nc = bass.Bass("TRN2", target_bir_lowering=False)
    xT_d = nc.dram_tensor("xT", [D, S], F32, kind="ExternalInput").ap()
    win_d = nc.dram_tensor("win", [NL, 128, 8, NCOL_IN], F32, kind="ExternalInput").ap()
    wout_d = nc.dram_tensor("wout", [NL, 128, 8, D], F32, kind="ExternalInput").ap()
    wup_d = nc.dram_tensor("wup", [NL, 128, 8, 2 * DFF], F32, kind="ExternalInput").ap()
    wdn_d = nc.dram_tensor("wdn", [NL, 8, 2, 128, 11, 128], F32, kind="ExternalInput").ap()
    gw_d = nc.dram_tensor("gatew", [NL, 128, 8, 128], F32, kind="ExternalInput").ap()
    pv_d = nc.dram_tensor("pvec", [128, PV["_n"]], F32, kind="ExternalInput").ap()
    tab_d = nc.dram_tensor("tabs", [128, 8, S], F32, kind="ExternalInput").ap()
    cst_d = nc.dram_tensor("cst", [128, 6, 128], F32, kind="ExternalInput").ap()
    rm_d = nc.dram_tensor("rmask", [128, T], F32, kind="ExternalInput").ap()
    yT_d = nc.dram_tensor("yT", [D, S], F32, kind="ExternalOutput").ap()
    snd_t = nc.dram_tensor("snd", [D, T], F32)
    rcv_t = [nc.dram_tensor("rcv%d" % i, [2 * D, T], F32) for i in range(2)]
    GROUPS = [[0, 1], [2, 3], [4, 5], [6, 7]]

    sch = Sched(nc)
    sb, ps = sch.sb, sch.ps
    op = sch.op

    pv = sb("pv", [128, PV["_n"]])
    sch.dma("sp", pv[:, :], pv_d[:, :], w=[pv], nbytes=1 << 18)
    cstf = sb("cstf", [128, 6, 128])
    sch.dma("sp", cstf[:, :, :], cst_d[:, :, :], w=[cstf], nbytes=1 << 18)
    cst = sb("cstb", [128, 6, 128], BF16)
    op("dve", lambda e: e.tensor_copy(out=cst[:, :, :], in_=cstf[:, :, :]), r=[cstf], w=[cst], n=768)
    ONES, IDENT, BD64, MASK128, MASK32, PSWAP = range(6)
    rmask = sb("rmask_s", [128, T])
    sch.dma("sp", rmask[:, :], rm_d[:, :], w=[rmask], nbytes=1 << 18)
    gwf = sb("gwf", [128, NL * 8, 128])
    gwb = sb("gwb", [128, NL * 8, 128], BF16)
    for l in range(NL):
        sch.dma("sp", gwf[:, l * 8:(l + 1) * 8, :], gw_d[l], w=[gwf], nbytes=1 << 19)
    op("dve", lambda e: e.tensor_copy(out=gwb[:, :, :], in_=gwf[:, :, :]), r=[gwf], w=[gwb], n=2048)

    def pcol(name, c=0, lo=0, hi=128):
        k = PV[name] + c
        return pv[lo:hi, k:k + 1]

    der = sb("der", [128, 32])
    for l in range(NL):
        k = PV["lam%d" % l]
        op("act", lambda e, k=k, l=l: e.activation(out=der[:, l * 4:l * 4 + 4], in_=pv[:, k:k + 4], func=AF.Exp, scale=-1.0), r=[pv], w=[der], n=4, tbl="exp")
        op("act", lambda e, l=l: e.activation(out=der[:, l * 4:l * 4 + 4], in_=der[:, l * 4:l * 4 + 4], func=AF.Ln, bias=pcol("one"), scale=1.0), r=[der, pv], w=[der], n=4, tbl="exp")
        op("dve", lambda e, l=l: e.tensor_scalar(out=der[:, l * 4:l * 4 + 4], in0=der[:, l * 4:l * 4 + 4], scalar1=-8.0, scalar2=None, op0=ALU.mult), r=[der], w=[der], n=4)
    LB, OML, NOML = 8, 12, 16
    kA, kB = PV["hbA"], PV["hbB"]
    op("dve", lambda e: e.tensor_tensor(out=der[:, 20:22], in0=pv[:, kB:kB + 2], in1=pv[:, kA:kA + 2], op=ALU.subtract), r=[pv, der], w=[der], n=2)
    op("act", lambda e: e.activation(out=der[:, 22:24], in_=der[:, 20:22], func=AF.Sigmoid), r=[der], w=[der], n=2, tbl="sig")
    op("dve", lambda e: e.tensor_scalar(out=der[:, LB:LB + 2], in0=der[:, 22:24], scalar1=pcol("lbm"), scalar2=None, op0=ALU.mult), r=[pv], w=[der], n=2)
    op("dve", lambda e: e.tensor_scalar(out=der[:, OML:OML + 2], in0=der[:, LB:LB + 2], scalar1=-1.0, scalar2=1.0, op0=ALU.mult, op1=ALU.add), r=[], w=[der], n=2)
    op("dve", lambda e: e.tensor_scalar(out=der[:, NOML:NOML + 2], in0=der[:, OML:OML + 2], scalar1=-1.0, scalar2=None, op0=ALU.mult), r=[], w=[der], n=2)

    def dcol(base, l, c, lo=0, hi=128):
        k = base + l * (4 if base == 0 else 2) + c
        return der[lo:hi, k:k + 1]

    hst_t = [sb("hst%d" % l, [128, 4]) for l in range(NL)]
    lxh_t = [sb("lxh%d" % l, [128, 4, 4]) for l in range(NL)]
    fh_t = [sb("fh%d" % l, [128, NFC, 2]) for l in range(NL)]
    hst = {}
    lxh = {}
    fh = {}
    st_f = {}
    st_b = {}
    for l in range(NL):
        for c in range(4):
            hst[(l, c)] = hst_t[l].sub("hst%d_%d" % (l, c))
            lxh[(l, c)] = lxh_t[l].sub("lxh%d_%d" % (l, c))
        for ch in range(NFC):
            fh[(l, ch)] = fh_t[l].sub("fh%d_%d" % (l, ch))
        op("dve", lambda e, l=l: e.memset(hst_t[l][:, :], 0.0), w=[hst[(l, c)] for c in range(4)], n=4)
        op("dve", lambda e, l=l: e.memset(lxh_t[l][:, :, :], 0.0), w=[lxh[(l, c)] for c in range(4)], n=16)
        op("dve", lambda e, l=l: e.memset(fh_t[l][:, :, :], 0.0), w=[fh[(l, ch)] for ch in range(NFC)], n=88)
        for mix in ("r", "h"):
            for p in range(2):
                tf = sb("stf_%s%d%d" % (mix, l, p), [128, 64])
                tb = sb("stb_%s%d%d" % (mix, l, p), [128, 64], BF16)
                for hh in range(2):
                    bf_ = tf.sub("stf_%s%d%d%d" % (mix, l, p, hh))
                    bb_ = tb.sub("stb_%s%d%d%d" % (mix, l, p, hh))
                    st_f[(mix, l, p, hh)] = bf_
                    st_b[(mix, l, p, hh)] = bb_
                    lo = hh * 64
                    op("dve", lambda e, bf_=bf_, lo=lo: e.memset(bf_[lo:lo + 64, :], 0.0), w=[bf_], n=64)
                    op("dve", lambda e, bb_=bb_, lo=lo: e.memset(bb_[lo:lo + 64, :], 0.0), w=[bb_], n=64)

    xres = [sb("xres%d" % i, [128, 8, T]) for i in range(2)]
    rv = sb("rv", [128, 8, T])
    tabs = sb("tabs_s", [128, 8, T])
    hT = sb("hT", [128, 8, T], BF16)
    yT = sb("yTs", [128, 8, T], BF16)
    yTc = [yT.sub("yT%d" % c) for c in range(8)]
    gT = sb("gT", [128, 22, T], BF16)
    gTc = [gT.sub("gT%d" % c) for c in range(22)]
    NSLOT = 5
    wsl = [sb("wsl%d" % i, [128, 4096], BF16) for i in range(NSLOT)]

    class FB:
        def __init__(self, name):
            self.b = sb(name, [128, T + 4])
            self.h = self.b.sub(name + "_h")

        @property
        def d(self):
            return self.b[:, 4:4 + T]

    NF = 14
    Fring = [FB("F%d" % i) for i in range(NF)]
    fstate = {"f": 0, "b": 0, "a": 0, "x": 0, "at": 0}

    def getF():
        f = Fring[fstate["f"] % NF]
        fstate["f"] += 1
        return f

    NB = 8
    Bring = [sb("B%d" % i, [128, T], BF16) for i in range(NB)]

    def getB():
        b = Bring[fstate["b"] % NB]
        fstate["b"] += 1
        return b

    vtok = sb("vtok", [128, 8, 256], BF16)
    ktok = sb("ktok", [128, 8, 2, 128], BF16)
    NAT = 4
    ATb = [sb("AT%d" % i, [128, 128], BF16) for i in range(NAT)]
    ebl = [sb("ebl%d" % p, [128, 16]) for p in range(2)]
    stmp = [sb("stmp%d" % i, [128, 64]) for i in range(4)]

    banks = [ps("bank%d" % i, [128, T]) for i in range(8)]
    for i_, b_ in enumerate(banks):
        b_.touch = -100 + i_
        b_.held = False

    def getA(hold=False):
        a = min((b for b in banks if not b.held), key=lambda b: b.touch)
        a.touch = len(sch.rec)
        a.held = hold
        return a

    wstate = {"n": 0}

    def wload(src_ap, shape):
        s = wsl[wstate["n"] % NSLOT]
        wstate["n"] += 1
        n = 1
        for d_ in shape[1:]:
            n *= d_
        view = s[:, 0:n].rearrange("p (a b) -> p a b", b=shape[2])
        sch.dma("pool", view, src_ap, w=[s], nbytes=128 * n * 4)
        return s, view

    def proj(slot, wview, col, rhs_bufs, rhs_of_kc, nk=8, m=128, ncols=T):
        pbuf = getA()

        def fn(e):
            ins = None
            for kc in range(nk):
                ins = e.matmul(pbuf[0:m, 0:ncols], lhsT=wview[:, kc, col:col + m], rhs=rhs_of_kc(kc), start=(kc == 0), stop=(kc == nk - 1))
            return ins

        op("pe", fn, r=[slot] + list(rhs_bufs), w=[pbuf], n=ncols, nmm=nk)
        return pbuf

    def sq_chunk(xr, m):
        op("act", lambda e, m=m: e.activation(out=gT[:, m, :], in_=xr[:, m, :], func=AF.Square), r=[xr], w=[gTc[m]])

    def rms_rstd(xr, use_hT=False, presq=False):
        scr = hT if use_hT else gT
        scr_b = [hT] if use_hT else gTc[0:8]
        if not presq:
            op("act", lambda e: e.activation(out=scr[:, 0:8, :], in_=xr[:, :, :], func=AF.Square), r=[xr], w=scr_b, n=8 * T)
        pS_ = getA()

        def fn(e):
            ins = None
            for c in range(8):
                ins = e.matmul(pS_[:, :], lhsT=cst[:, ONES, :], rhs=scr[:, c, :], start=(c == 0), stop=(c == 7))
            return ins

        op("pe", fn, r=[cst] + scr_b, w=[pS_], nmm=8)
        op("act", lambda e: e.activation(out=pS_[:, :], in_=pS_[:, :], func=AF.Ln, bias=pcol("eps"), scale=1.0 / D), r=[pv], w=[pS_], tbl="exp")
        op("act", lambda e: e.activation(out=pS_[:, :], in_=pS_[:, :], func=AF.Exp, scale=-0.5), r=[], w=[pS_], tbl="exp")
        pS_.held = True
        return pS_

    def norm_to_hT(xr, nwname, use_hT=False, presq=False):
        rs = rms_rstd(xr, use_hT, presq)
        for c in range(8):
            op("dve", lambda e, c=c: e.scalar_tensor_tensor(out=hT[:, c, :], in0=xr[:, c, :], scalar=pcol(nwname, c), in1=rs[:, :],
                                                              op0=ALU.mult, op1=ALU.mult), r=[xr, pv], w=[hT, rs])
        rs.held = False

    hk = lambda kc: hT[:, kc, :]

    def emit_layer(l, xr, ti):
        norm_to_hT(xr, "n1w%d" % l, use_hT=True)

        s0, w0 = wload(win_d[l, :, :, 0:512], [128, 8, 512])
        s1, w1 = wload(win_d[l, :, :, 512:1024], [128, 8, 512])
        for c in range(4):
            lxb, xc, rr, ii, aa, mm = getF(), getF(), getF(), getF(), getF(), getF()
            xcb = getB()
            pa = proj(s0, w0, c * 128, [hT], hk)
            op("pool", lambda e, c=c, lxb=lxb: e.tensor_copy(out=lxb.b[:, 1:4], in_=lxh_t[l][:, c, 0:3]), r=[lxh[(l, c)]], w=[lxb.h], n=3)
            op("act", lambda e, lxb=lxb, pa=pa: e.activation(out=lxb.d, in_=pa[:, :], func=AF.Copy), r=[], w=[lxb.b, pa])
            op("pool", lambda e, c=c, lxb=lxb: e.tensor_copy(out=lxh_t[l][:, c, 0:3], in_=lxb.b[:, T + 1:T + 4]), r=[lxb.b], w=[lxh[(l, c)]], n=3)
            op("dve", lambda e, c=c, lxb=lxb, xc=xc: e.tensor_scalar(out=xc.d, in0=lxb.b[:, 1:1 + T], scalar1=pcol("lcw%d_0" % l, c), scalar2=pcol("lcb%d" % l, c),
                                                                      op0=ALU.mult, op1=ALU.add), r=[lxb.b, lxb.h, pv], w=[xc.b])
            for k in range(1, 4):
                op("dve", lambda e, c=c, k=k, lxb=lxb, xc=xc: e.scalar_tensor_tensor(out=xc.d, in0=lxb.b[:, 1 + k:1 + k + T], scalar=pcol("lcw%d_%d" % (l, k), c), in1=xc.d,
                                                                                      op0=ALU.mult, op1=ALU.add), r=[lxb.b, lxb.h, pv], w=[xc.b], k=2)
            op("act", lambda e, xc=xc, xcb=xcb: e.activation(out=xcb[:, :], in_=xc.d, func=AF.Copy), r=[xc.b], w=[xcb])
            pg = getA()
            op("pe", lambda e, c=c, pg=pg, xcb=xcb: e.matmul(pg[:, :], lhsT=gwb[:, l * 8 + c * 2, :], rhs=xcb[:, :], start=True, stop=True), r=[gwb, xcb], w=[pg])
            op("act", lambda e, c=c, pg=pg, rr=rr: e.activation(out=rr.d, in_=pg[:, :], func=AF.Sigmoid, bias=pcol("ba%d" % l, c)), r=[pv], w=[rr.b, pg], tbl="sig")
            pg2 = getA()
            op("pe", lambda e, c=c, pg2=pg2, xcb=xcb: e.matmul(pg2[:, :], lhsT=gwb[:, l * 8 + c * 2 + 1, :], rhs=xcb[:, :], start=True, stop=True), r=[gwb, xcb], w=[pg2])
            op("act", lambda e, c=c, pg2=pg2, ii=ii: e.activation(out=ii.d, in_=pg2[:, :], func=AF.Sigmoid, bias=pcol("bx%d" % l, c)), r=[pv], w=[ii.b, pg2], tbl="sig")
            op("act", lambda e, c=c, rr=rr, aa=aa: e.activation(out=aa.d, in_=rr.d, func=AF.Exp, scale=dcol(0, l, c)), r=[rr.b, der], w=[aa.b], tbl="exp")
            op("act", lambda e, aa=aa, mm=mm: e.activation(out=mm.d, in_=aa.d, func=AF.Square), r=[aa.b], w=[mm.b])
            op("act", lambda e, mm=mm: e.activation(out=mm.d, in_=mm.d, func=AF.Sqrt, bias=pcol("one"), scale=-1.0), r=[pv], w=[mm.b], tbl="sqrt")
            op("pool", lambda e, ii=ii, xc=xc: e.tensor_tensor(out=ii.d, in0=ii.d, in1=xc.d, op=ALU.mult), r=[xc.b], w=[ii.b], k=2)
            op("pool", lambda e, ii=ii, mm=mm: e.tensor_tensor(out=ii.d, in0=ii.d, in1=mm.d, op=ALU.mult), r=[mm.b], w=[ii.b], k=2)
            hs = hst[(l, c)]
            op("dve", lambda e, c=c, rr=rr, aa=aa, ii=ii: e.tensor_tensor_scan(out=rr.d, data0=aa.d, data1=ii.d, initial=hst_t[l][:, c:c + 1],
                                                                                op0=ALU.mult, op1=ALU.add), r=[aa.b, ii.b, hs], w=[rr.b], k=2)
            op("dve", lambda e, c=c, rr=rr: e.tensor_copy(out=hst_t[l][:, c:c + 1], in_=rr.b[:, T + 3:T + 4]), r=[rr.b], w=[hs], n=1)
            pb = proj(s1, w1, c * 128, [hT], hk)
            op("act", lambda e, mm=mm, pb=pb: e.activation(out=mm.d, in_=pb[:, :], func=AF.Gelu_apprx_tanh), r=[], w=[mm.b, pb], tbl="gelu")
            op("dve", lambda e, c=c, rr=rr, mm=mm: e.tensor_tensor(out=yT[:, c, :], in0=rr.d, in1=mm.d, op=ALU.mult), r=[rr.b, mm.b], w=[yTc[c]], k=2)

        def gla(mix, qT, kT, C, BLK, sgate, nwname, ycol0, decay_const):
            nblk = T // BLK
            nch = T // C
            msk = MASK128 if C == 128 else MASK32
            pO = [getA(True), getA(True)]
            pX = [getA(True), getA(True)]
            pUT = getA(True)
            pTr = pUT[:, 256:512].bitcast(BF16)
            for p in range(2):
                for b_ in range(nblk):
                    op("pe", lambda e, p=p, b_=b_: e.transpose(pTr[0:BLK, 0:128], kT[p][:, b_ * BLK:(b_ + 1) * BLK], cst[:, IDENT, :]), r=[kT[p], cst], w=[pUT], n=128)
                    op("act", lambda e, p=p, b_=b_: e.activation(out=ktok[0:BLK, b_, p, :], in_=pTr[0:BLK, 0:128], func=AF.Copy), r=[], w=[ktok, pUT], n=128)
            for j in range(nch):
                blk = (j * C) // BLK
                base = (j * C) % BLK
                tsl = slice(j * C, (j + 1) * C)
                for p in range(2):
                    for hh in range(2):
                        h = 2 * p + hh
                        off = hh * 64
                        px = pX[fstate["x"] % 2]
                        fstate["x"] += 1
                        at = ATb[fstate["at"] % NAT]
                        fstate["at"] += 1
                        Sb = st_b[(mix, l, p, hh)]
                        op("pe", lambda e, p=p, off=off, tsl=tsl, px=px, base=base: e.matmul(
                            px[base:base + C, 0:C], lhsT=kT[p][off:off + 64, tsl], rhs=qT[p][off:off + 64, tsl], start=True, stop=True),
                            r=[kT[p], qT[p]], w=[px], n=C)
                        op("dve", lambda e, px=px, at=at, base=base: e.tensor_tensor(
                            out=at[base:base + C, 0:C], in0=px[base:base + C, 0:C], in1=cst[base:base + C, msk, 0:C], op=ALU.mult),
                            r=[cst], w=[at, px], n=C)

                        def fn_o(e, p=p, off=off, tsl=tsl, at=at, base=base, blk=blk, h=h, Sb=Sb):
                            e.matmul(pO[p][off:off + 64, tsl], lhsT=vtok[base:base + C, blk, h * 64:(h + 1) * 64], rhs=at[base:base + C, 0:C], start=True, stop=False)
                            return e.matmul(pO[p][off:off + 64, tsl], lhsT=Sb[off:off + 64, :], rhs=qT[p][off:off + 64, tsl], start=False, stop=True)

                        op("pe", fn_o, r=[vtok, at, Sb, qT[p]], w=[pO[p]], n=C, nmm=2)
                    op("pe", lambda e, p=p, base=base, blk=blk: e.matmul(
                        pUT[:, 0:128], lhsT=ktok[base:base + C, blk, p, :], rhs=vtok[base:base + C, blk, p * 128:(p + 1) * 128],
                        start=True, stop=True), r=[ktok, vtok], w=[pUT], n=128)
                    for hh in range(2):
                        h = 2 * p + hh
                        off = hh * 64
                        Sf = st_f[(mix, l, p, hh)]
                        Sb = st_b[(mix, l, p, hh)]
                        if decay_const is not None:
                            g = decay_const[h]
                            op("dve", lambda e, Sf=Sf, off=off, g=g: e.scalar_tensor_tensor(
                                out=Sf[off:off + 64, :], in0=Sf[off:off + 64, :], scalar=g, in1=pUT[off:off + 64, off:off + 64], op0=ALU.mult, op1=ALU.add),
                                r=[], w=[Sf, pUT], n=64)
                            op("dve", lambda e, Sf=Sf, Sb=Sb, off=off, g=g: e.tensor_scalar(
                                out=Sb[off:off + 64, :], in0=Sf[off:off + 64, :], scalar1=g, scalar2=None, op0=ALU.mult), r=[Sf], w=[Sb], n=64)
                        else:
                            tmp = stmp[h]
                            op("dve", lambda e, Sf=Sf, off=off, tmp=tmp: e.tensor_tensor(
                                out=tmp[off:off + 64, :], in0=pUT[off:off + 64, off:off + 64], in1=Sf[off:off + 64, :], op=ALU.add), r=[Sf], w=[tmp, pUT], n=64)
                            op("dve", lambda e, Sf=Sf, off=off, tmp=tmp, p=p, j=j: e.tensor_scalar(
                                out=Sf[off:off + 64, :], in0=tmp[off:off + 64, :], scalar1=ebl[p][off:off + 64, j:j + 1], scalar2=None, op0=ALU.mult),
                                r=[tmp, ebl[p]], w=[Sf], n=64)
                            op("act", lambda e, Sf=Sf, Sb=Sb, off=off: e.activation(out=Sb[off:off + 64, :], in_=Sf[off:off + 64, :], func=AF.Copy), r=[Sf], w=[Sb], n=64)
            for p in range(2):
                o, rt = getF(), getF()
                osq = getB()
                op("act", lambda e, p=p, o=o: e.activation(out=o.d, in_=pO[p][:, :], func=AF.Copy), r=[], w=[o.b, pO[p]])
                op("act", lambda e, o=o, osq=osq: e.activation(out=osq[:, :], in_=o.d, func=AF.Square), r=[o.b], w=[osq])
                pS_ = getA()
                op("pe", lambda e, pS_=pS_, osq=osq: e.matmul(pS_[:, :], lhsT=cst[:, BD64, :], rhs=osq[:, :], start=True, stop=True), r=[cst, osq], w=[pS_])
                op("act", lambda e, pS_=pS_, rt=rt: e.activation(out=rt.d, in_=pS_[:, :], func=AF.Ln, bias=pcol("eps"), scale=1.0), r=[pv], w=[rt.b, pS_], tbl="exp")
                op("act", lambda e, rt=rt: e.activation(out=rt.d, in_=rt.d, func=AF.Exp, scale=-0.5), r=[], w=[rt.b], tbl="exp")
                op("dve", lambda e, p=p, o=o, rt=rt: e.scalar_tensor_tensor(out=o.d, in0=o.d, scalar=pcol(nwname, p), in1=rt.d, op0=ALU.mult, op1=ALU.mult),
                   r=[rt.b, pv], w=[o.b], k=2)
                op("pool", lambda e, p=p, o=o: e.tensor_tensor(out=yT[:, ycol0 + p, :], in0=o.d, in1=sgate[p].d, op=ALU.mult), r=[o.b, sgate[p].b], w=[yTc[ycol0 + p]], k=2)
            for b_ in pO + pX + [pUT]:
                b_.held = False

        s2, w2 = wload(win_d[l, :, :, 1024:1536], [128, 8, 512])
        s3, w3 = wload(win_d[l, :, :, 1536:2048], [128, 8, 512])
        if l == 0:
            sch.dma("sp", tabs[:, :, :], tab_d[:, :, ti * T:(ti + 1) * T], w=[tabs], nbytes=2 << 20)
        qT = [getB(), getB()]
        kT = [getB(), getB()]
        for dst, cbase, tb in ((qT, 0, 0), (kT, 256, 2)):
            for p in range(2):
                pa = proj(s2, w2, cbase + p * 128, [hT], hk)
                qb = getB()
                op("act", lambda e, qb=qb, pa=pa: e.activation(out=qb[:, :], in_=pa[:, :], func=AF.Copy), r=[], w=[qb, pa])
                pb = getA()
                op("pe", lambda e, pb=pb, qb=qb: e.matmul(pb[:, :], lhsT=cst[:, PSWAP, :], rhs=qb[:, :], start=True, stop=True), r=[cst, qb], w=[pb])
                t1, t2 = getF(), getF()
                op("dve", lambda e, p=p, tb=tb, t1=t1, pa=pa: e.tensor_tensor(out=t1.d, in0=pa[:, :], in1=tabs[:, p * 4 + tb, :], op=ALU.mult), r=[tabs], w=[t1.b, pa])
                op("dve", lambda e, p=p, tb=tb, t2=t2, pb=pb: e.tensor_tensor(out=t2.d, in0=pb[:, :], in1=tabs[:, p * 4 + tb + 1, :], op=ALU.mult), r=[tabs], w=[t2.b, pb])
                op("pool", lambda e, p=p, dst=dst, t1=t1, t2=t2: e.tensor_tensor(out=dst[p][:, :], in0=t1.d, in1=t2.d, op=ALU.add), r=[t1.b, t2.b], w=[dst[p]], k=2)
        for b_ in range(4):
            pg = getA()

            def fn_v(e, b_=b_, pg=pg):
                ins = None
                for kc in range(8):
                    ins = e.matmul(pg[:, 0:256], lhsT=hT[:, kc, b_ * 128:(b_ + 1) * 128], rhs=w3[:, kc, 0:256], start=(kc == 0), stop=(kc == 7))
                return ins

            op("pe", fn_v, r=[hT, s3], w=[pg], n=256, nmm=8)
            op("act", lambda e, b_=b_, pg=pg: e.activation(out=vtok[:, b_, :], in_=pg[:, 0:256], func=AF.Copy), r=[], w=[vtok, pg], n=256)
        sg = [getF(), getF()]
        for p in range(2):
            pa = proj(s3, w3, 256 + p * 128, [hT], hk)
            op("act", lambda e, p=p, pa=pa, sg=sg: e.activation(out=sg[p].d, in_=pa[:, :], func=AF.Silu), r=[], w=[sg[p].b, pa], tbl="silu")
        gla("r", qT, kT, 128, 128, sg, "rnw%d" % l, 4, [g ** 128 for g in RET_GAMMA])

        s4, w4 = wload(win_d[l, :, :, 2048:2560], [128, 8, 512])
        s5, w5 = wload(win_d[l, :, :, 2560:3072], [128, 8, 512])
        qT = [getB(), getB()]
        kT = [getB(), getB()]
        for p in range(2):
            sig, ff, bb, en = getF(), getF(), getF(), getF()
            pa = proj(s4, w4, 256 + p * 128, [hT], hk)
            op("act", lambda e, sig=sig, pa=pa: e.activation(out=sig.d, in_=pa[:, :], func=AF.Sigmoid), r=[], w=[sig.b, pa], tbl="sig")
            op("dve", lambda e, p=p, sig=sig, ff=ff: e.tensor_scalar(out=ff.d, in0=sig.d, scalar1=dcol(OML, l, p), scalar2=dcol(LB, l, p), op0=ALU.mult, op1=ALU.add),
               r=[sig.b, der], w=[ff.b])
            op("act", lambda e, ff=ff: e.activation(out=ff.d, in_=ff.d, func=AF.Ln), r=[], w=[ff.b], tbl="exp")
            op("dve", lambda e, ff=ff, bb=bb: e.tensor_tensor_scan(out=bb.d, data0=rmask[:, :], data1=ff.d, initial=0.0, op0=ALU.mult, op1=ALU.add),
               r=[rmask, ff.b], w=[bb.b], k=2)
            op("pool", lambda e, p=p, sig=sig: e.tensor_scalar(out=sig.d, in0=sig.d, scalar1=dcol(NOML, l, p), scalar2=dcol(OML, l, p), op0=ALU.mult, op1=ALU.add),
               r=[der], w=[sig.b])
            op("act", lambda e, bb=bb, en=en: e.activation(out=en.d, in_=bb.d, func=AF.Exp, scale=-1.0), r=[bb.b], w=[en.b], tbl="exp")
            op("dve", lambda e, p=p, sig=sig, en=en, kT=kT: e.tensor_tensor(out=kT[p][:, :], in0=sig.d, in1=en.d, op=ALU.mult), r=[sig.b, en.b], w=[kT[p]], k=2)
            op("act", lambda e, bb=bb, ff=ff: e.activation(out=ff.d, in_=bb.d, func=AF.Exp), r=[bb.b], w=[ff.b], tbl="exp")
            op("dve", lambda e, p=p, ff=ff: e.tensor_copy(out=ebl[p][:, :], in_=ff.d.rearrange("p (c t) -> p c t", t=32)[:, :, 31]), r=[ff.b], w=[ebl[p]], n=16)
            pb = proj(s4, w4, p * 128, [hT], hk)
            op("act", lambda e, en=en, pb=pb: e.activation(out=en.d, in_=pb[:, :], func=AF.Silu), r=[], w=[en.b, pb], tbl="silu")
            op("dve", lambda e, p=p, en=en, ff=ff, qT=qT: e.tensor_tensor(out=qT[p][:, :], in0=en.d, in1=ff.d, op=ALU.mult), r=[en.b, ff.b], w=[qT[p]], k=2)
        for b_ in range(8):
            pg = getA()

            def fn_v2(e, b_=b_, pg=pg):
                ins = None
                for kc in range(8):
                    ins = e.matmul(pg[0:64, 0:256], lhsT=hT[:, kc, b_ * 64:(b_ + 1) * 64], rhs=w5[:, kc, 0:256], start=(kc == 0), stop=(kc == 7))
                return ins

            op("pe", fn_v2, r=[hT, s5], w=[pg], n=256, nmm=8)
            op("act", lambda e, b_=b_, pg=pg: e.activation(out=vtok[0:64, b_, :], in_=pg[0:64, 0:256], func=AF.Copy), r=[], w=[vtok, pg], n=256)
        sg = [getF(), getF()]
        for p in range(2):
            pa = proj(s5, w5, 256 + p * 128, [hT], hk)
            op("act", lambda e, p=p, pa=pa, sg=sg: e.activation(out=sg[p].d, in_=pa[:, :], func=AF.Silu), r=[], w=[sg[p].b, pa], tbl="silu")
        gla("h", qT, kT, 32, 64, sg, "hnw%d" % l, 6, None)

        for half in range(2):
            so, wo = wload(wout_d[l, :, :, half * 512:(half + 1) * 512], [128, 8, 512])
            for mq in range(4):
                m = half * 4 + mq
                pb = proj(so, wo, mq * 128, yTc, lambda kc: yT[:, kc, :])
                op("dve", lambda e, m=m, pb=pb: e.tensor_tensor(out=xr[:, m, :], in0=pb[:, :], in1=xr[:, m, :], op=ALU.add), r=[], w=[xr, pb])
                sq_chunk(xr, m)

        def down_half(hf):
            for m in range(8):
                sd, wd = wload(wdn_d[l, m, hf], [128, 11, 128])
                pb = getA()

                def fn_d(e, pb=pb, wd=wd):
                    ins = None
                    for kc in range(11):
                        ins = e.matmul(pb[:, :], lhsT=wd[:, kc, :], rhs=gT[:, hf * 11 + kc, :], start=(kc == 0), stop=(kc == 10))
                    return ins

                op("pe", fn_d, r=[sd] + gTc[hf * 11:(hf + 1) * 11], w=[pb], nmm=11)
                op("dve", lambda e, m=m, pb=pb: e.tensor_tensor(out=xr[:, m, :], in0=pb[:, :], in1=xr[:, m, :], op=ALU.add), r=[], w=[xr, pb])
                if hf == 1:
                    sq_chunk(xr, m)

        norm_to_hT(xr, "n2w%d" % l, presq=True)
        for q in range(11):
            su, wu = wload(wup_d[l, :, :, q * 512:(q + 1) * 512], [128, 8, 512])
            for jj in range(2):
                j = 2 * q + jj
                outs = []
                for (half, col) in ((0, jj * 128), (1, 256 + jj * 128)):
                    ch = half * 22 + j
                    cb = getF()
                    pb = proj(su, wu, col, [hT], hk)
                    fhb = fh[(l, ch)]
                    op("act", lambda e, cb=cb, pb=pb, ch=ch: e.activation(out=cb.d, in_=pb[:, :], func=AF.Identity, bias=pcol("fcb%d" % l, ch), scale=pcol("fcw%d_2" % l, ch)),
                       r=[pv], w=[cb.b, pb])
                    op("dve", lambda e, cb=cb, pb=pb, ch=ch: e.scalar_tensor_tensor(out=cb.b[:, 5:4 + T], in0=pb[:, 0:T - 1], scalar=pcol("fcw%d_1" % l, ch), in1=cb.b[:, 5:4 + T],
                                                                                     op0=ALU.mult, op1=ALU.add), r=[pv], w=[cb.b, pb])
                    op("dve", lambda e, cb=cb, pb=pb, ch=ch: e.scalar_tensor_tensor(out=cb.b[:, 6:4 + T], in0=pb[:, 0:T - 2], scalar=pcol("fcw%d_0" % l, ch), in1=cb.b[:, 6:4 + T],
                                                                                     op0=ALU.mult, op1=ALU.add), r=[pv], w=[cb.b, pb])
                    op("dve", lambda e, cb=cb, ch=ch: e.scalar_tensor_tensor(out=cb.b[:, 4:5], in0=fh_t[l][:, ch, 1:2], scalar=pcol("fcw%d_1" % l, ch), in1=cb.b[:, 4:5],
                                                                              op0=ALU.mult, op1=ALU.add), r=[pv, fhb], w=[cb.b], n=1)
                    op("dve", lambda e, cb=cb, ch=ch: e.scalar_tensor_tensor(out=cb.b[:, 4:6], in0=fh_t[l][:, ch, 0:2], scalar=pcol("fcw%d_0" % l, ch), in1=cb.b[:, 4:6],
                                                                              op0=ALU.mult, op1=ALU.add), r=[pv, fhb], w=[cb.b], n=2)
                    op("act", lambda e, pb=pb, ch=ch: e.activation(out=fh_t[l][:, ch, :], in_=pb[:, T - 2:T], func=AF.Copy), r=[], w=[fhb, pb], n=2)
                    outs.append(cb)
                cg, cv = outs
                op("act", lambda e, cg=cg: e.activation(out=cg.d, in_=cg.d, func=AF.Silu), r=[], w=[cg.b], tbl="silu")
                op("pool", lambda e, cg=cg, cv=cv, j=j: e.tensor_tensor(out=gT[:, j, :], in0=cg.d, in1=cv.d, op=ALU.mult), r=[cg.b, cv.b], w=[gTc[j]], k=2)
            if q == 5:
                down_half(0)
        down_half(1)

    xv = xT_d.rearrange("(c p) s -> p c s", p=128)
    yv = yT_d.rearrange("(c p) s -> p c s", p=128)
    sndB = Buf(snd_t, "snd")
    rcvB = [Buf(rcv_t[i], "rcv%d" % i) for i in range(2)]
    snd_v = snd_t.ap().rearrange("(c p) s -> p c s", p=128)
    out_toks = []
    for st in range(NS):
        xr = xres[st % 2]
        sch.dma("sp", xr[:, :, :], xv[:, :, st * T:(st + 1) * T], w=[xr], nbytes=2 << 20)
        op("act", lambda e, xr=xr: e.activation(out=xr[:, :, :], in_=xr[:, :, :], func=AF.Copy, scale=pcol("m")), r=[pv], w=[xr], n=8 * T)
        if st >= SKEW:
            rb = rcvB[(st - SKEW) % 2]
            rcv_v = rcv_t[(st - SKEW) % 2].ap()[0:D, :].rearrange("(c p) s -> p c s", p=128)
            sch.dma("sp", rv[:, :, :], rcv_v, r=[rb], w=[rv], nbytes=2 << 20)
            op("dve", lambda e, xr=xr: e.scalar_tensor_tensor(out=xr[:, :, :], in0=rv[:, :, :], scalar=pcol("om"), in1=xr[:, :, :], op0=ALU.mult, op1=ALU.add),
               r=[rv, pv], w=[xr], n=8 * T, k=2)
        emit_layer(0, xr, st)
        if st < NT:
            sch.dma("sp", snd_v, xr[:, :, :], r=[xr], w=[sndB], nbytes=2 << 20)
            rb = rcvB[st % 2]
            sch.raw("pool", lambda e, rb=rb: e.collective_compute("AllGather", ALU.bypass, replica_groups=GROUPS,
                                                                   ins=[snd_t.ap().opt()], outs=[rb.t.ap().opt()]),
                    r=[sndB], w=[rb], sem_buf=rb, inc=1, occ=1.0, lat=120.0)
        if st == SKEW - 1:
            kp = pcol("keep")
            op("dve", lambda e: e.tensor_scalar(out=hst_t[0][:, :], in0=hst_t[0][:, :], scalar1=kp, scalar2=None, op0=ALU.mult), r=[pv], w=[hst[(0, c)] for c in range(4)], n=4)
            op("dve", lambda e: e.tensor_scalar(out=lxh_t[0][:, :, :], in0=lxh_t[0][:, :, :], scalar1=kp, scalar2=None, op0=ALU.mult), r=[pv], w=[lxh[(0, c)] for c in range(4)], n=16)
            op("dve", lambda e: e.tensor_scalar(out=fh_t[0][:, :, :], in0=fh_t[0][:, :, :], scalar1=kp, scalar2=None, op0=ALU.mult), r=[pv], w=[fh[(0, ch)] for ch in range(NFC)], n=88)
            for key in list(st_f.keys()):
                lo = key[3] * 64
                for tbl in (st_f, st_b):
                    bb_ = tbl[key]
                    op("dve", lambda e, bb_=bb_, lo=lo: e.tensor_scalar(out=bb_[lo:lo + 64, :], in0=bb_[lo:lo + 64, :], scalar1=pv[lo:lo + 64, PV["keep"]:PV["keep"] + 1],
                                                                          scalar2=None, op0=ALU.mult), r=[pv], w=[bb_], n=64)
        rs = rms_rstd(xr, presq=True)
        for c in range(8):
            op("dve", lambda e, c=c, xr=xr, rs=rs: e.scalar_tensor_tensor(out=xr[:, c, :], in0=xr[:, c, :], scalar=pcol("fnw", c), in1=rs[:, :],
                                                                           op0=ALU.mult, op1=ALU.mult), r=[pv], w=[xr, rs])
        rs.held = False
        out_toks.append(sch.dma("sp", yv[:, :, st * T:(st + 1) * T], xr[:, :, :], r=[xr], nbytes=2 << 20))
    sch.wait_all_on("sp", out_toks)
    sch.emit()
    return nc, sch


def _chunks(v):
    v = np.asarray(v, np.float32)
    return np.ascontiguousarray(v.reshape(-1, 128).T)


def make_tables(S):
    t = np.arange(S, dtype=np.float64)
    inv = 10000.0 ** (-np.arange(0, 64, 2, dtype=np.float64) / 64.0)
    ang = t[None, :] * inv[:, None]
    cos = np.concatenate([np.cos(ang), np.cos(ang)], 0)
    sin = np.concatenate([-np.sin(ang), np.sin(ang)], 0)
    n1 = (t % 128) + 1.0
    tabs = np.zeros((128, 8, S), np.float64)
    for p in range(2):
        for hh in range(2):
            h = 2 * p + hh
            lg = math.log(RET_GAMMA[h])
            qd = np.exp(n1 * lg)[None, :]
            kd = np.exp(-n1 * lg)[None, :] * (64.0 ** -0.5)
            sl = slice(hh * 64, hh * 64 + 64)
            tabs[sl, p * 4 + 0] = cos * qd
            tabs[sl, p * 4 + 1] = sin * qd
            tabs[sl, p * 4 + 2] = cos * kd
            tabs[sl, p * 4 + 3] = sin * kd
    return tabs.astype(np.float32)


def make_consts():
    cst = np.zeros((128, 6, 128), np.float32)
    for m_ in range(128):
        cst[(m_ // 64) * 64 + ((m_ % 64) + 32) % 64, 5, m_] = 1.0
    cst[:, 0, :] = 1.0
    cst[:, 1, :] = np.eye(128, dtype=np.float32)
    for b in range(2):
        cst[b * 64:(b + 1) * 64, 2, b * 64:(b + 1) * 64] = 1.0 / 64.0
    m = np.arange(128)[:, None]
    n = np.arange(128)[None, :]
    cst[:, 3, :] = (n >= m).astype(np.float32)
    m32 = (np.arange(32)[None, :] >= np.arange(32)[:, None]).astype(np.float32)
    for b in range(4):
        cst[b * 32:(b + 1) * 32, 4, 0:32] = m32
    rmask = np.ones((128, T), np.float32)
    rmask[:, ::32] = 0.0
    return cst, rmask


def prep_role(inp, role, NT):
    lyr = role
    NS = NT + SKEW
    w_in = np.asarray(inp["w_in"], np.float32)[lyr:lyr + 1]
    win = w_in.reshape(1, 8, 128, NCOL_IN).transpose(0, 2, 1, 3)
    wout = np.asarray(inp["w_out"], np.float32)[lyr:lyr + 1].reshape(1, 8, 128, D).transpose(0, 2, 1, 3)
    perm = []
    for q in range(11):
        perm += list(range(2 * q * 128, (2 * q + 2) * 128))
        perm += list(range(DFF + 2 * q * 128, DFF + (2 * q + 2) * 128))
    wup = np.asarray(inp["ffn_w_up"], np.float32)[lyr:lyr + 1][:, :, perm].reshape(1, 8, 128, 2 * DFF).transpose(0, 2, 1, 3)
    wdn = np.asarray(inp["ffn_w_down"], np.float32)[lyr:lyr + 1].reshape(1, 2, 11, 128, 8, 128).transpose(0, 4, 1, 3, 2, 5)
    gw = np.zeros((1, 128, 8, 128), np.float32)
    wa = np.asarray(inp["lru_wa"], np.float32)
    wx = np.asarray(inp["lru_wx"], np.float32)
    for c in range(4):
        for b2 in range(2):
            sl = slice(b2 * 64, b2 * 64 + 64)
            gw[0, sl, c * 2 + 0, sl] = wa[lyr, 2 * c + b2]
            gw[0, sl, c * 2 + 1, sl] = wx[lyr, 2 * c + b2]
    pvec = np.zeros((128, PV["_n"]), np.float32)

    def put(name, v):
        a = _chunks(v)
        pvec[:, PV[name]:PV[name] + a.shape[1]] = a

    put("n1w0", inp["norm1_w"][lyr])
    put("n2w0", inp["norm2_w"][lyr])
    for k in range(4):
        put("lcw0_%d" % k, inp["lru_conv_w"][lyr][k])
    put("lcb0", inp["lru_conv_b"][lyr])
    put("ba0", inp["lru_ba"][lyr])
    put("bx0", inp["lru_bx"][lyr])
    put("lam0", inp["lru_lambda"][lyr])
    put("rnw0", inp["ret_norm_w"][lyr])
    put("hnw0", inp["hg_norm_w"][lyr])
    put("hbA", inp["hg_lower_bounds"][0])
    put("hbB", inp["hg_lower_bounds"][1])
    for k in range(3):
        put("fcw0_%d" % k, inp["ffn_conv_w"][lyr][k])
    put("fcb0", inp["ffn_conv_b"][lyr])
    put("fnw", inp["final_norm_w"])
    pvec[:, PV["eps"]] = EPS
    pvec[:, PV["one"]] = 1.0
    pvec[:, PV["m"]] = 1.0 if role == 0 else 0.0
    pvec[:, PV["om"]] = 0.0 if role == 0 else 1.0
    pvec[:, PV["keep"]] = 1.0 if role == 0 else 0.0
    pvec[:, PV["lbm"]] = 0.0 if role == 0 else 1.0
    cst, rmask = make_consts()
    tb = make_tables(NT * T)
    tabs = np.zeros((128, 8, NS * T), np.float32)
    for st in range(NS):
        ti = st if role == 0 else st - SKEW
        if ti < 0 or ti >= NT:
            ti = 0
        tabs[:, :, st * T:(st + 1) * T] = tb[:, :, ti * T:(ti + 1) * T]
    return {
        "win": np.ascontiguousarray(win), "wout": np.ascontiguousarray(wout),
        "wup": np.ascontiguousarray(wup), "wdn": np.ascontiguousarray(wdn),
        "gatew": gw, "pvec": pvec, "tabs": tabs, "cst": cst, "rmask": rmask,
    }


_CACHE = {}


def kernel(**inputs):
    x = np.asarray(inputs["x"], np.float32)
    B, S, _ = x.shape
    NT = S // T
    NS = NT + SKEW
    roles = [prep_role(inputs, r, NT) for r in range(2)]
    if NT not in _CACHE:
        _CACHE[NT] = build(NT)
    nc, sch = _CACHE[NT]
    ncores = 2 * B
    zeros = np.zeros((D, NS * T), np.float32)
    in_maps = []
    for c in range(ncores):
        b, r = c // 2, c % 2
        m = dict(roles[r])
        if r == 0:
            xp = np.zeros((D, NS * T), np.float32)
            xp[:, :S] = x[b].T
            m["xT"] = xp
        else:
            m["xT"] = zeros
        in_maps.append(m)
    res = run_bass_kernel_spmd(nc, in_maps, core_ids=list(range(ncores)))
    out = np.empty((B, S, D), np.float32)
    for b in range(B):
        out[b] = res.results[2 * b + 1]["yT"][:, SKEW * T:].T
    return out
```

```python
import contextlib
import math
import numpy as np
import concourse.bass as bass
import concourse.mybir as mybir
from concourse.bass_utils import run_bass_kernel_spmd

F32 = mybir.dt.float32
BF16 = mybir.dt.bfloat16
AF = mybir.ActivationFunctionType
ALU = mybir.AluOpType

EPOCH = 4000
PRIO = "cp"

D = 1024
SEQ = 8192
BATCH = 4
DEPTH = 2
NL = 1
SKEW = 2
T = 512
DFF = 2816
NFC = 44
EPS = 1e-6
NCOL_IN = 3072


import heapq
import sys


class Buf:
    def __init__(self, t, name):
        self.t = t
        self.name = name
        self.last_write = None
        self.reads = []
        self.dma_cnt = 0
        self.g_lw = None
        self.g_rd = []
        self.held = False

    def __getitem__(self, key):
        return self.t[key]

    def sub(self, name):
        return Buf(self.t, name)


class _Op:
    __slots__ = ("eng", "fn", "r", "w", "occ", "lat", "kind", "sem_buf", "preds", "succs", "inc", "line", "start", "tbl")


class Sched:
    ENG = ("pe", "act", "dve", "pool", "sp")

    def __init__(self, nc):
        self.nc = nc
        self.stack = contextlib.ExitStack()
        self.rec = []
        self.ops = {e: [] for e in self.ENG}
        self.cnt = {e: 0 for e in self.ENG}
        self.epoch = {e: 0 for e in self.ENG}
        self.sems = {}
        self.waited = {e: {} for e in self.ENG}
        self.nsem = 0
        self.final_waits = []
        self.sim_time = 0.0

    def sem(self, key):
        if key not in self.sems:
            self.sems[key] = self.stack.enter_context(self.nc.semaphore("s%d" % self.nsem))
            self.nsem += 1
        return self.sems[key]

    def sb(self, name, shape, dtype=F32):
        t = self.stack.enter_context(self.nc.sbuf_tensor(name, list(shape), dtype))
        return Buf(t, name)

    def ps(self, name, shape, dtype=F32):
        t = self.stack.enter_context(self.nc.psum_tensor(name, list(shape), dtype))
        return Buf(t, name)

    def _record(self, o):
        i = len(self.rec)
        try:
            f = sys._getframe(2)
            while f.f_code.co_name in ("proj", "wload", "getA", "op", "dma", "raw"):
                f = f.f_back
            o.line = f.f_lineno
        except Exception:
            o.line = 0
        preds = set()
        for b in o.r:
            if b.g_lw is not None:
                preds.add(b.g_lw)
        for b in o.w:
            if b.g_lw is not None:
                preds.add(b.g_lw)
            preds.update(b.g_rd)
        preds.discard(i)
        o.preds = preds
        o.succs = []
        for p in preds:
            self.rec[p].succs.append(i)
        for b in o.w:
            b.g_lw = i
            b.g_rd = []
            if hasattr(b, "touch"):
                b.touch = i
        for b in o.r:
            if b not in o.w:
                b.g_rd.append(i)
        self.rec.append(o)
        return i

    def op(self, eng, fn, r=(), w=(), n=T, k=1.0, nmm=1, tbl=None):
        o = _Op()
        o.tbl = tbl
        o.eng, o.fn, o.r, o.w, o.kind, o.sem_buf = eng, fn, tuple(r), tuple(w), "c", None
        if eng == "pe":
            o.occ = nmm * (0.19 + 0.0002 * n)
        elif eng == "act":
            o.occ = 0.22 + 0.00072 * n
        elif eng == "dve":
            o.occ = 0.07 + 0.00105 * n * k
        else:
            o.occ = 0.4 + 0.002 * n
        o.lat = o.occ + (0.1 if eng == "pe" else 0.3)
        return self._record(o)

    def dma(self, q, out_ap, in_ap, r=(), w=(), sem_buf=None, nbytes=1 << 20):
        o = _Op()
        o.eng, o.r, o.w, o.kind = q, tuple(r), tuple(w), "d"
        o.sem_buf = sem_buf if sem_buf is not None else (w[0] if len(w) else r[0])

        def fn(e, out_ap=out_ap, in_ap=in_ap):
            return e.dma_start(out=out_ap, in_=in_ap)

        o.fn = fn
        o.occ = 1.0 if q == "pool" else 0.15
        o.lat = o.occ + 2.0 + nbytes / 150e3
        o.inc = 16
        return self._record(o)

    def raw(self, q, fn, r=(), w=(), sem_buf=None, inc=1, occ=1.0, lat=50.0):
        o = _Op()
        o.eng, o.r, o.w, o.kind = q, tuple(r), tuple(w), "d"
        o.sem_buf = sem_buf
        o.fn = fn
        o.occ, o.lat = occ, lat
        o.inc = inc
        return self._record(o)

    def wait_all_on(self, eng, idxs):
        self.final_waits.append((eng, list(idxs)))

    def _schedule(self):
        ops = self.rec
        ENG = self.ENG
        ready = {e: [] for e in ENG}
        busy = {e: 0.0 for e in ENG}
        npred = [len(o.preds) for o in ops]
        N = len(ops)
        tail = [0.0] * N
        for i in range(N - 1, -1, -1):
            o = ops[i]
            t = 0.0
            for s_ in o.succs:
                if tail[s_] > t:
                    t = tail[s_]
            tail[i] = t + o.lat
        if PRIO == "cp":
            key = [(-tail[i], i) for i in range(N)]
        else:
            key = [(i, i) for i in range(N)]
        for i, o in enumerate(ops):
            if npred[i] == 0:
                heapq.heappush(ready[o.eng], (key[i], i))
        events = []
        now = 0.0
        order = []
        cur_tbl = None
        while True:
            for e in ENG:
                if ready[e] and busy[e] <= now + 1e-9:
                    extra = 0.0
                    if e == "act":
                        cand = [heapq.heappop(ready[e]) for _ in range(min(5, len(ready[e])))]
                        pick = 0
                        for ci, (_, ii) in enumerate(cand):
                            tb = getattr(ops[ii], "tbl", None)
                            if tb is None or tb == cur_tbl:
                                pick = ci
                                break
                        i = cand[pick][1]
                        for ci, c_ in enumerate(cand):
                            if ci != pick:
                                heapq.heappush(ready[e], c_)
                        tb = getattr(ops[i], "tbl", None)
                        if tb is not None and tb != cur_tbl:
                            extra = 1.3
                            cur_tbl = tb
                    else:
                        i = heapq.heappop(ready[e])[1]
                    o = ops[i]
                    order.append(i)
                    o.start = now
                    busy[e] = now + o.occ + extra
                    heapq.heappush(events, (now + o.lat + extra, 1, i))
                    heapq.heappush(events, (now + o.occ + extra, 0, i))
            if not events:
                break
            t, typ, i = heapq.heappop(events)
            now = t
            if typ == 1:
                for s_ in ops[i].succs:
                    npred[s_] -= 1
                    if npred[s_] == 0:
                        heapq.heappush(ready[ops[s_].eng], (key[s_], s_))
        assert len(order) == len(ops), (len(order), len(ops))
        self.sim_time = now
        return order

    def _deps(self, r, w):
        deps = []
        for b in r:
            if b.last_write is not None:
                deps.append(b.last_write)
        for b in w:
            if b.last_write is not None:
                deps.append(b.last_write)
            deps.extend(b.reads)
        return deps

    def _waits(self, eng, deps):
        need = {}
        wd = self.waited[eng]
        for (k, v) in deps:
            if wd.get(k, 0) >= v:
                continue
            if need.get(k, 0) < v:
                need[k] = v
        for k, v in need.items():
            wd[k] = v
        return [(self.sem(k), v) for k, v in need.items()]

    def _commit(self, r, w, tok):
        for b in w:
            b.last_write = tok
            b.reads = []
        for b in r:
            if b in w:
                continue
            b.reads = [x for x in b.reads if x[0] != tok[0]] + [tok]

    def _replay_one(self, o):
        eng = o.eng
        deps = self._deps(o.r, o.w)
        if o.kind == "c":
            if eng == "pe":
                deps = [d for d in deps if not (d[0][0] == "e" and d[0][1] == "pe")]
            waits = self._waits(eng, deps)
            if self.cnt[eng] >= EPOCH:
                self.epoch[eng] += 1
                self.cnt[eng] = 0
            self.cnt[eng] += 1
            key = ("e", eng, self.epoch[eng])
            tok = (key, self.cnt[eng])
            self.ops[eng].append((o.fn, waits, self.sem(key), 1))
        else:
            waits = self._waits(eng, deps)
            sb_ = o.sem_buf
            key = ("d", id(sb_))
            inc = getattr(o, "inc", 16)
            sb_.dma_cnt += inc
            tok = (key, sb_.dma_cnt)
            self.ops[eng].append((o.fn, waits, self.sem(key), inc))
        self._commit(o.r, o.w, tok)
        return tok

    def emit(self):
        nc = self.nc
        order = self._schedule()
        toks = {}
        for i in order:
            toks[i] = self._replay_one(self.rec[i])
        for eng, idxs in self.final_waits:
            waits = self._waits(eng, [toks[i] for i in idxs])
            self.ops[eng].append((None, waits, None, 0))

        def replay(engobj, lst):
            for (fn, waits, s, inc) in lst:
                for (ws, v) in waits:
                    engobj.wait_ge(ws, v)
                if fn is not None:
                    fn(engobj).then_inc(s, inc)

        with nc.Block() as block:
            @block.tensor
            def _(e):
                replay(e, self.ops["pe"])

            @block.scalar
            def _(e):
                replay(e, self.ops["act"])

            @block.vector
            def _(e):
                replay(e, self.ops["dve"])

            @block.gpsimd
            def _(e):
                replay(e, self.ops["pool"])

            @block.sync
            def _(e):
                replay(e, self.ops["sp"])

    def close(self):
        self.stack.close()


def _pvec_layout():
    lay = {}
    col = 0

    def add(name, n):
        nonlocal col
        lay[name] = col
        col += n

    for l in range(NL):
        add("n1w%d" % l, 8)
        add("n2w%d" % l, 8)
        for k in range(4):
            add("lcw%d_%d" % (l, k), 4)
        add("lcb%d" % l, 4)
        add("ba%d" % l, 4)
        add("bx%d" % l, 4)
        add("lam%d" % l, 4)
        add("rnw%d" % l, 2)
        add("hnw%d" % l, 2)
        for k in range(3):
            add("fcw%d_%d" % (l, k), NFC)
        add("fcb%d" % l, NFC)
    add("fnw", 8)
    add("hbA", 2)
    add("hbB", 2)
    add("m", 1)
    add("om", 1)
    add("keep", 1)
    add("lbm", 1)
    add("eps", 1)
    add("one", 1)
    lay["_n"] = col
    return lay


PV = _pvec_layout()
RET_GAMMA = [1.0 - 2.0 ** (-5.0 - h) for h in range(4)]


def build(NT, nlayers=NL, debug_out=None):
    NS = NT + SKEW
    S = NS * T
    nc = bass.Bass("TRN2", target_bir_lowering=False)
    xT_d = nc.dram_tensor("xT", [D, S], F32, kind="ExternalInput").ap()
    win_d = nc.dram_tensor("win", [NL, 128, 8, NCOL_IN], F32, kind="ExternalInput").ap()
    wout_d = nc.dram_tensor("wout", [NL, 128, 8, D], F32, kind="ExternalInput").ap()
    wup_d = nc.dram_tensor("wup", [NL, 128, 8, 2 * DFF], F32, kind="ExternalInput").ap()
    wdn_d = nc.dram_tensor("wdn", [NL, 8, 2, 128, 11, 128], F32, kind="ExternalInput").ap()
    gw_d = nc.dram_tensor("gatew", [NL, 128, 8, 128], F32, kind="ExternalInput").ap()
    pv_d = nc.dram_tensor("pvec", [128, PV["_n"]], F32, kind="ExternalInput").ap()
    tab_d = nc.dram_tensor("tabs", [128, 8, S], F32, kind="ExternalInput").ap()
    cst_d = nc.dram_tensor("cst", [128, 6, 128], F32, kind="ExternalInput").ap()
    rm_d = nc.dram_tensor("rmask", [128, T], F32, kind="ExternalInput").ap()
    yT_d = nc.dram_tensor("yT", [D, S], F32, kind="ExternalOutput").ap()
    snd_t = nc.dram_tensor("snd", [D, T], F32)
    rcv_t = [nc.dram_tensor("rcv%d" % i, [2 * D, T], F32) for i in range(2)]
    GROUPS = [[0, 1], [2, 3], [4, 5], [6, 7]]

    sch = Sched(nc)
    sb, ps = sch.sb, sch.ps
    op = sch.op

    pv = sb("pv", [128, PV["_n"]])
    sch.dma("sp", pv[:, :], pv_d[:, :], w=[pv], nbytes=1 << 18)
    cstf = sb("cstf", [128, 6, 128])
    sch.dma("sp", cstf[:, :, :], cst_d[:, :, :], w=[cstf], nbytes=1 << 18)
    cst = sb("cstb", [128, 6, 128], BF16)
    op("dve", lambda e: e.tensor_copy(out=cst[:, :, :], in_=cstf[:, :, :]), r=[cstf], w=[cst], n=768)
    ONES, IDENT, BD64, MASK128, MASK32, PSWAP = range(6)
    rmask = sb("rmask_s", [128, T])
    sch.dma("sp", rmask[:, :], rm_d[:, :], w=[rmask], nbytes=1 << 18)
    gwf = sb("gwf", [128, NL * 8, 128])
    gwb = sb("gwb", [128, NL * 8, 128], BF16)
    for l in range(NL):
        sch.dma("sp", gwf[:, l * 8:(l + 1) * 8, :], gw_d[l], w=[gwf], nbytes=1 << 19)
    op("dve", lambda e: e.tensor_copy(out=gwb[:, :, :], in_=gwf[:, :, :]), r=[gwf], w=[gwb], n=2048)

    def pcol(name, c=0, lo=0, hi=128):
        k = PV[name] + c
        return pv[lo:hi, k:k + 1]

    der = sb("der", [128, 32])
    for l in range(NL):
        k = PV["lam%d" % l]
        op("act", lambda e, k=k, l=l: e.activation(out=der[:, l * 4:l * 4 + 4], in_=pv[:, k:k + 4], func=AF.Exp, scale=-1.0), r=[pv], w=[der], n=4, tbl="exp")
        op("act", lambda e, l=l: e.activation(out=der[:, l * 4:l * 4 + 4], in_=der[:, l * 4:l * 4 + 4], func=AF.Ln, bias=pcol("one"), scale=1.0), r=[der, pv], w=[der], n=4, tbl="exp")
        op("dve", lambda e, l=l: e.tensor_scalar(out=der[:, l * 4:l * 4 + 4], in0=der[:, l * 4:l * 4 + 4], scalar1=-8.0, scalar2=None, op0=ALU.mult), r=[der], w=[der], n=4)
    LB, OML, NOML = 8, 12, 16
    kA, kB = PV["hbA"], PV["hbB"]
    op("dve", lambda e: e.tensor_tensor(out=der[:, 20:22], in0=pv[:, kB:kB + 2], in1=pv[:, kA:kA + 2], op=ALU.subtract), r=[pv, der], w=[der], n=2)
    op("act", lambda e: e.activation(out=der[:, 22:24], in_=der[:, 20:22], func=AF.Sigmoid), r=[der], w=[der], n=2, tbl="sig")
    op("dve", lambda e: e.tensor_scalar(out=der[:, LB:LB + 2], in0=der[:, 22:24], scalar1=pcol("lbm"), scalar2=None, op0=ALU.mult), r=[pv], w=[der], n=2)
    op("dve", lambda e: e.tensor_scalar(out=der[:, OML:OML + 2], in0=der[:, LB:LB + 2], scalar1=-1.0, scalar2=1.0, op0=ALU.mult, op1=ALU.add), r=[], w=[der], n=2)
    op("dve", lambda e: e.tensor_scalar(out=der[:, NOML:NOML + 2], in0=der[:, OML:OML + 2], scalar1=-1.0, scalar2=None, op0=ALU.mult), r=[], w=[der], n=2)

    def dcol(base, l, c, lo=0, hi=128):
        k = base + l * (4 if base == 0 else 2) + c
        return der[lo:hi, k:k + 1]

    hst_t = [sb("hst%d" % l, [128, 4]) for l in range(NL)]
    lxh_t = [sb("lxh%d" % l, [128, 4, 4]) for l in range(NL)]
    fh_t = [sb("fh%d" % l, [128, NFC, 2]) for l in range(NL)]
    hst = {}
    lxh = {}
    fh = {}
    st_f = {}
    st_b = {}
    for l in range(NL):
        for c in range(4):
            hst[(l, c)] = hst_t[l].sub("hst%d_%d" % (l, c))
            lxh[(l, c)] = lxh_t[l].sub("lxh%d_%d" % (l, c))
        for ch in range(NFC):
            fh[(l, ch)] = fh_t[l].sub("fh%d_%d" % (l, ch))
        op("dve", lambda e, l=l: e.memset(hst_t[l][:, :], 0.0), w=[hst[(l, c)] for c in range(4)], n=4)
        op("dve", lambda e, l=l: e.memset(lxh_t[l][:, :, :], 0.0), w=[lxh[(l, c)] for c in range(4)], n=16)
        op("dve", lambda e, l=l: e.memset(fh_t[l][:, :, :], 0.0), w=[fh[(l, ch)] for ch in range(NFC)], n=88)
        for mix in ("r", "h"):
            for p in range(2):
                tf = sb("stf_%s%d%d" % (mix, l, p), [128, 64])
                tb = sb("stb_%s%d%d" % (mix, l, p), [128, 64], BF16)
                for hh in range(2):
                    bf_ = tf.sub("stf_%s%d%d%d" % (mix, l, p, hh))
                    bb_ = tb.sub("stb_%s%d%d%d" % (mix, l, p, hh))
                    st_f[(mix, l, p, hh)] = bf_
                    st_b[(mix, l, p, hh)] = bb_
                    lo = hh * 64
                    op("dve", lambda e, bf_=bf_, lo=lo: e.memset(bf_[lo:lo + 64, :], 0.0), w=[bf_], n=64)
                    op("dve", lambda e, bb_=bb_, lo=lo: e.memset(bb_[lo:lo + 64, :], 0.0), w=[bb_], n=64)

    xres = [sb("xres%d" % i, [128, 8, T]) for i in range(2)]
    rv = sb("rv", [128, 8, T])
    tabs = sb("tabs_s", [128, 8, T])
    hT = sb("hT", [128, 8, T], BF16)
    yT = sb("yTs", [128, 8, T], BF16)
    yTc = [yT.sub("yT%d" % c) for c in range(8)]
    gT = sb("gT", [128, 22, T], BF16)
    gTc = [gT.sub("gT%d" % c) for c in range(22)]
    NSLOT = 5
    wsl = [sb("wsl%d" % i, [128, 4096], BF16) for i in range(NSLOT)]

    class FB:
        def __init__(self, name):
            self.b = sb(name, [128, T + 4])
            self.h = self.b.sub(name + "_h")

        @property
        def d(self):
            return self.b[:, 4:4 + T]

    NF = 14
    Fring = [FB("F%d" % i) for i in range(NF)]
    fstate = {"f": 0, "b": 0, "a": 0, "x": 0, "at": 0}

    def getF():
        f = Fring[fstate["f"] % NF]
        fstate["f"] += 1
        return f

    NB = 10
    Bring = [sb("B%d" % i, [128, T], BF16) for i in range(NB)]

    def getB():
        b = Bring[fstate["b"] % NB]
        fstate["b"] += 1
        return b

    vtok = sb("vtok", [128, 8, 256], BF16)
    ktok = sb("ktok", [128, 8, 2, 128], BF16)
    NAT = 4
    ATb = [sb("AT%d" % i, [128, 128], BF16) for i in range(NAT)]
    ebl = [sb("ebl%d" % p, [128, 16]) for p in range(2)]
    stmp = [sb("stmp%d" % i, [128, 64]) for i in range(4)]

    banks = [ps("bank%d" % i, [128, T]) for i in range(8)]
    for i_, b_ in enumerate(banks):
        b_.touch = -100 + i_
        b_.held = False

    def getA(hold=False):
        a = min((b for b in banks if not b.held), key=lambda b: b.touch)
        a.touch = len(sch.rec)
        a.held = hold
        return a

    wstate = {"n": 0}

    def wload(src_ap, shape):
        s = wsl[wstate["n"] % NSLOT]
        wstate["n"] += 1
        n = 1
        for d_ in shape[1:]:
            n *= d_
        view = s[:, 0:n].rearrange("p (a b) -> p a b", b=shape[2])
        sch.dma("pool", view, src_ap, w=[s], nbytes=128 * n * 4)
        return s, view

    def proj(slot, wview, col, rhs_bufs, rhs_of_kc, nk=8, m=128, ncols=T):
        pbuf = getA()

        def fn(e):
            ins = None
            for kc in range(nk):
                ins = e.matmul(pbuf[0:m, 0:ncols], lhsT=wview[:, kc, col:col + m], rhs=rhs_of_kc(kc), start=(kc == 0), stop=(kc == nk - 1))
            return ins

        op("pe", fn, r=[slot] + list(rhs_bufs), w=[pbuf], n=ncols, nmm=nk)
        return pbuf

    def sq_chunk(xr, m):
        op("act", lambda e, m=m: e.activation(out=gT[:, m, :], in_=xr[:, m, :], func=AF.Square), r=[xr], w=[gTc[m]])

    def rms_rstd(xr, use_hT=False, presq=False):
        scr = hT if use_hT else gT
        scr_b = [hT] if use_hT else gTc[0:8]
        if not presq:
            op("act", lambda e: e.activation(out=scr[:, 0:8, :], in_=xr[:, :, :], func=AF.Square), r=[xr], w=scr_b, n=8 * T)
        pS_ = getA()

        def fn(e):
            ins = None
            for c in range(8):
                ins = e.matmul(pS_[:, :], lhsT=cst[:, ONES, :], rhs=scr[:, c, :], start=(c == 0), stop=(c == 7))
            return ins

        op("pe", fn, r=[cst] + scr_b, w=[pS_], nmm=8)
        op("act", lambda e: e.activation(out=pS_[:, :], in_=pS_[:, :], func=AF.Ln, bias=pcol("eps"), scale=1.0 / D), r=[pv], w=[pS_], tbl="exp")
        op("act", lambda e: e.activation(out=pS_[:, :], in_=pS_[:, :], func=AF.Exp, scale=-0.5), r=[], w=[pS_], tbl="exp")
        pS_.held = True
        return pS_

    def norm_to_hT(xr, nwname, use_hT=False, presq=False):
        rs = rms_rstd(xr, use_hT, presq)
        for c in range(8):
            op("dve", lambda e, c=c: e.scalar_tensor_tensor(out=hT[:, c, :], in0=xr[:, c, :], scalar=pcol(nwname, c), in1=rs[:, :],
                                                              op0=ALU.mult, op1=ALU.mult), r=[xr, pv], w=[hT, rs])
        rs.held = False

    hk = lambda kc: hT[:, kc, :]

    def emit_layer(l, xr, ti):
        norm_to_hT(xr, "n1w%d" % l, use_hT=True)

        s0, w0 = wload(win_d[l, :, :, 0:512], [128, 8, 512])
        s1, w1 = wload(win_d[l, :, :, 512:1024], [128, 8, 512])
        for c in range(4):
            lxb, xc, rr, ii, aa, mm = getF(), getF(), getF(), getF(), getF(), getF()
            xcb = getB()
            pa = proj(s0, w0, c * 128, [hT], hk)
            op("pool", lambda e, c=c, lxb=lxb: e.tensor_copy(out=lxb.b[:, 1:4], in_=lxh_t[l][:, c, 0:3]), r=[lxh[(l, c)]], w=[lxb.h], n=3)
            op("act", lambda e, lxb=lxb, pa=pa: e.activation(out=lxb.d, in_=pa[:, :], func=AF.Copy), r=[], w=[lxb.b, pa])
            op("pool", lambda e, c=c, lxb=lxb: e.tensor_copy(out=lxh_t[l][:, c, 0:3], in_=lxb.b[:, T + 1:T + 4]), r=[lxb.b], w=[lxh[(l, c)]], n=3)
            op("dve", lambda e, c=c, lxb=lxb, xc=xc: e.tensor_scalar(out=xc.d, in0=lxb.b[:, 1:1 + T], scalar1=pcol("lcw%d_0" % l, c), scalar2=pcol("lcb%d" % l, c),
                                                                      op0=ALU.mult, op1=ALU.add), r=[lxb.b, lxb.h, pv], w=[xc.b])
            for k in range(1, 4):
                op("dve", lambda e, c=c, k=k, lxb=lxb, xc=xc: e.scalar_tensor_tensor(out=xc.d, in0=lxb.b[:, 1 + k:1 + k + T], scalar=pcol("lcw%d_%d" % (l, k), c), in1=xc.d,
                                                                                      op0=ALU.mult, op1=ALU.add), r=[lxb.b, lxb.h, pv], w=[xc.b], k=2)
            op("act", lambda e, xc=xc, xcb=xcb: e.activation(out=xcb[:, :], in_=xc.d, func=AF.Copy), r=[xc.b], w=[xcb])
            pg = getA()
            op("pe", lambda e, c=c, pg=pg, xcb=xcb: e.matmul(pg[:, :], lhsT=gwb[:, l * 8 + c * 2, :], rhs=xcb[:, :], start=True, stop=True), r=[gwb, xcb], w=[pg])
            op("act", lambda e, c=c, pg=pg, rr=rr: e.activation(out=rr.d, in_=pg[:, :], func=AF.Sigmoid, bias=pcol("ba%d" % l, c)), r=[pv], w=[rr.b, pg], tbl="sig")
            pg2 = getA()
            op("pe", lambda e, c=c, pg2=pg2, xcb=xcb: e.matmul(pg2[:, :], lhsT=gwb[:, l * 8 + c * 2 + 1, :], rhs=xcb[:, :], start=True, stop=True), r=[gwb, xcb], w=[pg2])
            op("act", lambda e, c=c, pg2=pg2, ii=ii: e.activation(out=ii.d, in_=pg2[:, :], func=AF.Sigmoid, bias=pcol("bx%d" % l, c)), r=[pv], w=[ii.b, pg2], tbl="sig")
            op("act", lambda e, c=c, rr=rr, aa=aa: e.activation(out=aa.d, in_=rr.d, func=AF.Exp, scale=dcol(0, l, c)), r=[rr.b, der], w=[aa.b], tbl="exp")
            op("act", lambda e, aa=aa, mm=mm: e.activation(out=mm.d, in_=aa.d, func=AF.Square), r=[aa.b], w=[mm.b])
            op("act", lambda e, mm=mm: e.activation(out=mm.d, in_=mm.d, func=AF.Sqrt, bias=pcol("one"), scale=-1.0), r=[pv], w=[mm.b], tbl="sqrt")
            op("pool", lambda e, ii=ii, xc=xc: e.tensor_tensor(out=ii.d, in0=ii.d, in1=xc.d, op=ALU.mult), r=[xc.b], w=[ii.b], k=2)
            op("pool", lambda e, ii=ii, mm=mm: e.tensor_tensor(out=ii.d, in0=ii.d, in1=mm.d, op=ALU.mult), r=[mm.b], w=[ii.b], k=2)
            hs = hst[(l, c)]
            op("dve", lambda e, c=c, rr=rr, aa=aa, ii=ii: e.tensor_tensor_scan(out=rr.d, data0=aa.d, data1=ii.d, initial=hst_t[l][:, c:c + 1],
                                                                                op0=ALU.mult, op1=ALU.add), r=[aa.b, ii.b, hs], w=[rr.b], k=2)
            op("dve", lambda e, c=c, rr=rr: e.tensor_copy(out=hst_t[l][:, c:c + 1], in_=rr.b[:, T + 3:T + 4]), r=[rr.b], w=[hs], n=1)
            pb = proj(s1, w1, c * 128, [hT], hk)
            op("act", lambda e, mm=mm, pb=pb: e.activation(out=mm.d, in_=pb[:, :], func=AF.Gelu_apprx_tanh), r=[], w=[mm.b, pb], tbl="gelu")
            op("dve", lambda e, c=c, rr=rr, mm=mm: e.tensor_tensor(out=yT[:, c, :], in0=rr.d, in1=mm.d, op=ALU.mult), r=[rr.b, mm.b], w=[yTc[c]], k=2)

        def gla(mix, qT, kT, vT, C, BLK, sgate, nwname, ycol0, decay_const):
            nblk = T // BLK
            nch = T // C
            msk = MASK128 if C == 128 else MASK32
            for p in range(2):
                for b_ in range(nblk):
                    for (srcT, is_k) in ((kT, True), (vT, False)):
                        pt = getA()
                        ptv = pt[:, 0:64].bitcast(BF16)
                        op("pe", lambda e, p=p, b_=b_, ptv=ptv, srcT=srcT: e.transpose(ptv[0:BLK, 0:128], srcT[p][:, b_ * BLK:(b_ + 1) * BLK], cst[:, IDENT, :]),
                           r=[srcT[p], cst], w=[pt], n=128)
                        if is_k:
                            op("act", lambda e, p=p, b_=b_, ptv=ptv: e.activation(out=ktok[0:BLK, b_, p, :], in_=ptv[0:BLK, 0:128], func=AF.Copy), r=[], w=[ktok, pt], n=128)
                        else:
                            op("act", lambda e, p=p, b_=b_, ptv=ptv: e.activation(out=vtok[0:BLK, b_, p * 128:(p + 1) * 128], in_=ptv[0:BLK, 0:128], func=AF.Copy),
                               r=[], w=[vtok, pt], n=128)
            pO = [getA(True), getA(True)]
            pX = [getA(True), getA(True)]
            pUT = getA(True)
            for j in range(nch):
                blk = (j * C) // BLK
                base = (j * C) % BLK
                tsl = slice(j * C, (j + 1) * C)
                for p in range(2):
                    for hh in range(2):
                        h = 2 * p + hh
                        off = hh * 64
                        px = pX[fstate["x"] % 2]
                        fstate["x"] += 1
                        at = ATb[fstate["at"] % NAT]
                        fstate["at"] += 1
                        Sb = st_b[(mix, l, p, hh)]
                        op("pe", lambda e, p=p, off=off, tsl=tsl, px=px, base=base: e.matmul(
                            px[base:base + C, 0:C], lhsT=kT[p][off:off + 64, tsl], rhs=qT[p][off:off + 64, tsl], start=True, stop=True),
                            r=[kT[p], qT[p]], w=[px], n=C)
                        op("dve", lambda e, px=px, at=at, base=base: e.tensor_tensor(
                            out=at[base:base + C, 0:C], in0=px[base:base + C, 0:C], in1=cst[base:base + C, msk, 0:C], op=ALU.mult),
                            r=[cst], w=[at, px], n=C)

                        def fn_o(e, p=p, off=off, tsl=tsl, at=at, base=base, blk=blk, h=h, Sb=Sb):
                            e.matmul(pO[p][off:off + 64, tsl], lhsT=vtok[base:base + C, blk, h * 64:(h + 1) * 64], rhs=at[base:base + C, 0:C], start=True, stop=False)
                            return e.matmul(pO[p][off:off + 64, tsl], lhsT=Sb[off:off + 64, :], rhs=qT[p][off:off + 64, tsl], start=False, stop=True)

                        op("pe", fn_o, r=[vtok, at, Sb, qT[p]], w=[pO[p]], n=C, nmm=2)
                    op("pe", lambda e, p=p, base=base, blk=blk: e.matmul(
                        pUT[:, 0:128], lhsT=ktok[base:base + C, blk, p, :], rhs=vtok[base:base + C, blk, p * 128:(p + 1) * 128],
                        start=True, stop=True), r=[ktok, vtok], w=[pUT], n=128)
                    for hh in range(2):
                        h = 2 * p + hh
                        off = hh * 64
                        Sf = st_f[(mix, l, p, hh)]
                        Sb = st_b[(mix, l, p, hh)]
                        if decay_const is not None:
                            g = decay_const[h]
                            op("dve", lambda e, Sf=Sf, off=off, g=g: e.scalar_tensor_tensor(
                                out=Sf[off:off + 64, :], in0=Sf[off:off + 64, :], scalar=g, in1=pUT[off:off + 64, off:off + 64], op0=ALU.mult, op1=ALU.add),
                                r=[], w=[Sf, pUT], n=64)
                            op("dve", lambda e, Sf=Sf, Sb=Sb, off=off, g=g: e.tensor_scalar(
                                out=Sb[off:off + 64, :], in0=Sf[off:off + 64, :], scalar1=g, scalar2=None, op0=ALU.mult), r=[Sf], w=[Sb], n=64)
                        else:
                            tmp = stmp[h]
                            op("dve", lambda e, Sf=Sf, off=off, tmp=tmp: e.tensor_tensor(
                                out=tmp[off:off + 64, :], in0=pUT[off:off + 64, off:off + 64], in1=Sf[off:off + 64, :], op=ALU.add), r=[Sf], w=[tmp, pUT], n=64)
                            op("dve", lambda e, Sf=Sf, off=off, tmp=tmp, p=p, j=j: e.tensor_scalar(
                                out=Sf[off:off + 64, :], in0=tmp[off:off + 64, :], scalar1=ebl[p][off:off + 64, j:j + 1], scalar2=None, op0=ALU.mult),
                                r=[tmp, ebl[p]], w=[Sf], n=64)
                            op("act", lambda e, Sf=Sf, Sb=Sb, off=off: e.activation(out=Sb[off:off + 64, :], in_=Sf[off:off + 64, :], func=AF.Copy), r=[Sf], w=[Sb], n=64)
            for p in range(2):
                o, rt = getF(), getF()
                osq = getB()
                op("act", lambda e, p=p, o=o: e.activation(out=o.d, in_=pO[p][:, :], func=AF.Copy), r=[], w=[o.b, pO[p]])
                op("act", lambda e, o=o, osq=osq: e.activation(out=osq[:, :], in_=o.d, func=AF.Square), r=[o.b], w=[osq])
                pS_ = getA()
                op("pe", lambda e, pS_=pS_, osq=osq: e.matmul(pS_[:, :], lhsT=cst[:, BD64, :], rhs=osq[:, :], start=True, stop=True), r=[cst, osq], w=[pS_])
                op("act", lambda e, pS_=pS_, rt=rt: e.activation(out=rt.d, in_=pS_[:, :], func=AF.Ln, bias=pcol("eps"), scale=1.0), r=[pv], w=[rt.b, pS_], tbl="exp")
                op("act", lambda e, rt=rt: e.activation(out=rt.d, in_=rt.d, func=AF.Exp, scale=-0.5), r=[], w=[rt.b], tbl="exp")
                op("dve", lambda e, p=p, o=o, rt=rt: e.scalar_tensor_tensor(out=o.d, in0=o.d, scalar=pcol(nwname, p), in1=rt.d, op0=ALU.mult, op1=ALU.mult),
                   r=[rt.b, pv], w=[o.b], k=2)
                op("pool", lambda e, p=p, o=o: e.tensor_tensor(out=yT[:, ycol0 + p, :], in0=o.d, in1=sgate[p].d, op=ALU.mult), r=[o.b, sgate[p].b], w=[yTc[ycol0 + p]], k=2)
            for b_ in pO + pX + [pUT]:
                b_.held = False

        s2, w2 = wload(win_d[l, :, :, 1024:1536], [128, 8, 512])
        s3, w3 = wload(win_d[l, :, :, 1536:2048], [128, 8, 512])
        if l == 0:
            sch.dma("sp", tabs[:, :, :], tab_d[:, :, ti * T:(ti + 1) * T], w=[tabs], nbytes=2 << 20)
        qT = [getB(), getB()]
        kT = [getB(), getB()]
        for dst, cbase, tb in ((qT, 0, 0), (kT, 256, 2)):
            for p in range(2):
                pa = proj(s2, w2, cbase + p * 128, [hT], hk)
                qb = getB()
                op("act", lambda e, qb=qb, pa=pa: e.activation(out=qb[:, :], in_=pa[:, :], func=AF.Copy), r=[], w=[qb, pa])
                pb = getA()
                op("pe", lambda e, pb=pb, qb=qb: e.matmul(pb[:, :], lhsT=cst[:, PSWAP, :], rhs=qb[:, :], start=True, stop=True), r=[cst, qb], w=[pb])
                t1, t2 = getF(), getF()
                op("dve", lambda e, p=p, tb=tb, t1=t1, pa=pa: e.tensor_tensor(out=t1.d, in0=pa[:, :], in1=tabs[:, p * 4 + tb, :], op=ALU.mult), r=[tabs], w=[t1.b, pa])
                op("dve", lambda e, p=p, tb=tb, t2=t2, pb=pb: e.tensor_tensor(out=t2.d, in0=pb[:, :], in1=tabs[:, p * 4 + tb + 1, :], op=ALU.mult), r=[tabs], w=[t2.b, pb])
                op("pool", lambda e, p=p, dst=dst, t1=t1, t2=t2: e.tensor_tensor(out=dst[p][:, :], in0=t1.d, in1=t2.d, op=ALU.add), r=[t1.b, t2.b], w=[dst[p]], k=2)
        vT = [getB(), getB()]
        for p in range(2):
            pa = proj(s3, w3, p * 128, [hT], hk)
            op("act", lambda e, p=p, pa=pa, vT=vT: e.activation(out=vT[p][:, :], in_=pa[:, :], func=AF.Copy), r=[], w=[vT[p], pa])
        sg = [getF(), getF()]
        for p in range(2):
            pa = proj(s3, w3, 256 + p * 128, [hT], hk)
            op("act", lambda e, p=p, pa=pa, sg=sg: e.activation(out=sg[p].d, in_=pa[:, :], func=AF.Silu), r=[], w=[sg[p].b, pa], tbl="silu")
        gla("r", qT, kT, vT, 128, 128, sg, "rnw%d" % l, 4, [g ** 128 for g in RET_GAMMA])

        s4, w4 = wload(win_d[l, :, :, 2048:2560], [128, 8, 512])
        s5, w5 = wload(win_d[l, :, :, 2560:3072], [128, 8, 512])
        qT = [getB(), getB()]
        kT = [getB(), getB()]
        for p in range(2):
            sig, ff, bb, en = getF(), getF(), getF(), getF()
            pa = proj(s4, w4, 256 + p * 128, [hT], hk)
            op("act", lambda e, sig=sig, pa=pa: e.activation(out=sig.d, in_=pa[:, :], func=AF.Sigmoid), r=[], w=[sig.b, pa], tbl="sig")
            op("dve", lambda e, p=p, sig=sig, ff=ff: e.tensor_scalar(out=ff.d, in0=sig.d, scalar1=dcol(OML, l, p), scalar2=dcol(LB, l, p), op0=ALU.mult, op1=ALU.add),
               r=[sig.b, der], w=[ff.b])
            op("act", lambda e, ff=ff: e.activation(out=ff.d, in_=ff.d, func=AF.Ln), r=[], w=[ff.b], tbl="exp")
            op("dve", lambda e, ff=ff, bb=bb: e.tensor_tensor_scan(out=bb.d, data0=rmask[:, :], data1=ff.d, initial=0.0, op0=ALU.mult, op1=ALU.add),
               r=[rmask, ff.b], w=[bb.b], k=2)
            op("pool", lambda e, p=p, sig=sig: e.tensor_scalar(out=sig.d, in0=sig.d, scalar1=dcol(NOML, l, p), scalar2=dcol(OML, l, p), op0=ALU.mult, op1=ALU.add),
               r=[der], w=[sig.b])
            op("act", lambda e, bb=bb, en=en: e.activation(out=en.d, in_=bb.d, func=AF.Exp, scale=-1.0), r=[bb.b], w=[en.b], tbl="exp")
            op("dve", lambda e, p=p, sig=sig, en=en, kT=kT: e.tensor_tensor(out=kT[p][:, :], in0=sig.d, in1=en.d, op=ALU.mult), r=[sig.b, en.b], w=[kT[p]], k=2)
            op("act", lambda e, bb=bb, ff=ff: e.activation(out=ff.d, in_=bb.d, func=AF.Exp), r=[bb.b], w=[ff.b], tbl="exp")
            op("dve", lambda e, p=p, ff=ff: e.tensor_copy(out=ebl[p][:, :], in_=ff.d.rearrange("p (c t) -> p c t", t=32)[:, :, 31]), r=[ff.b], w=[ebl[p]], n=16)
            pb = proj(s4, w4, p * 128, [hT], hk)
            op("act", lambda e, en=en, pb=pb: e.activation(out=en.d, in_=pb[:, :], func=AF.Silu), r=[], w=[en.b, pb], tbl="silu")
            op("dve", lambda e, p=p, en=en, ff=ff, qT=qT: e.tensor_tensor(out=qT[p][:, :], in0=en.d, in1=ff.d, op=ALU.mult), r=[en.b, ff.b], w=[qT[p]], k=2)
        vT = [getB(), getB()]
        for p in range(2):
            pa = proj(s5, w5, p * 128, [hT], hk)
            op("act", lambda e, p=p, pa=pa, vT=vT: e.activation(out=vT[p][:, :], in_=pa[:, :], func=AF.Copy), r=[], w=[vT[p], pa])
        sg = [getF(), getF()]
        for p in range(2):
            pa = proj(s5, w5, 256 + p * 128, [hT], hk)
            op("act", lambda e, p=p, pa=pa, sg=sg: e.activation(out=sg[p].d, in_=pa[:, :], func=AF.Silu), r=[], w=[sg[p].b, pa], tbl="silu")
        gla("h", qT, kT, vT, 32, 64, sg, "hnw%d" % l, 6, None)

        for half in range(2):
            so, wo = wload(wout_d[l, :, :, half * 512:(half + 1) * 512], [128, 8, 512])
            for mq in range(4):
                m = half * 4 + mq
                pb = proj(so, wo, mq * 128, yTc, lambda kc: yT[:, kc, :])
                op("dve", lambda e, m=m, pb=pb: e.tensor_tensor(out=xr[:, m, :], in0=pb[:, :], in1=xr[:, m, :], op=ALU.add), r=[], w=[xr, pb])
                sq_chunk(xr, m)

        def down_half(hf):
            for m in range(8):
                sd, wd = wload(wdn_d[l, m, hf], [128, 11, 128])
                pb = getA()

                def fn_d(e, pb=pb, wd=wd):
                    ins = None
                    for kc in range(11):
                        ins = e.matmul(pb[:, :], lhsT=wd[:, kc, :], rhs=gT[:, hf * 11 + kc, :], start=(kc == 0), stop=(kc == 10))
                    return ins

                op("pe", fn_d, r=[sd] + gTc[hf * 11:(hf + 1) * 11], w=[pb], nmm=11)
                op("dve", lambda e, m=m, pb=pb: e.tensor_tensor(out=xr[:, m, :], in0=pb[:, :], in1=xr[:, m, :], op=ALU.add), r=[], w=[xr, pb])
                if hf == 1:
                    sq_chunk(xr, m)

        norm_to_hT(xr, "n2w%d" % l, presq=True)
        for q in range(11):
            su, wu = wload(wup_d[l, :, :, q * 512:(q + 1) * 512], [128, 8, 512])
            for jj in range(2):
                j = 2 * q + jj
                outs = []
                for (half, col) in ((0, jj * 128), (1, 256 + jj * 128)):
                    ch = half * 22 + j
                    cb = getF()
                    pb = proj(su, wu, col, [hT], hk)
                    fhb = fh[(l, ch)]
                    op("act", lambda e, cb=cb, pb=pb, ch=ch: e.activation(out=cb.d, in_=pb[:, :], func=AF.Identity, bias=pcol("fcb%d" % l, ch), scale=pcol("fcw%d_2" % l, ch)),
                       r=[pv], w=[cb.b, pb])
                    op("dve", lambda e, cb=cb, pb=pb, ch=ch: e.scalar_tensor_tensor(out=cb.b[:, 5:4 + T], in0=pb[:, 0:T - 1], scalar=pcol("fcw%d_1" % l, ch), in1=cb.b[:, 5:4 + T],
                                                                                     op0=ALU.mult, op1=ALU.add), r=[pv], w=[cb.b, pb])
                    op("dve", lambda e, cb=cb, pb=pb, ch=ch: e.scalar_tensor_tensor(out=cb.b[:, 6:4 + T], in0=pb[:, 0:T - 2], scalar=pcol("fcw%d_0" % l, ch), in1=cb.b[:, 6:4 + T],
                                                                                     op0=ALU.mult, op1=ALU.add), r=[pv], w=[cb.b, pb])
                    op("dve", lambda e, cb=cb, ch=ch: e.scalar_tensor_tensor(out=cb.b[:, 4:5], in0=fh_t[l][:, ch, 1:2], scalar=pcol("fcw%d_1" % l, ch), in1=cb.b[:, 4:5],
                                                                              op0=ALU.mult, op1=ALU.add), r=[pv, fhb], w=[cb.b], n=1)
                    op("dve", lambda e, cb=cb, ch=ch: e.scalar_tensor_tensor(out=cb.b[:, 4:6], in0=fh_t[l][:, ch, 0:2], scalar=pcol("fcw%d_0" % l, ch), in1=cb.b[:, 4:6],
                                                                              op0=ALU.mult, op1=ALU.add), r=[pv, fhb], w=[cb.b], n=2)
                    op("act", lambda e, pb=pb, ch=ch: e.activation(out=fh_t[l][:, ch, :], in_=pb[:, T - 2:T], func=AF.Copy), r=[], w=[fhb, pb], n=2)
                    outs.append(cb)
                cg, cv = outs
                op("act", lambda e, cg=cg: e.activation(out=cg.d, in_=cg.d, func=AF.Silu), r=[], w=[cg.b], tbl="silu")
                op("pool", lambda e, cg=cg, cv=cv, j=j: e.tensor_tensor(out=gT[:, j, :], in0=cg.d, in1=cv.d, op=ALU.mult), r=[cg.b, cv.b], w=[gTc[j]], k=2)
            if q == 5:
                down_half(0)
        down_half(1)

    xv = xT_d.rearrange("(c p) s -> p c s", p=128)
    yv = yT_d.rearrange("(c p) s -> p c s", p=128)
    sndB = Buf(snd_t, "snd")
    rcvB = [Buf(rcv_t[i], "rcv%d" % i) for i in range(2)]
    snd_v = snd_t.ap().rearrange("(c p) s -> p c s", p=128)
    out_toks = []
    for st in range(NS):
        xr = xres[st % 2]
        sch.dma("sp", xr[:, :, :], xv[:, :, st * T:(st + 1) * T], w=[xr], nbytes=2 << 20)
        op("act", lambda e, xr=xr: e.activation(out=xr[:, :, :], in_=xr[:, :, :], func=AF.Copy, scale=pcol("m")), r=[pv], w=[xr], n=8 * T)
        if st >= SKEW:
            rb = rcvB[(st - SKEW) % 2]
            rcv_v = rcv_t[(st - SKEW) % 2].ap()[0:D, :].rearrange("(c p) s -> p c s", p=128)
            sch.dma("sp", rv[:, :, :], rcv_v, r=[rb], w=[rv], nbytes=2 << 20)
            op("dve", lambda e, xr=xr: e.scalar_tensor_tensor(out=xr[:, :, :], in0=rv[:, :, :], scalar=pcol("om"), in1=xr[:, :, :], op0=ALU.mult, op1=ALU.add),
               r=[rv, pv], w=[xr], n=8 * T, k=2)
        emit_layer(0, xr, st)
        if st < NT:
            sch.dma("sp", snd_v, xr[:, :, :], r=[xr], w=[sndB], nbytes=2 << 20)
            rb = rcvB[st % 2]
            sch.raw("pool", lambda e, rb=rb: e.collective_compute("AllGather", ALU.bypass, replica_groups=GROUPS,
                                                                   ins=[snd_t.ap().opt()], outs=[rb.t.ap().opt()]),
                    r=[sndB], w=[rb], sem_buf=rb, inc=1, occ=1.0, lat=120.0)
        if st == SKEW - 1:
            kp = pcol("keep")
            op("dve", lambda e: e.tensor_scalar(out=hst_t[0][:, :], in0=hst_t[0][:, :], scalar1=kp, scalar2=None, op0=ALU.mult), r=[pv], w=[hst[(0, c)] for c in range(4)], n=4)
            op("dve", lambda e: e.tensor_scalar(out=lxh_t[0][:, :, :], in0=lxh_t[0][:, :, :], scalar1=kp, scalar2=None, op0=ALU.mult), r=[pv], w=[lxh[(0, c)] for c in range(4)], n=16)
            op("dve", lambda e: e.tensor_scalar(out=fh_t[0][:, :, :], in0=fh_t[0][:, :, :], scalar1=kp, scalar2=None, op0=ALU.mult), r=[pv], w=[fh[(0, ch)] for ch in range(NFC)], n=88)
            for key in list(st_f.keys()):
                lo = key[3] * 64
                for tbl in (st_f, st_b):
                    bb_ = tbl[key]
                    op("dve", lambda e, bb_=bb_, lo=lo: e.tensor_scalar(out=bb_[lo:lo + 64, :], in0=bb_[lo:lo + 64, :], scalar1=pv[lo:lo + 64, PV["keep"]:PV["keep"] + 1],
                                                                          scalar2=None, op0=ALU.mult), r=[pv], w=[bb_], n=64)
        rs = rms_rstd(xr, presq=True)
        for c in range(8):
            op("dve", lambda e, c=c, xr=xr, rs=rs: e.scalar_tensor_tensor(out=xr[:, c, :], in0=xr[:, c, :], scalar=pcol("fnw", c), in1=rs[:, :],
                                                                           op0=ALU.mult, op1=ALU.mult), r=[pv], w=[xr, rs])
        rs.held = False
        out_toks.append(sch.dma("sp", yv[:, :, st * T:(st + 1) * T], xr[:, :, :], r=[xr], nbytes=2 << 20))
    sch.wait_all_on("sp", out_toks)
    sch.emit()
    return nc, sch


def _chunks(v):
    v = np.asarray(v, np.float32)
    return np.ascontiguousarray(v.reshape(-1, 128).T)


def make_tables(S):
    t = np.arange(S, dtype=np.float64)
    inv = 10000.0 ** (-np.arange(0, 64, 2, dtype=np.float64) / 64.0)
    ang = t[None, :] * inv[:, None]
    cos = np.concatenate([np.cos(ang), np.cos(ang)], 0)
    sin = np.concatenate([-np.sin(ang), np.sin(ang)], 0)
    n1 = (t % 128) + 1.0
    tabs = np.zeros((128, 8, S), np.float64)
    for p in range(2):
        for hh in range(2):
            h = 2 * p + hh
            lg = math.log(RET_GAMMA[h])
            qd = np.exp(n1 * lg)[None, :]
            kd = np.exp(-n1 * lg)[None, :] * (64.0 ** -0.5)
            sl = slice(hh * 64, hh * 64 + 64)
            tabs[sl, p * 4 + 0] = cos * qd
            tabs[sl, p * 4 + 1] = sin * qd
            tabs[sl, p * 4 + 2] = cos * kd
            tabs[sl, p * 4 + 3] = sin * kd
    return tabs.astype(np.float32)


def make_consts():
    cst = np.zeros((128, 6, 128), np.float32)
    for m_ in range(128):
        cst[(m_ // 64) * 64 + ((m_ % 64) + 32) % 64, 5, m_] = 1.0
    cst[:, 0, :] = 1.0
    cst[:, 1, :] = np.eye(128, dtype=np.float32)
    for b in range(2):
        cst[b * 64:(b + 1) * 64, 2, b * 64:(b + 1) * 64] = 1.0 / 64.0
    m = np.arange(128)[:, None]
    n = np.arange(128)[None, :]
    cst[:, 3, :] = (n >= m).astype(np.float32)
    m32 = (np.arange(32)[None, :] >= np.arange(32)[:, None]).astype(np.float32)
    for b in range(4):
        cst[b * 32:(b + 1) * 32, 4, 0:32] = m32
    rmask = np.ones((128, T), np.float32)
    rmask[:, ::32] = 0.0
    return cst, rmask


def prep_role(inp, role, NT):
    lyr = role
    NS = NT + SKEW
    w_in = np.asarray(inp["w_in"], np.float32)[lyr:lyr + 1]
    win = w_in.reshape(1, 8, 128, NCOL_IN).transpose(0, 2, 1, 3)
    wout = np.asarray(inp["w_out"], np.float32)[lyr:lyr + 1].reshape(1, 8, 128, D).transpose(0, 2, 1, 3)
    perm = []
    for q in range(11):
        perm += list(range(2 * q * 128, (2 * q + 2) * 128))
        perm += list(range(DFF + 2 * q * 128, DFF + (2 * q + 2) * 128))
    wup = np.asarray(inp["ffn_w_up"], np.float32)[lyr:lyr + 1][:, :, perm].reshape(1, 8, 128, 2 * DFF).transpose(0, 2, 1, 3)
    wdn = np.asarray(inp["ffn_w_down"], np.float32)[lyr:lyr + 1].reshape(1, 2, 11, 128, 8, 128).transpose(0, 4, 1, 3, 2, 5)
    gw = np.zeros((1, 128, 8, 128), np.float32)
    wa = np.asarray(inp["lru_wa"], np.float32)
    wx = np.asarray(inp["lru_wx"], np.float32)
    for c in range(4):
        for b2 in range(2):
            sl = slice(b2 * 64, b2 * 64 + 64)
            gw[0, sl, c * 2 + 0, sl] = wa[lyr, 2 * c + b2]
            gw[0, sl, c * 2 + 1, sl] = wx[lyr, 2 * c + b2]
    pvec = np.zeros((128, PV["_n"]), np.float32)

    def put(name, v):
        a = _chunks(v)
        pvec[:, PV[name]:PV[name] + a.shape[1]] = a

    put("n1w0", inp["norm1_w"][lyr])
    put("n2w0", inp["norm2_w"][lyr])
    for k in range(4):
        put("lcw0_%d" % k, inp["lru_conv_w"][lyr][k])
    put("lcb0", inp["lru_conv_b"][lyr])
    put("ba0", inp["lru_ba"][lyr])
    put("bx0", inp["lru_bx"][lyr])
    put("lam0", inp["lru_lambda"][lyr])
    put("rnw0", inp["ret_norm_w"][lyr])
    put("hnw0", inp["hg_norm_w"][lyr])
    put("hbA", inp["hg_lower_bounds"][0])
    put("hbB", inp["hg_lower_bounds"][1])
    for k in range(3):
        put("fcw0_%d" % k, inp["ffn_conv_w"][lyr][k])
    put("fcb0", inp["ffn_conv_b"][lyr])
    put("fnw", inp["final_norm_w"])
    pvec[:, PV["eps"]] = EPS
    pvec[:, PV["one"]] = 1.0
    pvec[:, PV["m"]] = 1.0 if role == 0 else 0.0
    pvec[:, PV["om"]] = 0.0 if role == 0 else 1.0
    pvec[:, PV["keep"]] = 1.0 if role == 0 else 0.0
    pvec[:, PV["lbm"]] = 0.0 if role == 0 else 1.0
    cst, rmask = make_consts()
    tb = make_tables(NT * T)
    tabs = np.zeros((128, 8, NS * T), np.float32)
    for st in range(NS):
        ti = st if role == 0 else st - SKEW
        if ti < 0 or ti >= NT:
            ti = 0
        tabs[:, :, st * T:(st + 1) * T] = tb[:, :, ti * T:(ti + 1) * T]
    return {
        "win": np.ascontiguousarray(win), "wout": np.ascontiguousarray(wout),
        "wup": np.ascontiguousarray(wup), "wdn": np.ascontiguousarray(wdn),
        "gatew": gw, "pvec": pvec, "tabs": tabs, "cst": cst, "rmask": rmask,
    }


_CACHE = {}


def kernel(**inputs):
    x = np.asarray(inputs["x"], np.float32)
    B, S, _ = x.shape
    NT = S // T
    NS = NT + SKEW
    roles = [prep_role(inputs, r, NT) for r in range(2)]
    if NT not in _CACHE:
        _CACHE[NT] = build(NT)
    nc, sch = _CACHE[NT]
    ncores = 2 * B
    zeros = np.zeros((D, NS * T), np.float32)
    in_maps = []
    for c in range(ncores):
        b, r = c // 2, c % 2
        m = dict(roles[r])
        if r == 0:
            xp = np.zeros((D, NS * T), np.float32)
            xp[:, :S] = x[b].T
            m["xT"] = xp
        else:
            m["xT"] = zeros
        in_maps.append(m)
    res = run_bass_kernel_spmd(nc, in_maps, core_ids=list(range(ncores)))
    out = np.empty((B, S, D), np.float32)
    for b in range(B):
        out[b] = res.results[2 * b + 1]["yT"][:, SKEW * T:].T
    return out
```

```python
import contextlib
import math
import numpy as np
import concourse.bass as bass
import concourse.mybir as mybir
from concourse.bass_utils import run_bass_kernel_spmd

F32 = mybir.dt.float32
BF16 = mybir.dt.bfloat16
AF = mybir.ActivationFunctionType
ALU = mybir.AluOpType

EPOCH = 4000
PRIO = "cp"

D = 1024
SEQ = 8192
BATCH = 4
DEPTH = 2
NL = 1
SKEW = 2
T = 512
DFF = 2816
NFC = 44
EPS = 1e-6
NCOL_IN = 3072


import heapq
import sys


class Buf:
    def __init__(self, t, name):
        self.t = t
        self.name = name
        self.last_write = None
        self.reads = []
        self.dma_cnt = 0
        self.g_lw = None
        self.g_rd = []
        self.held = False

    def __getitem__(self, key):
        return self.t[key]

    def sub(self, name):
        return Buf(self.t, name)


class _Op:
    __slots__ = ("eng", "fn", "r", "w", "occ", "lat", "kind", "sem_buf", "preds", "succs", "inc", "line", "start", "tbl")


class Sched:
    ENG = ("pe", "act", "dve", "pool", "sp")

    def __init__(self, nc):
        self.nc = nc
        self.stack = contextlib.ExitStack()
        self.rec = []
        self.ops = {e: [] for e in self.ENG}
        self.cnt = {e: 0 for e in self.ENG}
        self.epoch = {e: 0 for e in self.ENG}
        self.sems = {}
        self.waited = {e: {} for e in self.ENG}
        self.nsem = 0
        self.final_waits = []
        self.sim_time = 0.0

    def sem(self, key):
        if key not in self.sems:
            self.sems[key] = self.stack.enter_context(self.nc.semaphore("s%d" % self.nsem))
            self.nsem += 1
        return self.sems[key]

    def sb(self, name, shape, dtype=F32):
        t = self.stack.enter_context(self.nc.sbuf_tensor(name, list(shape), dtype))
        return Buf(t, name)

    def ps(self, name, shape, dtype=F32):
        t = self.stack.enter_context(self.nc.psum_tensor(name, list(shape), dtype))
        return Buf(t, name)

    def _record(self, o):
        i = len(self.rec)
        try:
            f = sys._getframe(2)
            while f.f_code.co_name in ("proj", "wload", "getA", "op", "dma", "raw"):
                f = f.f_back
            o.line = f.f_lineno
        except Exception:
            o.line = 0
        preds = set()
        for b in o.r:
            if b.g_lw is not None:
                preds.add(b.g_lw)
        for b in o.w:
            if b.g_lw is not None:
                preds.add(b.g_lw)
            preds.update(b.g_rd)
        preds.discard(i)
        o.preds = preds
        o.succs = []
        for p in preds:
            self.rec[p].succs.append(i)
        for b in o.w:
            b.g_lw = i
            b.g_rd = []
            if hasattr(b, "touch"):
                b.touch = i
        for b in o.r:
            if b not in o.w:
                b.g_rd.append(i)
        self.rec.append(o)
        return i

    def op(self, eng, fn, r=(), w=(), n=T, k=1.0, nmm=1, tbl=None):
        o = _Op()
        o.tbl = tbl
        o.eng, o.fn, o.r, o.w, o.kind, o.sem_buf = eng, fn, tuple(r), tuple(w), "c", None
        if eng == "pe":
            o.occ = nmm * (0.19 + 0.0002 * n)
        elif eng == "act":
            o.occ = 0.22 + 0.00072 * n
        elif eng == "dve":
            o.occ = 0.07 + 0.00105 * n * k
        else:
            o.occ = 0.4 + 0.002 * n
        o.lat = o.occ + (0.1 if eng == "pe" else 0.3)
        return self._record(o)

    def dma(self, q, out_ap, in_ap, r=(), w=(), sem_buf=None, nbytes=1 << 20):
        o = _Op()
        o.eng, o.r, o.w, o.kind = q, tuple(r), tuple(w), "d"
        o.sem_buf = sem_buf if sem_buf is not None else (w[0] if len(w) else r[0])

        def fn(e, out_ap=out_ap, in_ap=in_ap):
            return e.dma_start(out=out_ap, in_=in_ap)

        o.fn = fn
        o.occ = 1.0 if q == "pool" else 0.15
        o.lat = o.occ + 2.0 + nbytes / 150e3
        o.inc = 16
        return self._record(o)

    def raw(self, q, fn, r=(), w=(), sem_buf=None, inc=1, occ=1.0, lat=50.0):
        o = _Op()
        o.eng, o.r, o.w, o.kind = q, tuple(r), tuple(w), "d"
        o.sem_buf = sem_buf
        o.fn = fn
        o.occ, o.lat = occ, lat
        o.inc = inc
        return self._record(o)

    def wait_all_on(self, eng, idxs):
        self.final_waits.append((eng, list(idxs)))

    def _schedule(self):
        ops = self.rec
        ENG = self.ENG
        ready = {e: [] for e in ENG}
        busy = {e: 0.0 for e in ENG}
        npred = [len(o.preds) for o in ops]
        N = len(ops)
        tail = [0.0] * N
        for i in range(N - 1, -1, -1):
            o = ops[i]
            t = 0.0
            for s_ in o.succs:
                if tail[s_] > t:
                    t = tail[s_]
            tail[i] = t + o.lat
        if PRIO == "cp":
            key = [(-tail[i], i) for i in range(N)]
        else:
            key = [(i, i) for i in range(N)]
        for i, o in enumerate(ops):
            if npred[i] == 0:
                heapq.heappush(ready[o.eng], (key[i], i))
        events = []
        now = 0.0
        order = []
        cur_tbl = None
        while True:
            for e in ENG:
                if ready[e] and busy[e] <= now + 1e-9:
                    extra = 0.0
                    if e == "act":
                        cand = [heapq.heappop(ready[e]) for _ in range(min(5, len(ready[e])))]
                        pick = 0
                        for ci, (_, ii) in enumerate(cand):
                            tb = getattr(ops[ii], "tbl", None)
                            if tb is None or tb == cur_tbl:
                                pick = ci
                                break
                        i = cand[pick][1]
                        for ci, c_ in enumerate(cand):
                            if ci != pick:
                                heapq.heappush(ready[e], c_)
                        tb = getattr(ops[i], "tbl", None)
                        if tb is not None and tb != cur_tbl:
                            extra = 1.3
                            cur_tbl = tb
                    else:
                        i = heapq.heappop(ready[e])[1]
                    o = ops[i]
                    order.append(i)
                    o.start = now
                    busy[e] = now + o.occ + extra
                    heapq.heappush(events, (now + o.lat + extra, 1, i))
                    heapq.heappush(events, (now + o.occ + extra, 0, i))
            if not events:
                break
            t, typ, i = heapq.heappop(events)
            now = t
            if typ == 1:
                for s_ in ops[i].succs:
                    npred[s_] -= 1
                    if npred[s_] == 0:
                        heapq.heappush(ready[ops[s_].eng], (key[s_], s_))
        assert len(order) == len(ops), (len(order), len(ops))
        self.sim_time = now
        return order

    def _deps(self, r, w):
        deps = []
        for b in r:
            if b.last_write is not None:
                deps.append(b.last_write)
        for b in w:
            if b.last_write is not None:
                deps.append(b.last_write)
            deps.extend(b.reads)
        return deps

    def _waits(self, eng, deps):
        need = {}
        wd = self.waited[eng]
        for (k, v) in deps:
            if wd.get(k, 0) >= v:
                continue
            if need.get(k, 0) < v:
                need[k] = v
        for k, v in need.items():
            wd[k] = v
        return [(self.sem(k), v) for k, v in need.items()]

    def _commit(self, r, w, tok):
        for b in w:
            b.last_write = tok
            b.reads = []
        for b in r:
            if b in w:
                continue
            b.reads = [x for x in b.reads if x[0] != tok[0]] + [tok]

    def _replay_one(self, o):
        eng = o.eng
        deps = self._deps(o.r, o.w)
        if o.kind == "c":
            if eng == "pe":
                deps = [d for d in deps if not (d[0][0] == "e" and d[0][1] == "pe")]
            waits = self._waits(eng, deps)
            if self.cnt[eng] >= EPOCH:
                self.epoch[eng] += 1
                self.cnt[eng] = 0
            self.cnt[eng] += 1
            key = ("e", eng, self.epoch[eng])
            tok = (key, self.cnt[eng])
            self.ops[eng].append((o.fn, waits, self.sem(key), 1))
        else:
            waits = self._waits(eng, deps)
            sb_ = o.sem_buf
            key = ("d", id(sb_))
            inc = getattr(o, "inc", 16)
            sb_.dma_cnt += inc
            tok = (key, sb_.dma_cnt)
            self.ops[eng].append((o.fn, waits, self.sem(key), inc))
        self._commit(o.r, o.w, tok)
        return tok

    def emit(self):
        nc = self.nc
        order = self._schedule()
        toks = {}
        for i in order:
            toks[i] = self._replay_one(self.rec[i])
        for eng, idxs in self.final_waits:
            waits = self._waits(eng, [toks[i] for i in idxs])
            self.ops[eng].append((None, waits, None, 0))

        def replay(engobj, lst):
            for (fn, waits, s, inc) in lst:
                for (ws, v) in waits:
                    engobj.wait_ge(ws, v)
                if fn is not None:
                    fn(engobj).then_inc(s, inc)

        with nc.Block() as block:
            @block.tensor
            def _(e):
                replay(e, self.ops["pe"])

            @block.scalar
            def _(e):
                replay(e, self.ops["act"])

            @block.vector
            def _(e):
                replay(e, self.ops["dve"])

            @block.gpsimd
            def _(e):
                replay(e, self.ops["pool"])

            @block.sync
            def _(e):
                replay(e, self.ops["sp"])

    def close(self):
        self.stack.close()


def _pvec_layout():
    lay = {}
    col = 0

    def add(name, n):
        nonlocal col
        lay[name] = col
        col += n

    for l in range(NL):
        add("n1w%d" % l, 8)
        add("n2w%d" % l, 8)
        for k in range(4):
            add("lcw%d_%d" % (l, k), 4)
        add("lcb%d" % l, 4)
        add("ba%d" % l, 4)
        add("bx%d" % l, 4)
        add("lam%d" % l, 4)
        add("rnw%d" % l, 2)
        add("hnw%d" % l, 2)
        for k in range(3):
            add("fcw%d_%d" % (l, k), NFC)
        add("fcb%d" % l, NFC)
    add("fnw", 8)
    add("hbA", 2)
    add("hbB", 2)
    add("m", 1)
    add("om", 1)
    add("keep", 1)
    add("lbm", 1)
    add("eps", 1)
    add("one", 1)
    lay["_n"] = col
    return lay


PV = _pvec_layout()
RET_GAMMA = [1.0 - 2.0 ** (-5.0 - h) for h in range(4)]


def build(NT, nlayers=NL, debug_out=None):
    NS = NT + SKEW
    S = NS * T
    nc = bass.Bass("TRN2", target_bir_lowering=False)
    xT_d = nc.dram_tensor("xT", [D, S], F32, kind="ExternalInput").ap()
    win_d = nc.dram_tensor("win", [NL, 128, 8, NCOL_IN], F32, kind="ExternalInput").ap()
    wout_d = nc.dram_tensor("wout", [NL, 128, 8, D], F32, kind="ExternalInput").ap()
    wup_d = nc.dram_tensor("wup", [NL, 128, 8, 2 * DFF], F32, kind="ExternalInput").ap()
    wdn_d = nc.dram_tensor("wdn", [NL, 8, 2, 128, 11, 128], F32, kind="ExternalInput").ap()
    gw_d = nc.dram_tensor("gatew", [NL, 128, 8, 128], F32, kind="ExternalInput").ap()
    pv_d = nc.dram_tensor("pvec", [128, PV["_n"]], F32, kind="ExternalInput").ap()
    tab_d = nc.dram_tensor("tabs", [128, 8, S], F32, kind="ExternalInput").ap()
    cst_d = nc.dram_tensor("cst", [128, 6, 128], F32, kind="ExternalInput").ap()
    rm_d = nc.dram_tensor("rmask", [128, T], F32, kind="ExternalInput").ap()
    yT_d = nc.dram_tensor("yT", [D, S], F32, kind="ExternalOutput").ap()
    snd_t = nc.dram_tensor("snd", [D, T], F32)
    rcv_t = [nc.dram_tensor("rcv%d" % i, [2 * D, T], F32) for i in range(2)]
    GROUPS = [[0, 1], [2, 3], [4, 5], [6, 7]]

    sch = Sched(nc)
    sb, ps = sch.sb, sch.ps
    op = sch.op

    pv = sb("pv", [128, PV["_n"]])
    sch.dma("sp", pv[:, :], pv_d[:, :], w=[pv], nbytes=1 << 18)
    cstf = sb("cstf", [128, 6, 128])
    sch.dma("sp", cstf[:, :, :], cst_d[:, :, :], w=[cstf], nbytes=1 << 18)
    cst = sb("cstb", [128, 6, 128], BF16)
    op("dve", lambda e: e.tensor_copy(out=cst[:, :, :], in_=cstf[:, :, :]), r=[cstf], w=[cst], n=768)
    ONES, IDENT, BD64, MASK128, MASK32, PSWAP = range(6)
    rmask = sb("rmask_s", [128, T])
    sch.dma("sp", rmask[:, :], rm_d[:, :], w=[rmask], nbytes=1 << 18)
    gwf = sb("gwf", [128, NL * 8, 128])
    gwb = sb("gwb", [128, NL * 8, 128], BF16)
    for l in range(NL):
        sch.dma("sp", gwf[:, l * 8:(l + 1) * 8, :], gw_d[l], w=[gwf], nbytes=1 << 19)
    op("dve", lambda e: e.tensor_copy(out=gwb[:, :, :], in_=gwf[:, :, :]), r=[gwf], w=[gwb], n=2048)

    def pcol(name, c=0, lo=0, hi=128):
        k = PV[name] + c
        return pv[lo:hi, k:k + 1]

    der = sb("der", [128, 32])
    for l in range(NL):
        k = PV["lam%d" % l]
        op("act", lambda e, k=k, l=l: e.activation(out=der[:, l * 4:l * 4 + 4], in_=pv[:, k:k + 4], func=AF.Exp, scale=-1.0), r=[pv], w=[der], n=4, tbl="exp")
        op("act", lambda e, l=l: e.activation(out=der[:, l * 4:l * 4 + 4], in_=der[:, l * 4:l * 4 + 4], func=AF.Ln, bias=pcol("one"), scale=1.0), r=[der, pv], w=[der], n=4, tbl="exp")
        op("dve", lambda e, l=l: e.tensor_scalar(out=der[:, l * 4:l * 4 + 4], in0=der[:, l * 4:l * 4 + 4], scalar1=-8.0, scalar2=None, op0=ALU.mult), r=[der], w=[der], n=4)
    LB, OML, NOML = 8, 12, 16
    kA, kB = PV["hbA"], PV["hbB"]
    op("dve", lambda e: e.tensor_tensor(out=der[:, 20:22], in0=pv[:, kB:kB + 2], in1=pv[:, kA:kA + 2], op=ALU.subtract), r=[pv, der], w=[der], n=2)
    op("act", lambda e: e.activation(out=der[:, 22:24], in_=der[:, 20:22], func=AF.Sigmoid), r=[der], w=[der], n=2, tbl="sig")
    op("dve", lambda e: e.tensor_scalar(out=der[:, LB:LB + 2], in0=der[:, 22:24], scalar1=pcol("lbm"), scalar2=None, op0=ALU.mult), r=[pv], w=[der], n=2)
    op("dve", lambda e: e.tensor_scalar(out=der[:, OML:OML + 2], in0=der[:, LB:LB + 2], scalar1=-1.0, scalar2=1.0, op0=ALU.mult, op1=ALU.add), r=[], w=[der], n=2)
    op("dve", lambda e: e.tensor_scalar(out=der[:, NOML:NOML + 2], in0=der[:, OML:OML + 2], scalar1=-1.0, scalar2=None, op0=ALU.mult), r=[], w=[der], n=2)

    def dcol(base, l, c, lo=0, hi=128):
        k = base + l * (4 if base == 0 else 2) + c
        return der[lo:hi, k:k + 1]

    hst_t = [sb("hst%d" % l, [128, 4]) for l in range(NL)]
    lxh_t = [sb("lxh%d" % l, [128, 4, 4]) for l in range(NL)]
    fh_t = [sb("fh%d" % l, [128, NFC, 2]) for l in range(NL)]
    hst = {}
    lxh = {}
    fh = {}
    st_f = {}
    st_b = {}
    for l in range(NL):
        for c in range(4):
            hst[(l, c)] = hst_t[l].sub("hst%d_%d" % (l, c))
            lxh[(l, c)] = lxh_t[l].sub("lxh%d_%d" % (l, c))
        for ch in range(NFC):
            fh[(l, ch)] = fh_t[l].sub("fh%d_%d" % (l, ch))
        op("dve", lambda e, l=l: e.memset(hst_t[l][:, :], 0.0), w=[hst[(l, c)] for c in range(4)], n=4)
        op("dve", lambda e, l=l: e.memset(lxh_t[l][:, :, :], 0.0), w=[lxh[(l, c)] for c in range(4)], n=16)
        op("dve", lambda e, l=l: e.memset(fh_t[l][:, :, :], 0.0), w=[fh[(l, ch)] for ch in range(NFC)], n=88)
        for mix in ("r", "h"):
            for p in range(2):
                tf = sb("stf_%s%d%d" % (mix, l, p), [128, 64])
                tb = sb("stb_%s%d%d" % (mix, l, p), [128, 64], BF16)
                for hh in range(2):
                    bf_ = tf.sub("stf_%s%d%d%d" % (mix, l, p, hh))
                    bb_ = tb.sub("stb_%s%d%d%d" % (mix, l, p, hh))
                    st_f[(mix, l, p, hh)] = bf_
                    st_b[(mix, l, p, hh)] = bb_
                    lo = hh * 64
                    op("dve", lambda e, bf_=bf_, lo=lo: e.memset(bf_[lo:lo + 64, :], 0.0), w=[bf_], n=64)
                    op("dve", lambda e, bb_=bb_, lo=lo: e.memset(bb_[lo:lo + 64, :], 0.0), w=[bb_], n=64)

    xres = [sb("xres%d" % i, [128, 8, T]) for i in range(2)]
    rv = sb("rv", [128, 8, T])
    tabs = sb("tabs_s", [128, 8, T])
    hT = sb("hT", [128, 8, T], BF16)
    yT = sb("yTs", [128, 8, T], BF16)
    yTc = [yT.sub("yT%d" % c) for c in range(8)]
    gT = sb("gT", [128, 22, T], BF16)
    gTc = [gT.sub("gT%d" % c) for c in range(22)]
    NSLOT = 5
    wsl = [sb("wsl%d" % i, [128, 4096], BF16) for i in range(NSLOT)]

    class FB:
        def __init__(self, name):
            self.b = sb(name, [128, T + 4])
            self.h = self.b.sub(name + "_h")

        @property
        def d(self):
            return self.b[:, 4:4 + T]

    NF = 14
    Fring = [FB("F%d" % i) for i in range(NF)]
    fstate = {"f": 0, "b": 0, "a": 0, "x": 0, "at": 0}

    def getF():
        f = Fring[fstate["f"] % NF]
        fstate["f"] += 1
        return f

    NB = 10
    Bring = [sb("B%d" % i, [128, T], BF16) for i in range(NB)]

    def getB():
        b = Bring[fstate["b"] % NB]
        fstate["b"] += 1
        return b

    vtok = sb("vtok", [128, 8, 256], BF16)
    ktok = sb("ktok", [128, 8, 2, 128], BF16)
    NAT = 6
    ATb = [sb("AT%d" % i, [128, 128], BF16) for i in range(NAT)]
    ebl = [sb("ebl%d" % p, [128, 16]) for p in range(2)]
    stmp = [sb("stmp%d" % i, [128, 64]) for i in range(4)]

    banks = [ps("bank%d" % i, [128, T]) for i in range(8)]
    for i_, b_ in enumerate(banks):
        b_.touch = -100 + i_
        b_.held = False

    def getA(hold=False):
        a = min((b for b in banks if not b.held), key=lambda b: b.touch)
        a.touch = len(sch.rec)
        a.held = hold
        return a

    wstate = {"n": 0}

    def wload(src_ap, shape):
        s = wsl[wstate["n"] % NSLOT]
        wstate["n"] += 1
        n = 1
        for d_ in shape[1:]:
            n *= d_
        view = s[:, 0:n].rearrange("p (a b) -> p a b", b=shape[2])
        sch.dma("pool", view, src_ap, w=[s], nbytes=128 * n * 4)
        return s, view

    def proj(slot, wview, col, rhs_bufs, rhs_of_kc, nk=8, m=128, ncols=T):
        pbuf = getA()

        def fn(e):
            ins = None
            for kc in range(nk):
                ins = e.matmul(pbuf[0:m, 0:ncols], lhsT=wview[:, kc, col:col + m], rhs=rhs_of_kc(kc), start=(kc == 0), stop=(kc == nk - 1))
            return ins

        op("pe", fn, r=[slot] + list(rhs_bufs), w=[pbuf], n=ncols, nmm=nk)
        return pbuf

    def sq_chunk(xr, m):
        op("act", lambda e, m=m: e.activation(out=gT[:, m, :], in_=xr[:, m, :], func=AF.Square), r=[xr], w=[gTc[m]])

    def rms_rstd(xr, use_hT=False, presq=False):
        scr = hT if use_hT else gT
        scr_b = [hT] if use_hT else gTc[0:8]
        if not presq:
            op("act", lambda e: e.activation(out=scr[:, 0:8, :], in_=xr[:, :, :], func=AF.Square), r=[xr], w=scr_b, n=8 * T)
        pS_ = getA()

        def fn(e):
            ins = None
            for c in range(8):
                ins = e.matmul(pS_[:, :], lhsT=cst[:, ONES, :], rhs=scr[:, c, :], start=(c == 0), stop=(c == 7))
            return ins

        op("pe", fn, r=[cst] + scr_b, w=[pS_], nmm=8)
        op("act", lambda e: e.activation(out=pS_[:, :], in_=pS_[:, :], func=AF.Ln, bias=pcol("eps"), scale=1.0 / D), r=[pv], w=[pS_], tbl="exp")
        op("act", lambda e: e.activation(out=pS_[:, :], in_=pS_[:, :], func=AF.Exp, scale=-0.5), r=[], w=[pS_], tbl="exp")
        pS_.held = True
        return pS_

    def norm_to_hT(xr, nwname, use_hT=False, presq=False):
        rs = rms_rstd(xr, use_hT, presq)
        for c in range(8):
            op("dve", lambda e, c=c: e.scalar_tensor_tensor(out=hT[:, c, :], in0=xr[:, c, :], scalar=pcol(nwname, c), in1=rs[:, :],
                                                              op0=ALU.mult, op1=ALU.mult), r=[xr, pv], w=[hT, rs])
        rs.held = False

    hk = lambda kc: hT[:, kc, :]

    def emit_layer(l, xr, ti):
        norm_to_hT(xr, "n1w%d" % l, use_hT=True)

        s0, w0 = wload(win_d[l, :, :, 0:512], [128, 8, 512])
        s1, w1 = wload(win_d[l, :, :, 512:1024], [128, 8, 512])
        for c in range(4):
            lxb, xc, rr, ii, aa, mm = getF(), getF(), getF(), getF(), getF(), getF()
            xcb = getB()
            pa = proj(s0, w0, c * 128, [hT], hk)
            op("pool", lambda e, c=c, lxb=lxb: e.tensor_copy(out=lxb.b[:, 1:4], in_=lxh_t[l][:, c, 0:3]), r=[lxh[(l, c)]], w=[lxb.h], n=3)
            op("act", lambda e, lxb=lxb, pa=pa: e.activation(out=lxb.d, in_=pa[:, :], func=AF.Copy), r=[], w=[lxb.b, pa])
            op("pool", lambda e, c=c, lxb=lxb: e.tensor_copy(out=lxh_t[l][:, c, 0:3], in_=lxb.b[:, T + 1:T + 4]), r=[lxb.b], w=[lxh[(l, c)]], n=3)
            op("dve", lambda e, c=c, lxb=lxb, xc=xc: e.tensor_scalar(out=xc.d, in0=lxb.b[:, 1:1 + T], scalar1=pcol("lcw%d_0" % l, c), scalar2=pcol("lcb%d" % l, c),
                                                                      op0=ALU.mult, op1=ALU.add), r=[lxb.b, lxb.h, pv], w=[xc.b])
            for k in range(1, 4):
                op("dve", lambda e, c=c, k=k, lxb=lxb, xc=xc: e.scalar_tensor_tensor(out=xc.d, in0=lxb.b[:, 1 + k:1 + k + T], scalar=pcol("lcw%d_%d" % (l, k), c), in1=xc.d,
                                                                                      op0=ALU.mult, op1=ALU.add), r=[lxb.b, lxb.h, pv], w=[xc.b], k=2)
            op("act", lambda e, xc=xc, xcb=xcb: e.activation(out=xcb[:, :], in_=xc.d, func=AF.Copy), r=[xc.b], w=[xcb])
            pg = getA()
            op("pe", lambda e, c=c, pg=pg, xcb=xcb: e.matmul(pg[:, :], lhsT=gwb[:, l * 8 + c * 2, :], rhs=xcb[:, :], start=True, stop=True), r=[gwb, xcb], w=[pg])
            op("act", lambda e, c=c, pg=pg, rr=rr: e.activation(out=rr.d, in_=pg[:, :], func=AF.Sigmoid, bias=pcol("ba%d" % l, c)), r=[pv], w=[rr.b, pg], tbl="sig")
            pg2 = getA()
            op("pe", lambda e, c=c, pg2=pg2, xcb=xcb: e.matmul(pg2[:, :], lhsT=gwb[:, l * 8 + c * 2 + 1, :], rhs=xcb[:, :], start=True, stop=True), r=[gwb, xcb], w=[pg2])
            op("act", lambda e, c=c, pg2=pg2, ii=ii: e.activation(out=ii.d, in_=pg2[:, :], func=AF.Sigmoid, bias=pcol("bx%d" % l, c)), r=[pv], w=[ii.b, pg2], tbl="sig")
            op("act", lambda e, c=c, rr=rr, aa=aa: e.activation(out=aa.d, in_=rr.d, func=AF.Exp, scale=dcol(0, l, c)), r=[rr.b, der], w=[aa.b], tbl="exp")
            op("act", lambda e, aa=aa, mm=mm: e.activation(out=mm.d, in_=aa.d, func=AF.Square), r=[aa.b], w=[mm.b])
            op("act", lambda e, mm=mm: e.activation(out=mm.d, in_=mm.d, func=AF.Sqrt, bias=pcol("one"), scale=-1.0), r=[pv], w=[mm.b], tbl="sqrt")
            op("pool", lambda e, ii=ii, xc=xc: e.tensor_tensor(out=ii.d, in0=ii.d, in1=xc.d, op=ALU.mult), r=[xc.b], w=[ii.b], k=2)
            op("pool", lambda e, ii=ii, mm=mm: e.tensor_tensor(out=ii.d, in0=ii.d, in1=mm.d, op=ALU.mult), r=[mm.b], w=[ii.b], k=2)
            hs = hst[(l, c)]
            op("dve", lambda e, c=c, rr=rr, aa=aa, ii=ii: e.tensor_tensor_scan(out=rr.d, data0=aa.d, data1=ii.d, initial=hst_t[l][:, c:c + 1],
                                                                                op0=ALU.mult, op1=ALU.add), r=[aa.b, ii.b, hs], w=[rr.b], k=2)
            op("dve", lambda e, c=c, rr=rr: e.tensor_copy(out=hst_t[l][:, c:c + 1], in_=rr.b[:, T + 3:T + 4]), r=[rr.b], w=[hs], n=1)
            pb = proj(s1, w1, c * 128, [hT], hk)
            op("act", lambda e, mm=mm, pb=pb: e.activation(out=mm.d, in_=pb[:, :], func=AF.Gelu_apprx_tanh), r=[], w=[mm.b, pb], tbl="gelu")
            op("dve", lambda e, c=c, rr=rr, mm=mm: e.tensor_tensor(out=yT[:, c, :], in0=rr.d, in1=mm.d, op=ALU.mult), r=[rr.b, mm.b], w=[yTc[c]], k=2)

        def gla(mix, qT, kT, vT, C, BLK, sgate, nwname, ycol0, decay_const):
            nblk = T // BLK
            nch = T // C
            msk = MASK128 if C == 128 else MASK32
            for p in range(2):
                for b_ in range(nblk):
                    for (srcT, is_k) in ((kT, True), (vT, False)):
                        pt = getA()
                        ptv = pt[:, 0:64].bitcast(BF16)
                        op("pe", lambda e, p=p, b_=b_, ptv=ptv, srcT=srcT: e.transpose(ptv[0:BLK, 0:128], srcT[p][:, b_ * BLK:(b_ + 1) * BLK], cst[:, IDENT, :]),
                           r=[srcT[p], cst], w=[pt], n=128)
                        if is_k:
                            op("act", lambda e, p=p, b_=b_, ptv=ptv: e.activation(out=ktok[0:BLK, b_, p, :], in_=ptv[0:BLK, 0:128], func=AF.Copy), r=[], w=[ktok, pt], n=128)
                        else:
                            op("act", lambda e, p=p, b_=b_, ptv=ptv: e.activation(out=vtok[0:BLK, b_, p * 128:(p + 1) * 128], in_=ptv[0:BLK, 0:128], func=AF.Copy),
                               r=[], w=[vtok, pt], n=128)
            pO = [getA(True), getA(True)]
            pX = [getA(True), getA(True), getA(True)]
            pUT = getA(True)
            for j in range(nch):
                blk = (j * C) // BLK
                base = (j * C) % BLK
                tsl = slice(j * C, (j + 1) * C)
                for p in range(2):
                    for hh in range(2):
                        h = 2 * p + hh
                        off = hh * 64
                        px = pX[fstate["x"] % 3]
                        fstate["x"] += 1
                        at = ATb[fstate["at"] % NAT]
                        fstate["at"] += 1
                        Sb = st_b[(mix, l, p, hh)]
                        op("pe", lambda e, p=p, off=off, tsl=tsl, px=px, base=base: e.matmul(
                            px[base:base + C, 0:C], lhsT=kT[p][off:off + 64, tsl], rhs=qT[p][off:off + 64, tsl], start=True, stop=True),
                            r=[kT[p], qT[p]], w=[px], n=C)
                        op("dve", lambda e, px=px, at=at, base=base: e.tensor_tensor(
                            out=at[base:base + C, 0:C], in0=px[base:base + C, 0:C], in1=cst[base:base + C, msk, 0:C], op=ALU.mult),
                            r=[cst], w=[at, px], n=C)

                        def fn_o(e, p=p, off=off, tsl=tsl, at=at, base=base, blk=blk, h=h, Sb=Sb):
                            e.matmul(pO[p][off:off + 64, tsl], lhsT=vtok[base:base + C, blk, h * 64:(h + 1) * 64], rhs=at[base:base + C, 0:C], start=True, stop=False)
                            return e.matmul(pO[p][off:off + 64, tsl], lhsT=Sb[off:off + 64, :], rhs=qT[p][off:off + 64, tsl], start=False, stop=True)

                        op("pe", fn_o, r=[vtok, at, Sb, qT[p]], w=[pO[p]], n=C, nmm=2)
                    op("pe", lambda e, p=p, base=base, blk=blk: e.matmul(
                        pUT[:, 0:128], lhsT=ktok[base:base + C, blk, p, :], rhs=vtok[base:base + C, blk, p * 128:(p + 1) * 128],
                        start=True, stop=True), r=[ktok, vtok], w=[pUT], n=128)
                    for hh in range(2):
                        h = 2 * p + hh
                        off = hh * 64
                        Sf = st_f[(mix, l, p, hh)]
                        Sb = st_b[(mix, l, p, hh)]
                        if decay_const is not None:
                            g = decay_const[h]
                            op("dve", lambda e, Sf=Sf, off=off, g=g: e.scalar_tensor_tensor(
                                out=Sf[off:off + 64, :], in0=Sf[off:off + 64, :], scalar=g, in1=pUT[off:off + 64, off:off + 64], op0=ALU.mult, op1=ALU.add),
                                r=[], w=[Sf, pUT], n=64)
                            op("dve", lambda e, Sf=Sf, Sb=Sb, off=off, g=g: e.tensor_scalar(
                                out=Sb[off:off + 64, :], in0=Sf[off:off + 64, :], scalar1=g, scalar2=None, op0=ALU.mult), r=[Sf], w=[Sb], n=64)
                        else:
                            tmp = stmp[h]
                            op("dve", lambda e, Sf=Sf, off=off, tmp=tmp: e.tensor_tensor(
                                out=tmp[off:off + 64, :], in0=pUT[off:off + 64, off:off + 64], in1=Sf[off:off + 64, :], op=ALU.add), r=[Sf], w=[tmp, pUT], n=64)
                            op("dve", lambda e, Sf=Sf, off=off, tmp=tmp, p=p, j=j: e.tensor_scalar(
                                out=Sf[off:off + 64, :], in0=tmp[off:off + 64, :], scalar1=ebl[p][off:off + 64, j:j + 1], scalar2=None, op0=ALU.mult),
                                r=[tmp, ebl[p]], w=[Sf], n=64)
                            op("act", lambda e, Sf=Sf, Sb=Sb, off=off: e.activation(out=Sb[off:off + 64, :], in_=Sf[off:off + 64, :], func=AF.Copy), r=[Sf], w=[Sb], n=64)
            for p in range(2):
                o, rt = getF(), getF()
                osq = getB()
                op("act", lambda e, p=p, o=o: e.activation(out=o.d, in_=pO[p][:, :], func=AF.Copy), r=[], w=[o.b, pO[p]])
                op("act", lambda e, o=o, osq=osq: e.activation(out=osq[:, :], in_=o.d, func=AF.Square), r=[o.b], w=[osq])
                pS_ = getA()
                op("pe", lambda e, pS_=pS_, osq=osq: e.matmul(pS_[:, :], lhsT=cst[:, BD64, :], rhs=osq[:, :], start=True, stop=True), r=[cst, osq], w=[pS_])
                op("act", lambda e, pS_=pS_, rt=rt: e.activation(out=rt.d, in_=pS_[:, :], func=AF.Ln, bias=pcol("eps"), scale=1.0), r=[pv], w=[rt.b, pS_], tbl="exp")
                op("act", lambda e, rt=rt: e.activation(out=rt.d, in_=rt.d, func=AF.Exp, scale=-0.5), r=[], w=[rt.b], tbl="exp")
                op("dve", lambda e, p=p, o=o, rt=rt: e.scalar_tensor_tensor(out=o.d, in0=o.d, scalar=pcol(nwname, p), in1=rt.d, op0=ALU.mult, op1=ALU.mult),
                   r=[rt.b, pv], w=[o.b], k=2)
                op("pool", lambda e, p=p, o=o: e.tensor_tensor(out=yT[:, ycol0 + p, :], in0=o.d, in1=sgate[p].d, op=ALU.mult), r=[o.b, sgate[p].b], w=[yTc[ycol0 + p]], k=2)
            for b_ in pO + pX + [pUT]:
                b_.held = False

        s2, w2 = wload(win_d[l, :, :, 1024:1536], [128, 8, 512])
        s3, w3 = wload(win_d[l, :, :, 1536:2048], [128, 8, 512])
        if l == 0:
            sch.dma("sp", tabs[:, :, :], tab_d[:, :, ti * T:(ti + 1) * T], w=[tabs], nbytes=2 << 20)
        qT = [getB(), getB()]
        kT = [getB(), getB()]
        for dst, cbase, tb in ((qT, 0, 0), (kT, 256, 2)):
            for p in range(2):
                pa = proj(s2, w2, cbase + p * 128, [hT], hk)
                qb = getB()
                op("act", lambda e, qb=qb, pa=pa: e.activation(out=qb[:, :], in_=pa[:, :], func=AF.Copy), r=[], w=[qb, pa])
                pb = getA()
                op("pe", lambda e, pb=pb, qb=qb: e.matmul(pb[:, :], lhsT=cst[:, PSWAP, :], rhs=qb[:, :], start=True, stop=True), r=[cst, qb], w=[pb])
                t1, t2 = getF(), getF()
                op("dve", lambda e, p=p, tb=tb, t1=t1, pa=pa: e.tensor_tensor(out=t1.d, in0=pa[:, :], in1=tabs[:, p * 4 + tb, :], op=ALU.mult), r=[tabs], w=[t1.b, pa])
                op("dve", lambda e, p=p, tb=tb, t2=t2, pb=pb: e.tensor_tensor(out=t2.d, in0=pb[:, :], in1=tabs[:, p * 4 + tb + 1, :], op=ALU.mult), r=[tabs], w=[t2.b, pb])
                op("pool", lambda e, p=p, dst=dst, t1=t1, t2=t2: e.tensor_tensor(out=dst[p][:, :], in0=t1.d, in1=t2.d, op=ALU.add), r=[t1.b, t2.b], w=[dst[p]], k=2)
        vT = [getB(), getB()]
        for p in range(2):
            pa = proj(s3, w3, p * 128, [hT], hk)
            op("act", lambda e, p=p, pa=pa, vT=vT: e.activation(out=vT[p][:, :], in_=pa[:, :], func=AF.Copy), r=[], w=[vT[p], pa])
        sg = [getF(), getF()]
        for p in range(2):
            pa = proj(s3, w3, 256 + p * 128, [hT], hk)
            op("act", lambda e, p=p, pa=pa, sg=sg: e.activation(out=sg[p].d, in_=pa[:, :], func=AF.Silu), r=[], w=[sg[p].b, pa], tbl="silu")
        gla("r", qT, kT, vT, 128, 128, sg, "rnw%d" % l, 4, [g ** 128 for g in RET_GAMMA])

        s4, w4 = wload(win_d[l, :, :, 2048:2560], [128, 8, 512])
        s5, w5 = wload(win_d[l, :, :, 2560:3072], [128, 8, 512])
        qT = [getB(), getB()]
        kT = [getB(), getB()]
        for p in range(2):
            sig, ff, bb, en = getF(), getF(), getF(), getF()
            pa = proj(s4, w4, 256 + p * 128, [hT], hk)
            op("act", lambda e, sig=sig, pa=pa: e.activation(out=sig.d, in_=pa[:, :], func=AF.Sigmoid), r=[], w=[sig.b, pa], tbl="sig")
            op("dve", lambda e, p=p, sig=sig, ff=ff: e.tensor_scalar(out=ff.d, in0=sig.d, scalar1=dcol(OML, l, p), scalar2=dcol(LB, l, p), op0=ALU.mult, op1=ALU.add),
               r=[sig.b, der], w=[ff.b])
            op("act", lambda e, ff=ff: e.activation(out=ff.d, in_=ff.d, func=AF.Ln), r=[], w=[ff.b], tbl="exp")
            op("dve", lambda e, ff=ff, bb=bb: e.tensor_tensor_scan(out=bb.d, data0=rmask[:, :], data1=ff.d, initial=0.0, op0=ALU.mult, op1=ALU.add),
               r=[rmask, ff.b], w=[bb.b], k=2)
            op("pool", lambda e, p=p, sig=sig: e.tensor_scalar(out=sig.d, in0=sig.d, scalar1=dcol(NOML, l, p), scalar2=dcol(OML, l, p), op0=ALU.mult, op1=ALU.add),
               r=[der], w=[sig.b])
            op("act", lambda e, bb=bb, en=en: e.activation(out=en.d, in_=bb.d, func=AF.Exp, scale=-1.0), r=[bb.b], w=[en.b], tbl="exp")
            op("dve", lambda e, p=p, sig=sig, en=en, kT=kT: e.tensor_tensor(out=kT[p][:, :], in0=sig.d, in1=en.d, op=ALU.mult), r=[sig.b, en.b], w=[kT[p]], k=2)
            op("act", lambda e, bb=bb, ff=ff: e.activation(out=ff.d, in_=bb.d, func=AF.Exp), r=[bb.b], w=[ff.b], tbl="exp")
            op("dve", lambda e, p=p, ff=ff: e.tensor_copy(out=ebl[p][:, :], in_=ff.d.rearrange("p (c t) -> p c t", t=32)[:, :, 31]), r=[ff.b], w=[ebl[p]], n=16)
            pb = proj(s4, w4, p * 128, [hT], hk)
            op("act", lambda e, en=en, pb=pb: e.activation(out=en.d, in_=pb[:, :], func=AF.Silu), r=[], w=[en.b, pb], tbl="silu")
            op("dve", lambda e, p=p, en=en, ff=ff, qT=qT: e.tensor_tensor(out=qT[p][:, :], in0=en.d, in1=ff.d, op=ALU.mult), r=[en.b, ff.b], w=[qT[p]], k=2)
        vT = [getB(), getB()]
        for p in range(2):
            pa = proj(s5, w5, p * 128, [hT], hk)
            op("act", lambda e, p=p, pa=pa, vT=vT: e.activation(out=vT[p][:, :], in_=pa[:, :], func=AF.Copy), r=[], w=[vT[p], pa])
        sg = [getF(), getF()]
        for p in range(2):
            pa = proj(s5, w5, 256 + p * 128, [hT], hk)
            op("act", lambda e, p=p, pa=pa, sg=sg: e.activation(out=sg[p].d, in_=pa[:, :], func=AF.Silu), r=[], w=[sg[p].b, pa], tbl="silu")
        gla("h", qT, kT, vT, 32, 64, sg, "hnw%d" % l, 6, None)

        for half in range(2):
            so, wo = wload(wout_d[l, :, :, half * 512:(half + 1) * 512], [128, 8, 512])
            for mq in range(4):
                m = half * 4 + mq
                pb = proj(so, wo, mq * 128, yTc, lambda kc: yT[:, kc, :])
                op("dve", lambda e, m=m, pb=pb: e.tensor_tensor(out=xr[:, m, :], in0=pb[:, :], in1=xr[:, m, :], op=ALU.add), r=[], w=[xr, pb])
                sq_chunk(xr, m)

        def down_half(hf):
            for m in range(8):
                sd, wd = wload(wdn_d[l, m, hf], [128, 11, 128])
                pb = getA()

                def fn_d(e, pb=pb, wd=wd):
                    ins = None
                    for kc in range(11):
                        ins = e.matmul(pb[:, :], lhsT=wd[:, kc, :], rhs=gT[:, hf * 11 + kc, :], start=(kc == 0), stop=(kc == 10))
                    return ins

                op("pe", fn_d, r=[sd] + gTc[hf * 11:(hf + 1) * 11], w=[pb], nmm=11)
                op("dve", lambda e, m=m, pb=pb: e.tensor_tensor(out=xr[:, m, :], in0=pb[:, :], in1=xr[:, m, :], op=ALU.add), r=[], w=[xr, pb])
                if hf == 1:
                    sq_chunk(xr, m)

        norm_to_hT(xr, "n2w%d" % l, presq=True)
        for q in range(11):
            su, wu = wload(wup_d[l, :, :, q * 512:(q + 1) * 512], [128, 8, 512])
            for jj in range(2):
                j = 2 * q + jj
                outs = []
                for (half, col) in ((0, jj * 128), (1, 256 + jj * 128)):
                    ch = half * 22 + j
                    cb = getF()
                    pb = proj(su, wu, col, [hT], hk)
                    fhb = fh[(l, ch)]
                    op("act", lambda e, cb=cb, pb=pb, ch=ch: e.activation(out=cb.d, in_=pb[:, :], func=AF.Identity, bias=pcol("fcb%d" % l, ch), scale=pcol("fcw%d_2" % l, ch)),
                       r=[pv], w=[cb.b, pb])
                    op("dve", lambda e, cb=cb, pb=pb, ch=ch: e.scalar_tensor_tensor(out=cb.b[:, 5:4 + T], in0=pb[:, 0:T - 1], scalar=pcol("fcw%d_1" % l, ch), in1=cb.b[:, 5:4 + T],
                                                                                     op0=ALU.mult, op1=ALU.add), r=[pv], w=[cb.b, pb])
                    op("dve", lambda e, cb=cb, pb=pb, ch=ch: e.scalar_tensor_tensor(out=cb.b[:, 6:4 + T], in0=pb[:, 0:T - 2], scalar=pcol("fcw%d_0" % l, ch), in1=cb.b[:, 6:4 + T],
                                                                                     op0=ALU.mult, op1=ALU.add), r=[pv], w=[cb.b, pb])
                    op("dve", lambda e, cb=cb, ch=ch: e.scalar_tensor_tensor(out=cb.b[:, 4:5], in0=fh_t[l][:, ch, 1:2], scalar=pcol("fcw%d_1" % l, ch), in1=cb.b[:, 4:5],
                                                                              op0=ALU.mult, op1=ALU.add), r=[pv, fhb], w=[cb.b], n=1)
                    op("dve", lambda e, cb=cb, ch=ch: e.scalar_tensor_tensor(out=cb.b[:, 4:6], in0=fh_t[l][:, ch, 0:2], scalar=pcol("fcw%d_0" % l, ch), in1=cb.b[:, 4:6],
                                                                              op0=ALU.mult, op1=ALU.add), r=[pv, fhb], w=[cb.b], n=2)
                    op("act", lambda e, pb=pb, ch=ch: e.activation(out=fh_t[l][:, ch, :], in_=pb[:, T - 2:T], func=AF.Copy), r=[], w=[fhb, pb], n=2)
                    outs.append(cb)
                cg, cv = outs
                op("act", lambda e, cg=cg: e.activation(out=cg.d, in_=cg.d, func=AF.Silu), r=[], w=[cg.b], tbl="silu")
                op("pool", lambda e, cg=cg, cv=cv, j=j: e.tensor_tensor(out=gT[:, j, :], in0=cg.d, in1=cv.d, op=ALU.mult), r=[cg.b, cv.b], w=[gTc[j]], k=2)
            if q == 5:
                down_half(0)
        down_half(1)

    xv = xT_d.rearrange("(c p) s -> p c s", p=128)
    yv = yT_d.rearrange("(c p) s -> p c s", p=128)
    sndB = Buf(snd_t, "snd")
    rcvB = [Buf(rcv_t[i], "rcv%d" % i) for i in range(2)]
    snd_v = snd_t.ap().rearrange("(c p) s -> p c s", p=128)
    out_toks = []
    for st in range(NS):
        xr = xres[st % 2]
        sch.dma("sp", xr[:, :, :], xv[:, :, st * T:(st + 1) * T], w=[xr], nbytes=2 << 20)
        op("act", lambda e, xr=xr: e.activation(out=xr[:, :, :], in_=xr[:, :, :], func=AF.Copy, scale=pcol("m")), r=[pv], w=[xr], n=8 * T)
        if st >= SKEW:
            rb = rcvB[(st - SKEW) % 2]
            rcv_v = rcv_t[(st - SKEW) % 2].ap()[0:D, :].rearrange("(c p) s -> p c s", p=128)
            sch.dma("sp", rv[:, :, :], rcv_v, r=[rb], w=[rv], nbytes=2 << 20)
            op("dve", lambda e, xr=xr: e.scalar_tensor_tensor(out=xr[:, :, :], in0=rv[:, :, :], scalar=pcol("om"), in1=xr[:, :, :], op0=ALU.mult, op1=ALU.add),
               r=[rv, pv], w=[xr], n=8 * T, k=2)
        emit_layer(0, xr, st)
        if st < NT:
            sch.dma("sp", snd_v, xr[:, :, :], r=[xr], w=[sndB], nbytes=2 << 20)
            rb = rcvB[st % 2]
            sch.raw("pool", lambda e, rb=rb: e.collective_compute("AllGather", ALU.bypass, replica_groups=GROUPS,
                                                                   ins=[snd_t.ap().opt()], outs=[rb.t.ap().opt()]),
                    r=[sndB], w=[rb], sem_buf=rb, inc=1, occ=1.0, lat=120.0)
        if st == SKEW - 1:
            kp = pcol("keep")
            op("dve", lambda e: e.tensor_scalar(out=hst_t[0][:, :], in0=hst_t[0][:, :], scalar1=kp, scalar2=None, op0=ALU.mult), r=[pv], w=[hst[(0, c)] for c in range(4)], n=4)
            op("dve", lambda e: e.tensor_scalar(out=lxh_t[0][:, :, :], in0=lxh_t[0][:, :, :], scalar1=kp, scalar2=None, op0=ALU.mult), r=[pv], w=[lxh[(0, c)] for c in range(4)], n=16)
            op("dve", lambda e: e.tensor_scalar(out=fh_t[0][:, :, :], in0=fh_t[0][:, :, :], scalar1=kp, scalar2=None, op0=ALU.mult), r=[pv], w=[fh[(0, ch)] for ch in range(NFC)], n=88)
            for key in list(st_f.keys()):
                lo = key[3] * 64
                for tbl in (st_f, st_b):
                    bb_ = tbl[key]
                    op("dve", lambda e, bb_=bb_, lo=lo: e.tensor_scalar(out=bb_[lo:lo + 64, :], in0=bb_[lo:lo + 64, :], scalar1=pv[lo:lo + 64, PV["keep"]:PV["keep"] + 1],
                                                                          scalar2=None, op0=ALU.mult), r=[pv], w=[bb_], n=64)
        rs = rms_rstd(xr, presq=True)
        for c in range(8):
            op("dve", lambda e, c=c, xr=xr, rs=rs: e.scalar_tensor_tensor(out=xr[:, c, :], in0=xr[:, c, :], scalar=pcol("fnw", c), in1=rs[:, :],
                                                                           op0=ALU.mult, op1=ALU.mult), r=[pv], w=[xr, rs])
        rs.held = False
        out_toks.append(sch.dma("sp", yv[:, :, st * T:(st + 1) * T], xr[:, :, :], r=[xr], nbytes=2 << 20))
    sch.wait_all_on("sp", out_toks)
    sch.emit()
    return nc, sch


def _chunks(v):
    v = np.asarray(v, np.float32)
    return np.ascontiguousarray(v.reshape(-1, 128).T)


def make_tables(S):
    t = np.arange(S, dtype=np.float64)
    inv = 10000.0 ** (-np.arange(0, 64, 2, dtype=np.float64) / 64.0)
    ang = t[None, :] * inv[:, None]
    cos = np.concatenate([np.cos(ang), np.cos(ang)], 0)
    sin = np.concatenate([-np.sin(ang), np.sin(ang)], 0)
    n1 = (t % 128) + 1.0
    tabs = np.zeros((128, 8, S), np.float64)
    for p in range(2):
        for hh in range(2):
            h = 2 * p + hh
            lg = math.log(RET_GAMMA[h])
            qd = np.exp(n1 * lg)[None, :]
            kd = np.exp(-n1 * lg)[None, :] * (64.0 ** -0.5)
            sl = slice(hh * 64, hh * 64 + 64)
            tabs[sl, p * 4 + 0] = cos * qd
            tabs[sl, p * 4 + 1] = sin * qd
            tabs[sl, p * 4 + 2] = cos * kd
            tabs[sl, p * 4 + 3] = sin * kd
    return tabs.astype(np.float32)


def make_consts():
    cst = np.zeros((128, 6, 128), np.float32)
    for m_ in range(128):
        cst[(m_ // 64) * 64 + ((m_ % 64) + 32) % 64, 5, m_] = 1.0
    cst[:, 0, :] = 1.0
    cst[:, 1, :] = np.eye(128, dtype=np.float32)
    for b in range(2):
        cst[b * 64:(b + 1) * 64, 2, b * 64:(b + 1) * 64] = 1.0 / 64.0
    m = np.arange(128)[:, None]
    n = np.arange(128)[None, :]
    cst[:, 3, :] = (n >= m).astype(np.float32)
    m32 = (np.arange(32)[None, :] >= np.arange(32)[:, None]).astype(np.float32)
    for b in range(4):
        cst[b * 32:(b + 1) * 32, 4, 0:32] = m32
    rmask = np.ones((128, T), np.float32)
    rmask[:, ::32] = 0.0
    return cst, rmask


def prep_role(inp, role, NT):
    lyr = role
    NS = NT + SKEW
    w_in = np.asarray(inp["w_in"], np.float32)[lyr:lyr + 1]
    win = w_in.reshape(1, 8, 128, NCOL_IN).transpose(0, 2, 1, 3)
    wout = np.asarray(inp["w_out"], np.float32)[lyr:lyr + 1].reshape(1, 8, 128, D).transpose(0, 2, 1, 3)
    perm = []
    for q in range(11):
        perm += list(range(2 * q * 128, (2 * q + 2) * 128))
        perm += list(range(DFF + 2 * q * 128, DFF + (2 * q + 2) * 128))
    wup = np.asarray(inp["ffn_w_up"], np.float32)[lyr:lyr + 1][:, :, perm].reshape(1, 8, 128, 2 * DFF).transpose(0, 2, 1, 3)
    wdn = np.asarray(inp["ffn_w_down"], np.float32)[lyr:lyr + 1].reshape(1, 2, 11, 128, 8, 128).transpose(0, 4, 1, 3, 2, 5)
    gw = np.zeros((1, 128, 8, 128), np.float32)
    wa = np.asarray(inp["lru_wa"], np.float32)
    wx = np.asarray(inp["lru_wx"], np.float32)
    for c in range(4):
        for b2 in range(2):
            sl = slice(b2 * 64, b2 * 64 + 64)
            gw[0, sl, c * 2 + 0, sl] = wa[lyr, 2 * c + b2]
            gw[0, sl, c * 2 + 1, sl] = wx[lyr, 2 * c + b2]
    pvec = np.zeros((128, PV["_n"]), np.float32)

    def put(name, v):
        a = _chunks(v)
        pvec[:, PV[name]:PV[name] + a.shape[1]] = a

    put("n1w0", inp["norm1_w"][lyr])
    put("n2w0", inp["norm2_w"][lyr])
    for k in range(4):
        put("lcw0_%d" % k, inp["lru_conv_w"][lyr][k])
    put("lcb0", inp["lru_conv_b"][lyr])
    put("ba0", inp["lru_ba"][lyr])
    put("bx0", inp["lru_bx"][lyr])
    put("lam0", inp["lru_lambda"][lyr])
    put("rnw0", inp["ret_norm_w"][lyr])
    put("hnw0", inp["hg_norm_w"][lyr])
    put("hbA", inp["hg_lower_bounds"][0])
    put("hbB", inp["hg_lower_bounds"][1])
    for k in range(3):
        put("fcw0_%d" % k, inp["ffn_conv_w"][lyr][k])
    put("fcb0", inp["ffn_conv_b"][lyr])
    put("fnw", inp["final_norm_w"])
    pvec[:, PV["eps"]] = EPS
    pvec[:, PV["one"]] = 1.0
    pvec[:, PV["m"]] = 1.0 if role == 0 else 0.0
    pvec[:, PV["om"]] = 0.0 if role == 0 else 1.0
    pvec[:, PV["keep"]] = 1.0 if role == 0 else 0.0
    pvec[:, PV["lbm"]] = 0.0 if role == 0 else 1.0
    cst, rmask = make_consts()
    tb = make_tables(NT * T)
    tabs = np.zeros((128, 8, NS * T), np.float32)
    for st in range(NS):
        ti = st if role == 0 else st - SKEW
        if ti < 0 or ti >= NT:
            ti = 0
        tabs[:, :, st * T:(st + 1) * T] = tb[:, :, ti * T:(ti + 1) * T]
    return {
        "win": np.ascontiguousarray(win), "wout": np.ascontiguousarray(wout),
        "wup": np.ascontiguousarray(wup), "wdn": np.ascontiguousarray(wdn),
        "gatew": gw, "pvec": pvec, "tabs": tabs, "cst": cst, "rmask": rmask,
    }


_CACHE = {}


def kernel(**inputs):
    x = np.asarray(inputs["x"], np.float32)
    B, S, _ = x.shape
    NT = S // T
    NS = NT + SKEW
    roles = [prep_role(inputs, r, NT) for r in range(2)]
    if NT not in _CACHE:
        _CACHE[NT] = build(NT)
    nc, sch = _CACHE[NT]
    ncores = 2 * B
    zeros = np.zeros((D, NS * T), np.float32)
    in_maps = []
    for c in range(ncores):
        b, r = c // 2, c % 2
        m = dict(roles[r])
        if r == 0:
            xp = np.zeros((D, NS * T), np.float32)
            xp[:, :S] = x[b].T
            m["xT"] = xp
        else:
            m["xT"] = zeros
        in_maps.append(m)
    res = run_bass_kernel_spmd(nc, in_maps, core_ids=list(range(ncores)))
    out = np.empty((B, S, D), np.float32)
    for b in range(B):
        out[b] = res.results[2 * b + 1]["yT"][:, SKEW * T:].T
    return out
```
